# Optimizing a Trainium2 kernel written in Bass

```python
import math
import jax, jax.numpy as jnp
from jax import lax
import numpy as np

D_MODEL = 1024
BATCH = 2
SEQ = 8192
DEPTH = 2

CHUNK = 64
PLE_DIM = 256
N_A = max(1, DEPTH // 2)
N_B = DEPTH - N_A
SSM_WIDTH = D_MODEL
GROUP_SIZE = 16
N_GROUPS = SSM_WIDTH // GROUP_SIZE
STATE = 64
DT_MIN = 0.001
DT_MAX = 0.1
HEAD_DIM = 64
SB_WIDTH = D_MODEL
N_HEADS = SB_WIDTH // HEAD_DIM
Q_BLOCK = 128
EPS = 1e-6

kernel_name = "hybrid_s5_stickbreaking_yoco"


def rms_norm(x, g):
    xf = x.astype(jnp.float32)
    y = xf * lax.rsqrt(jnp.mean(xf * xf, axis=-1, keepdims=True) + EPS)
    return (y * g.astype(jnp.float32)).astype(x.dtype)


def _linear_recurrence_combine(left, right):
    ar1, ai1, br1, bi1 = left
    ar2, ai2, br2, bi2 = right
    return (ar2 * ar1 - ai2 * ai1,
            ar2 * ai1 + ai2 * ar1,
            ar2 * br1 - ai2 * bi1 + br2,
            ar2 * bi1 + ai2 * br1 + bi2)


def s5_scan(u, lam_re, lam_im, log_dt, b_re, b_im, c_re, c_im):
    bsz, seq, _ = u.shape
    f32 = jnp.float32
    lr = jnp.minimum(lam_re.astype(f32), -1e-4)
    li = lam_im.astype(f32)
    dt = jnp.exp(log_dt.astype(f32))[:, None]
    mag = jnp.exp(lr * dt)
    a_re = mag * jnp.cos(li * dt)
    a_im = mag * jnp.sin(li * dt)
    den = lr * lr + li * li
    nr = a_re - 1.0
    f_re = (nr * lr + a_im * li) / den
    f_im = (a_im * lr - nr * li) / den
    br = b_re.astype(f32)
    bi = b_im.astype(f32)
    bb_re = f_re[..., None] * br - f_im[..., None] * bi
    bb_im = f_re[..., None] * bi + f_im[..., None] * br
    cr = c_re.astype(f32)
    ci = c_im.astype(f32)
    n_chunks = seq // CHUNK
    uc = u.astype(f32).reshape(bsz, n_chunks, CHUNK, N_GROUPS, GROUP_SIZE).transpose(1, 0, 2, 3, 4)

    def step(carry, u_chunk):
        h_re, h_im = carry
        x_re = jnp.einsum('bcgh,gph->bcgp', u_chunk, bb_re)
        x_im = jnp.einsum('bcgh,gph->bcgp', u_chunk, bb_im)
        x_re = x_re.at[:, 0].add(a_re * h_re - a_im * h_im)
        x_im = x_im.at[:, 0].add(a_re * h_im + a_im * h_re)
        ar = jnp.broadcast_to(a_re, x_re.shape)
        ai = jnp.broadcast_to(a_im, x_im.shape)
        _, _, s_re, s_im = lax.associative_scan(_linear_recurrence_combine, (ar, ai, x_re, x_im), axis=1)
        y = jnp.einsum('bcgp,ghp->bcgh', s_re, cr) - jnp.einsum('bcgp,ghp->bcgh', s_im, ci)
        return (s_re[:, -1], s_im[:, -1]), y

    h0 = jnp.zeros((bsz, N_GROUPS, STATE), f32)
    _, ys = lax.scan(step, (h0, h0), uc)
    return ys.transpose(1, 0, 2, 3, 4).reshape(bsz, seq, N_GROUPS * GROUP_SIZE)


def s5_mixer(h, w_in, lam_re, lam_im, log_dt, b_re, b_im, c_re, c_im, d_skip, w_glu, b_glu, w_out):
    f32 = jnp.float32
    proj = h @ w_in
    u, gate = jnp.split(proj, 2, axis=-1)
    y = s5_scan(u, lam_re, lam_im, log_dt, b_re, b_im, c_re, c_im) + d_skip.astype(f32) * u.astype(f32)
    g = jax.nn.gelu(y)
    y = g * jax.nn.sigmoid(g @ w_glu.astype(f32) + b_glu.astype(f32))
    y = y.astype(h.dtype) * jax.nn.silu(gate)
    return y @ w_out


def stick_breaking_attention(q, k, v):
    f32 = jnp.float32
    seq = q.shape[2]
    scale = HEAD_DIM ** -0.5
    outs = []
    for start in range(0, seq, Q_BLOCK):
        end = start + Q_BLOCK
        qb = q[:, :, start:end].astype(f32)
        kp = k[:, :, :end].astype(f32)
        vp = v[:, :, :end].astype(f32)
        z = jnp.einsum('bhqd,bhkd->bhqk', qb, kp) * scale
        mask = jnp.arange(end)[None, :] < jnp.arange(start, end)[:, None]
        log_keep = jnp.where(mask, jax.nn.log_sigmoid(-z), 0.0)
        tail = lax.cumsum(log_keep, axis=3, reverse=True) - log_keep
        w = jnp.where(mask, jnp.exp(jax.nn.log_sigmoid(z) + tail), 0.0)
        outs.append(jnp.einsum('bhqk,bhkd->bhqd', w, vp))
    return jnp.concatenate(outs, axis=2)


def split_heads(t):
    bsz, seq, _ = t.shape
    return t.reshape(bsz, seq, N_HEADS, HEAD_DIM).transpose(0, 2, 1, 3)


def shared_kv(x, kv_norm, w_kv):
    kv = rms_norm(x, kv_norm) @ w_kv
    k, v = jnp.split(kv, 2, axis=-1)
    return split_heads(k), split_heads(v)


def stick_breaking_mixer(h, k, v, w_in, w_out):
    bsz, seq, _ = h.shape
    proj = h @ w_in
    q, gate = jnp.split(proj, 2, axis=-1)
    o = stick_breaking_attention(split_heads(q), k, v)
    o = o.transpose(0, 2, 1, 3).reshape(bsz, seq, SB_WIDTH).astype(h.dtype)
    return (o * jax.nn.silu(gate)) @ w_out


def setup_inputs(seed: int = 0) -> dict:
    key = jax.random.key(seed)
    ks = jax.random.split(key, 24)
    f32 = jnp.float32
    nrm = lambda k, shape, s: jax.random.normal(k, shape, f32) * s
    gain = lambda k, shape: 1.0 + 0.05 * jax.random.normal(k, shape, f32)
    lam_im_base = jnp.pi * jnp.arange(STATE, dtype=f32)
    return {
        'x': jax.random.normal(ks[0], (BATCH, SEQ, D_MODEL), f32),
        'p': jax.random.normal(ks[1], (DEPTH, BATCH, SEQ, PLE_DIM), f32),
        'a_norm_pre': gain(ks[2], (N_A, D_MODEL)),
        'a_norm_post': gain(ks[3], (N_A, D_MODEL)),
        'a_w_in': nrm(ks[4], (N_A, D_MODEL, 2 * SSM_WIDTH), D_MODEL ** -0.5),
        'a_lam_re': -0.5 + nrm(ks[5], (N_A, N_GROUPS, STATE), 0.01),
        'a_lam_im': lam_im_base + nrm(ks[6], (N_A, N_GROUPS, STATE), 0.01),
        'a_log_dt': jax.random.uniform(ks[7], (N_A, N_GROUPS), f32, math.log(DT_MIN), math.log(DT_MAX)),
        'a_b_re': nrm(ks[8], (N_A, N_GROUPS, STATE, GROUP_SIZE), (2.0 * GROUP_SIZE) ** -0.5),
        'a_b_im': nrm(ks[9], (N_A, N_GROUPS, STATE, GROUP_SIZE), (2.0 * GROUP_SIZE) ** -0.5),
        'a_c_re': nrm(ks[10], (N_A, N_GROUPS, GROUP_SIZE, STATE), (2.0 * STATE) ** -0.5),
        'a_c_im': nrm(ks[11], (N_A, N_GROUPS, GROUP_SIZE, STATE), (2.0 * STATE) ** -0.5),
        'a_d_skip': nrm(ks[12], (N_A, SSM_WIDTH), 1.0),
        'a_w_glu': nrm(ks[13], (N_A, SSM_WIDTH, SSM_WIDTH), SSM_WIDTH ** -0.5),
        'a_b_glu': nrm(ks[14], (N_A, SSM_WIDTH), 0.01),
        'a_w_out': nrm(ks[15], (N_A, SSM_WIDTH, D_MODEL), SSM_WIDTH ** -0.5),
        'kv_norm': gain(ks[16], (D_MODEL,)),
        'w_kv': nrm(ks[17], (D_MODEL, 2 * SB_WIDTH), D_MODEL ** -0.5),
        'b_norm_pre': gain(ks[18], (N_B, D_MODEL)),
        'b_norm_post': gain(ks[19], (N_B, D_MODEL)),
        'b_w_in': nrm(ks[20], (N_B, D_MODEL, 2 * SB_WIDTH), D_MODEL ** -0.5),
        'b_w_out': nrm(ks[21], (N_B, SB_WIDTH, D_MODEL), SB_WIDTH ** -0.5),
        'ple_w_proj': nrm(ks[22], (DEPTH, PLE_DIM, D_MODEL), PLE_DIM ** -0.5),
        'ple_w_gate': nrm(ks[23], (DEPTH, D_MODEL, D_MODEL), D_MODEL ** -0.5),
    }


def reference(x, p, a_norm_pre, a_norm_post, a_w_in, a_lam_re, a_lam_im, a_log_dt, a_b_re, a_b_im,
              a_c_re, a_c_im, a_d_skip, a_w_glu, a_b_glu, a_w_out, kv_norm, w_kv,
              b_norm_pre, b_norm_post, b_w_in, b_w_out, ple_w_proj, ple_w_gate):
    k = None
    v = None
    for i in range(DEPTH):
        if i < N_A:
            j = i
            h = rms_norm(x, a_norm_pre[j])
            y = s5_mixer(h, a_w_in[j], a_lam_re[j], a_lam_im[j], a_log_dt[j], a_b_re[j], a_b_im[j],
                         a_c_re[j], a_c_im[j], a_d_skip[j], a_w_glu[j], a_b_glu[j], a_w_out[j])
            x = x + rms_norm(y, a_norm_post[j])
        else:
            j = i - N_A
            h = rms_norm(x, b_norm_pre[j])
            y = stick_breaking_mixer(h, k, v, b_w_in[j], b_w_out[j])
            x = x + rms_norm(y, b_norm_post[j])
        x = x + jax.nn.sigmoid(x @ ple_w_gate[i]) * (p[i] @ ple_w_proj[i])
        if i == N_A - 1:
            k, v = shared_kv(x, kv_norm, w_kv)
    return x
```

```python
from contextlib import ExitStack
import numpy as np
import concourse.bass as bass
import concourse.mybir as mybir
from concourse.bass_utils import run_bass_kernel_spmd

F32 = mybir.dt.float32
BF16 = mybir.dt.bfloat16
AF = mybir.ActivationFunctionType
ALU = mybir.AluOpType

ENGS = ("pe", "act", "dve", "pool", "sp")
NCORE = 8
D = 1024
SEQ = 8192
NT = 512
EPS = 1e-6
PI = float(np.pi)


class Res:
    __slots__ = ("name", "w", "r", "dsem", "dcnt", "t")

    def __init__(self, name, t=None):
        self.name = name
        self.w = {}
        self.r = {}
        self.dsem = None
        self.dcnt = 0
        self.t = t

    def __getitem__(self, idx):
        return self.t[idx]


class Prog:
    _n = 0
    G = None

    def __init__(self, nc):
        Prog._n += 1
        self.pfx = "f%d_" % Prog._n
        self.nc = nc
        if Prog.G is None or Prog.G["nc"] is not nc:
            ges = ExitStack()
            Prog.G = {"nc": nc, "es": ges, "sems": {}, "cnt": {}}
            for e in ENGS:
                Prog.G["sems"][e] = ges.enter_context(nc.semaphore("s_" + e))
                Prog.G["cnt"][e] = 0
        G = Prog.G
        self.es = ExitStack()
        self.lists = {e: [] for e in ENGS}
        self.sems = G["sems"]
        self.cnt = G["cnt"]
        self.seen = {e: dict(self.cnt) for e in ENGS}
        self.nd = 0
        self.banks = []
        self.bi = 0
        self.touched = {}

    @staticmethod
    def finish():
        if Prog.G is not None:
            Prog.G["es"].close()
            Prog.G = None

    def _newsem(self):
        key = "d%d" % self.nd
        self.nd += 1
        if key not in self.sems:
            self.sems[key] = Prog.G["es"].enter_context(self.nc.semaphore("sd_" + key))
            self.cnt[key] = 0
        return key

    def sb(self, name, shape, dt):
        return Res(name, self.es.enter_context(self.nc.sbuf_tensor(self.pfx + "sb_" + name, list(shape), dt)))

    def ps(self, name, shape, dt=F32):
        return Res(name, self.es.enter_context(self.nc.psum_tensor(self.pfx + "ps_" + name, list(shape), dt)))

    def dram(self, name, shape, dt, kind="Internal"):
        return Res(name, self.nc.dram_tensor(name, list(shape), dt, kind=kind).ap())

    def mk_banks(self, n):
        self.banks = [self.ps("bank%d" % i, [128, NT], F32) for i in range(n)]

    def bank(self):
        b = self.banks[self.bi % len(self.banks)]
        self.bi += 1
        return b

    def _dsem(self, res):
        if res.dsem is None:
            res.dsem = self._newsem()
        return res.dsem

    def _waits(self, eng, reads, writes):
        for x_ in reads:
            self.touched[id(x_)] = x_
        for x_ in writes:
            self.touched[id(x_)] = x_
        deps = {}
        for r in reads:
            for k, v in r.w.items():
                if v > deps.get(k, 0):
                    deps[k] = v
        for w in writes:
            for k, v in w.w.items():
                if k != eng and v > deps.get(k, 0):
                    deps[k] = v
            for k, v in w.r.items():
                if k != eng and v > deps.get(k, 0):
                    deps[k] = v
        seen = self.seen[eng]
        for k, v in deps.items():
            if v > seen.get(k, 0):
                seen[k] = v
                sem = self.sems[k]
                self.lists[eng].append(lambda E, sem=sem, v=v: E.wait_ge(sem, v))

    def op(self, eng, fn, reads=(), writes=()):
        self._waits(eng, reads, writes)
        self.cnt[eng] += 1
        n = self.cnt[eng]
        sem = self.sems[eng]
        self.lists[eng].append(lambda E, fn=fn, sem=sem: fn(E).then_inc(sem, 1))
        for r in reads:
            r.r[eng] = n
        for w in writes:
            w.w[eng] = n

    def dma(self, eng, out_ap, in_ap, reads=(), writes=()):
        wres = writes[0]
        self._waits(eng, reads, writes)
        key = self._dsem(wres)
        self.cnt[key] += 16
        v = self.cnt[key]
        sem = self.sems[key]
        self.lists[eng].append(
            lambda E, o=out_ap, i=in_ap, sem=sem: E.dma_start(out=o, in_=i).then_inc(sem, 16))
        for r in reads:
            r.r[key] = v
        wres.w[key] = v

    def coll(self, kind, src, dst, groups):
        self._waits("pool", [src], [dst])
        key = self._newsem()
        self.cnt[key] += 1
        v = self.cnt[key]
        sem = self.sems[key]
        self.lists["pool"].append(lambda E, sem=sem: E.collective_compute(
            kind, ALU.bypass, replica_groups=groups, ins=[src.t.opt()], outs=[dst.t.opt()]).then_inc(sem))
        src.r[key] = v
        dst.w[key] = v

    def wait_all(self, eng, ress):
        self._waits(eng, ress, ())

    def emit(self):
        L = self.lists
        for e in ENGS:
            for k, sem in self.sems.items():
                tgt = self.cnt[k]
                if k != e and tgt > self.seen[e].get(k, 0):
                    self.seen[e][k] = tgt
                    L[e].append(lambda E, sem=sem, tgt=tgt: E.wait_ge(sem, tgt))
        with self.nc.Block() as block:
            @block.tensor
            def _(E):
                for f in L["pe"]:
                    f(E)

            @block.scalar
            def _(E):
                for f in L["act"]:
                    f(E)

            @block.vector
            def _(E):
                for f in L["dve"]:
                    f(E)

            @block.gpsimd
            def _(E):
                for f in L["pool"]:
                    f(E)

            @block.sync
            def _(E):
                for f in L["sp"]:
                    f(E)
        self.es.close()
        for x_ in self.touched.values():
            x_.w = {}
            x_.r = {}
            x_.dsem = None


def tt(P, eng, out, in0, in1, op, reads, writes):
    P.op(eng, lambda E: E.tensor_tensor(out=out, in0=in0, in1=in1, op=op), reads=reads, writes=writes)


def ts(P, eng, out, in0, s1, s2, op0, op1, reads, writes):
    if s2 is None:
        P.op(eng, lambda E: E.tensor_scalar(out=out, in0=in0, scalar1=s1, scalar2=None, op0=op0), reads=reads, writes=writes)
    else:
        P.op(eng, lambda E: E.tensor_scalar(out=out, in0=in0, scalar1=s1, scalar2=s2, op0=op0, op1=op1), reads=reads, writes=writes)


def act(P, out, in_, func, reads, writes, scale=1.0, bias=None):
    if bias is None:
        P.op("act", lambda E: E.activation(out=out, in_=in_, func=func, scale=scale), reads=reads, writes=writes)
    else:
        P.op("act", lambda E: E.activation(out=out, in_=in_, func=func, scale=scale, bias=bias), reads=reads, writes=writes)


def scale_cast(P, i, out, in_, col, reads, writes):
    if i % 2:
        P.op("act", lambda E: E.activation(out=out, in_=in_, func=AF.Copy, scale=col), reads=reads, writes=writes)
    else:
        ts(P, "dve", out, in_, col, None, ALU.mult, None, reads, writes)


def cp(P, eng, out, in_, reads, writes):
    if eng == "act":
        P.op(eng, lambda E: E.activation(out=out, in_=in_, func=AF.Copy), reads=reads, writes=writes)
    else:
        P.op(eng, lambda E: E.tensor_copy(out=out, in_=in_), reads=reads, writes=writes)


class Rot:
    def __init__(self, P, name, n, shape, dt):
        self.bufs = [P.sb("%s%d" % (name, i), shape, dt) for i in range(n)]
        self.i = 0

    def get(self):
        b = self.bufs[self.i % len(self.bufs)]
        self.i += 1
        return b


class Dense:
    def __init__(self, P, T):
        self.P = P
        self.T = T
        self.wst = Rot(P, "wst", 2, [128, 8, 128], F32)
        self.wbf = Rot(P, "wbf", 2, [128, 8, 128], BF16)
        self.ost = Rot(P, "ost", 3, [128, NT], F32)
        self.ist = Rot(P, "ist", 4, [128, NT], F32)
        self.tmp = Rot(P, "tmp", 4, [128, NT], F32)
        self.ones = P.sb("ones", [128, 128], BF16)
        P.op("pool", lambda E: E.memset(self.ones[:], 1.0), writes=[self.ones])
        self.sq = P.sb("sq", [128, 8, T], BF16)

    def load_w(self, W, m, KC, rowscale=None, wb=None):
        P = self.P
        assert rowscale is None
        st = self.wst.get()
        if wb is None:
            wb = self.wbf.get()
        P.dma("sp", st[:, 0:KC, :], W.t[:, m * 128:(m + 1) * 128].rearrange("(kc p) m -> p kc m", p=128), reads=[W], writes=[st])
        self.wi = getattr(self, "wi", 0) + 1
        cp(P, "act" if self.wi % 2 else "pool", wb[:, 0:KC, :], st[:, 0:KC, :], [st], [wb])
        return wb

    def prep_w(self, name, W, KC, MC):
        wbs = []
        for m in range(MC):
            wb = self.P.sb("%s_w%d" % (name, m), [128, KC, 128], BF16)
            wbs.append(self.load_w(W, m, KC, wb=wb))
        return wbs

    def linear(self, W, KC, MC, a, epi, rowscale=None, wbs=None):
        P = self.P
        for m in range(MC):
            wb = wbs[m] if wbs is not None else self.load_w(W, m, KC, rowscale)
            for n in range(self.T // NT):
                ps = P.bank()
                for kc in range(KC):
                    P.op("pe", lambda E, ps=ps, wb=wb, kc=kc, n=n: E.matmul(
                        ps[:, :], lhsT=wb[:, kc, :], rhs=a[:, kc, n * NT:(n + 1) * NT], start=(kc == 0), stop=(kc == KC - 1)),
                        reads=[wb, a], writes=[ps])
                epi(m, n, ps)

    def rstd(self, src, out):
        P = self.P
        sq = self.sq
        for kc in range(8):
            act(P, sq[:, kc, :], src[:, kc, :], AF.Square, [src], [sq])
        for n in range(self.T // NT):
            ps = P.bank()
            for kc in range(8):
                P.op("pe", lambda E, ps=ps, kc=kc, n=n: E.matmul(
                    ps[:, :], lhsT=self.ones[:, :], rhs=sq[:, kc, n * NT:(n + 1) * NT], start=(kc == 0), stop=(kc == 7)),
                    reads=[self.ones, sq], writes=[ps])
            t = self.tmp.get()
            act(P, t[:, :], ps[:, :], AF.Ln, [ps], [t], scale=1.0 / D, bias=EPS)
            act(P, out[:, n * NT:(n + 1) * NT], t[:, :], AF.Exp, [t], [out], scale=-0.5)

    def store(self, dst, m, n, t0, src_res, src_ap):
        self.P.dma("act", dst.t[m * 128:(m + 1) * 128, t0 + n * NT:t0 + (n + 1) * NT], src_ap, reads=[src_res], writes=[dst])

    def load_act(self, src, t0, dst, KC=8):
        for kc in range(KC):
            self.P.dma("sp", dst[:, kc, :], src.t[kc * 128:(kc + 1) * 128, t0:t0 + self.T], reads=[src], writes=[dst])


def colvec(P, name, dram_res, ncol):
    t = P.sb(name, [128, ncol], F32)
    P.dma("sp", t[:, :], dram_res.t[:, :], reads=[dram_res], writes=[t])
    return t


def _run(nc, in_maps):
    res = run_bass_kernel_spmd(nc, in_maps, core_ids=list(range(NCORE)))
    return res.results


def _cols(v):
    return np.ascontiguousarray(np.asarray(v, np.float32).reshape(-1, 128).T)


TS = 256
NG = 16
M_MAGIC = 12582912.0


def range_reduce(P, eng, out, in_, tmp, reads, writes, shift=0.0):
    res_in = reads
    if shift != 0.0:
        ts(P, eng, out, in_, shift, None, ALU.add, None, res_in, writes)
        in_ = out
        res_in = writes
    ts(P, eng, tmp[0], in_, 1.0 / (2 * PI), M_MAGIC, ALU.mult, ALU.add, res_in, [tmp[1]])
    ts(P, eng, tmp[0], tmp[0], -M_MAGIC, -2 * PI, ALU.add, ALU.mult, [tmp[1]], [tmp[1]])
    tt(P, eng, out, tmp[0], in_, ALU.add, [tmp[1]] + list(res_in), writes)
    ts(P, eng, out, out, 3.14159, -3.14159, ALU.min, ALU.max, writes, writes)


NAMES_T = ["lamre_T", "lamim_T", "logdt_T", "bre_T", "bim_T"]
NAMES_S = ["lamre_S", "lamim_S", "logdt_S"]


def phase_B(nc, dr):
    P = Prog(nc)
    uT = dr["uT_cs"]
    names_T = NAMES_T
    dT = {n: dr[n] for n in names_T}
    names_S = NAMES_S
    dS = {n: dr[n] for n in names_S}
    c1_d, c2_d, iota_d, gmask_d, sgn_d, J_d = dr["c1_S"], dr["c2_S"], dr["iota"], dr["gmask"], dr["sgn"], dr["Jm"]
    dsk = colvec(P, "dskip_cs", dr["dskip_cs"], 2)
    P.mk_banks(4)
    ybank = [P.ps("ybank%d" % i, [128, NT], F32) for i in range(2)]
    igb = P.ps("igb", [128, NT], F32)

    def ld(name, d, ncol):
        return colvec(P, name, d, ncol)

    lt = {n: ld("s_" + n, dT[n], 128) for n in names_T}
    cnt = [0]

    def newT(nm, ncol=128):
        cnt[0] += 1
        return P.sb("%s_%d" % (nm, cnt[0]), [128, ncol], F32)

    def derive(lamre, lamim, logdt, ncol, pfx):
        o = {}
        lr = newT(pfx + "lr", ncol)
        ts(P, "dve", lr[:, :], lamre[:, :], -1e-4, None, ALU.min, None, [lamre], [lr])
        dt = newT(pfx + "dt", ncol)
        act(P, dt[:, :], logdt[:, :], AF.Exp, [logdt], [dt])
        e = newT(pfx + "e", ncol)
        tt(P, "dve", e[:, :], lr[:, :], dt[:, :], ALU.mult, [lr, dt], [e])
        th = newT(pfx + "th", ncol)
        tt(P, "dve", th[:, :], lamim[:, :], dt[:, :], ALU.mult, [lamim, dt], [th])
        thr = newT(pfx + "thr", ncol)
        tmp = newT(pfx + "tmp", ncol)
        range_reduce(P, "dve", thr[:, :], th[:, :], (tmp[:, :], tmp), [th], [thr])
        thc = newT(pfx + "thc", ncol)
        range_reduce(P, "dve", thc[:, :], th[:, :], (tmp[:, :], tmp), [th], [thc], shift=PI / 2)
        mag = newT(pfx + "mag", ncol)
        act(P, mag[:, :], e[:, :], AF.Exp, [e], [mag])
        sn = newT(pfx + "sin", ncol)
        act(P, sn[:, :], thr[:, :], AF.Sin, [thr], [sn])
        cs = newT(pfx + "cos", ncol)
        act(P, cs[:, :], thc[:, :], AF.Sin, [thc], [cs])
        o.update(lr=lr, li=lamim, e=e, thr=thr, mag=mag, sin=sn, cos=cs)
        return o

    dl = derive(lt["lamre_T"], lt["lamim_T"], lt["logdt_T"], 128, "T")

    def mul(a, b, nm):
        t = newT(nm)
        tt(P, "dve", t[:, :], a[:, :], b[:, :], ALU.mult, [a, b], [t])
        return t

    def addsub(a, b, op, nm):
        t = newT(nm)
        tt(P, "dve", t[:, :], a[:, :], b[:, :], op, [a, b], [t])
        return t

    are = mul(dl["mag"], dl["cos"], "are")
    aim = mul(dl["mag"], dl["sin"], "aim")
    den = addsub(mul(dl["lr"], dl["lr"], "lr2"), mul(dl["li"], dl["li"], "li2"), ALU.add, "den")
    rden = newT("rden")
    P.op("dve", lambda E: E.reciprocal(out=rden[:, :], in_=den[:, :]), reads=[den], writes=[rden])
    nr = newT("nr")
    ts(P, "dve", nr[:, :], are[:, :], -1.0, None, ALU.add, None, [are], [nr])
    fre = mul(addsub(mul(nr, dl["lr"], "f1"), mul(aim, dl["li"], "f2"), ALU.add, "f3"), rden, "fre")
    fim = mul(addsub(mul(aim, dl["lr"], "f4"), mul(nr, dl["li"], "f5"), ALU.subtract, "f6"), rden, "fim")
    bbre = addsub(mul(fre, lt["bre_T"], "b1"), mul(fim, lt["bim_T"], "b2"), ALU.subtract, "bbre")
    bbim = addsub(mul(fre, lt["bim_T"], "b3"), mul(fim, lt["bre_T"], "b4"), ALU.add, "bbim")
    nbbim = newT("nbbim")
    ts(P, "dve", nbbim[:, :], bbim[:, :], -1.0, None, ALU.mult, None, [bbim], [nbbim])
    gmask = ld("gmask", gmask_d, 8)
    W1 = P.sb("W1pad", [128, NG, 128], BF16)
    W2 = P.sb("W2pad", [128, NG, 128], BF16)
    for jj in range(2):
        for gl in range(8):
            gg = jj * 8 + gl
            ms = gmask[:, gl:gl + 1]
            sl = slice(jj * 64, (jj + 1) * 64)
            ts(P, "dve", W1[:, gg, 0:64], bbre[:, sl], ms, None, ALU.mult, None, [bbre, gmask], [W1])
            ts(P, "dve", W1[:, gg, 64:128], bbim[:, sl], ms, None, ALU.mult, None, [bbim, gmask], [W1])
            ts(P, "dve", W2[:, gg, 0:64], nbbim[:, sl], ms, None, ALU.mult, None, [nbbim, gmask], [W2])
            ts(P, "dve", W2[:, gg, 64:128], bbre[:, sl], ms, None, ALU.mult, None, [bbre, gmask], [W2])

    ls_ = {n: ld("s_" + n, dS[n], NG) for n in names_S}
    ds = derive(ls_["lamre_S"], ls_["lamim_S"], ls_["logdt_S"], NG, "S")
    rho = ds["mag"]
    iota = ld("iota", iota_d, TS + 1)
    COS = P.sb("COS", [128, NG, TS + 1], F32)
    SIN = P.sb("SIN", [128, NG, TS + 1], F32)
    ARG = P.sb("ARG", [128, NG, TS + 1], F32)
    TMP = P.sb("TMPA", [128, NG, TS + 1], F32)
    Gg = [P.sb("Gg%d" % i, [128, TS], F32) for i in range(NG)]
    GT = [P.sb("GT%d" % i, [128, NG], F32) for i in range(2)]
    for gg in range(NG):
        ts(P, "dve", ARG[:, gg, :], iota[:, :], ds["thr"][:, gg:gg + 1], None, ALU.mult, None, [iota, ds["thr"]], [ARG])
    range_reduce(P, "dve", SIN[:, :, :], ARG[:, :, :], (TMP[:, :, :], TMP), [ARG], [SIN])
    range_reduce(P, "dve", COS[:, :, :], ARG[:, :, :], (TMP[:, :, :], TMP), [ARG], [COS], shift=PI / 2)
    act(P, SIN[:, :, :], SIN[:, :, :], AF.Sin, [SIN], [SIN])
    act(P, COS[:, :, :], COS[:, :, :], AF.Sin, [COS], [COS])
    c1 = ld("c1", c1_d, NG * 16)
    c2 = ld("c2", c2_d, NG * 16)
    sgn = ld("sgn", sgn_d, 1)
    L1 = P.sb("L1pad", [128, NG, 128], BF16)
    L2 = P.sb("L2pad", [128, NG, 128], BF16)
    P.op("pool", lambda E: E.memset(L1[:, :, :], 0.0), writes=[L1])
    P.op("pool", lambda E: E.memset(L2[:, :, :], 0.0), writes=[L2])
    for gg in range(NG):
        gl = gg % 8
        ts(P, "dve", L1[:, gg, gl * 16:(gl + 1) * 16], c1[:, gg * 16:(gg + 1) * 16], sgn[:, 0:1], None, ALU.mult, None, [c1, sgn], [L1])
        ts(P, "dve", L2[:, gg, gl * 16:(gl + 1) * 16], c2[:, gg * 16:(gg + 1) * 16], -1.0, None, ALU.mult, None, [c2], [L2])
    Jm = P.sb("Jm", [128, 128], F32)
    P.dma("sp", Jm[:, :], J_d.t[:, :], reads=[J_d], writes=[Jm])

    ust = Rot(P, "ust", 3, [128, 2, TS], F32)
    ubf = Rot(P, "ubf", 3, [128, 2, TS], BF16)
    t1r = Rot(P, "t1r", 4, [128, TS], F32)
    t2r = Rot(P, "t2r", 4, [128, TS], F32)
    xtr = Rot(P, "xtr", 4, [128, TS], F32)
    h1r = Rot(P, "h1r", 4, [128, TS], BF16)
    h2r = Rot(P, "h2r", 4, [128, TS], BF16)
    S0 = [P.sb("S0_%d" % i, [128, NG], F32) for i in range(2)]
    ys = Rot(P, "ys", 3, [128, 2, TS], F32)
    P.op("pool", lambda E: E.memset(S0[0][:, :], 0.0), writes=[S0[0]])
    nch = SEQ // TS
    units = []
    chunk_res = {}
    for ch in range(nch):
        for jj in range(2):
            for gl in range(8):
                units.append(dict(ch=ch, jj=jj, gl=gl, gg=jj * 8 + gl))
    nu = len(units)

    def chunk_setup(ch):
        c0 = ch * TS
        us = ust.get()
        ub = ubf.get()
        P.dma("sp", us[:, :, :], uT.t[:, c0:c0 + TS].rearrange("(j p) t -> p j t", p=128), reads=[uT], writes=[us])
        cp(P, "act", ub[:, :, :], us[:, :, :], [us], [ub])
        chunk_res[ch] = dict(us=us, ub=ub, yo=ys.get())

    def s1(i):
        U = units[i]
        ch, jj, gg = U["ch"], U["jj"], U["gg"]
        if jj == 0 and U["gl"] == 0:
            chunk_setup(ch)
        ub = chunk_res[ch]["ub"]
        p1 = P.bank()
        p2 = P.bank()
        P.op("pe", lambda E, p1=p1, gg=gg, ub=ub, jj=jj: E.matmul(p1[:, 0:TS], lhsT=W1[:, gg, :], rhs=ub[:, jj, :], start=True, stop=True), reads=[W1, ub], writes=[p1])
        P.op("pe", lambda E, p2=p2, gg=gg, ub=ub, jj=jj: E.matmul(p2[:, 0:TS], lhsT=W2[:, gg, :], rhs=ub[:, jj, :], start=True, stop=True), reads=[W2, ub], writes=[p2])
        t1 = t1r.get()
        t2 = t2r.get()
        xt = xtr.get()
        tt(P, "dve", t1[:, :], p1[:, 0:TS], COS[:, gg, 1:TS + 1], ALU.mult, [p1, COS], [t1])
        tt(P, "dve", t2[:, :], p2[:, 0:TS], SIN[:, gg, 1:TS + 1], ALU.mult, [p2, SIN], [t2])
        tt(P, "pool", xt[:, :], t1[:, :], t2[:, :], ALU.subtract, [t1, t2], [xt])
        U["xt"] = xt

    def s2(i):
        U = units[i]
        ch, gg = U["ch"], U["gg"]
        gt, s0 = GT[ch % 2], S0[ch % 2]
        xt = U["xt"]
        G = Gg[gg]
        P.op("dve", lambda E, G=G, gg=gg, xt=xt, s0=s0: E.tensor_tensor_scan(
            out=G[:, :], data0=rho[:, gg:gg + 1].to_broadcast([128, TS]), data1=xt[:, :],
            initial=s0[:, gg:gg + 1], op0=ALU.mult, op1=ALU.add), reads=[rho, xt, s0], writes=[G])
        if ch + 1 < nch:
            cp(P, "act", gt[:, gg:gg + 1], G[:, TS - 1:TS], [G], [gt])
        h1 = h1r.get()
        h2 = h2r.get()
        tt(P, "dve", h1[:, :], G[:, :], COS[:, gg, 1:TS + 1], ALU.mult, [G, COS], [h1])
        tt(P, "pool", h2[:, :], G[:, :], SIN[:, gg, 1:TS + 1], ALU.mult, [G, SIN], [h2])
        U["h1"], U["h2"] = h1, h2
        if gg == NG - 1 and ch + 1 < nch:
            s1_ = S0[(ch + 1) % 2]
            P.op("pe", lambda E, gt=gt: E.matmul(igb[:, 0:NG], lhsT=Jm[:, :], rhs=gt[:, :], start=True, stop=True), reads=[Jm, gt], writes=[igb])
            ta = P.sb("bta%d" % ch, [128, NG], F32)
            tb = P.sb("btb%d" % ch, [128, NG], F32)
            tt(P, "dve", ta[:, :], gt[:, :], COS[:, :, TS], ALU.mult, [gt, COS], [ta])
            tt(P, "dve", tb[:, :], igb[:, 0:NG], SIN[:, :, TS], ALU.mult, [igb, SIN], [tb])
            tt(P, "dve", s1_[:, :], ta[:, :], tb[:, :], ALU.add, [ta, tb], [s1_])

    def s3(i):
        U = units[i]
        ch, jj, gl, gg = U["ch"], U["jj"], U["gl"], U["gg"]
        yb = ybank[jj]
        h1, h2 = U["h1"], U["h2"]
        P.op("pe", lambda E, yb=yb, gg=gg, h1=h1, gl=gl: E.matmul(yb[:, 0:TS], lhsT=L1[:, gg, :], rhs=h1[:, :], start=(gl == 0), stop=False), reads=[L1, h1], writes=[yb])
        P.op("pe", lambda E, yb=yb, gg=gg, h2=h2, gl=gl: E.matmul(yb[:, 0:TS], lhsT=L2[:, gg, :], rhs=h2[:, :], start=False, stop=(gl == 7)), reads=[L2, h2], writes=[yb])
        if gl == 7:
            cr = chunk_res[ch]
            yo, us = cr["yo"], cr["us"]
            P.op("dve", lambda E, yo=yo, us=us, yb=yb, jj=jj: E.scalar_tensor_tensor(
                out=yo[:, jj, :], in0=us[:, jj, :], scalar=dsk[:, jj:jj + 1], in1=yb[:, 0:TS], op0=ALU.mult, op1=ALU.add),
                reads=[us, dsk, yb], writes=[yo])
            if jj == 1:
                c0 = ch * TS
                ysrc = dr["y_src"][c0 // 1024]
                P.dma("act", ysrc.t[:, c0 % 1024:c0 % 1024 + TS].rearrange("(j p) t -> p j t", p=128), yo[:, :, :], reads=[yo], writes=[ysrc])

    for j in range(nu + 2):
        if j < nu:
            s1(j)
        if 0 <= j - 1 < nu:
            s2(j - 1)
        if 0 <= j - 2 < nu:
            s3(j - 2)
    for k in range(8):
        P.coll("AllGather", dr["y_src"][k], dr["y_all"][k], GROUPS)
    P.wait_all("act", dr["y_all"])
    P.emit()


def gelu_tanh(P, dn, out_bf, y, reads):
    s = dn.tmp.get()
    s2 = dn.tmp.get()
    act(P, s[:, :], y, AF.Square, reads, [s])
    ts(P, "dve", s[:, :], s[:, :], 0.044715, 1.0, ALU.mult, ALU.add, [s], [s])
    tt(P, "dve", s2[:, :], s[:, :], y, ALU.mult, [s] + list(reads), [s2])
    act(P, s2[:, :], s2[:, :], AF.Sigmoid, [s2], [s2], scale=1.5957691216057308)
    return s2


def ple(P, dn, Wpg, Wpp, Xb, Pb, X1):
    for m in range(8):
        wg = dn.load_w(Wpg, m, 8)
        wp = dn.load_w(Wpp, m, 2)
        for n in range(dn.T // NT):
            pg = P.bank()
            pp = P.bank()
            for kc in range(8):
                P.op("pe", lambda E, pg=pg, wg=wg, kc=kc, n=n: E.matmul(
                    pg[:, :], lhsT=wg[:, kc, :], rhs=Xb[:, kc, n * NT:(n + 1) * NT], start=(kc == 0), stop=(kc == 7)),
                    reads=[wg, Xb], writes=[pg])
            for kc in range(2):
                P.op("pe", lambda E, pp=pp, wp=wp, kc=kc, n=n: E.matmul(
                    pp[:, :], lhsT=wp[:, kc, :], rhs=Pb[:, kc, n * NT:(n + 1) * NT], start=(kc == 0), stop=(kc == 1)),
                    reads=[wp, Pb], writes=[pp])
            sg = dn.tmp.get()
            act(P, sg[:, :], pg[:, :], AF.Sigmoid, [pg], [sg])
            t = dn.tmp.get()
            tt(P, "dve", t[:, :], sg[:, :], pp[:, :], ALU.mult, [sg, pp], [t])
            tt(P, "pool", X1[:, m, n * NT:(n + 1) * NT], X1[:, m, n * NT:(n + 1) * NT], t[:, :], ALU.add, [X1, t], [X1])


NH = 4
NQT = SEQ // NT


def phase_D(nc, dr):
    P = Prog(nc)
    qT, kT, vT, tri_d = dr["qT_cs"], dr["kT_cs"], dr["vT_cs"], dr["ntri"]
    ident = P.sb("ident", [128, 128], BF16)
    P.op("pool", lambda E: E.memset(ident[:, :], 0.0), writes=[ident])
    P.op("pool", lambda E: E.affine_select(out=ident[:, :], in_=ident[:, :], pattern=[[-1, 128]], compare_op=ALU.not_equal,
                                           fill=1.0, base=0, channel_multiplier=1), reads=[ident], writes=[ident])
    vtb = Rot(P, "vtb", 2, [64, 2048], BF16)
    tpb = P.ps("tpb", [128, NT], BF16)
    zb = [P.ps("zb%d" % i, [128, NT], F32) for i in range(4)]
    ob = [P.ps("ob%d" % i, [64, NT], F32) for i in range(2)]
    st = P.sb("tri_st", [128, 128], F32)
    P.dma("sp", st[:, :], tri_d.t[:, :], reads=[tri_d], writes=[st])
    ntri = P.sb("ntri", [128, 128], BF16)
    cp(P, "dve", ntri[:, :], st[:, :], [st], [ntri])
    nones = P.sb("nones", [128, 128], BF16)
    P.op("pool", lambda E: E.memset(nones[:, :], -1.0), writes=[nones])
    Qb = P.sb("Qb", [128, 2, SEQ], BF16)
    Kb = P.sb("Kb", [128, 2, SEQ], BF16)
    Vb = [P.sb("Vb%d" % h, [128, 64 * 64], BF16) for h in range(NH)]
    ldst = Rot(P, "ldst", 2, [128, 2048], F32)
    er = Rot(P, "er", 3, [128, NT], F32)
    spr = Rot(P, "spr", 4, [128, NT], BF16)
    wr = Rot(P, "wr", 4, [128, NT], BF16)
    racc = Rot(P, "racc", 3, [128, NT], BF16)
    ost = Rot(P, "ost", 2, [64, NT], BF16)
    for pr in range(2):
        for c in range(SEQ // 2048):
            s = ldst.get()
            P.dma("sp", s[:, :], qT.t[pr * 128:(pr + 1) * 128, c * 2048:(c + 1) * 2048], reads=[qT], writes=[s])
            cp(P, "pool", Qb[:, pr, c * 2048:(c + 1) * 2048], s[:, :], [s], [Qb])
            s = ldst.get()
            P.dma("sp", s[:, :], kT.t[pr * 128:(pr + 1) * 128, c * 2048:(c + 1) * 2048], reads=[kT], writes=[s])
            ts(P, "dve", Kb[:, pr, c * 2048:(c + 1) * 2048], s[:, :], 0.125, None, ALU.mult, None, [s], [Kb])
    for h in range(NH):
        for c in range(SEQ // 2048):
            s = ldst.get()
            P.dma("sp", s[0:64, :], vT.t[h * 64:(h + 1) * 64, c * 2048:(c + 1) * 2048], reads=[vT], writes=[s])
            vb_ = vtb.get()
            cp(P, "pool", vb_[:, :], s[0:64, :], [s], [vb_])
            for k8 in range(2):
                for j in range(8):
                    blk = k8 * 8 + j
                    P.op("pe", lambda E, vb_=vb_, j=j, blk=blk: E.transpose(
                        out=tpb[:, j * 64:(j + 1) * 64], in_=vb_[:, blk * 128:(blk + 1) * 128], identity=ident[0:64, 0:64]),
                        reads=[vb_, ident], writes=[tpb])
                kb0 = c * 16 + k8 * 8
                cp(P, "dve", Vb[h][:, kb0 * 64:(kb0 + 8) * 64], tpb[:, :], [tpb], [Vb[h]])
    blocks = []
    for h in range(NH):
        for qt in range(NQT):
            kbs = list(range(4 * qt + 3, -1, -1))
            for idx, kb in enumerate(kbs):
                blocks.append(dict(h=h, qt=qt, kb=kb, idx=idx, n=len(kbs), g=h * NQT + qt))
    nb = len(blocks)

    def operands(B):
        hp, pr = B["h"] % 2, B["h"] // 2
        ksl = Kb[hp * 64:(hp + 1) * 64, pr, B["kb"] * 128:(B["kb"] + 1) * 128]
        qsl = Qb[hp * 64:(hp + 1) * 64, pr, B["qt"] * NT:(B["qt"] + 1) * NT]
        return ksl, qsl

    def mask(t, B):
        base = B["qt"] * NT - 128 * B["kb"]
        P.op("pool", lambda E, t=t, base=base: E.affine_select(
            out=t[:, :], in_=t[:, :], pattern=[[1, NT]], compare_op=ALU.is_gt, fill=0.0, base=base, channel_multiplier=-1),
            reads=[t], writes=[t])

    def stage1a(i):
        B = blocks[i]
        ksl, qsl = operands(B)
        z = zb[i % 4]
        P.op("pe", lambda E, z=z, ksl=ksl, qsl=qsl: E.matmul(z[:, :], lhsT=ksl, rhs=qsl, start=True, stop=False), reads=[Kb, Qb], writes=[z])
        e = er.get()
        act(P, e[:, :], z[:, :], AF.Exp, [z], [e])
        B["e"] = e

    def stage1b(i):
        B = blocks[i]
        e = B["e"]
        sp = spr.get()
        act(P, sp[:, :], e[:, :], AF.Ln, [e], [sp], bias=1.0)
        if B["kb"] >= 4 * B["qt"]:
            mask(sp, B)
        B["sp"] = sp

    def stage2(i):
        B = blocks[i]
        ksl, qsl = operands(B)
        b = zb[i % 4]
        sp = B["sp"]
        first = B["idx"] == 0
        ra_prev = None if first else blocks[i - 1]["ra"]
        P.op("pe", lambda E, b=b, sp=sp, first=first: E.matmul(b[:, :], lhsT=ntri[:, :], rhs=sp[:, :], start=False, stop=first), reads=[ntri, sp], writes=[b])
        if not first:
            P.op("pe", lambda E, b=b, ra=ra_prev: E.matmul(b[:, :], lhsT=nones[:, :], rhs=ra[:, :], start=False, stop=True), reads=[nones, ra_prev], writes=[b])
        w = wr.get()
        act(P, w[:, :], b[:, :], AF.Exp, [b], [w])
        if B["kb"] >= 4 * B["qt"]:
            mask(w, B)
        B["w"] = w
        if B["idx"] + 1 < B["n"]:
            ra = racc.get()
            if first:
                cp(P, "dve", ra[:, :], sp[:, :], [sp], [ra])
            else:
                tt(P, "dve", ra[:, :], ra_prev[:, :], sp[:, :], ALU.add, [ra_prev, sp], [ra])
            B["ra"] = ra

    def stage3(i):
        B = blocks[i]
        o_ps = ob[B["g"] % 2]
        w = B["w"]
        h, kb = B["h"], B["kb"]
        P.op("pe", lambda E, o_ps=o_ps, w=w, h=h, kb=kb, B=B: E.matmul(
            o_ps[:, :], lhsT=Vb[h][:, kb * 64:(kb + 1) * 64], rhs=w[:, :], start=(B["idx"] == 0), stop=(B["idx"] == B["n"] - 1)),
            reads=[Vb[h], w], writes=[o_ps])
        if B["idx"] == B["n"] - 1:
            o = ost.get()
            cp(P, "dve", o[:, :], o_ps[:, :], [o_ps], [o])
            q0 = B["qt"] * NT
            osrc = dr["o_src"][q0 // 2048]
            P.dma("act", osrc.t[h * 64:(h + 1) * 64, q0 % 2048:q0 % 2048 + NT], o[:, :], reads=[o], writes=[osrc])

    for j in range(nb + 2):
        if j < nb:
            stage1a(j)
        if 0 <= j - 1 < nb:
            stage2(j - 1)
        if j < nb:
            stage1b(j)
        if 0 <= j - 2 < nb:
            stage3(j - 2)
    for k in range(4):
        P.coll("AllGather", dr["o_src"][k], dr["o_all"][k], GROUPS)
    P.wait_all("act", dr["o_all"])
    P.emit()


GROUPS = [[0, 1, 2, 3], [4, 5, 6, 7]]
TOKC = 2048
TP = 1024


def proj_pass(P, dn, src_fn, xs, rs, jobs, col0):
    for kc in range(8):
        srcs = src_fn(kc)
        if isinstance(srcs, tuple):
            P.dma("sp", xs[:, kc, :], srcs[1], reads=[srcs[0]], writes=[xs])
        else:
            w_ = TP // len(srcs)
            for i_, (sr_, sa_) in enumerate(srcs):
                P.dma("sp", xs[:, kc, i_ * w_:(i_ + 1) * w_], sa_, reads=[sr_], writes=[xs])
    dn.rstd(xs, rs)
    done = {}
    for (xb, gain, wbs, MC, dst) in jobs:
        if id(xb) not in done:
            done[id(xb)] = 1
            for kc in range(8):
                scale_cast(P, kc, xb[:, kc, :], xs[:, kc, :], gain[:, kc:kc + 1], [xs, gain], [xb])

        def epi(m, n, ps, dst=dst):
            o = dn.ost.get()
            tt(P, "dve", o[:, :], ps[:, :], rs[:, n * NT:(n + 1) * NT], ALU.mult, [ps, rs], [o])
            dn.store(dst, m, n, col0, o, o[:, :])
        dn.linear(None, 8, MC, xb, epi, wbs=wbs)


def phase_A(nc, dr):
    P = Prog(nc)
    P.mk_banks(6)
    dn = Dense(P, TP)
    g = colvec(P, "g_pre", dr["g_pre"], 8)
    xs = P.sb("xs", [128, 8, TP], F32)
    xb = P.sb("xb", [128, 8, TP], BF16)
    rs = P.sb("rs", [128, TP], F32)
    xT = dr["xT_full"]
    wbs = dn.prep_w("wu", dr["w_in_u"], 8, 2)
    for pa in range(SEQ // TP):
        t0 = pa * TP
        proj_pass(P, dn, lambda kc, t0=t0: (xT, xT.t[kc * 128:(kc + 1) * 128, t0:t0 + TP]), xs, rs,
                  [(xb, g, wbs, 2, dr["uT_cs"])], t0)
    P.wait_all("act", [dr["uT_cs"]])
    P.emit()


def select4(P, dn, src_fn, sel, dt):
    acc = dn.tmp.get()
    for s in range(4):
        it = dn.ist.get() if dt == F32 else dn.istb.get()
        sres, sap = src_fn(s)
        P.dma("sp", it[:, :], sap, reads=[sres], writes=[it])
        if s == 0:
            ts(P, "dve", acc[:, :], it[:, :], sel[:, 0:1], None, ALU.mult, None, [it, sel], [acc])
        else:
            P.op("dve", lambda E, it=it, s=s, acc=acc: E.scalar_tensor_tensor(
                out=acc[:, :], in0=it[:, :], scalar=sel[:, s:s + 1], in1=acc[:, :], op0=ALU.mult, op1=ALU.add),
                reads=[it, sel, acc], writes=[acc])
    return acc


def phase_C(nc, dr):
    P = Prog(nc)
    P.mk_banks(7)
    dn = Dense(P, TP)
    dn.istb = Rot(P, "istb", 4, [128, NT], BF16)
    sel = colvec(P, "sel", dr["sel"], 4)
    gpre = colvec(P, "g_pre", dr["g_pre"], 8)
    vl = []
    for i in range(4):
        t_ = P.sb("vec%d" % i, [128, 8], F32)
        P.dma("sp", t_[:, :], dr["vecsC"].t[:, i * 8:(i + 1) * 8], reads=[dr["vecsC"]], writes=[t_])
        vl.append(t_)
    bglu, gpost, gbpre, _unused = vl
    xT, y_all, pT = dr["xT_own"], dr["y_all"], dr["p0T"]
    Gb = P.sb("Gb", [128, 8, TP], BF16)
    SGb = P.sb("SGb", [128, 8, TP], BF16)
    Y2b = P.sb("Y2b", [128, 8, TP], BF16)
    X1 = P.sb("X1", [128, 8, TP], F32)
    Pb = P.sb("Pb", [128, 2, TP], BF16)
    rs = P.sb("rs", [128, TP], F32)
    for pa in range(TOKC // TP):
        t0 = pa * TP
        for kc in range(8):
            P.dma("sp", X1[:, kc, :], xT.t[kc * 128:(kc + 1) * 128, t0:t0 + TP], reads=[xT], writes=[X1])
        dn.rstd(X1, rs)
        for kc in range(8):
            scale_cast(P, kc, Y2b[:, kc, :], X1[:, kc, :], gpre[:, kc:kc + 1], [X1, gpre], [Y2b])

        def epi_gate(m, n, ps):
            t = dn.tmp.get()
            tt(P, "dve", t[:, :], ps[:, :], rs[:, n * NT:(n + 1) * NT], ALU.mult, [ps, rs], [t])
            act(P, SGb[:, m, n * NT:(n + 1) * NT], t[:, :], AF.Silu, [t], [SGb])
        dn.linear(dr["w_in_g"], 8, 8, Y2b, epi_gate)
        for kc in range(8):
            for n in range(TP // NT):
                def ysrc_fn(s, n=n, kc=kc):
                    g0 = s * TOKC + t0 + n * NT
                    ya = y_all[g0 // 1024]
                    return ya, ya.t[kc * 128:(kc + 1) * 128, g0 % 1024:g0 % 1024 + NT]
                y = select4(P, dn, ysrc_fn, sel, F32)
                s2 = gelu_tanh(P, dn, None, y[:, :], [y])
                tt(P, "pool", Gb[:, kc, n * NT:(n + 1) * NT], s2[:, :], y[:, :], ALU.mult, [s2, y], [Gb])

        def epi_glu(m, n, ps):
            sg = dn.tmp.get()
            act(P, sg[:, :], ps[:, :], AF.Sigmoid, [ps, bglu], [sg], bias=bglu[:, m:m + 1])
            t = dn.tmp.get()
            tt(P, "dve", t[:, :], sg[:, :], Gb[:, m, n * NT:(n + 1) * NT], ALU.mult, [sg, Gb], [t])
            tt(P, "pool", Y2b[:, m, n * NT:(n + 1) * NT], t[:, :], SGb[:, m, n * NT:(n + 1) * NT], ALU.mult, [t, SGb], [Y2b])
        dn.linear(dr["w_glu"], 8, 8, Gb, epi_glu)

        def epi_out(m, n, ps):
            act(P, X1[:, m, n * NT:(n + 1) * NT], ps[:, :], AF.Copy, [ps], [X1])
        dn.linear(dr["w_out0"], 8, 8, Y2b, epi_out)
        dn.rstd(X1, rs)
        for kc in range(8):
            for n in range(TP // NT):
                c0 = t0 + n * NT
                ix = dn.ist.get()
                P.dma("sp", ix[:, :], xT.t[kc * 128:(kc + 1) * 128, c0:c0 + NT], reads=[xT], writes=[ix])
                t = dn.tmp.get()
                P.op("dve", lambda E, t=t, kc=kc, n=n: E.scalar_tensor_tensor(
                    out=t[:, :], in0=X1[:, kc, n * NT:(n + 1) * NT], scalar=gpost[:, kc:kc + 1], in1=rs[:, n * NT:(n + 1) * NT],
                    op0=ALU.mult, op1=ALU.mult), reads=[X1, gpost, rs], writes=[t])
                tt(P, "pool", X1[:, kc, n * NT:(n + 1) * NT], t[:, :], ix[:, :], ALU.add, [t, ix], [X1])
                cp(P, "act", Gb[:, kc, n * NT:(n + 1) * NT], X1[:, kc, n * NT:(n + 1) * NT], [X1], [Gb])
        for kc in range(2):
            for n in range(TP // NT):
                c0 = t0 + n * NT
                ip = dn.ist.get()
                P.dma("sp", ip[:, :], pT.t[kc * 128:(kc + 1) * 128, c0:c0 + NT], reads=[pT], writes=[ip])
                cp(P, "dve", Pb[:, kc, n * NT:(n + 1) * NT], ip[:, :], [ip], [Pb])
        ple(P, dn, dr["w_pg0"], dr["w_pp0"], Gb, Pb, X1)
        dn.rstd(X1, rs)
        for kc in range(8):
            cp(P, "dve", Y2b[:, kc, :], X1[:, kc, :], [X1], [Y2b])
            scale_cast(P, kc + 1, SGb[:, kc, :], X1[:, kc, :], gbpre[:, kc:kc + 1], [X1, gbpre], [SGb])
            P.dma("act", dr["x1_own"].t[kc * 128:(kc + 1) * 128, t0:t0 + TP], X1[:, kc, :], reads=[X1], writes=[dr["x1_own"]])
            for n in range(TP // NT):
                xsrc = dr["x1_src"][(t0 + n * NT) // NT]
                P.dma("act", xsrc.t[kc * 128:(kc + 1) * 128, :], Y2b[:, kc, n * NT:(n + 1) * NT], reads=[Y2b], writes=[xsrc])

        def epi_g1(m, n, ps):
            o = dn.ost.get()
            tt(P, "dve", o[:, :], ps[:, :], rs[:, n * NT:(n + 1) * NT], ALU.mult, [ps, rs], [o])
            dn.store(dr["g1T"], m, n, t0, o, o[:, :])
        dn.linear(dr["w_bin_g"], 8, 8, SGb, epi_g1)
    for k in range(4):
        P.coll("AllGather", dr["x1_src"][k], dr["x1_all"][k], GROUPS)
    P.wait_all("act", dr["x1_all"] + [dr["x1_own"], dr["g1T"]])
    P.emit()


def phase_QKV(nc, dr):
    P = Prog(nc)
    P.mk_banks(6)
    dn = Dense(P, TP)
    gkv = colvec(P, "g_kv", dr["g_kv"], 8)
    gbpre = colvec(P, "g_bpre", dr["g_bpre"], 8)
    xs = P.sb("xs", [128, 8, TP], BF16)
    xq = P.sb("xq", [128, 8, TP], BF16)
    xk = P.sb("xk", [128, 8, TP], BF16)
    rs = P.sb("rs", [128, TP], F32)
    xa = dr["x1_all"]
    wq = dn.prep_w("wq", dr["w_q"], 8, 2)
    wk = dn.prep_w("wk", dr["w_k"], 8, 2)
    wv = dn.prep_w("wv", dr["w_v"], 8, 2)
    for pa in range(SEQ // TP):
        t0 = pa * TP
        s, tl = t0 // TOKC, t0 % TOKC
        proj_pass(P, dn, lambda kc, s=s, tl=tl: [(xa[(tl + h_ * NT) // NT], xa[(tl + h_ * NT) // NT].t[s * D + kc * 128:s * D + (kc + 1) * 128, :]) for h_ in range(TP // NT)], xs, rs,
                  [(xq, gbpre, wq, 2, dr["qT_cs"]), (xk, gkv, wk, 2, dr["kT_cs"]), (xk, gkv, wv, 2, dr["vT_cs"])], t0)
    P.wait_all("act", [dr["qT_cs"], dr["kT_cs"], dr["vT_cs"]])
    P.emit()


def phase_E(nc, dr):
    P = Prog(nc)
    P.mk_banks(7)
    dn = Dense(P, TP)
    dn.istb = Rot(P, "istb", 4, [128, NT], BF16)
    sel = colvec(P, "sel", dr["sel"], 4)
    gpost = colvec(P, "g_bpost", dr["g_bpost"], 8)
    Ob = P.sb("Ob", [128, 8, TP], BF16)
    Xb = P.sb("Xb", [128, 8, TP], BF16)
    X1 = P.sb("X1", [128, 8, TP], F32)
    Pb = P.sb("Pb", [128, 2, TP], BF16)
    rs = P.sb("rs", [128, TP], F32)
    x1T, gT, pT, outT = dr["x1_own"], dr["g1T"], dr["p1T"], dr["outT"]
    for pa in range(TOKC // TP):
        t0 = pa * TP
        for kc in range(8):
            for n in range(TP // NT):
                c0 = t0 + n * NT
                def osrc_fn(s, n=n, kc=kc):
                    g0 = s * TOKC + t0 + n * NT
                    oa = dr["o_all"][g0 // 2048]
                    return oa, oa.t[kc * 128:(kc + 1) * 128, g0 % 2048:g0 % 2048 + NT]
                o = select4(P, dn, osrc_fn, sel, BF16)
                ig = dn.ist.get()
                P.dma("sp", ig[:, :], gT.t[kc * 128:(kc + 1) * 128, c0:c0 + NT], reads=[gT], writes=[ig])
                sg = dn.tmp.get()
                act(P, sg[:, :], ig[:, :], AF.Silu, [ig], [sg])
                tt(P, "pool", Ob[:, kc, n * NT:(n + 1) * NT], sg[:, :], o[:, :], ALU.mult, [sg, o], [Ob])

        def epi_out(m, n, ps):
            act(P, X1[:, m, n * NT:(n + 1) * NT], ps[:, :], AF.Copy, [ps], [X1])
        dn.linear(dr["w_out1"], 8, 8, Ob, epi_out)
        dn.rstd(X1, rs)
        for kc in range(8):
            for n in range(TP // NT):
                c0 = t0 + n * NT
                ix = dn.ist.get()
                P.dma("sp", ix[:, :], x1T.t[kc * 128:(kc + 1) * 128, c0:c0 + NT], reads=[x1T], writes=[ix])
                t = dn.tmp.get()
                P.op("dve", lambda E, t=t, kc=kc, n=n: E.scalar_tensor_tensor(
                    out=t[:, :], in0=X1[:, kc, n * NT:(n + 1) * NT], scalar=gpost[:, kc:kc + 1], in1=rs[:, n * NT:(n + 1) * NT],
                    op0=ALU.mult, op1=ALU.mult), reads=[X1, gpost, rs], writes=[t])
                tt(P, "pool", X1[:, kc, n * NT:(n + 1) * NT], t[:, :], ix[:, :], ALU.add, [t, ix], [X1])
                cp(P, "act", Xb[:, kc, n * NT:(n + 1) * NT], X1[:, kc, n * NT:(n + 1) * NT], [X1], [Xb])
        for kc in range(2):
            for n in range(TP // NT):
                c0 = t0 + n * NT
                ip = dn.ist.get()
                P.dma("sp", ip[:, :], pT.t[kc * 128:(kc + 1) * 128, c0:c0 + NT], reads=[pT], writes=[ip])
                cp(P, "dve", Pb[:, kc, n * NT:(n + 1) * NT], ip[:, :], [ip], [Pb])
        ple(P, dn, dr["w_pg1"], dr["w_pp1"], Xb, Pb, X1)
        for kc in range(8):
            P.dma("act", outT.t[kc * 128:(kc + 1) * 128, t0:t0 + TP], X1[:, kc, :], reads=[X1], writes=[outT])
    P.wait_all("act", [outT])
    P.emit()


IN_SPECS = {
    "xT_full": ([D, SEQ], F32), "xT_own": ([D, TOKC], F32), "p0T": ([256, TOKC], F32), "p1T": ([256, TOKC], F32),
    "sel": ([128, 4], F32), "g_pre": ([128, 8], F32), "w_in_u": ([D, 256], F32), "w_in_g": ([D, D], F32),
    "lamre_T": ([128, 128], F32), "lamim_T": ([128, 128], F32), "logdt_T": ([128, 128], F32), "bre_T": ([128, 128], F32),
    "bim_T": ([128, 128], F32), "lamre_S": ([128, NG], F32), "lamim_S": ([128, NG], F32), "logdt_S": ([128, NG], F32),
    "c1_S": ([128, NG * 16], F32), "c2_S": ([128, NG * 16], F32), "iota": ([128, TS + 1], F32), "gmask": ([128, 8], F32),
    "sgn": ([128, 1], F32), "Jm": ([128, 128], F32), "dskip_cs": ([128, 2], F32), "vecsC": ([128, 32], F32),
    "w_glu": ([D, D], F32), "w_out0": ([D, D], F32), "w_pg0": ([D, D], F32), "w_pp0": ([256, D], F32),
    "w_bin_g": ([D, D], F32), "g_kv": ([128, 8], F32), "g_bpre": ([128, 8], F32), "w_q": ([D, 256], F32),
    "w_k": ([D, 256], F32), "w_v": ([D, 256], F32), "ntri": ([128, 128], F32), "g_bpost": ([128, 8], F32),
    "w_out1": ([D, D], F32), "w_pg1": ([D, D], F32), "w_pp1": ([256, D], F32),
}
SCRATCH = {
    "uT_cs": ([256, SEQ], F32), "y_src": ([256, 1024], F32, 8), "y_all": ([D, 1024], F32, 8), "x1_own": ([D, TOKC], F32),
    "x1_src": ([D, NT], BF16, 4), "x1_all": ([4 * D, NT], BF16, 4), "g1T": ([D, TOKC], F32), "qT_cs": ([256, SEQ], F32),
    "kT_cs": ([256, SEQ], F32), "vT_cs": ([256, SEQ], F32), "o_src": ([256, 2048], BF16, 4), "o_all": ([D, 2048], BF16, 4),
}


def build_fused():
    nc = bass.Bass("TRN2", target_bir_lowering=False)
    dr = {}
    for n, (shp, dt) in IN_SPECS.items():
        dr[n] = Res(n, nc.dram_tensor(n, list(shp), dt, kind="ExternalInput").ap())
    for n, spec in SCRATCH.items():
        shp, dt = spec[0], spec[1]
        if len(spec) == 3:
            dr[n] = [Res("%s%d" % (n, i), nc.dram_tensor("%s%d" % (n, i), list(shp), dt, kind="Internal").ap()) for i in range(spec[2])]
        else:
            dr[n] = Res(n, nc.dram_tensor(n, list(shp), dt, kind="Internal").ap())
    dr["outT"] = Res("outT", nc.dram_tensor("outT", [D, TOKC], F32, kind="ExternalOutput").ap())
    phase_A(nc, dr)
    phase_B(nc, dr)
    phase_C(nc, dr)
    phase_QKV(nc, dr)
    phase_D(nc, dr)
    phase_E(nc, dr)
    Prog.finish()
    return nc


def _f(a):
    return np.ascontiguousarray(np.asarray(a, dtype=np.float32))


def kernel(**inputs):
    inp = {k: np.asarray(v) for k, v in inputs.items()}
    x, p = inp["x"], inp["p"]
    lam_re, lam_im, log_dt = _f(inp["a_lam_re"][0]), _f(inp["a_lam_im"][0]), _f(inp["a_log_dt"][0])
    b_re, b_im, c_re, c_im = _f(inp["a_b_re"][0]), _f(inp["a_b_im"][0]), _f(inp["a_c_re"][0]), _f(inp["a_c_im"][0])
    iota = _f(np.broadcast_to(np.arange(TS + 1, dtype=np.float32), (128, TS + 1)))
    gmask = np.zeros((128, 8), np.float32)
    for gl in range(8):
        gmask[gl * 16:(gl + 1) * 16, gl] = 1.0
    sgn = np.ones((128, 1), np.float32)
    sgn[64:] = -1.0
    J = np.zeros((128, 128), np.float32)
    for q in range(64):
        J[64 + q, q] = -1.0
        J[q, 64 + q] = 1.0
    ntri = np.zeros((128, 128), np.float32)
    for j in range(128):
        ntri[j, :j + 1] = -1.0
    vecsC = np.zeros((128, 32), np.float32)
    for i, v in enumerate([inp["a_b_glu"][0], inp["a_norm_post"][0], inp["b_norm_pre"][0]]):
        vecsC[:, i * 8:(i + 1) * 8] = _cols(v)
    a_w_in, b_w_in, w_kv = _f(inp["a_w_in"][0]), _f(inp["b_w_in"][0]), _f(inp["w_kv"])
    common = {
        "g_pre": _cols(inp["a_norm_pre"][0]), "w_in_g": _f(a_w_in[:, D:]), "iota": iota, "gmask": gmask, "sgn": sgn, "Jm": J,
        "vecsC": vecsC, "w_glu": _f(inp["a_w_glu"][0]), "w_out0": _f(inp["a_w_out"][0]), "w_pg0": _f(inp["ple_w_gate"][0]),
        "w_pp0": _f(inp["ple_w_proj"][0]), "w_bin_g": _f(b_w_in[:, D:]), "g_kv": _cols(inp["kv_norm"]),
        "g_bpre": _cols(inp["b_norm_pre"][0]), "ntri": ntri, "g_bpost": _cols(inp["b_norm_post"][0]),
        "w_out1": _f(inp["b_w_out"][0]), "w_pg1": _f(inp["ple_w_gate"][1]), "w_pp1": _f(inp["ple_w_proj"][1]),
    }
    xT_full = [_f(np.asarray(x[b], np.float32).T) for b in range(2)]
    maps = []
    for c in range(NCORE):
        b, r = c // 4, c % 4
        gs = np.arange(16 * r, 16 * r + 16)
        tsl = slice(r * TOKC, (r + 1) * TOKC)
        csl = slice(256 * r, 256 * r + 256)
        m = dict(common)
        m["xT_full"] = xT_full[b]
        m["xT_own"] = _f(xT_full[b][:, tsl])
        m["p0T"] = _f(np.asarray(p[0, b, tsl, :], np.float32).T)
        m["p1T"] = _f(np.asarray(p[1, b, tsl, :], np.float32).T)
        sel = np.zeros((128, 4), np.float32)
        sel[:, r] = 1.0
        m["sel"] = sel
        m["w_in_u"] = _f(a_w_in[:, csl])
        m["dskip_cs"] = _cols(inp["a_d_skip"][0][csl])
        m["w_q"] = _f(b_w_in[:, csl])
        m["w_k"] = _f(w_kv[:, csl])
        m["w_v"] = _f(w_kv[:, D + 256 * r:D + 256 * r + 256])

        def lt_gp(a):
            t = a[gs].reshape(2, 8, 64)
            t = np.broadcast_to(t[:, :, None, :], (2, 8, 16, 64))
            return _f(t.transpose(1, 2, 0, 3).reshape(128, 128))

        def lt_b(a):
            t = a[gs].reshape(2, 8, 64, 16)
            return _f(t.transpose(1, 3, 0, 2).reshape(128, 128))

        def sp_gp(a):
            t = a[gs].T
            return _f(np.concatenate([t, t], axis=0))
        ldt = np.broadcast_to(log_dt[:, None], (64, 64))
        m["lamre_T"], m["lamim_T"], m["logdt_T"] = lt_gp(lam_re), lt_gp(lam_im), lt_gp(ldt)
        m["bre_T"], m["bim_T"] = lt_b(b_re), lt_b(b_im)
        m["lamre_S"], m["lamim_S"], m["logdt_S"] = sp_gp(lam_re), sp_gp(lam_im), sp_gp(ldt)
        cr = c_re[gs].transpose(2, 0, 1).reshape(64, 256)
        ci = c_im[gs].transpose(2, 0, 1).reshape(64, 256)
        m["c1_S"] = _f(np.concatenate([cr, ci], axis=0))
        m["c2_S"] = _f(np.concatenate([ci, cr], axis=0))
        maps.append(m)
    res = _run(build_fused(), maps)
    out = np.empty((2, SEQ, D), np.float32)
    for c in range(NCORE):
        b, r = c // 4, c % 4
        out[b, r * TOKC:(r + 1) * TOKC, :] = res[c]["outT"].T
    return out
```

```python
from contextlib import ExitStack
import numpy as np
import concourse.bass as bass
import concourse.mybir as mybir
from concourse.bass_utils import run_bass_kernel_spmd

F32 = mybir.dt.float32
BF16 = mybir.dt.bfloat16
AF = mybir.ActivationFunctionType
ALU = mybir.AluOpType

ENGS = ("pe", "act", "dve", "pool", "sp")
NCORE = 8
D = 1024
SEQ = 8192
NT = 512
EPS = 1e-6
PI = float(np.pi)


class Res:
    __slots__ = ("name", "w", "r", "dsem", "dcnt", "t")

    def __init__(self, name, t=None):
        self.name = name
        self.w = {}
        self.r = {}
        self.dsem = None
        self.dcnt = 0
        self.t = t

    def __getitem__(self, idx):
        return self.t[idx]


class Prog:
    _n = 0
    G = None

    def __init__(self, nc):
        Prog._n += 1
        self.pfx = "f%d_" % Prog._n
        self.nc = nc
        if Prog.G is None or Prog.G["nc"] is not nc:
            ges = ExitStack()
            Prog.G = {"nc": nc, "es": ges, "sems": {}, "cnt": {}}
            for e in ENGS:
                Prog.G["sems"][e] = ges.enter_context(nc.semaphore("s_" + e))
                Prog.G["cnt"][e] = 0
        G = Prog.G
        self.es = ExitStack()
        self.lists = {e: [] for e in ENGS}
        self.sems = G["sems"]
        self.cnt = G["cnt"]
        self.seen = {e: dict(self.cnt) for e in ENGS}
        self.nd = 0
        self.banks = []
        self.bi = 0
        self.touched = {}

    @staticmethod
    def finish():
        if Prog.G is not None:
            Prog.G["es"].close()
            Prog.G = None

    def _newsem(self):
        key = "d%d" % self.nd
        self.nd += 1
        if key not in self.sems:
            self.sems[key] = Prog.G["es"].enter_context(self.nc.semaphore("sd_" + key))
            self.cnt[key] = 0
        return key

    def sb(self, name, shape, dt):
        return Res(name, self.es.enter_context(self.nc.sbuf_tensor(self.pfx + "sb_" + name, list(shape), dt)))

    def ps(self, name, shape, dt=F32):
        return Res(name, self.es.enter_context(self.nc.psum_tensor(self.pfx + "ps_" + name, list(shape), dt)))

    def dram(self, name, shape, dt, kind="Internal"):
        return Res(name, self.nc.dram_tensor(name, list(shape), dt, kind=kind).ap())

    def mk_banks(self, n):
        self.banks = [self.ps("bank%d" % i, [128, NT], F32) for i in range(n)]

    def bank(self):
        b = self.banks[self.bi % len(self.banks)]
        self.bi += 1
        return b

    def _dsem(self, res):
        if res.dsem is None:
            res.dsem = self._newsem()
        return res.dsem

    def _waits(self, eng, reads, writes, skip_same=False):
        for x_ in reads:
            self.touched[id(x_)] = x_
        for x_ in writes:
            self.touched[id(x_)] = x_
        deps = {}
        for r in reads:
            for k, v in r.w.items():
                if skip_same and k == eng:
                    continue
                if v > deps.get(k, 0):
                    deps[k] = v
        for w in writes:
            for k, v in w.w.items():
                if k != eng and v > deps.get(k, 0):
                    deps[k] = v
            for k, v in w.r.items():
                if k != eng and v > deps.get(k, 0):
                    deps[k] = v
        seen = self.seen[eng]
        for k, v in deps.items():
            if v > seen.get(k, 0):
                seen[k] = v
                sem = self.sems[k]
                self.lists[eng].append(lambda E, sem=sem, v=v: E.wait_ge(sem, v))

    def op(self, eng, fn, reads=(), writes=(), skip_same=False):
        self._waits(eng, reads, writes, skip_same)
        self.cnt[eng] += 1
        n = self.cnt[eng]
        sem = self.sems[eng]
        self.lists[eng].append(lambda E, fn=fn, sem=sem: fn(E).then_inc(sem, 1))
        for r in reads:
            r.r[eng] = n
        for w in writes:
            w.w[eng] = n

    def dma(self, eng, out_ap, in_ap, reads=(), writes=()):
        wres = writes[0]
        self._waits(eng, reads, writes)
        key = self._dsem(wres)
        self.cnt[key] += 16
        v = self.cnt[key]
        sem = self.sems[key]
        self.lists[eng].append(
            lambda E, o=out_ap, i=in_ap, sem=sem: E.dma_start(out=o, in_=i).then_inc(sem, 16))
        for r in reads:
            r.r[key] = v
        wres.w[key] = v

    def coll(self, kind, src, dst, groups):
        self._waits("pool", [src], [dst])
        key = self._newsem()
        self.cnt[key] += 1
        v = self.cnt[key]
        sem = self.sems[key]
        self.lists["pool"].append(lambda E, sem=sem: E.collective_compute(
            kind, ALU.bypass, replica_groups=groups, ins=[src.t.opt()], outs=[dst.t.opt()]).then_inc(sem))
        src.r[key] = v
        dst.w[key] = v

    def wait_all(self, eng, ress):
        self._waits(eng, ress, ())

    def emit(self):
        L = self.lists
        for e in ENGS:
            for k, sem in self.sems.items():
                tgt = self.cnt[k]
                if k != e and tgt > self.seen[e].get(k, 0):
                    self.seen[e][k] = tgt
                    L[e].append(lambda E, sem=sem, tgt=tgt: E.wait_ge(sem, tgt))
        with self.nc.Block() as block:
            @block.tensor
            def _(E):
                for f in L["pe"]:
                    f(E)

            @block.scalar
            def _(E):
                for f in L["act"]:
                    f(E)

            @block.vector
            def _(E):
                for f in L["dve"]:
                    f(E)

            @block.gpsimd
            def _(E):
                for f in L["pool"]:
                    f(E)

            @block.sync
            def _(E):
                for f in L["sp"]:
                    f(E)
        self.es.close()
        for x_ in self.touched.values():
            x_.w = {}
            x_.r = {}
            x_.dsem = None


def tt(P, eng, out, in0, in1, op, reads, writes):
    P.op(eng, lambda E: E.tensor_tensor(out=out, in0=in0, in1=in1, op=op), reads=reads, writes=writes)


def ts(P, eng, out, in0, s1, s2, op0, op1, reads, writes):
    if s2 is None:
        P.op(eng, lambda E: E.tensor_scalar(out=out, in0=in0, scalar1=s1, scalar2=None, op0=op0), reads=reads, writes=writes)
    else:
        P.op(eng, lambda E: E.tensor_scalar(out=out, in0=in0, scalar1=s1, scalar2=s2, op0=op0, op1=op1), reads=reads, writes=writes)


def act(P, out, in_, func, reads, writes, scale=1.0, bias=None):
    if bias is None:
        P.op("act", lambda E: E.activation(out=out, in_=in_, func=func, scale=scale), reads=reads, writes=writes)
    else:
        P.op("act", lambda E: E.activation(out=out, in_=in_, func=func, scale=scale, bias=bias), reads=reads, writes=writes)


def scale_cast(P, i, out, in_, col, reads, writes):
    if i % 2:
        P.op("act", lambda E: E.activation(out=out, in_=in_, func=AF.Copy, scale=col), reads=reads, writes=writes)
    else:
        ts(P, "dve", out, in_, col, None, ALU.mult, None, reads, writes)


def cp(P, eng, out, in_, reads, writes):
    if eng == "act":
        P.op(eng, lambda E: E.activation(out=out, in_=in_, func=AF.Copy), reads=reads, writes=writes)
    else:
        P.op(eng, lambda E: E.tensor_copy(out=out, in_=in_), reads=reads, writes=writes)


class Rot:
    def __init__(self, P, name, n, shape, dt):
        self.bufs = [P.sb("%s%d" % (name, i), shape, dt) for i in range(n)]
        self.i = 0

    def get(self):
        b = self.bufs[self.i % len(self.bufs)]
        self.i += 1
        return b


class Dense:
    def __init__(self, P, T):
        self.P = P
        self.T = T
        self.wst = Rot(P, "wst", 2, [128, 8, 128], F32)
        self.wbf = Rot(P, "wbf", 2, [128, 8, 128], BF16)
        self.ost = Rot(P, "ost", 3, [128, NT], F32)
        self.ist = Rot(P, "ist", 4, [128, NT], F32)
        self.tmp = Rot(P, "tmp", 4, [128, NT], F32)
        self.ones = P.sb("ones", [128, 128], BF16)
        P.op("pool", lambda E: E.memset(self.ones[:], 1.0), writes=[self.ones])
        self.sq = P.sb("sq", [128, 8, T], BF16)

    def load_w(self, W, m, KC, rowscale=None, wb=None):
        P = self.P
        assert rowscale is None
        st = self.wst.get()
        if wb is None:
            wb = self.wbf.get()
        P.dma("sp", st[:, 0:KC, :], W.t[:, m * 128:(m + 1) * 128].rearrange("(kc p) m -> p kc m", p=128), reads=[W], writes=[st])
        self.wi = getattr(self, "wi", 0) + 1
        cp(P, "act" if self.wi % 2 else "pool", wb[:, 0:KC, :], st[:, 0:KC, :], [st], [wb])
        return wb

    def prep_w(self, name, W, KC, MC):
        wbs = []
        for m in range(MC):
            wb = self.P.sb("%s_w%d" % (name, m), [128, KC, 128], BF16)
            wbs.append(self.load_w(W, m, KC, wb=wb))
        return wbs

    def linear(self, W, KC, MC, a, epi, rowscale=None, wbs=None):
        P = self.P
        for m in range(MC):
            wb = wbs[m] if wbs is not None else self.load_w(W, m, KC, rowscale)
            for n in range(self.T // NT):
                ps = P.bank()
                for kc in range(KC):
                    P.op("pe", lambda E, ps=ps, wb=wb, kc=kc, n=n: E.matmul(
                        ps[:, :], lhsT=wb[:, kc, :], rhs=a[:, kc, n * NT:(n + 1) * NT], start=(kc == 0), stop=(kc == KC - 1)),
                        reads=[wb, a], writes=[ps])
                epi(m, n, ps)

    def rstd(self, src, out):
        P = self.P
        sq = self.sq
        for kc in range(8):
            act(P, sq[:, kc, :], src[:, kc, :], AF.Square, [src], [sq])
        for n in range(self.T // NT):
            ps = P.bank()
            for kc in range(8):
                P.op("pe", lambda E, ps=ps, kc=kc, n=n: E.matmul(
                    ps[:, :], lhsT=self.ones[:, :], rhs=sq[:, kc, n * NT:(n + 1) * NT], start=(kc == 0), stop=(kc == 7)),
                    reads=[self.ones, sq], writes=[ps])
            t = self.tmp.get()
            act(P, t[:, :], ps[:, :], AF.Ln, [ps], [t], scale=1.0 / D, bias=EPS)
            act(P, out[:, n * NT:(n + 1) * NT], t[:, :], AF.Exp, [t], [out], scale=-0.5)

    def store(self, dst, m, n, t0, src_res, src_ap):
        self.P.dma("act", dst.t[m * 128:(m + 1) * 128, t0 + n * NT:t0 + (n + 1) * NT], src_ap, reads=[src_res], writes=[dst])

    def load_act(self, src, t0, dst, KC=8):
        for kc in range(KC):
            self.P.dma("sp", dst[:, kc, :], src.t[kc * 128:(kc + 1) * 128, t0:t0 + self.T], reads=[src], writes=[dst])


def colvec(P, name, dram_res, ncol):
    t = P.sb(name, [128, ncol], F32)
    P.dma("sp", t[:, :], dram_res.t[:, :], reads=[dram_res], writes=[t])
    return t


def _run(nc, in_maps):
    res = run_bass_kernel_spmd(nc, in_maps, core_ids=list(range(NCORE)))
    return res.results


def _cols(v):
    return np.ascontiguousarray(np.asarray(v, np.float32).reshape(-1, 128).T)


TS = 256
NG = 16
M_MAGIC = 12582912.0


def range_reduce(P, eng, out, in_, tmp, reads, writes, shift=0.0):
    res_in = reads
    if shift != 0.0:
        ts(P, eng, out, in_, shift, None, ALU.add, None, res_in, writes)
        in_ = out
        res_in = writes
    ts(P, eng, tmp[0], in_, 1.0 / (2 * PI), M_MAGIC, ALU.mult, ALU.add, res_in, [tmp[1]])
    ts(P, eng, tmp[0], tmp[0], -M_MAGIC, -2 * PI, ALU.add, ALU.mult, [tmp[1]], [tmp[1]])
    tt(P, eng, out, tmp[0], in_, ALU.add, [tmp[1]] + list(res_in), writes)
    ts(P, eng, out, out, 3.14159, -3.14159, ALU.min, ALU.max, writes, writes)


NAMES_T = ["lamre_T", "lamim_T", "logdt_T", "bre_T", "bim_T"]
NAMES_S = ["lamre_S", "lamim_S", "logdt_S"]


def phase_B(nc, dr):
    P = Prog(nc)
    uT = dr["uT_cs"]
    names_T = NAMES_T
    dT = {n: dr[n] for n in names_T}
    names_S = NAMES_S
    dS = {n: dr[n] for n in names_S}
    c1_d, c2_d, iota_d, gmask_d, sgn_d, J_d = dr["c1_S"], dr["c2_S"], dr["iota"], dr["gmask"], dr["sgn"], dr["Jm"]
    dsk = colvec(P, "dskip_cs", dr["dskip_cs"], 2)
    P.mk_banks(4)
    ybank = [P.ps("ybank%d" % i, [128, NT], F32) for i in range(2)]
    igb = P.ps("igb", [128, NT], F32)

    def ld(name, d, ncol):
        return colvec(P, name, d, ncol)

    lt = {n: ld("s_" + n, dT[n], 128) for n in names_T}
    cnt = [0]

    def newT(nm, ncol=128):
        cnt[0] += 1
        return P.sb("%s_%d" % (nm, cnt[0]), [128, ncol], F32)

    def derive(lamre, lamim, logdt, ncol, pfx):
        o = {}
        lr = newT(pfx + "lr", ncol)
        ts(P, "dve", lr[:, :], lamre[:, :], -1e-4, None, ALU.min, None, [lamre], [lr])
        dt = newT(pfx + "dt", ncol)
        act(P, dt[:, :], logdt[:, :], AF.Exp, [logdt], [dt])
        e = newT(pfx + "e", ncol)
        tt(P, "dve", e[:, :], lr[:, :], dt[:, :], ALU.mult, [lr, dt], [e])
        th = newT(pfx + "th", ncol)
        tt(P, "dve", th[:, :], lamim[:, :], dt[:, :], ALU.mult, [lamim, dt], [th])
        thr = newT(pfx + "thr", ncol)
        tmp = newT(pfx + "tmp", ncol)
        range_reduce(P, "dve", thr[:, :], th[:, :], (tmp[:, :], tmp), [th], [thr])
        thc = newT(pfx + "thc", ncol)
        range_reduce(P, "dve", thc[:, :], th[:, :], (tmp[:, :], tmp), [th], [thc], shift=PI / 2)
        mag = newT(pfx + "mag", ncol)
        act(P, mag[:, :], e[:, :], AF.Exp, [e], [mag])
        sn = newT(pfx + "sin", ncol)
        act(P, sn[:, :], thr[:, :], AF.Sin, [thr], [sn])
        cs = newT(pfx + "cos", ncol)
        act(P, cs[:, :], thc[:, :], AF.Sin, [thc], [cs])
        o.update(lr=lr, li=lamim, e=e, thr=thr, mag=mag, sin=sn, cos=cs)
        return o

    dl = derive(lt["lamre_T"], lt["lamim_T"], lt["logdt_T"], 128, "T")

    def mul(a, b, nm):
        t = newT(nm)
        tt(P, "dve", t[:, :], a[:, :], b[:, :], ALU.mult, [a, b], [t])
        return t

    def addsub(a, b, op, nm):
        t = newT(nm)
        tt(P, "dve", t[:, :], a[:, :], b[:, :], op, [a, b], [t])
        return t

    are = mul(dl["mag"], dl["cos"], "are")
    aim = mul(dl["mag"], dl["sin"], "aim")
    den = addsub(mul(dl["lr"], dl["lr"], "lr2"), mul(dl["li"], dl["li"], "li2"), ALU.add, "den")
    rden = newT("rden")
    P.op("dve", lambda E: E.reciprocal(out=rden[:, :], in_=den[:, :]), reads=[den], writes=[rden])
    nr = newT("nr")
    ts(P, "dve", nr[:, :], are[:, :], -1.0, None, ALU.add, None, [are], [nr])
    fre = mul(addsub(mul(nr, dl["lr"], "f1"), mul(aim, dl["li"], "f2"), ALU.add, "f3"), rden, "fre")
    fim = mul(addsub(mul(aim, dl["lr"], "f4"), mul(nr, dl["li"], "f5"), ALU.subtract, "f6"), rden, "fim")
    bbre = addsub(mul(fre, lt["bre_T"], "b1"), mul(fim, lt["bim_T"], "b2"), ALU.subtract, "bbre")
    bbim = addsub(mul(fre, lt["bim_T"], "b3"), mul(fim, lt["bre_T"], "b4"), ALU.add, "bbim")
    nbbim = newT("nbbim")
    ts(P, "dve", nbbim[:, :], bbim[:, :], -1.0, None, ALU.mult, None, [bbim], [nbbim])
    gmask = ld("gmask", gmask_d, 8)
    W1 = P.sb("W1pad", [128, NG, 128], BF16)
    W2 = P.sb("W2pad", [128, NG, 128], BF16)
    for jj in range(2):
        for gl in range(8):
            gg = jj * 8 + gl
            ms = gmask[:, gl:gl + 1]
            sl = slice(jj * 64, (jj + 1) * 64)
            ts(P, "dve", W1[:, gg, 0:64], bbre[:, sl], ms, None, ALU.mult, None, [bbre, gmask], [W1])
            ts(P, "dve", W1[:, gg, 64:128], bbim[:, sl], ms, None, ALU.mult, None, [bbim, gmask], [W1])
            ts(P, "dve", W2[:, gg, 0:64], nbbim[:, sl], ms, None, ALU.mult, None, [nbbim, gmask], [W2])
            ts(P, "dve", W2[:, gg, 64:128], bbre[:, sl], ms, None, ALU.mult, None, [bbre, gmask], [W2])

    ls_ = {n: ld("s_" + n, dS[n], NG) for n in names_S}
    ds = derive(ls_["lamre_S"], ls_["lamim_S"], ls_["logdt_S"], NG, "S")
    rho = ds["mag"]
    iota = ld("iota", iota_d, TS + 1)
    COS = P.sb("COS", [128, NG, TS + 1], F32)
    SIN = P.sb("SIN", [128, NG, TS + 1], F32)
    ARG = P.sb("ARG", [128, NG, TS + 1], F32)
    TMP = P.sb("TMPA", [128, NG, TS + 1], F32)
    Gg = [P.sb("Gg%d" % i, [128, TS], F32) for i in range(NG)]
    GT = [P.sb("GT%d" % i, [128, NG], F32) for i in range(2)]
    for gg in range(NG):
        ts(P, "dve", ARG[:, gg, :], iota[:, :], ds["thr"][:, gg:gg + 1], None, ALU.mult, None, [iota, ds["thr"]], [ARG])
    range_reduce(P, "dve", SIN[:, :, :], ARG[:, :, :], (TMP[:, :, :], TMP), [ARG], [SIN])
    range_reduce(P, "dve", COS[:, :, :], ARG[:, :, :], (TMP[:, :, :], TMP), [ARG], [COS], shift=PI / 2)
    act(P, SIN[:, :, :], SIN[:, :, :], AF.Sin, [SIN], [SIN])
    act(P, COS[:, :, :], COS[:, :, :], AF.Sin, [COS], [COS])
    c1 = ld("c1", c1_d, NG * 16)
    c2 = ld("c2", c2_d, NG * 16)
    sgn = ld("sgn", sgn_d, 1)
    L1 = P.sb("L1pad", [128, NG, 128], BF16)
    L2 = P.sb("L2pad", [128, NG, 128], BF16)
    P.op("pool", lambda E: E.memset(L1[:, :, :], 0.0), writes=[L1])
    P.op("pool", lambda E: E.memset(L2[:, :, :], 0.0), writes=[L2])
    for gg in range(NG):
        gl = gg % 8
        ts(P, "dve", L1[:, gg, gl * 16:(gl + 1) * 16], c1[:, gg * 16:(gg + 1) * 16], sgn[:, 0:1], None, ALU.mult, None, [c1, sgn], [L1])
        ts(P, "dve", L2[:, gg, gl * 16:(gl + 1) * 16], c2[:, gg * 16:(gg + 1) * 16], -1.0, None, ALU.mult, None, [c2], [L2])
    Jm = P.sb("Jm", [128, 128], F32)
    P.dma("sp", Jm[:, :], J_d.t[:, :], reads=[J_d], writes=[Jm])

    ust = Rot(P, "ust", 3, [128, 2, TS], F32)
    ubf = Rot(P, "ubf", 3, [128, 2, TS], BF16)
    t1r = Rot(P, "t1r", 4, [128, TS], F32)
    t2r = Rot(P, "t2r", 4, [128, TS], F32)
    xtr = Rot(P, "xtr", 4, [128, TS], F32)
    h1r = Rot(P, "h1r", 4, [128, TS], BF16)
    h2r = Rot(P, "h2r", 4, [128, TS], BF16)
    S0 = [P.sb("S0_%d" % i, [128, NG], F32) for i in range(2)]
    ys = Rot(P, "ys", 3, [128, 2, TS], F32)
    P.op("pool", lambda E: E.memset(S0[0][:, :], 0.0), writes=[S0[0]])
    nch = SEQ // TS
    units = []
    chunk_res = {}
    for ch in range(nch):
        for jj in range(2):
            for gl in range(8):
                units.append(dict(ch=ch, jj=jj, gl=gl, gg=jj * 8 + gl))
    nu = len(units)

    def chunk_setup(ch):
        c0 = ch * TS
        us = ust.get()
        ub = ubf.get()
        P.dma("sp", us[:, :, :], uT.t[:, c0:c0 + TS].rearrange("(j p) t -> p j t", p=128), reads=[uT], writes=[us])
        cp(P, "act", ub[:, :, :], us[:, :, :], [us], [ub])
        chunk_res[ch] = dict(us=us, ub=ub, yo=ys.get())

    def s1(i):
        U = units[i]
        ch, jj, gg = U["ch"], U["jj"], U["gg"]
        if jj == 0 and U["gl"] == 0:
            chunk_setup(ch)
        ub = chunk_res[ch]["ub"]
        p1 = P.bank()
        p2 = P.bank()
        P.op("pe", lambda E, p1=p1, gg=gg, ub=ub, jj=jj: E.matmul(p1[:, 0:TS], lhsT=W1[:, gg, :], rhs=ub[:, jj, :], start=True, stop=True), reads=[W1, ub], writes=[p1])
        P.op("pe", lambda E, p2=p2, gg=gg, ub=ub, jj=jj: E.matmul(p2[:, 0:TS], lhsT=W2[:, gg, :], rhs=ub[:, jj, :], start=True, stop=True), reads=[W2, ub], writes=[p2])
        t1 = t1r.get()
        t2 = t2r.get()
        xt = xtr.get()
        tt(P, "dve", t1[:, :], p1[:, 0:TS], COS[:, gg, 1:TS + 1], ALU.mult, [p1, COS], [t1])
        tt(P, "dve", t2[:, :], p2[:, 0:TS], SIN[:, gg, 1:TS + 1], ALU.mult, [p2, SIN], [t2])
        tt(P, "pool", xt[:, :], t1[:, :], t2[:, :], ALU.subtract, [t1, t2], [xt])
        U["xt"] = xt

    def s2(i):
        U = units[i]
        ch, gg = U["ch"], U["gg"]
        gt, s0 = GT[ch % 2], S0[ch % 2]
        xt = U["xt"]
        G = Gg[gg]
        P.op("dve", lambda E, G=G, gg=gg, xt=xt, s0=s0: E.tensor_tensor_scan(
            out=G[:, :], data0=rho[:, gg:gg + 1].to_broadcast([128, TS]), data1=xt[:, :],
            initial=s0[:, gg:gg + 1], op0=ALU.mult, op1=ALU.add), reads=[rho, xt, s0], writes=[G])
        if ch + 1 < nch:
            cp(P, "act", gt[:, gg:gg + 1], G[:, TS - 1:TS], [G], [gt])
        h1 = h1r.get()
        h2 = h2r.get()
        tt(P, "dve", h1[:, :], G[:, :], COS[:, gg, 1:TS + 1], ALU.mult, [G, COS], [h1])
        tt(P, "pool", h2[:, :], G[:, :], SIN[:, gg, 1:TS + 1], ALU.mult, [G, SIN], [h2])
        U["h1"], U["h2"] = h1, h2
        if gg == NG - 1 and ch + 1 < nch:
            s1_ = S0[(ch + 1) % 2]
            P.op("pe", lambda E, gt=gt: E.matmul(igb[:, 0:NG], lhsT=Jm[:, :], rhs=gt[:, :], start=True, stop=True), reads=[Jm, gt], writes=[igb])
            ta = P.sb("bta%d" % ch, [128, NG], F32)
            tb = P.sb("btb%d" % ch, [128, NG], F32)
            tt(P, "dve", ta[:, :], gt[:, :], COS[:, :, TS], ALU.mult, [gt, COS], [ta])
            tt(P, "dve", tb[:, :], igb[:, 0:NG], SIN[:, :, TS], ALU.mult, [igb, SIN], [tb])
            tt(P, "dve", s1_[:, :], ta[:, :], tb[:, :], ALU.add, [ta, tb], [s1_])

    def s3(i):
        U = units[i]
        ch, jj, gl, gg = U["ch"], U["jj"], U["gl"], U["gg"]
        yb = ybank[jj]
        h1, h2 = U["h1"], U["h2"]
        P.op("pe", lambda E, yb=yb, gg=gg, h1=h1, gl=gl: E.matmul(yb[:, 0:TS], lhsT=L1[:, gg, :], rhs=h1[:, :], start=(gl == 0), stop=False), reads=[L1, h1], writes=[yb])
        P.op("pe", lambda E, yb=yb, gg=gg, h2=h2, gl=gl: E.matmul(yb[:, 0:TS], lhsT=L2[:, gg, :], rhs=h2[:, :], start=False, stop=(gl == 7)), reads=[L2, h2], writes=[yb])
        if gl == 7:
            cr = chunk_res[ch]
            yo, us = cr["yo"], cr["us"]
            P.op("dve", lambda E, yo=yo, us=us, yb=yb, jj=jj: E.scalar_tensor_tensor(
                out=yo[:, jj, :], in0=us[:, jj, :], scalar=dsk[:, jj:jj + 1], in1=yb[:, 0:TS], op0=ALU.mult, op1=ALU.add),
                reads=[us, dsk, yb], writes=[yo])
            if jj == 1:
                c0 = ch * TS
                ysrc = dr["y_src"][c0 // 1024]
                P.dma("act", ysrc.t[:, c0 % 1024:c0 % 1024 + TS].rearrange("(j p) t -> p j t", p=128), yo[:, :, :], reads=[yo], writes=[ysrc])

    for j in range(nu + 2):
        if j < nu:
            s1(j)
        if 0 <= j - 1 < nu:
            s2(j - 1)
        if 0 <= j - 2 < nu:
            s3(j - 2)
    for k in range(8):
        P.coll("AllGather", dr["y_src"][k], dr["y_all"][k], GROUPS)
    P.wait_all("act", dr["y_all"])
    P.emit()


def gelu_tanh(P, dn, out_bf, y, reads):
    s = dn.tmp.get()
    s2 = dn.tmp.get()
    act(P, s[:, :], y, AF.Square, reads, [s])
    ts(P, "dve", s[:, :], s[:, :], 0.044715, 1.0, ALU.mult, ALU.add, [s], [s])
    tt(P, "dve", s2[:, :], s[:, :], y, ALU.mult, [s] + list(reads), [s2])
    act(P, s2[:, :], s2[:, :], AF.Sigmoid, [s2], [s2], scale=1.5957691216057308)
    return s2


def ple(P, dn, Wpg, Wpp, Xb, Pb, X1):
    for m in range(8):
        wg = dn.load_w(Wpg, m, 8)
        wp = dn.load_w(Wpp, m, 2)
        for n in range(dn.T // NT):
            pg = P.bank()
            pp = P.bank()
            for kc in range(8):
                P.op("pe", lambda E, pg=pg, wg=wg, kc=kc, n=n: E.matmul(
                    pg[:, :], lhsT=wg[:, kc, :], rhs=Xb[:, kc, n * NT:(n + 1) * NT], start=(kc == 0), stop=(kc == 7)),
                    reads=[wg, Xb], writes=[pg])
            for kc in range(2):
                P.op("pe", lambda E, pp=pp, wp=wp, kc=kc, n=n: E.matmul(
                    pp[:, :], lhsT=wp[:, kc, :], rhs=Pb[:, kc, n * NT:(n + 1) * NT], start=(kc == 0), stop=(kc == 1)),
                    reads=[wp, Pb], writes=[pp])
            sg = dn.tmp.get()
            act(P, sg[:, :], pg[:, :], AF.Sigmoid, [pg], [sg])
            t = dn.tmp.get()
            tt(P, "dve", t[:, :], sg[:, :], pp[:, :], ALU.mult, [sg, pp], [t])
            tt(P, "pool", X1[:, m, n * NT:(n + 1) * NT], X1[:, m, n * NT:(n + 1) * NT], t[:, :], ALU.add, [X1, t], [X1])


NH = 4
NQT = SEQ // NT


def phase_D(nc, dr):
    P = Prog(nc)
    qT, kT, vT, tri_d = dr["qT_cs"], dr["kT_cs"], dr["vT_cs"], dr["ntri"]
    ident = P.sb("ident", [128, 128], BF16)
    P.op("pool", lambda E: E.memset(ident[:, :], 0.0), writes=[ident])
    P.op("pool", lambda E: E.affine_select(out=ident[:, :], in_=ident[:, :], pattern=[[-1, 128]], compare_op=ALU.not_equal,
                                           fill=1.0, base=0, channel_multiplier=1), reads=[ident], writes=[ident])
    vtb = Rot(P, "vtb", 2, [64, 2048], BF16)
    tpb = P.ps("tpb", [128, NT], BF16)
    zb = [P.ps("zb%d" % i, [128, NT], F32) for i in range(4)]
    ob = [P.ps("ob%d" % i, [64, NT], F32) for i in range(2)]
    st = P.sb("tri_st", [128, 128], F32)
    P.dma("sp", st[:, :], tri_d.t[:, :], reads=[tri_d], writes=[st])
    ntri = P.sb("ntri", [128, 128], BF16)
    cp(P, "dve", ntri[:, :], st[:, :], [st], [ntri])
    nones = P.sb("nones", [128, 128], BF16)
    P.op("pool", lambda E: E.memset(nones[:, :], -1.0), writes=[nones])
    Qb = P.sb("Qb", [128, 2, SEQ], BF16)
    Kb = P.sb("Kb", [128, 2, SEQ], BF16)
    Vb = [P.sb("Vb%d" % h, [128, 64 * 64], BF16) for h in range(NH)]
    ldst = Rot(P, "ldst", 2, [128, 2048], F32)
    er = Rot(P, "er", 3, [128, NT], F32)
    spr = Rot(P, "spr", 4, [128, NT], BF16)
    wr = Rot(P, "wr", 4, [128, NT], BF16)
    racc = Rot(P, "racc", 3, [128, NT], BF16)
    ost = Rot(P, "ost", 2, [64, NT], BF16)
    for pr in range(2):
        for c in range(SEQ // 2048):
            s = ldst.get()
            P.dma("sp", s[:, :], qT.t[pr * 128:(pr + 1) * 128, c * 2048:(c + 1) * 2048], reads=[qT], writes=[s])
            cp(P, "pool", Qb[:, pr, c * 2048:(c + 1) * 2048], s[:, :], [s], [Qb])
            s = ldst.get()
            P.dma("sp", s[:, :], kT.t[pr * 128:(pr + 1) * 128, c * 2048:(c + 1) * 2048], reads=[kT], writes=[s])
            ts(P, "dve", Kb[:, pr, c * 2048:(c + 1) * 2048], s[:, :], 0.125, None, ALU.mult, None, [s], [Kb])
    for h in range(NH):
        for c in range(SEQ // 2048):
            s = ldst.get()
            P.dma("sp", s[0:64, :], vT.t[h * 64:(h + 1) * 64, c * 2048:(c + 1) * 2048], reads=[vT], writes=[s])
            vb_ = vtb.get()
            cp(P, "pool", vb_[:, :], s[0:64, :], [s], [vb_])
            for k8 in range(2):
                for j in range(8):
                    blk = k8 * 8 + j
                    P.op("pe", lambda E, vb_=vb_, j=j, blk=blk: E.transpose(
                        out=tpb[:, j * 64:(j + 1) * 64], in_=vb_[:, blk * 128:(blk + 1) * 128], identity=ident[0:64, 0:64]),
                        reads=[vb_, ident], writes=[tpb])
                kb0 = c * 16 + k8 * 8
                cp(P, "dve", Vb[h][:, kb0 * 64:(kb0 + 8) * 64], tpb[:, :], [tpb], [Vb[h]])
    blocks = []
    for h in range(NH):
        for qt in range(NQT):
            kbs = list(range(4 * qt + 3, -1, -1))
            for idx, kb in enumerate(kbs):
                blocks.append(dict(h=h, qt=qt, kb=kb, idx=idx, n=len(kbs), g=h * NQT + qt))
    nb = len(blocks)

    def operands(B):
        hp, pr = B["h"] % 2, B["h"] // 2
        ksl = Kb[hp * 64:(hp + 1) * 64, pr, B["kb"] * 128:(B["kb"] + 1) * 128]
        qsl = Qb[hp * 64:(hp + 1) * 64, pr, B["qt"] * NT:(B["qt"] + 1) * NT]
        return ksl, qsl

    def mask(t, B):
        base = B["qt"] * NT - 128 * B["kb"]
        P.op("pool", lambda E, t=t, base=base: E.affine_select(
            out=t[:, :], in_=t[:, :], pattern=[[1, NT]], compare_op=ALU.is_gt, fill=0.0, base=base, channel_multiplier=-1),
            reads=[t], writes=[t])

    def stage1a(i):
        B = blocks[i]
        ksl, qsl = operands(B)
        z = zb[i % 4]
        P.op("pe", lambda E, z=z, ksl=ksl, qsl=qsl: E.matmul(z[:, :], lhsT=ksl, rhs=qsl, start=True, stop=False), reads=[Kb, Qb], writes=[z])
        e = er.get()
        act(P, e[:, :], z[:, :], AF.Exp, [z], [e])
        B["e"] = e

    def stage1b(i):
        B = blocks[i]
        e = B["e"]
        sp = spr.get()
        P.op("act", lambda E, sp=sp, e=e: E.activation(out=sp[:, :], in_=e[:, :], func=AF.Ln, scale=1.0, bias=1.0),
             reads=[e], writes=[sp], skip_same=True)
        if B["kb"] >= 4 * B["qt"]:
            mask(sp, B)
        B["sp"] = sp

    def stage2(i):
        B = blocks[i]
        ksl, qsl = operands(B)
        b = zb[i % 4]
        sp = B["sp"]
        first = B["idx"] == 0
        ra_prev = None if first else blocks[i - 1]["ra"]
        P.op("pe", lambda E, b=b, sp=sp, first=first: E.matmul(b[:, :], lhsT=ntri[:, :], rhs=sp[:, :], start=False, stop=first), reads=[ntri, sp], writes=[b])
        if not first:
            P.op("pe", lambda E, b=b, ra=ra_prev: E.matmul(b[:, :], lhsT=nones[:, :], rhs=ra[:, :], start=False, stop=True), reads=[nones, ra_prev], writes=[b])
        w = wr.get()
        act(P, w[:, :], b[:, :], AF.Exp, [b], [w])
        if B["kb"] >= 4 * B["qt"]:
            mask(w, B)
        B["w"] = w
        if B["idx"] + 1 < B["n"]:
            ra = racc.get()
            if first:
                cp(P, "dve", ra[:, :], sp[:, :], [sp], [ra])
            else:
                tt(P, "dve", ra[:, :], ra_prev[:, :], sp[:, :], ALU.add, [ra_prev, sp], [ra])
            B["ra"] = ra

    def stage3(i):
        B = blocks[i]
        o_ps = ob[B["g"] % 2]
        w = B["w"]
        h, kb = B["h"], B["kb"]
        P.op("pe", lambda E, o_ps=o_ps, w=w, h=h, kb=kb, B=B: E.matmul(
            o_ps[:, :], lhsT=Vb[h][:, kb * 64:(kb + 1) * 64], rhs=w[:, :], start=(B["idx"] == 0), stop=(B["idx"] == B["n"] - 1)),
            reads=[Vb[h], w], writes=[o_ps])
        if B["idx"] == B["n"] - 1:
            o = ost.get()
            cp(P, "dve", o[:, :], o_ps[:, :], [o_ps], [o])
            q0 = B["qt"] * NT
            osrc = dr["o_src"][q0 // 2048]
            P.dma("act", osrc.t[h * 64:(h + 1) * 64, q0 % 2048:q0 % 2048 + NT], o[:, :], reads=[o], writes=[osrc])

    for j in range(nb + 2):
        if j < nb:
            stage1a(j)
            stage1b(j)
        if 0 <= j - 1 < nb:
            stage2(j - 1)
        if 0 <= j - 2 < nb:
            stage3(j - 2)
    for k in range(4):
        P.coll("AllGather", dr["o_src"][k], dr["o_all"][k], GROUPS)
    P.wait_all("act", dr["o_all"])
    P.emit()


GROUPS = [[0, 1, 2, 3], [4, 5, 6, 7]]
TOKC = 2048
TP = 1024


def proj_pass(P, dn, src_fn, xs, rs, jobs, col0):
    for kc in range(8):
        srcs = src_fn(kc)
        if isinstance(srcs, tuple):
            P.dma("sp", xs[:, kc, :], srcs[1], reads=[srcs[0]], writes=[xs])
        else:
            w_ = TP // len(srcs)
            for i_, (sr_, sa_) in enumerate(srcs):
                P.dma("sp", xs[:, kc, i_ * w_:(i_ + 1) * w_], sa_, reads=[sr_], writes=[xs])
    dn.rstd(xs, rs)
    done = {}
    for (xb, gain, wbs, MC, dst) in jobs:
        if id(xb) not in done:
            done[id(xb)] = 1
            for kc in range(8):
                scale_cast(P, kc, xb[:, kc, :], xs[:, kc, :], gain[:, kc:kc + 1], [xs, gain], [xb])

        def epi(m, n, ps, dst=dst):
            o = dn.ost.get()
            tt(P, "dve", o[:, :], ps[:, :], rs[:, n * NT:(n + 1) * NT], ALU.mult, [ps, rs], [o])
            dn.store(dst, m, n, col0, o, o[:, :])
        dn.linear(None, 8, MC, xb, epi, wbs=wbs)


def phase_A(nc, dr):
    P = Prog(nc)
    P.mk_banks(6)
    dn = Dense(P, TP)
    g = colvec(P, "g_pre", dr["g_pre"], 8)
    xs = P.sb("xs", [128, 8, TP], F32)
    xb = P.sb("xb", [128, 8, TP], BF16)
    rs = P.sb("rs", [128, TP], F32)
    xT = dr["xT_full"]
    wbs = dn.prep_w("wu", dr["w_in_u"], 8, 2)
    for pa in range(SEQ // TP):
        t0 = pa * TP
        proj_pass(P, dn, lambda kc, t0=t0: (xT, xT.t[kc * 128:(kc + 1) * 128, t0:t0 + TP]), xs, rs,
                  [(xb, g, wbs, 2, dr["uT_cs"])], t0)
    P.wait_all("act", [dr["uT_cs"]])
    P.emit()


def select4(P, dn, src_fn, sel, dt):
    acc = dn.tmp.get()
    for s in range(4):
        it = dn.ist.get() if dt == F32 else dn.istb.get()
        sres, sap = src_fn(s)
        P.dma("sp", it[:, :], sap, reads=[sres], writes=[it])
        if s == 0:
            ts(P, "dve", acc[:, :], it[:, :], sel[:, 0:1], None, ALU.mult, None, [it, sel], [acc])
        else:
            P.op("dve", lambda E, it=it, s=s, acc=acc: E.scalar_tensor_tensor(
                out=acc[:, :], in0=it[:, :], scalar=sel[:, s:s + 1], in1=acc[:, :], op0=ALU.mult, op1=ALU.add),
                reads=[it, sel, acc], writes=[acc])
    return acc


def phase_C(nc, dr):
    P = Prog(nc)
    P.mk_banks(7)
    dn = Dense(P, TP)
    dn.istb = Rot(P, "istb", 4, [128, NT], BF16)
    sel = colvec(P, "sel", dr["sel"], 4)
    gpre = colvec(P, "g_pre", dr["g_pre"], 8)
    vl = []
    for i in range(4):
        t_ = P.sb("vec%d" % i, [128, 8], F32)
        P.dma("sp", t_[:, :], dr["vecsC"].t[:, i * 8:(i + 1) * 8], reads=[dr["vecsC"]], writes=[t_])
        vl.append(t_)
    bglu, gpost, gbpre, _unused = vl
    xT, y_all, pT = dr["xT_own"], dr["y_all"], dr["p0T"]
    Gb = P.sb("Gb", [128, 8, TP], BF16)
    SGb = P.sb("SGb", [128, 8, TP], BF16)
    Y2b = P.sb("Y2b", [128, 8, TP], BF16)
    X1 = P.sb("X1", [128, 8, TP], F32)
    Pb = P.sb("Pb", [128, 2, TP], BF16)
    rs = P.sb("rs", [128, TP], F32)
    for pa in range(TOKC // TP):
        t0 = pa * TP
        for kc in range(8):
            P.dma("sp", X1[:, kc, :], xT.t[kc * 128:(kc + 1) * 128, t0:t0 + TP], reads=[xT], writes=[X1])
        dn.rstd(X1, rs)
        for kc in range(8):
            scale_cast(P, kc, Y2b[:, kc, :], X1[:, kc, :], gpre[:, kc:kc + 1], [X1, gpre], [Y2b])

        def epi_gate(m, n, ps):
            t = dn.tmp.get()
            tt(P, "dve", t[:, :], ps[:, :], rs[:, n * NT:(n + 1) * NT], ALU.mult, [ps, rs], [t])
            act(P, SGb[:, m, n * NT:(n + 1) * NT], t[:, :], AF.Silu, [t], [SGb])
        dn.linear(dr["w_in_g"], 8, 8, Y2b, epi_gate)
        for kc in range(8):
            for n in range(TP // NT):
                def ysrc_fn(s, n=n, kc=kc):
                    g0 = s * TOKC + t0 + n * NT
                    ya = y_all[g0 // 1024]
                    return ya, ya.t[kc * 128:(kc + 1) * 128, g0 % 1024:g0 % 1024 + NT]
                y = select4(P, dn, ysrc_fn, sel, F32)
                s2 = gelu_tanh(P, dn, None, y[:, :], [y])
                tt(P, "pool", Gb[:, kc, n * NT:(n + 1) * NT], s2[:, :], y[:, :], ALU.mult, [s2, y], [Gb])

        def epi_glu(m, n, ps):
            sg = dn.tmp.get()
            act(P, sg[:, :], ps[:, :], AF.Sigmoid, [ps, bglu], [sg], bias=bglu[:, m:m + 1])
            t = dn.tmp.get()
            tt(P, "dve", t[:, :], sg[:, :], Gb[:, m, n * NT:(n + 1) * NT], ALU.mult, [sg, Gb], [t])
            tt(P, "pool", Y2b[:, m, n * NT:(n + 1) * NT], t[:, :], SGb[:, m, n * NT:(n + 1) * NT], ALU.mult, [t, SGb], [Y2b])
        dn.linear(dr["w_glu"], 8, 8, Gb, epi_glu)

        def epi_out(m, n, ps):
            act(P, X1[:, m, n * NT:(n + 1) * NT], ps[:, :], AF.Copy, [ps], [X1])
        dn.linear(dr["w_out0"], 8, 8, Y2b, epi_out)
        dn.rstd(X1, rs)
        for kc in range(8):
            for n in range(TP // NT):
                c0 = t0 + n * NT
                ix = dn.ist.get()
                P.dma("sp", ix[:, :], xT.t[kc * 128:(kc + 1) * 128, c0:c0 + NT], reads=[xT], writes=[ix])
                t = dn.tmp.get()
                P.op("dve", lambda E, t=t, kc=kc, n=n: E.scalar_tensor_tensor(
                    out=t[:, :], in0=X1[:, kc, n * NT:(n + 1) * NT], scalar=gpost[:, kc:kc + 1], in1=rs[:, n * NT:(n + 1) * NT],
                    op0=ALU.mult, op1=ALU.mult), reads=[X1, gpost, rs], writes=[t])
                tt(P, "pool", X1[:, kc, n * NT:(n + 1) * NT], t[:, :], ix[:, :], ALU.add, [t, ix], [X1])
                cp(P, "act", Gb[:, kc, n * NT:(n + 1) * NT], X1[:, kc, n * NT:(n + 1) * NT], [X1], [Gb])
        for kc in range(2):
            for n in range(TP // NT):
                c0 = t0 + n * NT
                ip = dn.ist.get()
                P.dma("sp", ip[:, :], pT.t[kc * 128:(kc + 1) * 128, c0:c0 + NT], reads=[pT], writes=[ip])
                cp(P, "dve", Pb[:, kc, n * NT:(n + 1) * NT], ip[:, :], [ip], [Pb])
        ple(P, dn, dr["w_pg0"], dr["w_pp0"], Gb, Pb, X1)
        dn.rstd(X1, rs)
        for kc in range(8):
            cp(P, "dve", Y2b[:, kc, :], X1[:, kc, :], [X1], [Y2b])
            scale_cast(P, kc + 1, SGb[:, kc, :], X1[:, kc, :], gbpre[:, kc:kc + 1], [X1, gbpre], [SGb])
            P.dma("act", dr["x1_own"].t[kc * 128:(kc + 1) * 128, t0:t0 + TP], X1[:, kc, :], reads=[X1], writes=[dr["x1_own"]])
            for n in range(TP // NT):
                xsrc = dr["x1_src"][(t0 + n * NT) // NT]
                P.dma("act", xsrc.t[kc * 128:(kc + 1) * 128, :], Y2b[:, kc, n * NT:(n + 1) * NT], reads=[Y2b], writes=[xsrc])

        def epi_g1(m, n, ps):
            o = dn.ost.get()
            tt(P, "dve", o[:, :], ps[:, :], rs[:, n * NT:(n + 1) * NT], ALU.mult, [ps, rs], [o])
            dn.store(dr["g1T"], m, n, t0, o, o[:, :])
        dn.linear(dr["w_bin_g"], 8, 8, SGb, epi_g1)
    for k in range(4):
        P.coll("AllGather", dr["x1_src"][k], dr["x1_all"][k], GROUPS)
    P.wait_all("act", dr["x1_all"] + [dr["x1_own"], dr["g1T"]])
    P.emit()


def phase_QKV(nc, dr):
    P = Prog(nc)
    P.mk_banks(6)
    dn = Dense(P, TP)
    gkv = colvec(P, "g_kv", dr["g_kv"], 8)
    gbpre = colvec(P, "g_bpre", dr["g_bpre"], 8)
    xs = P.sb("xs", [128, 8, TP], BF16)
    xq = P.sb("xq", [128, 8, TP], BF16)
    xk = P.sb("xk", [128, 8, TP], BF16)
    rs = P.sb("rs", [128, TP], F32)
    xa = dr["x1_all"]
    wq = dn.prep_w("wq", dr["w_q"], 8, 2)
    wk = dn.prep_w("wk", dr["w_k"], 8, 2)
    wv = dn.prep_w("wv", dr["w_v"], 8, 2)
    for pa in range(SEQ // TP):
        t0 = pa * TP
        s, tl = t0 // TOKC, t0 % TOKC
        proj_pass(P, dn, lambda kc, s=s, tl=tl: [(xa[(tl + h_ * NT) // NT], xa[(tl + h_ * NT) // NT].t[s * D + kc * 128:s * D + (kc + 1) * 128, :]) for h_ in range(TP // NT)], xs, rs,
                  [(xq, gbpre, wq, 2, dr["qT_cs"]), (xk, gkv, wk, 2, dr["kT_cs"]), (xk, gkv, wv, 2, dr["vT_cs"])], t0)
    P.wait_all("act", [dr["qT_cs"], dr["kT_cs"], dr["vT_cs"]])
    P.emit()


def phase_E(nc, dr):
    P = Prog(nc)
    P.mk_banks(7)
    dn = Dense(P, TP)
    dn.istb = Rot(P, "istb", 4, [128, NT], BF16)
    sel = colvec(P, "sel", dr["sel"], 4)
    gpost = colvec(P, "g_bpost", dr["g_bpost"], 8)
    Ob = P.sb("Ob", [128, 8, TP], BF16)
    Xb = P.sb("Xb", [128, 8, TP], BF16)
    X1 = P.sb("X1", [128, 8, TP], F32)
    Pb = P.sb("Pb", [128, 2, TP], BF16)
    rs = P.sb("rs", [128, TP], F32)
    x1T, gT, pT, outT = dr["x1_own"], dr["g1T"], dr["p1T"], dr["outT"]
    for pa in range(TOKC // TP):
        t0 = pa * TP
        for kc in range(8):
            for n in range(TP // NT):
                c0 = t0 + n * NT
                def osrc_fn(s, n=n, kc=kc):
                    g0 = s * TOKC + t0 + n * NT
                    oa = dr["o_all"][g0 // 2048]
                    return oa, oa.t[kc * 128:(kc + 1) * 128, g0 % 2048:g0 % 2048 + NT]
                o = select4(P, dn, osrc_fn, sel, BF16)
                ig = dn.ist.get()
                P.dma("sp", ig[:, :], gT.t[kc * 128:(kc + 1) * 128, c0:c0 + NT], reads=[gT], writes=[ig])
                sg = dn.tmp.get()
                act(P, sg[:, :], ig[:, :], AF.Silu, [ig], [sg])
                tt(P, "pool", Ob[:, kc, n * NT:(n + 1) * NT], sg[:, :], o[:, :], ALU.mult, [sg, o], [Ob])

        def epi_out(m, n, ps):
            act(P, X1[:, m, n * NT:(n + 1) * NT], ps[:, :], AF.Copy, [ps], [X1])
        dn.linear(dr["w_out1"], 8, 8, Ob, epi_out)
        dn.rstd(X1, rs)
        for kc in range(8):
            for n in range(TP // NT):
                c0 = t0 + n * NT
                ix = dn.ist.get()
                P.dma("sp", ix[:, :], x1T.t[kc * 128:(kc + 1) * 128, c0:c0 + NT], reads=[x1T], writes=[ix])
                t = dn.tmp.get()
                P.op("dve", lambda E, t=t, kc=kc, n=n: E.scalar_tensor_tensor(
                    out=t[:, :], in0=X1[:, kc, n * NT:(n + 1) * NT], scalar=gpost[:, kc:kc + 1], in1=rs[:, n * NT:(n + 1) * NT],
                    op0=ALU.mult, op1=ALU.mult), reads=[X1, gpost, rs], writes=[t])
                tt(P, "pool", X1[:, kc, n * NT:(n + 1) * NT], t[:, :], ix[:, :], ALU.add, [t, ix], [X1])
                cp(P, "act", Xb[:, kc, n * NT:(n + 1) * NT], X1[:, kc, n * NT:(n + 1) * NT], [X1], [Xb])
        for kc in range(2):
            for n in range(TP // NT):
                c0 = t0 + n * NT
                ip = dn.ist.get()
                P.dma("sp", ip[:, :], pT.t[kc * 128:(kc + 1) * 128, c0:c0 + NT], reads=[pT], writes=[ip])
                cp(P, "dve", Pb[:, kc, n * NT:(n + 1) * NT], ip[:, :], [ip], [Pb])
        ple(P, dn, dr["w_pg1"], dr["w_pp1"], Xb, Pb, X1)
        for kc in range(8):
            P.dma("act", outT.t[kc * 128:(kc + 1) * 128, t0:t0 + TP], X1[:, kc, :], reads=[X1], writes=[outT])
    P.wait_all("act", [outT])
    P.emit()


IN_SPECS = {
    "xT_full": ([D, SEQ], F32), "xT_own": ([D, TOKC], F32), "p0T": ([256, TOKC], F32), "p1T": ([256, TOKC], F32),
    "sel": ([128, 4], F32), "g_pre": ([128, 8], F32), "w_in_u": ([D, 256], F32), "w_in_g": ([D, D], F32),
    "lamre_T": ([128, 128], F32), "lamim_T": ([128, 128], F32), "logdt_T": ([128, 128], F32), "bre_T": ([128, 128], F32),
    "bim_T": ([128, 128], F32), "lamre_S": ([128, NG], F32), "lamim_S": ([128, NG], F32), "logdt_S": ([128, NG], F32),
    "c1_S": ([128, NG * 16], F32), "c2_S": ([128, NG * 16], F32), "iota": ([128, TS + 1], F32), "gmask": ([128, 8], F32),
    "sgn": ([128, 1], F32), "Jm": ([128, 128], F32), "dskip_cs": ([128, 2], F32), "vecsC": ([128, 32], F32),
    "w_glu": ([D, D], F32), "w_out0": ([D, D], F32), "w_pg0": ([D, D], F32), "w_pp0": ([256, D], F32),
    "w_bin_g": ([D, D], F32), "g_kv": ([128, 8], F32), "g_bpre": ([128, 8], F32), "w_q": ([D, 256], F32),
    "w_k": ([D, 256], F32), "w_v": ([D, 256], F32), "ntri": ([128, 128], F32), "g_bpost": ([128, 8], F32),
    "w_out1": ([D, D], F32), "w_pg1": ([D, D], F32), "w_pp1": ([256, D], F32),
}
SCRATCH = {
    "uT_cs": ([256, SEQ], F32), "y_src": ([256, 1024], F32, 8), "y_all": ([D, 1024], F32, 8), "x1_own": ([D, TOKC], F32),
    "x1_src": ([D, NT], BF16, 4), "x1_all": ([4 * D, NT], BF16, 4), "g1T": ([D, TOKC], F32), "qT_cs": ([256, SEQ], F32),
    "kT_cs": ([256, SEQ], F32), "vT_cs": ([256, SEQ], F32), "o_src": ([256, 2048], BF16, 4), "o_all": ([D, 2048], BF16, 4),
}


def build_fused():
    nc = bass.Bass("TRN2", target_bir_lowering=False)
    dr = {}
    for n, (shp, dt) in IN_SPECS.items():
        dr[n] = Res(n, nc.dram_tensor(n, list(shp), dt, kind="ExternalInput").ap())
    for n, spec in SCRATCH.items():
        shp, dt = spec[0], spec[1]
        if len(spec) == 3:
            dr[n] = [Res("%s%d" % (n, i), nc.dram_tensor("%s%d" % (n, i), list(shp), dt, kind="Internal").ap()) for i in range(spec[2])]
        else:
            dr[n] = Res(n, nc.dram_tensor(n, list(shp), dt, kind="Internal").ap())
    dr["outT"] = Res("outT", nc.dram_tensor("outT", [D, TOKC], F32, kind="ExternalOutput").ap())
    phase_A(nc, dr)
    phase_B(nc, dr)
    phase_C(nc, dr)
    phase_QKV(nc, dr)
    phase_D(nc, dr)
    phase_E(nc, dr)
    Prog.finish()
    return nc


def _f(a):
    return np.ascontiguousarray(np.asarray(a, dtype=np.float32))


def kernel(**inputs):
    inp = {k: np.asarray(v) for k, v in inputs.items()}
    x, p = inp["x"], inp["p"]
    lam_re, lam_im, log_dt = _f(inp["a_lam_re"][0]), _f(inp["a_lam_im"][0]), _f(inp["a_log_dt"][0])
    b_re, b_im, c_re, c_im = _f(inp["a_b_re"][0]), _f(inp["a_b_im"][0]), _f(inp["a_c_re"][0]), _f(inp["a_c_im"][0])
    iota = _f(np.broadcast_to(np.arange(TS + 1, dtype=np.float32), (128, TS + 1)))
    gmask = np.zeros((128, 8), np.float32)
    for gl in range(8):
        gmask[gl * 16:(gl + 1) * 16, gl] = 1.0
    sgn = np.ones((128, 1), np.float32)
    sgn[64:] = -1.0
    J = np.zeros((128, 128), np.float32)
    for q in range(64):
        J[64 + q, q] = -1.0
        J[q, 64 + q] = 1.0
    ntri = np.zeros((128, 128), np.float32)
    for j in range(128):
        ntri[j, :j + 1] = -1.0
    vecsC = np.zeros((128, 32), np.float32)
    for i, v in enumerate([inp["a_b_glu"][0], inp["a_norm_post"][0], inp["b_norm_pre"][0]]):
        vecsC[:, i * 8:(i + 1) * 8] = _cols(v)
    a_w_in, b_w_in, w_kv = _f(inp["a_w_in"][0]), _f(inp["b_w_in"][0]), _f(inp["w_kv"])
    common = {
        "g_pre": _cols(inp["a_norm_pre"][0]), "w_in_g": _f(a_w_in[:, D:]), "iota": iota, "gmask": gmask, "sgn": sgn, "Jm": J,
        "vecsC": vecsC, "w_glu": _f(inp["a_w_glu"][0]), "w_out0": _f(inp["a_w_out"][0]), "w_pg0": _f(inp["ple_w_gate"][0]),
        "w_pp0": _f(inp["ple_w_proj"][0]), "w_bin_g": _f(b_w_in[:, D:]), "g_kv": _cols(inp["kv_norm"]),
        "g_bpre": _cols(inp["b_norm_pre"][0]), "ntri": ntri, "g_bpost": _cols(inp["b_norm_post"][0]),
        "w_out1": _f(inp["b_w_out"][0]), "w_pg1": _f(inp["ple_w_gate"][1]), "w_pp1": _f(inp["ple_w_proj"][1]),
    }
    xT_full = [_f(np.asarray(x[b], np.float32).T) for b in range(2)]
    maps = []
    for c in range(NCORE):
        b, r = c // 4, c % 4
        gs = np.arange(16 * r, 16 * r + 16)
        tsl = slice(r * TOKC, (r + 1) * TOKC)
        csl = slice(256 * r, 256 * r + 256)
        m = dict(common)
        m["xT_full"] = xT_full[b]
        m["xT_own"] = _f(xT_full[b][:, tsl])
        m["p0T"] = _f(np.asarray(p[0, b, tsl, :], np.float32).T)
        m["p1T"] = _f(np.asarray(p[1, b, tsl, :], np.float32).T)
        sel = np.zeros((128, 4), np.float32)
        sel[:, r] = 1.0
        m["sel"] = sel
        m["w_in_u"] = _f(a_w_in[:, csl])
        m["dskip_cs"] = _cols(inp["a_d_skip"][0][csl])
        m["w_q"] = _f(b_w_in[:, csl])
        m["w_k"] = _f(w_kv[:, csl])
        m["w_v"] = _f(w_kv[:, D + 256 * r:D + 256 * r + 256])

        def lt_gp(a):
            t = a[gs].reshape(2, 8, 64)
            t = np.broadcast_to(t[:, :, None, :], (2, 8, 16, 64))
            return _f(t.transpose(1, 2, 0, 3).reshape(128, 128))

        def lt_b(a):
            t = a[gs].reshape(2, 8, 64, 16)
            return _f(t.transpose(1, 3, 0, 2).reshape(128, 128))

        def sp_gp(a):
            t = a[gs].T
            return _f(np.concatenate([t, t], axis=0))
        ldt = np.broadcast_to(log_dt[:, None], (64, 64))
        m["lamre_T"], m["lamim_T"], m["logdt_T"] = lt_gp(lam_re), lt_gp(lam_im), lt_gp(ldt)
        m["bre_T"], m["bim_T"] = lt_b(b_re), lt_b(b_im)
        m["lamre_S"], m["lamim_S"], m["logdt_S"] = sp_gp(lam_re), sp_gp(lam_im), sp_gp(ldt)
        cr = c_re[gs].transpose(2, 0, 1).reshape(64, 256)
        ci = c_im[gs].transpose(2, 0, 1).reshape(64, 256)
        m["c1_S"] = _f(np.concatenate([cr, ci], axis=0))
        m["c2_S"] = _f(np.concatenate([ci, cr], axis=0))
        maps.append(m)
    res = _run(build_fused(), maps)
    out = np.empty((2, SEQ, D), np.float32)
    for c in range(NCORE):
        b, r = c // 4, c % 4
        out[b, r * TOKC:(r + 1) * TOKC, :] = res[c]["outT"].T
    return out
```

```python
from contextlib import ExitStack
import numpy as np
import concourse.bass as bass
import concourse.mybir as mybir
from concourse.bass_utils import run_bass_kernel_spmd

F32 = mybir.dt.float32
BF16 = mybir.dt.bfloat16
AF = mybir.ActivationFunctionType
ALU = mybir.AluOpType

ENGS = ("pe", "act", "dve", "pool", "sp")
NCORE = 8
D = 1024
SEQ = 8192
NT = 512
EPS = 1e-6
PI = float(np.pi)


class Res:
    __slots__ = ("name", "w", "r", "dsem", "dcnt", "t")

    def __init__(self, name, t=None):
        self.name = name
        self.w = {}
        self.r = {}
        self.dsem = None
        self.dcnt = 0
        self.t = t

    def __getitem__(self, idx):
        return self.t[idx]


class Prog:
    _n = 0
    G = None

    def __init__(self, nc):
        Prog._n += 1
        self.pfx = "f%d_" % Prog._n
        self.nc = nc
        if Prog.G is None or Prog.G["nc"] is not nc:
            ges = ExitStack()
            Prog.G = {"nc": nc, "es": ges, "sems": {}, "cnt": {}}
            for e in ENGS:
                Prog.G["sems"][e] = ges.enter_context(nc.semaphore("s_" + e))
                Prog.G["cnt"][e] = 0
        G = Prog.G
        self.es = ExitStack()
        self.lists = {e: [] for e in ENGS}
        self.sems = G["sems"]
        self.cnt = G["cnt"]
        self.seen = {e: dict(self.cnt) for e in ENGS}
        self.nd = 0
        self.banks = []
        self.bi = 0
        self.touched = {}

    @staticmethod
    def finish():
        if Prog.G is not None:
            Prog.G["es"].close()
            Prog.G = None

    def _newsem(self):
        key = "d%d" % self.nd
        self.nd += 1
        if key not in self.sems:
            self.sems[key] = Prog.G["es"].enter_context(self.nc.semaphore("sd_" + key))
            self.cnt[key] = 0
        return key

    def sb(self, name, shape, dt):
        return Res(name, self.es.enter_context(self.nc.sbuf_tensor(self.pfx + "sb_" + name, list(shape), dt)))

    def ps(self, name, shape, dt=F32):
        return Res(name, self.es.enter_context(self.nc.psum_tensor(self.pfx + "ps_" + name, list(shape), dt)))

    def dram(self, name, shape, dt, kind="Internal"):
        return Res(name, self.nc.dram_tensor(name, list(shape), dt, kind=kind).ap())

    def mk_banks(self, n):
        self.banks = [self.ps("bank%d" % i, [128, NT], F32) for i in range(n)]

    def bank(self):
        b = self.banks[self.bi % len(self.banks)]
        self.bi += 1
        return b

    def _dsem(self, res):
        if res.dsem is None:
            res.dsem = self._newsem()
        return res.dsem

    def _waits(self, eng, reads, writes, skip_same=False):
        for x_ in reads:
            self.touched[id(x_)] = x_
        for x_ in writes:
            self.touched[id(x_)] = x_
        deps = {}
        for r in reads:
            for k, v in r.w.items():
                if skip_same and k == eng:
                    continue
                if v > deps.get(k, 0):
                    deps[k] = v
        for w in writes:
            for k, v in w.w.items():
                if k != eng and v > deps.get(k, 0):
                    deps[k] = v
            for k, v in w.r.items():
                if k != eng and v > deps.get(k, 0):
                    deps[k] = v
        seen = self.seen[eng]
        for k, v in deps.items():
            if v > seen.get(k, 0):
                seen[k] = v
                sem = self.sems[k]
                self.lists[eng].append(lambda E, sem=sem, v=v: E.wait_ge(sem, v))

    def op(self, eng, fn, reads=(), writes=(), skip_same=False):
        self._waits(eng, reads, writes, skip_same)
        self.cnt[eng] += 1
        n = self.cnt[eng]
        sem = self.sems[eng]
        self.lists[eng].append(lambda E, fn=fn, sem=sem: fn(E).then_inc(sem, 1))
        for r in reads:
            r.r[eng] = n
        for w in writes:
            w.w[eng] = n

    def dma(self, eng, out_ap, in_ap, reads=(), writes=()):
        wres = writes[0]
        self._waits(eng, reads, writes)
        key = self._dsem(wres)
        self.cnt[key] += 16
        v = self.cnt[key]
        sem = self.sems[key]
        self.lists[eng].append(
            lambda E, o=out_ap, i=in_ap, sem=sem: E.dma_start(out=o, in_=i).then_inc(sem, 16))
        for r in reads:
            r.r[key] = v
        wres.w[key] = v

    def coll(self, kind, src, dst, groups):
        self._waits("pool", [src], [dst])
        key = self._newsem()
        self.cnt[key] += 1
        v = self.cnt[key]
        sem = self.sems[key]
        self.lists["pool"].append(lambda E, sem=sem: E.collective_compute(
            kind, ALU.bypass, replica_groups=groups, ins=[src.t.opt()], outs=[dst.t.opt()]).then_inc(sem))
        src.r[key] = v
        dst.w[key] = v

    def wait_all(self, eng, ress):
        self._waits(eng, ress, ())

    def emit(self):
        L = self.lists
        for e in ENGS:
            for k, sem in self.sems.items():
                tgt = self.cnt[k]
                if k != e and tgt > self.seen[e].get(k, 0):
                    self.seen[e][k] = tgt
                    L[e].append(lambda E, sem=sem, tgt=tgt: E.wait_ge(sem, tgt))
        with self.nc.Block() as block:
            @block.tensor
            def _(E):
                for f in L["pe"]:
                    f(E)

            @block.scalar
            def _(E):
                for f in L["act"]:
                    f(E)

            @block.vector
            def _(E):
                for f in L["dve"]:
                    f(E)

            @block.gpsimd
            def _(E):
                for f in L["pool"]:
                    f(E)

            @block.sync
            def _(E):
                for f in L["sp"]:
                    f(E)
        self.es.close()
        for x_ in self.touched.values():
            x_.w = {}
            x_.r = {}
            x_.dsem = None


def tt(P, eng, out, in0, in1, op, reads, writes):
    P.op(eng, lambda E: E.tensor_tensor(out=out, in0=in0, in1=in1, op=op), reads=reads, writes=writes)


def ts(P, eng, out, in0, s1, s2, op0, op1, reads, writes):
    if s2 is None:
        P.op(eng, lambda E: E.tensor_scalar(out=out, in0=in0, scalar1=s1, scalar2=None, op0=op0), reads=reads, writes=writes)
    else:
        P.op(eng, lambda E: E.tensor_scalar(out=out, in0=in0, scalar1=s1, scalar2=s2, op0=op0, op1=op1), reads=reads, writes=writes)


def act(P, out, in_, func, reads, writes, scale=1.0, bias=None):
    if bias is None:
        P.op("act", lambda E: E.activation(out=out, in_=in_, func=func, scale=scale), reads=reads, writes=writes)
    else:
        P.op("act", lambda E: E.activation(out=out, in_=in_, func=func, scale=scale, bias=bias), reads=reads, writes=writes)


def scale_cast(P, i, out, in_, col, reads, writes):
    if i % 2:
        P.op("act", lambda E: E.activation(out=out, in_=in_, func=AF.Copy, scale=col), reads=reads, writes=writes)
    else:
        ts(P, "dve", out, in_, col, None, ALU.mult, None, reads, writes)


def cp(P, eng, out, in_, reads, writes):
    if eng == "act":
        P.op(eng, lambda E: E.activation(out=out, in_=in_, func=AF.Copy), reads=reads, writes=writes)
    else:
        P.op(eng, lambda E: E.tensor_copy(out=out, in_=in_), reads=reads, writes=writes)


class Rot:
    def __init__(self, P, name, n, shape, dt):
        self.bufs = [P.sb("%s%d" % (name, i), shape, dt) for i in range(n)]
        self.i = 0

    def get(self):
        b = self.bufs[self.i % len(self.bufs)]
        self.i += 1
        return b


class Dense:
    def __init__(self, P, T):
        self.P = P
        self.T = T
        self.wst = Rot(P, "wst", 4, [128, 8, 128], F32)
        self.wbf = Rot(P, "wbf", 4, [128, 8, 128], BF16)
        self.ost = Rot(P, "ost", 4, [128, NT], F32)
        self.ostb = Rot(P, "ostb", 4, [128, NT], BF16)
        self.ist = Rot(P, "ist", 4, [128, NT], F32)
        self.tmp = Rot(P, "tmp", 4, [128, NT], F32)
        self.ones = P.sb("ones", [128, 128], BF16)
        P.op("pool", lambda E: E.memset(self.ones[:], 1.0), writes=[self.ones])
        self.sq = P.sb("sq", [128, 8, T], BF16)

    def load_w(self, W, m, KC, rowscale=None, wb=None):
        P = self.P
        assert rowscale is None
        st = self.wst.get()
        if wb is None:
            wb = self.wbf.get()
        P.dma("sp", st[:, 0:KC, :], W.t[:, m * 128:(m + 1) * 128].rearrange("(kc p) m -> p kc m", p=128), reads=[W], writes=[st])
        self.wi = getattr(self, "wi", 0) + 1
        cp(P, "act" if self.wi % 2 else "pool", wb[:, 0:KC, :], st[:, 0:KC, :], [st], [wb])
        return wb

    def prep_w(self, name, W, KC, MC):
        wbs = []
        for m in range(MC):
            wb = self.P.sb("%s_w%d" % (name, m), [128, KC, 128], BF16)
            wbs.append(self.load_w(W, m, KC, wb=wb))
        return wbs

    def linear(self, W, KC, MC, a, epi, rowscale=None, wbs=None):
        P = self.P
        for m in range(MC):
            wb = wbs[m] if wbs is not None else self.load_w(W, m, KC, rowscale)
            for n in range(self.T // NT):
                ps = P.bank()
                for kc in range(KC):
                    P.op("pe", lambda E, ps=ps, wb=wb, kc=kc, n=n: E.matmul(
                        ps[:, :], lhsT=wb[:, kc, :], rhs=a[:, kc, n * NT:(n + 1) * NT], start=(kc == 0), stop=(kc == KC - 1)),
                        reads=[wb, a], writes=[ps])
                epi(m, n, ps)

    def rstd(self, src, out):
        P = self.P
        sq = self.sq
        for kc in range(8):
            act(P, sq[:, kc, :], src[:, kc, :], AF.Square, [src], [sq])
        for n in range(self.T // NT):
            ps = P.bank()
            for kc in range(8):
                P.op("pe", lambda E, ps=ps, kc=kc, n=n: E.matmul(
                    ps[:, :], lhsT=self.ones[:, :], rhs=sq[:, kc, n * NT:(n + 1) * NT], start=(kc == 0), stop=(kc == 7)),
                    reads=[self.ones, sq], writes=[ps])
            t = self.tmp.get()
            act(P, t[:, :], ps[:, :], AF.Ln, [ps], [t], scale=1.0 / D, bias=EPS)
            act(P, out[:, n * NT:(n + 1) * NT], t[:, :], AF.Exp, [t], [out], scale=-0.5)

    def store(self, dst, m, n, t0, src_res, src_ap):
        self.P.dma("pool", dst.t[m * 128:(m + 1) * 128, t0 + n * NT:t0 + (n + 1) * NT], src_ap, reads=[src_res], writes=[dst])

    def load_act(self, src, t0, dst, KC=8):
        for kc in range(KC):
            self.P.dma("sp", dst[:, kc, :], src.t[kc * 128:(kc + 1) * 128, t0:t0 + self.T], reads=[src], writes=[dst])


def colvec(P, name, dram_res, ncol):
    t = P.sb(name, [128, ncol], F32)
    P.dma("sp", t[:, :], dram_res.t[:, :], reads=[dram_res], writes=[t])
    return t


def _run(nc, in_maps):
    res = run_bass_kernel_spmd(nc, in_maps, core_ids=list(range(NCORE)))
    return res.results


def _cols(v):
    return np.ascontiguousarray(np.asarray(v, np.float32).reshape(-1, 128).T)


TS = 256
NG = 16
M_MAGIC = 12582912.0


def range_reduce(P, eng, out, in_, tmp, reads, writes, shift=0.0):
    res_in = reads
    if shift != 0.0:
        ts(P, eng, out, in_, shift, None, ALU.add, None, res_in, writes)
        in_ = out
        res_in = writes
    ts(P, eng, tmp[0], in_, 1.0 / (2 * PI), M_MAGIC, ALU.mult, ALU.add, res_in, [tmp[1]])
    ts(P, eng, tmp[0], tmp[0], -M_MAGIC, -2 * PI, ALU.add, ALU.mult, [tmp[1]], [tmp[1]])
    tt(P, eng, out, tmp[0], in_, ALU.add, [tmp[1]] + list(res_in), writes)
    ts(P, eng, out, out, 3.14159, -3.14159, ALU.min, ALU.max, writes, writes)


NAMES_T = ["lamre_T", "lamim_T", "logdt_T", "bre_T", "bim_T"]
NAMES_S = ["lamre_S", "lamim_S", "logdt_S"]


def phase_B(nc, dr):
    P = Prog(nc)
    uT = dr["uT_cs"]
    names_T = NAMES_T
    dT = {n: dr[n] for n in names_T}
    names_S = NAMES_S
    dS = {n: dr[n] for n in names_S}
    c1_d, c2_d, iota_d, gmask_d, sgn_d, J_d = dr["c1_S"], dr["c2_S"], dr["iota"], dr["gmask"], dr["sgn"], dr["Jm"]
    dsk = colvec(P, "dskip_cs", dr["dskip_cs"], 2)
    P.mk_banks(4)
    ybank = [P.ps("ybank%d" % i, [128, NT], F32) for i in range(2)]
    igb = P.ps("igb", [128, NT], F32)

    def ld(name, d, ncol):
        return colvec(P, name, d, ncol)

    lt = {n: ld("s_" + n, dT[n], 128) for n in names_T}
    cnt = [0]

    def newT(nm, ncol=128):
        cnt[0] += 1
        return P.sb("%s_%d" % (nm, cnt[0]), [128, ncol], F32)

    def derive(lamre, lamim, logdt, ncol, pfx):
        o = {}
        lr = newT(pfx + "lr", ncol)
        ts(P, "dve", lr[:, :], lamre[:, :], -1e-4, None, ALU.min, None, [lamre], [lr])
        dt = newT(pfx + "dt", ncol)
        act(P, dt[:, :], logdt[:, :], AF.Exp, [logdt], [dt])
        e = newT(pfx + "e", ncol)
        tt(P, "dve", e[:, :], lr[:, :], dt[:, :], ALU.mult, [lr, dt], [e])
        th = newT(pfx + "th", ncol)
        tt(P, "dve", th[:, :], lamim[:, :], dt[:, :], ALU.mult, [lamim, dt], [th])
        thr = newT(pfx + "thr", ncol)
        tmp = newT(pfx + "tmp", ncol)
        range_reduce(P, "dve", thr[:, :], th[:, :], (tmp[:, :], tmp), [th], [thr])
        thc = newT(pfx + "thc", ncol)
        range_reduce(P, "dve", thc[:, :], th[:, :], (tmp[:, :], tmp), [th], [thc], shift=PI / 2)
        mag = newT(pfx + "mag", ncol)
        act(P, mag[:, :], e[:, :], AF.Exp, [e], [mag])
        sn = newT(pfx + "sin", ncol)
        act(P, sn[:, :], thr[:, :], AF.Sin, [thr], [sn])
        cs = newT(pfx + "cos", ncol)
        act(P, cs[:, :], thc[:, :], AF.Sin, [thc], [cs])
        o.update(lr=lr, li=lamim, e=e, thr=thr, mag=mag, sin=sn, cos=cs)
        return o

    dl = derive(lt["lamre_T"], lt["lamim_T"], lt["logdt_T"], 128, "T")

    def mul(a, b, nm):
        t = newT(nm)
        tt(P, "dve", t[:, :], a[:, :], b[:, :], ALU.mult, [a, b], [t])
        return t

    def addsub(a, b, op, nm):
        t = newT(nm)
        tt(P, "dve", t[:, :], a[:, :], b[:, :], op, [a, b], [t])
        return t

    are = mul(dl["mag"], dl["cos"], "are")
    aim = mul(dl["mag"], dl["sin"], "aim")
    den = addsub(mul(dl["lr"], dl["lr"], "lr2"), mul(dl["li"], dl["li"], "li2"), ALU.add, "den")
    rden = newT("rden")
    P.op("dve", lambda E: E.reciprocal(out=rden[:, :], in_=den[:, :]), reads=[den], writes=[rden])
    nr = newT("nr")
    ts(P, "dve", nr[:, :], are[:, :], -1.0, None, ALU.add, None, [are], [nr])
    fre = mul(addsub(mul(nr, dl["lr"], "f1"), mul(aim, dl["li"], "f2"), ALU.add, "f3"), rden, "fre")
    fim = mul(addsub(mul(aim, dl["lr"], "f4"), mul(nr, dl["li"], "f5"), ALU.subtract, "f6"), rden, "fim")
    bbre = addsub(mul(fre, lt["bre_T"], "b1"), mul(fim, lt["bim_T"], "b2"), ALU.subtract, "bbre")
    bbim = addsub(mul(fre, lt["bim_T"], "b3"), mul(fim, lt["bre_T"], "b4"), ALU.add, "bbim")
    nbbim = newT("nbbim")
    ts(P, "dve", nbbim[:, :], bbim[:, :], -1.0, None, ALU.mult, None, [bbim], [nbbim])
    gmask = ld("gmask", gmask_d, 8)
    W1 = P.sb("W1pad", [128, NG, 128], BF16)
    W2 = P.sb("W2pad", [128, NG, 128], BF16)
    for jj in range(2):
        for gl in range(8):
            gg = jj * 8 + gl
            ms = gmask[:, gl:gl + 1]
            sl = slice(jj * 64, (jj + 1) * 64)
            ts(P, "dve", W1[:, gg, 0:64], bbre[:, sl], ms, None, ALU.mult, None, [bbre, gmask], [W1])
            ts(P, "dve", W1[:, gg, 64:128], bbim[:, sl], ms, None, ALU.mult, None, [bbim, gmask], [W1])
            ts(P, "dve", W2[:, gg, 0:64], nbbim[:, sl], ms, None, ALU.mult, None, [nbbim, gmask], [W2])
            ts(P, "dve", W2[:, gg, 64:128], bbre[:, sl], ms, None, ALU.mult, None, [bbre, gmask], [W2])

    ls_ = {n: ld("s_" + n, dS[n], NG) for n in names_S}
    ds = derive(ls_["lamre_S"], ls_["lamim_S"], ls_["logdt_S"], NG, "S")
    rho = ds["mag"]
    iota = ld("iota", iota_d, TS + 1)
    COS = P.sb("COS", [128, NG, TS + 1], F32)
    SIN = P.sb("SIN", [128, NG, TS + 1], F32)
    ARG = P.sb("ARG", [128, NG, TS + 1], F32)
    TMP = P.sb("TMPA", [128, NG, TS + 1], F32)
    Gg = [P.sb("Gg%d" % i, [128, TS], F32) for i in range(NG)]
    GT = [P.sb("GT%d" % i, [128, NG], F32) for i in range(2)]
    for gg in range(NG):
        ts(P, "dve", ARG[:, gg, :], iota[:, :], ds["thr"][:, gg:gg + 1], None, ALU.mult, None, [iota, ds["thr"]], [ARG])
    range_reduce(P, "dve", SIN[:, :, :], ARG[:, :, :], (TMP[:, :, :], TMP), [ARG], [SIN])
    range_reduce(P, "dve", COS[:, :, :], ARG[:, :, :], (TMP[:, :, :], TMP), [ARG], [COS], shift=PI / 2)
    act(P, SIN[:, :, :], SIN[:, :, :], AF.Sin, [SIN], [SIN])
    act(P, COS[:, :, :], COS[:, :, :], AF.Sin, [COS], [COS])
    c1 = ld("c1", c1_d, NG * 16)
    c2 = ld("c2", c2_d, NG * 16)
    sgn = ld("sgn", sgn_d, 1)
    L1 = P.sb("L1pad", [128, NG, 128], BF16)
    L2 = P.sb("L2pad", [128, NG, 128], BF16)
    P.op("pool", lambda E: E.memset(L1[:, :, :], 0.0), writes=[L1])
    P.op("pool", lambda E: E.memset(L2[:, :, :], 0.0), writes=[L2])
    for gg in range(NG):
        gl = gg % 8
        ts(P, "dve", L1[:, gg, gl * 16:(gl + 1) * 16], c1[:, gg * 16:(gg + 1) * 16], sgn[:, 0:1], None, ALU.mult, None, [c1, sgn], [L1])
        ts(P, "dve", L2[:, gg, gl * 16:(gl + 1) * 16], c2[:, gg * 16:(gg + 1) * 16], -1.0, None, ALU.mult, None, [c2], [L2])
    Jm = P.sb("Jm", [128, 128], F32)
    P.dma("sp", Jm[:, :], J_d.t[:, :], reads=[J_d], writes=[Jm])

    ust = Rot(P, "ust", 3, [128, 2, TS], F32)
    ubf = Rot(P, "ubf", 3, [128, 2, TS], BF16)
    t1r = Rot(P, "t1r", 4, [128, TS], F32)
    t2r = Rot(P, "t2r", 4, [128, TS], F32)
    xtr = Rot(P, "xtr", 4, [128, TS], F32)
    h1r = Rot(P, "h1r", 4, [128, TS], BF16)
    h2r = Rot(P, "h2r", 4, [128, TS], BF16)
    S0 = [P.sb("S0_%d" % i, [128, NG], F32) for i in range(2)]
    ys = Rot(P, "ys", 3, [128, 2, TS], F32)
    P.op("pool", lambda E: E.memset(S0[0][:, :], 0.0), writes=[S0[0]])
    nch = SEQ // TS
    units = []
    chunk_res = {}
    for ch in range(nch):
        for jj in range(2):
            for gl in range(8):
                units.append(dict(ch=ch, jj=jj, gl=gl, gg=jj * 8 + gl))
    nu = len(units)

    def chunk_setup(ch):
        c0 = ch * TS
        us = ust.get()
        ub = ubf.get()
        P.dma("sp", us[:, :, :], uT.t[:, c0:c0 + TS].rearrange("(j p) t -> p j t", p=128), reads=[uT], writes=[us])
        cp(P, "act", ub[:, :, :], us[:, :, :], [us], [ub])
        chunk_res[ch] = dict(us=us, ub=ub, yo=ys.get())

    def s1(i):
        U = units[i]
        ch, jj, gg = U["ch"], U["jj"], U["gg"]
        if jj == 0 and U["gl"] == 0:
            chunk_setup(ch)
        ub = chunk_res[ch]["ub"]
        p1 = P.bank()
        p2 = P.bank()
        P.op("pe", lambda E, p1=p1, gg=gg, ub=ub, jj=jj: E.matmul(p1[:, 0:TS], lhsT=W1[:, gg, :], rhs=ub[:, jj, :], start=True, stop=True), reads=[W1, ub], writes=[p1])
        P.op("pe", lambda E, p2=p2, gg=gg, ub=ub, jj=jj: E.matmul(p2[:, 0:TS], lhsT=W2[:, gg, :], rhs=ub[:, jj, :], start=True, stop=True), reads=[W2, ub], writes=[p2])
        t1 = t1r.get()
        t2 = t2r.get()
        xt = xtr.get()
        tt(P, "dve", t1[:, :], p1[:, 0:TS], COS[:, gg, 1:TS + 1], ALU.mult, [p1, COS], [t1])
        tt(P, "dve", t2[:, :], p2[:, 0:TS], SIN[:, gg, 1:TS + 1], ALU.mult, [p2, SIN], [t2])
        tt(P, "pool", xt[:, :], t1[:, :], t2[:, :], ALU.subtract, [t1, t2], [xt])
        U["xt"] = xt

    def s2(i):
        U = units[i]
        ch, gg = U["ch"], U["gg"]
        gt, s0 = GT[ch % 2], S0[ch % 2]
        xt = U["xt"]
        G = Gg[gg]
        P.op("dve", lambda E, G=G, gg=gg, xt=xt, s0=s0: E.tensor_tensor_scan(
            out=G[:, :], data0=rho[:, gg:gg + 1].to_broadcast([128, TS]), data1=xt[:, :],
            initial=s0[:, gg:gg + 1], op0=ALU.mult, op1=ALU.add), reads=[rho, xt, s0], writes=[G])
        if ch + 1 < nch:
            cp(P, "act", gt[:, gg:gg + 1], G[:, TS - 1:TS], [G], [gt])
        h1 = h1r.get()
        h2 = h2r.get()
        tt(P, "dve", h1[:, :], G[:, :], COS[:, gg, 1:TS + 1], ALU.mult, [G, COS], [h1])
        tt(P, "pool", h2[:, :], G[:, :], SIN[:, gg, 1:TS + 1], ALU.mult, [G, SIN], [h2])
        U["h1"], U["h2"] = h1, h2
        if gg == NG - 1 and ch + 1 < nch:
            s1_ = S0[(ch + 1) % 2]
            P.op("pe", lambda E, gt=gt: E.matmul(igb[:, 0:NG], lhsT=Jm[:, :], rhs=gt[:, :], start=True, stop=True), reads=[Jm, gt], writes=[igb])
            ta = P.sb("bta%d" % ch, [128, NG], F32)
            tb = P.sb("btb%d" % ch, [128, NG], F32)
            tt(P, "dve", ta[:, :], gt[:, :], COS[:, :, TS], ALU.mult, [gt, COS], [ta])
            tt(P, "dve", tb[:, :], igb[:, 0:NG], SIN[:, :, TS], ALU.mult, [igb, SIN], [tb])
            tt(P, "dve", s1_[:, :], ta[:, :], tb[:, :], ALU.add, [ta, tb], [s1_])

    def s3(i):
        U = units[i]
        ch, jj, gl, gg = U["ch"], U["jj"], U["gl"], U["gg"]
        yb = ybank[jj]
        h1, h2 = U["h1"], U["h2"]
        P.op("pe", lambda E, yb=yb, gg=gg, h1=h1, gl=gl: E.matmul(yb[:, 0:TS], lhsT=L1[:, gg, :], rhs=h1[:, :], start=(gl == 0), stop=False), reads=[L1, h1], writes=[yb])
        P.op("pe", lambda E, yb=yb, gg=gg, h2=h2, gl=gl: E.matmul(yb[:, 0:TS], lhsT=L2[:, gg, :], rhs=h2[:, :], start=False, stop=(gl == 7)), reads=[L2, h2], writes=[yb])
        if gl == 7:
            cr = chunk_res[ch]
            yo, us = cr["yo"], cr["us"]
            P.op("dve", lambda E, yo=yo, us=us, yb=yb, jj=jj: E.scalar_tensor_tensor(
                out=yo[:, jj, :], in0=us[:, jj, :], scalar=dsk[:, jj:jj + 1], in1=yb[:, 0:TS], op0=ALU.mult, op1=ALU.add),
                reads=[us, dsk, yb], writes=[yo])
            if jj == 1:
                c0 = ch * TS
                ysrc = dr["y_src"][c0 // 1024]
                P.dma("act", ysrc.t[:, c0 % 1024:c0 % 1024 + TS].rearrange("(j p) t -> p j t", p=128), yo[:, :, :], reads=[yo], writes=[ysrc])

    for j in range(nu + 2):
        if j < nu:
            s1(j)
        if 0 <= j - 1 < nu:
            s2(j - 1)
        if 0 <= j - 2 < nu:
            s3(j - 2)
    for k in range(8):
        P.coll("AllGather", dr["y_src"][k], dr["y_all"][k], GROUPS)
    P.wait_all("act", dr["y_all"])
    P.emit()


def gelu_tanh(P, dn, out_bf, y, reads):
    s = dn.tmp.get()
    s2 = dn.tmp.get()
    act(P, s[:, :], y, AF.Square, reads, [s])
    ts(P, "dve", s[:, :], s[:, :], 0.044715, 1.0, ALU.mult, ALU.add, [s], [s])
    tt(P, "dve", s2[:, :], s[:, :], y, ALU.mult, [s] + list(reads), [s2])
    act(P, s2[:, :], s2[:, :], AF.Sigmoid, [s2], [s2], scale=1.5957691216057308)
    return s2


def ple(P, dn, Wpg, Wpp, Xb, Pb, X1):
    for m in range(8):
        wg = dn.load_w(Wpg, m, 8)
        wp = dn.load_w(Wpp, m, 2)
        for n in range(dn.T // NT):
            pg = P.bank()
            pp = P.bank()
            for kc in range(8):
                P.op("pe", lambda E, pg=pg, wg=wg, kc=kc, n=n: E.matmul(
                    pg[:, :], lhsT=wg[:, kc, :], rhs=Xb[:, kc, n * NT:(n + 1) * NT], start=(kc == 0), stop=(kc == 7)),
                    reads=[wg, Xb], writes=[pg])
            for kc in range(2):
                P.op("pe", lambda E, pp=pp, wp=wp, kc=kc, n=n: E.matmul(
                    pp[:, :], lhsT=wp[:, kc, :], rhs=Pb[:, kc, n * NT:(n + 1) * NT], start=(kc == 0), stop=(kc == 1)),
                    reads=[wp, Pb], writes=[pp])
            sg = dn.tmp.get()
            act(P, sg[:, :], pg[:, :], AF.Sigmoid, [pg], [sg])
            t = dn.tmp.get()
            tt(P, "dve", t[:, :], sg[:, :], pp[:, :], ALU.mult, [sg, pp], [t])
            tt(P, "pool", X1[:, m, n * NT:(n + 1) * NT], X1[:, m, n * NT:(n + 1) * NT], t[:, :], ALU.add, [X1, t], [X1])


NH = 4
NQT = SEQ // NT


def phase_D(nc, dr):
    P = Prog(nc)
    qT, kT, vT, tri_d = dr["qT_cs"], dr["kT_cs"], dr["vT_cs"], dr["ntri"]
    ident = P.sb("ident", [128, 128], BF16)
    P.op("pool", lambda E: E.memset(ident[:, :], 0.0), writes=[ident])
    P.op("pool", lambda E: E.affine_select(out=ident[:, :], in_=ident[:, :], pattern=[[-1, 128]], compare_op=ALU.not_equal,
                                           fill=1.0, base=0, channel_multiplier=1), reads=[ident], writes=[ident])
    vtb = Rot(P, "vtb", 3, [64, 2048], BF16)
    tpb = P.ps("tpb", [128, NT], BF16)
    zb = [P.ps("zb%d" % i, [128, NT], F32) for i in range(4)]
    ob = [P.ps("ob%d" % i, [64, NT], F32) for i in range(2)]
    st = P.sb("tri_st", [128, 128], F32)
    P.dma("sp", st[:, :], tri_d.t[:, :], reads=[tri_d], writes=[st])
    ntri = P.sb("ntri", [128, 128], BF16)
    cp(P, "dve", ntri[:, :], st[:, :], [st], [ntri])
    nones = P.sb("nones", [128, 128], BF16)
    P.op("pool", lambda E: E.memset(nones[:, :], -1.0), writes=[nones])
    Qb = P.sb("Qb", [128, 2, SEQ], BF16)
    Kb = P.sb("Kb", [128, 2, SEQ], BF16)
    Vb = [P.sb("Vb%d" % h, [128, 64 * 64], BF16) for h in range(NH)]
    er = Rot(P, "er", 3, [128, NT], F32)
    spr = Rot(P, "spr", 4, [128, NT], BF16)
    wr = Rot(P, "wr", 4, [128, NT], BF16)
    racc = Rot(P, "racc", 3, [128, NT], BF16)
    ost = Rot(P, "ost", 2, [64, NT], BF16)
    for pr in range(2):
        for c in range(SEQ // 2048):
            P.dma("sp", Qb[:, pr, c * 2048:(c + 1) * 2048], qT.t[pr * 128:(pr + 1) * 128, c * 2048:(c + 1) * 2048], reads=[qT], writes=[Qb])
            P.dma("sp", Kb[:, pr, c * 2048:(c + 1) * 2048], kT.t[pr * 128:(pr + 1) * 128, c * 2048:(c + 1) * 2048], reads=[kT], writes=[Kb])
    for h in range(NH):
        for c in range(SEQ // 2048):
            vb_ = vtb.get()
            P.dma("sp", vb_[:, :], vT.t[h * 64:(h + 1) * 64, c * 2048:(c + 1) * 2048], reads=[vT], writes=[vb_])
            for k8 in range(2):
                for j in range(8):
                    blk = k8 * 8 + j
                    P.op("pe", lambda E, vb_=vb_, j=j, blk=blk: E.transpose(
                        out=tpb[:, j * 64:(j + 1) * 64], in_=vb_[:, blk * 128:(blk + 1) * 128], identity=ident[0:64, 0:64]),
                        reads=[vb_, ident], writes=[tpb])
                kb0 = c * 16 + k8 * 8
                cp(P, "dve", Vb[h][:, kb0 * 64:(kb0 + 8) * 64], tpb[:, :], [tpb], [Vb[h]])
    blocks = []
    for h in range(NH):
        for qt in range(NQT):
            kbs = list(range(4 * qt + 3, -1, -1))
            for idx, kb in enumerate(kbs):
                blocks.append(dict(h=h, qt=qt, kb=kb, idx=idx, n=len(kbs), g=h * NQT + qt))
    nb = len(blocks)

    def operands(B):
        hp, pr = B["h"] % 2, B["h"] // 2
        ksl = Kb[hp * 64:(hp + 1) * 64, pr, B["kb"] * 128:(B["kb"] + 1) * 128]
        qsl = Qb[hp * 64:(hp + 1) * 64, pr, B["qt"] * NT:(B["qt"] + 1) * NT]
        return ksl, qsl

    def mask(t, B):
        base = B["qt"] * NT - 128 * B["kb"]
        P.op("pool", lambda E, t=t, base=base: E.affine_select(
            out=t[:, :], in_=t[:, :], pattern=[[1, NT]], compare_op=ALU.is_gt, fill=0.0, base=base, channel_multiplier=-1),
            reads=[t], writes=[t])

    def stage1a(i):
        B = blocks[i]
        ksl, qsl = operands(B)
        z = zb[i % 4]
        P.op("pe", lambda E, z=z, ksl=ksl, qsl=qsl: E.matmul(z[:, :], lhsT=ksl, rhs=qsl, start=True, stop=False), reads=[Kb, Qb], writes=[z])
        e = er.get()
        act(P, e[:, :], z[:, :], AF.Exp, [z], [e])
        B["e"] = e

    def stage1b(i):
        B = blocks[i]
        e = B["e"]
        sp = spr.get()
        P.op("act", lambda E, sp=sp, e=e: E.activation(out=sp[:, :], in_=e[:, :], func=AF.Ln, scale=1.0, bias=1.0),
             reads=[e], writes=[sp], skip_same=True)
        if B["kb"] >= 4 * B["qt"]:
            mask(sp, B)
        B["sp"] = sp

    def stage2(i):
        B = blocks[i]
        ksl, qsl = operands(B)
        b = zb[i % 4]
        sp = B["sp"]
        first = B["idx"] == 0
        ra_prev = None if first else blocks[i - 1]["ra"]
        P.op("pe", lambda E, b=b, sp=sp, first=first: E.matmul(b[:, :], lhsT=ntri[:, :], rhs=sp[:, :], start=False, stop=first), reads=[ntri, sp], writes=[b])
        if not first:
            P.op("pe", lambda E, b=b, ra=ra_prev: E.matmul(b[:, :], lhsT=nones[:, :], rhs=ra[:, :], start=False, stop=True), reads=[nones, ra_prev], writes=[b])
        w = wr.get()
        act(P, w[:, :], b[:, :], AF.Exp, [b], [w])
        if B["kb"] >= 4 * B["qt"]:
            mask(w, B)
        B["w"] = w
        if B["idx"] + 1 < B["n"]:
            ra = racc.get()
            if first:
                cp(P, "dve", ra[:, :], sp[:, :], [sp], [ra])
            else:
                tt(P, "dve", ra[:, :], ra_prev[:, :], sp[:, :], ALU.add, [ra_prev, sp], [ra])
            B["ra"] = ra

    def stage3(i):
        B = blocks[i]
        o_ps = ob[B["g"] % 2]
        w = B["w"]
        h, kb = B["h"], B["kb"]
        P.op("pe", lambda E, o_ps=o_ps, w=w, h=h, kb=kb, B=B: E.matmul(
            o_ps[:, :], lhsT=Vb[h][:, kb * 64:(kb + 1) * 64], rhs=w[:, :], start=(B["idx"] == 0), stop=(B["idx"] == B["n"] - 1)),
            reads=[Vb[h], w], writes=[o_ps])
        if B["idx"] == B["n"] - 1:
            o = ost.get()
            cp(P, "dve", o[:, :], o_ps[:, :], [o_ps], [o])
            q0 = B["qt"] * NT
            osrc = dr["o_src"][q0 // 2048]
            P.dma("act", osrc.t[h * 64:(h + 1) * 64, q0 % 2048:q0 % 2048 + NT], o[:, :], reads=[o], writes=[osrc])

    for j in range(nb + 2):
        if j < nb:
            stage1a(j)
            stage1b(j)
        if 0 <= j - 1 < nb:
            stage2(j - 1)
        if 0 <= j - 2 < nb:
            stage3(j - 2)
    for k in range(4):
        P.coll("AllGather", dr["o_src"][k], dr["o_all"][k], GROUPS)
    P.wait_all("act", dr["o_all"])
    P.emit()


GROUPS = [[0, 1, 2, 3], [4, 5, 6, 7]]
TOKC = 2048
TP = 1024


def proj_pass(P, dn, src_fn, xs, rs, jobs, col0):
    for kc in range(8):
        srcs = src_fn(kc)
        if isinstance(srcs, tuple):
            P.dma("sp", xs[:, kc, :], srcs[1], reads=[srcs[0]], writes=[xs])
        else:
            w_ = TP // len(srcs)
            for i_, (sr_, sa_) in enumerate(srcs):
                P.dma("sp", xs[:, kc, i_ * w_:(i_ + 1) * w_], sa_, reads=[sr_], writes=[xs])
    dn.rstd(xs, rs)
    done = {}
    for job in jobs:
        xb, gain, wbs, MC, dst = job[:5]
        oscale = job[5] if len(job) > 5 else None
        if id(xb) not in done:
            done[id(xb)] = 1
            for kc in range(8):
                scale_cast(P, kc, xb[:, kc, :], xs[:, kc, :], gain[:, kc:kc + 1], [xs, gain], [xb])

        def epi(m, n, ps, dst=dst, oscale=oscale):
            if oscale is None:
                o = dn.ost.get()
                tt(P, "dve", o[:, :], ps[:, :], rs[:, n * NT:(n + 1) * NT], ALU.mult, [ps, rs], [o])
            else:
                o = dn.ostb.get()
                P.op("dve", lambda E, o=o, ps=ps, n=n: E.scalar_tensor_tensor(
                    out=o[:, :], in0=ps[:, :], scalar=oscale, in1=rs[:, n * NT:(n + 1) * NT], op0=ALU.mult, op1=ALU.mult),
                    reads=[ps, rs], writes=[o])
            dn.store(dst, m, n, col0, o, o[:, :])
        dn.linear(None, 8, MC, xb, epi, wbs=wbs)


def phase_A(nc, dr):
    P = Prog(nc)
    P.mk_banks(6)
    dn = Dense(P, TP)
    g = colvec(P, "g_pre", dr["g_pre"], 8)
    xs = P.sb("xs", [128, 8, TP], F32)
    xb = P.sb("xb", [128, 8, TP], BF16)
    rs = P.sb("rs", [128, TP], F32)
    xT = dr["xT_full"]
    wbs = dn.prep_w("wu", dr["w_in_u"], 8, 2)
    for pa in range(SEQ // TP):
        t0 = pa * TP
        proj_pass(P, dn, lambda kc, t0=t0: (xT, xT.t[kc * 128:(kc + 1) * 128, t0:t0 + TP]), xs, rs,
                  [(xb, g, wbs, 2, dr["uT_cs"])], t0)
    P.wait_all("pool", [dr["uT_cs"]])
    P.emit()


def select4(P, dn, src_fn, sel, dt):
    acc = dn.tmp.get()
    for s in range(4):
        it = dn.ist.get() if dt == F32 else dn.istb.get()
        sres, sap = src_fn(s)
        P.dma("sp", it[:, :], sap, reads=[sres], writes=[it])
        if s == 0:
            ts(P, "dve", acc[:, :], it[:, :], sel[:, 0:1], None, ALU.mult, None, [it, sel], [acc])
        else:
            P.op("dve", lambda E, it=it, s=s, acc=acc: E.scalar_tensor_tensor(
                out=acc[:, :], in0=it[:, :], scalar=sel[:, s:s + 1], in1=acc[:, :], op0=ALU.mult, op1=ALU.add),
                reads=[it, sel, acc], writes=[acc])
    return acc


def phase_C(nc, dr):
    P = Prog(nc)
    P.mk_banks(7)
    dn = Dense(P, TP)
    dn.istb = Rot(P, "istb", 4, [128, NT], BF16)
    sel = colvec(P, "sel", dr["sel"], 4)
    gpre = colvec(P, "g_pre", dr["g_pre"], 8)
    vl = []
    for i in range(4):
        t_ = P.sb("vec%d" % i, [128, 8], F32)
        P.dma("sp", t_[:, :], dr["vecsC"].t[:, i * 8:(i + 1) * 8], reads=[dr["vecsC"]], writes=[t_])
        vl.append(t_)
    bglu, gpost, gbpre, _unused = vl
    xT, y_all, pT = dr["xT_own"], dr["y_all"], dr["p0T"]
    Gb = P.sb("Gb", [128, 8, TP], BF16)
    SGb = P.sb("SGb", [128, 8, TP], BF16)
    Y2b = P.sb("Y2b", [128, 8, TP], BF16)
    X1 = P.sb("X1", [128, 8, TP], F32)
    Pb = P.sb("Pb", [128, 2, TP], BF16)
    rs = P.sb("rs", [128, TP], F32)
    for pa in range(TOKC // TP):
        t0 = pa * TP
        for kc in range(8):
            P.dma("sp", X1[:, kc, :], xT.t[kc * 128:(kc + 1) * 128, t0:t0 + TP], reads=[xT], writes=[X1])
        dn.rstd(X1, rs)
        for kc in range(8):
            scale_cast(P, kc, Y2b[:, kc, :], X1[:, kc, :], gpre[:, kc:kc + 1], [X1, gpre], [Y2b])

        def epi_gate(m, n, ps):
            t = dn.tmp.get()
            tt(P, "dve", t[:, :], ps[:, :], rs[:, n * NT:(n + 1) * NT], ALU.mult, [ps, rs], [t])
            act(P, SGb[:, m, n * NT:(n + 1) * NT], t[:, :], AF.Silu, [t], [SGb])
        dn.linear(dr["w_in_g"], 8, 8, Y2b, epi_gate)
        for kc in range(8):
            for n in range(TP // NT):
                def ysrc_fn(s, n=n, kc=kc):
                    g0 = s * TOKC + t0 + n * NT
                    ya = y_all[g0 // 1024]
                    return ya, ya.t[kc * 128:(kc + 1) * 128, g0 % 1024:g0 % 1024 + NT]
                y = select4(P, dn, ysrc_fn, sel, F32)
                s2 = gelu_tanh(P, dn, None, y[:, :], [y])
                tt(P, "pool", Gb[:, kc, n * NT:(n + 1) * NT], s2[:, :], y[:, :], ALU.mult, [s2, y], [Gb])

        def epi_glu(m, n, ps):
            sg = dn.tmp.get()
            act(P, sg[:, :], ps[:, :], AF.Sigmoid, [ps, bglu], [sg], bias=bglu[:, m:m + 1])
            t = dn.tmp.get()
            tt(P, "dve", t[:, :], sg[:, :], Gb[:, m, n * NT:(n + 1) * NT], ALU.mult, [sg, Gb], [t])
            tt(P, "pool", Y2b[:, m, n * NT:(n + 1) * NT], t[:, :], SGb[:, m, n * NT:(n + 1) * NT], ALU.mult, [t, SGb], [Y2b])
        dn.linear(dr["w_glu"], 8, 8, Gb, epi_glu)

        def epi_out(m, n, ps):
            act(P, X1[:, m, n * NT:(n + 1) * NT], ps[:, :], AF.Copy, [ps], [X1])
        dn.linear(dr["w_out0"], 8, 8, Y2b, epi_out)
        dn.rstd(X1, rs)
        for kc in range(8):
            for n in range(TP // NT):
                c0 = t0 + n * NT
                ix = dn.ist.get()
                P.dma("sp", ix[:, :], xT.t[kc * 128:(kc + 1) * 128, c0:c0 + NT], reads=[xT], writes=[ix])
                t = dn.tmp.get()
                P.op("dve", lambda E, t=t, kc=kc, n=n: E.scalar_tensor_tensor(
                    out=t[:, :], in0=X1[:, kc, n * NT:(n + 1) * NT], scalar=gpost[:, kc:kc + 1], in1=rs[:, n * NT:(n + 1) * NT],
                    op0=ALU.mult, op1=ALU.mult), reads=[X1, gpost, rs], writes=[t])
                tt(P, "pool", X1[:, kc, n * NT:(n + 1) * NT], t[:, :], ix[:, :], ALU.add, [t, ix], [X1])
                cp(P, "act", Gb[:, kc, n * NT:(n + 1) * NT], X1[:, kc, n * NT:(n + 1) * NT], [X1], [Gb])
        for kc in range(2):
            for n in range(TP // NT):
                c0 = t0 + n * NT
                ip = dn.ist.get()
                P.dma("sp", ip[:, :], pT.t[kc * 128:(kc + 1) * 128, c0:c0 + NT], reads=[pT], writes=[ip])
                cp(P, "dve", Pb[:, kc, n * NT:(n + 1) * NT], ip[:, :], [ip], [Pb])
        ple(P, dn, dr["w_pg0"], dr["w_pp0"], Gb, Pb, X1)
        dn.rstd(X1, rs)
        for kc in range(8):
            cp(P, "dve", Y2b[:, kc, :], X1[:, kc, :], [X1], [Y2b])
            scale_cast(P, kc + 1, SGb[:, kc, :], X1[:, kc, :], gbpre[:, kc:kc + 1], [X1, gbpre], [SGb])
            P.dma("pool", dr["x1_own"].t[kc * 128:(kc + 1) * 128, t0:t0 + TP], X1[:, kc, :], reads=[X1], writes=[dr["x1_own"]])
            for n in range(TP // NT):
                xsrc = dr["x1_src"][(t0 + n * NT) // NT]
                P.dma("pool", xsrc.t[kc * 128:(kc + 1) * 128, :], Y2b[:, kc, n * NT:(n + 1) * NT], reads=[Y2b], writes=[xsrc])

        def epi_g1(m, n, ps):
            o = dn.ost.get()
            tt(P, "dve", o[:, :], ps[:, :], rs[:, n * NT:(n + 1) * NT], ALU.mult, [ps, rs], [o])
            dn.store(dr["g1T"], m, n, t0, o, o[:, :])
        dn.linear(dr["w_bin_g"], 8, 8, SGb, epi_g1)
    for k in range(4):
        P.coll("AllGather", dr["x1_src"][k], dr["x1_all"][k], GROUPS)
    P.wait_all("pool", dr["x1_all"] + [dr["x1_own"], dr["g1T"]])
    P.emit()


def phase_QKV(nc, dr):
    P = Prog(nc)
    P.mk_banks(6)
    dn = Dense(P, TP)
    gkv = colvec(P, "g_kv", dr["g_kv"], 8)
    gbpre = colvec(P, "g_bpre", dr["g_bpre"], 8)
    xs = P.sb("xs", [128, 8, TP], BF16)
    xq = P.sb("xq", [128, 8, TP], BF16)
    xk = P.sb("xk", [128, 8, TP], BF16)
    rs = P.sb("rs", [128, TP], F32)
    xa = dr["x1_all"]
    wq = dn.prep_w("wq", dr["w_q"], 8, 2)
    wk = dn.prep_w("wk", dr["w_k"], 8, 2)
    wv = dn.prep_w("wv", dr["w_v"], 8, 2)
    for pa in range(SEQ // TP):
        t0 = pa * TP
        s, tl = t0 // TOKC, t0 % TOKC
        proj_pass(P, dn, lambda kc, s=s, tl=tl: [(xa[(tl + h_ * NT) // NT], xa[(tl + h_ * NT) // NT].t[s * D + kc * 128:s * D + (kc + 1) * 128, :]) for h_ in range(TP // NT)], xs, rs,
                  [(xq, gbpre, wq, 2, dr["qT_cs"], 1.0), (xk, gkv, wk, 2, dr["kT_cs"], 0.125), (xk, gkv, wv, 2, dr["vT_cs"], 1.0)], t0)
    P.wait_all("pool", [dr["qT_cs"], dr["kT_cs"], dr["vT_cs"]])
    P.emit()


def phase_E(nc, dr):
    P = Prog(nc)
    P.mk_banks(7)
    dn = Dense(P, TP)
    dn.istb = Rot(P, "istb", 4, [128, NT], BF16)
    sel = colvec(P, "sel", dr["sel"], 4)
    gpost = colvec(P, "g_bpost", dr["g_bpost"], 8)
    Ob = P.sb("Ob", [128, 8, TP], BF16)
    Xb = P.sb("Xb", [128, 8, TP], BF16)
    X1 = P.sb("X1", [128, 8, TP], F32)
    Pb = P.sb("Pb", [128, 2, TP], BF16)
    rs = P.sb("rs", [128, TP], F32)
    x1T, gT, pT, outT = dr["x1_own"], dr["g1T"], dr["p1T"], dr["outT"]
    for pa in range(TOKC // TP):
        t0 = pa * TP
        for kc in range(8):
            for n in range(TP // NT):
                c0 = t0 + n * NT
                def osrc_fn(s, n=n, kc=kc):
                    g0 = s * TOKC + t0 + n * NT
                    oa = dr["o_all"][g0 // 2048]
                    return oa, oa.t[kc * 128:(kc + 1) * 128, g0 % 2048:g0 % 2048 + NT]
                o = select4(P, dn, osrc_fn, sel, BF16)
                ig = dn.ist.get()
                P.dma("sp", ig[:, :], gT.t[kc * 128:(kc + 1) * 128, c0:c0 + NT], reads=[gT], writes=[ig])
                sg = dn.tmp.get()
                act(P, sg[:, :], ig[:, :], AF.Silu, [ig], [sg])
                tt(P, "pool", Ob[:, kc, n * NT:(n + 1) * NT], sg[:, :], o[:, :], ALU.mult, [sg, o], [Ob])

        def epi_out(m, n, ps):
            act(P, X1[:, m, n * NT:(n + 1) * NT], ps[:, :], AF.Copy, [ps], [X1])
        dn.linear(dr["w_out1"], 8, 8, Ob, epi_out)
        dn.rstd(X1, rs)
        for kc in range(8):
            for n in range(TP // NT):
                c0 = t0 + n * NT
                ix = dn.ist.get()
                P.dma("sp", ix[:, :], x1T.t[kc * 128:(kc + 1) * 128, c0:c0 + NT], reads=[x1T], writes=[ix])
                t = dn.tmp.get()
                P.op("dve", lambda E, t=t, kc=kc, n=n: E.scalar_tensor_tensor(
                    out=t[:, :], in0=X1[:, kc, n * NT:(n + 1) * NT], scalar=gpost[:, kc:kc + 1], in1=rs[:, n * NT:(n + 1) * NT],
                    op0=ALU.mult, op1=ALU.mult), reads=[X1, gpost, rs], writes=[t])
                tt(P, "pool", X1[:, kc, n * NT:(n + 1) * NT], t[:, :], ix[:, :], ALU.add, [t, ix], [X1])
                cp(P, "act", Xb[:, kc, n * NT:(n + 1) * NT], X1[:, kc, n * NT:(n + 1) * NT], [X1], [Xb])
        for kc in range(2):
            for n in range(TP // NT):
                c0 = t0 + n * NT
                ip = dn.ist.get()
                P.dma("sp", ip[:, :], pT.t[kc * 128:(kc + 1) * 128, c0:c0 + NT], reads=[pT], writes=[ip])
                cp(P, "dve", Pb[:, kc, n * NT:(n + 1) * NT], ip[:, :], [ip], [Pb])
        ple(P, dn, dr["w_pg1"], dr["w_pp1"], Xb, Pb, X1)
        for kc in range(8):
            P.dma("pool", outT.t[kc * 128:(kc + 1) * 128, t0:t0 + TP], X1[:, kc, :], reads=[X1], writes=[outT])
    P.wait_all("pool", [outT])
    P.emit()


IN_SPECS = {
    "xT_full": ([D, SEQ], F32), "xT_own": ([D, TOKC], F32), "p0T": ([256, TOKC], F32), "p1T": ([256, TOKC], F32),
    "sel": ([128, 4], F32), "g_pre": ([128, 8], F32), "w_in_u": ([D, 256], F32), "w_in_g": ([D, D], F32),
    "lamre_T": ([128, 128], F32), "lamim_T": ([128, 128], F32), "logdt_T": ([128, 128], F32), "bre_T": ([128, 128], F32),
    "bim_T": ([128, 128], F32), "lamre_S": ([128, NG], F32), "lamim_S": ([128, NG], F32), "logdt_S": ([128, NG], F32),
    "c1_S": ([128, NG * 16], F32), "c2_S": ([128, NG * 16], F32), "iota": ([128, TS + 1], F32), "gmask": ([128, 8], F32),
    "sgn": ([128, 1], F32), "Jm": ([128, 128], F32), "dskip_cs": ([128, 2], F32), "vecsC": ([128, 32], F32),
    "w_glu": ([D, D], F32), "w_out0": ([D, D], F32), "w_pg0": ([D, D], F32), "w_pp0": ([256, D], F32),
    "w_bin_g": ([D, D], F32), "g_kv": ([128, 8], F32), "g_bpre": ([128, 8], F32), "w_q": ([D, 256], F32),
    "w_k": ([D, 256], F32), "w_v": ([D, 256], F32), "ntri": ([128, 128], F32), "g_bpost": ([128, 8], F32),
    "w_out1": ([D, D], F32), "w_pg1": ([D, D], F32), "w_pp1": ([256, D], F32),
}
SCRATCH = {
    "uT_cs": ([256, SEQ], F32), "y_src": ([256, 1024], F32, 8), "y_all": ([D, 1024], F32, 8), "x1_own": ([D, TOKC], F32),
    "x1_src": ([D, NT], BF16, 4), "x1_all": ([4 * D, NT], BF16, 4), "g1T": ([D, TOKC], F32), "qT_cs": ([256, SEQ], BF16),
    "kT_cs": ([256, SEQ], BF16), "vT_cs": ([256, SEQ], BF16), "o_src": ([256, 2048], BF16, 4), "o_all": ([D, 2048], BF16, 4),
}


def build_fused():
    nc = bass.Bass("TRN2", target_bir_lowering=False)
    dr = {}
    for n, (shp, dt) in IN_SPECS.items():
        dr[n] = Res(n, nc.dram_tensor(n, list(shp), dt, kind="ExternalInput").ap())
    for n, spec in SCRATCH.items():
        shp, dt = spec[0], spec[1]
        if len(spec) == 3:
            dr[n] = [Res("%s%d" % (n, i), nc.dram_tensor("%s%d" % (n, i), list(shp), dt, kind="Internal").ap()) for i in range(spec[2])]
        else:
            dr[n] = Res(n, nc.dram_tensor(n, list(shp), dt, kind="Internal").ap())
    dr["outT"] = Res("outT", nc.dram_tensor("outT", [D, TOKC], F32, kind="ExternalOutput").ap())
    phase_A(nc, dr)
    phase_B(nc, dr)
    phase_C(nc, dr)
    phase_QKV(nc, dr)
    phase_D(nc, dr)
    phase_E(nc, dr)
    Prog.finish()
    return nc


def _f(a):
    return np.ascontiguousarray(np.asarray(a, dtype=np.float32))


def kernel(**inputs):
    inp = {k: np.asarray(v) for k, v in inputs.items()}
    x, p = inp["x"], inp["p"]
    lam_re, lam_im, log_dt = _f(inp["a_lam_re"][0]), _f(inp["a_lam_im"][0]), _f(inp["a_log_dt"][0])
    b_re, b_im, c_re, c_im = _f(inp["a_b_re"][0]), _f(inp["a_b_im"][0]), _f(inp["a_c_re"][0]), _f(inp["a_c_im"][0])
    iota = _f(np.broadcast_to(np.arange(TS + 1, dtype=np.float32), (128, TS + 1)))
    gmask = np.zeros((128, 8), np.float32)
    for gl in range(8):
        gmask[gl * 16:(gl + 1) * 16, gl] = 1.0
    sgn = np.ones((128, 1), np.float32)
    sgn[64:] = -1.0
    J = np.zeros((128, 128), np.float32)
    for q in range(64):
        J[64 + q, q] = -1.0
        J[q, 64 + q] = 1.0
    ntri = np.zeros((128, 128), np.float32)
    for j in range(128):
        ntri[j, :j + 1] = -1.0
    vecsC = np.zeros((128, 32), np.float32)
    for i, v in enumerate([inp["a_b_glu"][0], inp["a_norm_post"][0], inp["b_norm_pre"][0]]):
        vecsC[:, i * 8:(i + 1) * 8] = _cols(v)
    a_w_in, b_w_in, w_kv = _f(inp["a_w_in"][0]), _f(inp["b_w_in"][0]), _f(inp["w_kv"])
    common = {
        "g_pre": _cols(inp["a_norm_pre"][0]), "w_in_g": _f(a_w_in[:, D:]), "iota": iota, "gmask": gmask, "sgn": sgn, "Jm": J,
        "vecsC": vecsC, "w_glu": _f(inp["a_w_glu"][0]), "w_out0": _f(inp["a_w_out"][0]), "w_pg0": _f(inp["ple_w_gate"][0]),
        "w_pp0": _f(inp["ple_w_proj"][0]), "w_bin_g": _f(b_w_in[:, D:]), "g_kv": _cols(inp["kv_norm"]),
        "g_bpre": _cols(inp["b_norm_pre"][0]), "ntri": ntri, "g_bpost": _cols(inp["b_norm_post"][0]),
        "w_out1": _f(inp["b_w_out"][0]), "w_pg1": _f(inp["ple_w_gate"][1]), "w_pp1": _f(inp["ple_w_proj"][1]),
    }
    xT_full = [_f(np.asarray(x[b], np.float32).T) for b in range(2)]
    maps = []
    for c in range(NCORE):
        b, r = c // 4, c % 4
        gs = np.arange(16 * r, 16 * r + 16)
        tsl = slice(r * TOKC, (r + 1) * TOKC)
        csl = slice(256 * r, 256 * r + 256)
        m = dict(common)
        m["xT_full"] = xT_full[b]
        m["xT_own"] = _f(xT_full[b][:, tsl])
        m["p0T"] = _f(np.asarray(p[0, b, tsl, :], np.float32).T)
        m["p1T"] = _f(np.asarray(p[1, b, tsl, :], np.float32).T)
        sel = np.zeros((128, 4), np.float32)
        sel[:, r] = 1.0
        m["sel"] = sel
        m["w_in_u"] = _f(a_w_in[:, csl])
        m["dskip_cs"] = _cols(inp["a_d_skip"][0][csl])
        m["w_q"] = _f(b_w_in[:, csl])
        m["w_k"] = _f(w_kv[:, csl])
        m["w_v"] = _f(w_kv[:, D + 256 * r:D + 256 * r + 256])

        def lt_gp(a):
            t = a[gs].reshape(2, 8, 64)
            t = np.broadcast_to(t[:, :, None, :], (2, 8, 16, 64))
            return _f(t.transpose(1, 2, 0, 3).reshape(128, 128))

        def lt_b(a):
            t = a[gs].reshape(2, 8, 64, 16)
            return _f(t.transpose(1, 3, 0, 2).reshape(128, 128))

        def sp_gp(a):
            t = a[gs].T
            return _f(np.concatenate([t, t], axis=0))
        ldt = np.broadcast_to(log_dt[:, None], (64, 64))
        m["lamre_T"], m["lamim_T"], m["logdt_T"] = lt_gp(lam_re), lt_gp(lam_im), lt_gp(ldt)
        m["bre_T"], m["bim_T"] = lt_b(b_re), lt_b(b_im)
        m["lamre_S"], m["lamim_S"], m["logdt_S"] = sp_gp(lam_re), sp_gp(lam_im), sp_gp(ldt)
        cr = c_re[gs].transpose(2, 0, 1).reshape(64, 256)
        ci = c_im[gs].transpose(2, 0, 1).reshape(64, 256)
        m["c1_S"] = _f(np.concatenate([cr, ci], axis=0))
        m["c2_S"] = _f(np.concatenate([ci, cr], axis=0))
        maps.append(m)
    res = _run(build_fused(), maps)
    out = np.empty((2, SEQ, D), np.float32)
    for c in range(NCORE):
        b, r = c // 4, c % 4
        out[b, r * TOKC:(r + 1) * TOKC, :] = res[c]["outT"].T
    return out
```

```python
from contextlib import ExitStack
import numpy as np
import concourse.bass as bass
import concourse.mybir as mybir
from concourse.bass_utils import run_bass_kernel_spmd

F32 = mybir.dt.float32
BF16 = mybir.dt.bfloat16
AF = mybir.ActivationFunctionType
ALU = mybir.AluOpType

ENGS = ("pe", "act", "dve", "pool", "sp")
NCORE = 8
D = 1024
SEQ = 8192
NT = 512
EPS = 1e-6
PI = float(np.pi)


class Res:
    __slots__ = ("name", "w", "r", "dsem", "dcnt", "t")

    def __init__(self, name, t=None):
        self.name = name
        self.w = {}
        self.r = {}
        self.dsem = None
        self.dcnt = 0
        self.t = t

    def __getitem__(self, idx):
        return self.t[idx]


class Prog:
    _n = 0
    G = None

    def __init__(self, nc):
        Prog._n += 1
        self.pfx = "f%d_" % Prog._n
        self.nc = nc
        if Prog.G is None or Prog.G["nc"] is not nc:
            ges = ExitStack()
            Prog.G = {"nc": nc, "es": ges, "sems": {}, "cnt": {}}
            for e in ENGS:
                Prog.G["sems"][e] = ges.enter_context(nc.semaphore("s_" + e))
                Prog.G["cnt"][e] = 0
        G = Prog.G
        self.es = ExitStack()
        self.lists = {e: [] for e in ENGS}
        self.sems = G["sems"]
        self.cnt = G["cnt"]
        self.seen = {e: dict(self.cnt) for e in ENGS}
        self.nd = 0
        self.banks = []
        self.bi = 0
        self.touched = {}

    @staticmethod
    def finish():
        if Prog.G is not None:
            Prog.G["es"].close()
            Prog.G = None

    def _newsem(self):
        key = "d%d" % self.nd
        self.nd += 1
        if key not in self.sems:
            self.sems[key] = Prog.G["es"].enter_context(self.nc.semaphore("sd_" + key))
            self.cnt[key] = 0
        return key

    def sb(self, name, shape, dt):
        return Res(name, self.es.enter_context(self.nc.sbuf_tensor(self.pfx + "sb_" + name, list(shape), dt)))

    def ps(self, name, shape, dt=F32):
        return Res(name, self.es.enter_context(self.nc.psum_tensor(self.pfx + "ps_" + name, list(shape), dt)))

    def dram(self, name, shape, dt, kind="Internal"):
        return Res(name, self.nc.dram_tensor(name, list(shape), dt, kind=kind).ap())

    def mk_banks(self, n):
        self.banks = [self.ps("bank%d" % i, [128, NT], F32) for i in range(n)]

    def bank(self):
        b = self.banks[self.bi % len(self.banks)]
        self.bi += 1
        return b

    def _dsem(self, res):
        if res.dsem is None:
            res.dsem = self._newsem()
        return res.dsem

    def _waits(self, eng, reads, writes, skip_same=False):
        for x_ in reads:
            self.touched[id(x_)] = x_
        for x_ in writes:
            self.touched[id(x_)] = x_
        deps = {}
        for r in reads:
            for k, v in r.w.items():
                if skip_same and k == eng:
                    continue
                if v > deps.get(k, 0):
                    deps[k] = v
        for w in writes:
            for k, v in w.w.items():
                if k != eng and v > deps.get(k, 0):
                    deps[k] = v
            for k, v in w.r.items():
                if k != eng and v > deps.get(k, 0):
                    deps[k] = v
        seen = self.seen[eng]
        for k, v in deps.items():
            if v > seen.get(k, 0):
                seen[k] = v
                sem = self.sems[k]
                self.lists[eng].append(lambda E, sem=sem, v=v: E.wait_ge(sem, v))

    def op(self, eng, fn, reads=(), writes=(), skip_same=False):
        self._waits(eng, reads, writes, skip_same)
        self.cnt[eng] += 1
        n = self.cnt[eng]
        sem = self.sems[eng]
        self.lists[eng].append(lambda E, fn=fn, sem=sem: fn(E).then_inc(sem, 1))
        for r in reads:
            r.r[eng] = n
        for w in writes:
            w.w[eng] = n

    def dma(self, eng, out_ap, in_ap, reads=(), writes=()):
        wres = writes[0]
        self._waits(eng, reads, writes)
        key = self._dsem(wres)
        self.cnt[key] += 16
        v = self.cnt[key]
        sem = self.sems[key]
        self.lists[eng].append(
            lambda E, o=out_ap, i=in_ap, sem=sem: E.dma_start(out=o, in_=i).then_inc(sem, 16))
        for r in reads:
            r.r[key] = v
        wres.w[key] = v

    def coll(self, kind, src, dst, groups):
        self._waits("pool", [src], [dst])
        key = self._newsem()
        self.cnt[key] += 1
        v = self.cnt[key]
        sem = self.sems[key]
        self.lists["pool"].append(lambda E, sem=sem: E.collective_compute(
            kind, ALU.bypass, replica_groups=groups, ins=[src.t.opt()], outs=[dst.t.opt()]).then_inc(sem))
        src.r[key] = v
        dst.w[key] = v

    def wait_all(self, eng, ress):
        self._waits(eng, ress, ())

    def emit(self):
        L = self.lists
        for e in ENGS:
            for k, sem in self.sems.items():
                tgt = self.cnt[k]
                if k != e and tgt > self.seen[e].get(k, 0):
                    self.seen[e][k] = tgt
                    L[e].append(lambda E, sem=sem, tgt=tgt: E.wait_ge(sem, tgt))
        with self.nc.Block() as block:
            @block.tensor
            def _(E):
                for f in L["pe"]:
                    f(E)

            @block.scalar
            def _(E):
                for f in L["act"]:
                    f(E)

            @block.vector
            def _(E):
                for f in L["dve"]:
                    f(E)

            @block.gpsimd
            def _(E):
                for f in L["pool"]:
                    f(E)

            @block.sync
            def _(E):
                for f in L["sp"]:
                    f(E)
        self.es.close()
        for x_ in self.touched.values():
            x_.w = {}
            x_.r = {}
            x_.dsem = None


def tt(P, eng, out, in0, in1, op, reads, writes):
    P.op(eng, lambda E: E.tensor_tensor(out=out, in0=in0, in1=in1, op=op), reads=reads, writes=writes)


def ts(P, eng, out, in0, s1, s2, op0, op1, reads, writes):
    if s2 is None:
        P.op(eng, lambda E: E.tensor_scalar(out=out, in0=in0, scalar1=s1, scalar2=None, op0=op0), reads=reads, writes=writes)
    else:
        P.op(eng, lambda E: E.tensor_scalar(out=out, in0=in0, scalar1=s1, scalar2=s2, op0=op0, op1=op1), reads=reads, writes=writes)


def act(P, out, in_, func, reads, writes, scale=1.0, bias=None):
    if bias is None:
        P.op("act", lambda E: E.activation(out=out, in_=in_, func=func, scale=scale), reads=reads, writes=writes)
    else:
        P.op("act", lambda E: E.activation(out=out, in_=in_, func=func, scale=scale, bias=bias), reads=reads, writes=writes)


def scale_cast(P, i, out, in_, col, reads, writes):
    if i % 2:
        P.op("act", lambda E: E.activation(out=out, in_=in_, func=AF.Copy, scale=col), reads=reads, writes=writes)
    else:
        ts(P, "dve", out, in_, col, None, ALU.mult, None, reads, writes)


def cp(P, eng, out, in_, reads, writes):
    if eng == "act":
        P.op(eng, lambda E: E.activation(out=out, in_=in_, func=AF.Copy), reads=reads, writes=writes)
    else:
        P.op(eng, lambda E: E.tensor_copy(out=out, in_=in_), reads=reads, writes=writes)


class Rot:
    def __init__(self, P, name, n, shape, dt):
        self.bufs = [P.sb("%s%d" % (name, i), shape, dt) for i in range(n)]
        self.i = 0

    def get(self):
        b = self.bufs[self.i % len(self.bufs)]
        self.i += 1
        return b


class Dense:
    def __init__(self, P, T):
        self.P = P
        self.T = T
        self.wst = Rot(P, "wst", 4, [128, 8, 128], F32)
        self.wbf = Rot(P, "wbf", 4, [128, 8, 128], BF16)
        self.ost = Rot(P, "ost", 4, [128, NT], F32)
        self.ostb = Rot(P, "ostb", 4, [128, NT], BF16)
        self.ist = Rot(P, "ist", 4, [128, NT], F32)
        self.tmp = Rot(P, "tmp", 4, [128, NT], F32)
        self.ones = P.sb("ones", [128, 128], BF16)
        P.op("pool", lambda E: E.memset(self.ones[:], 1.0), writes=[self.ones])
        self.sq = P.sb("sq", [128, 8, T], BF16)

    def load_w(self, W, m, KC, rowscale=None, wb=None):
        P = self.P
        assert rowscale is None
        st = self.wst.get()
        if wb is None:
            wb = self.wbf.get()
        P.dma("sp", st[:, 0:KC, :], W.t[:, m * 128:(m + 1) * 128].rearrange("(kc p) m -> p kc m", p=128), reads=[W], writes=[st])
        self.wi = getattr(self, "wi", 0) + 1
        cp(P, "act" if self.wi % 2 else "pool", wb[:, 0:KC, :], st[:, 0:KC, :], [st], [wb])
        return wb

    def prep_w(self, name, W, KC, MC):
        wbs = []
        for m in range(MC):
            wb = self.P.sb("%s_w%d" % (name, m), [128, KC, 128], BF16)
            wbs.append(self.load_w(W, m, KC, wb=wb))
        return wbs

    def linear(self, W, KC, MC, a, epi, rowscale=None, wbs=None):
        P = self.P
        for m in range(MC):
            wb = wbs[m] if wbs is not None else self.load_w(W, m, KC, rowscale)
            for n in range(self.T // NT):
                ps = P.bank()
                for kc in range(KC):
                    P.op("pe", lambda E, ps=ps, wb=wb, kc=kc, n=n: E.matmul(
                        ps[:, :], lhsT=wb[:, kc, :], rhs=a[:, kc, n * NT:(n + 1) * NT], start=(kc == 0), stop=(kc == KC - 1)),
                        reads=[wb, a], writes=[ps])
                epi(m, n, ps)

    def rstd(self, src, out):
        P = self.P
        sq = self.sq
        for kc in range(8):
            act(P, sq[:, kc, :], src[:, kc, :], AF.Square, [src], [sq])
        for n in range(self.T // NT):
            ps = P.bank()
            for kc in range(8):
                P.op("pe", lambda E, ps=ps, kc=kc, n=n: E.matmul(
                    ps[:, :], lhsT=self.ones[:, :], rhs=sq[:, kc, n * NT:(n + 1) * NT], start=(kc == 0), stop=(kc == 7)),
                    reads=[self.ones, sq], writes=[ps])
            t = self.tmp.get()
            act(P, t[:, :], ps[:, :], AF.Ln, [ps], [t], scale=1.0 / D, bias=EPS)
            act(P, out[:, n * NT:(n + 1) * NT], t[:, :], AF.Exp, [t], [out], scale=-0.5)

    def store(self, dst, m, n, t0, src_res, src_ap):
        self.P.dma("pool", dst.t[m * 128:(m + 1) * 128, t0 + n * NT:t0 + (n + 1) * NT], src_ap, reads=[src_res], writes=[dst])

    def load_act(self, src, t0, dst, KC=8):
        for kc in range(KC):
            self.P.dma("sp", dst[:, kc, :], src.t[kc * 128:(kc + 1) * 128, t0:t0 + self.T], reads=[src], writes=[dst])


def colvec(P, name, dram_res, ncol):
    t = P.sb(name, [128, ncol], F32)
    P.dma("sp", t[:, :], dram_res.t[:, :], reads=[dram_res], writes=[t])
    return t


def _run(nc, in_maps):
    res = run_bass_kernel_spmd(nc, in_maps, core_ids=list(range(NCORE)))
    return res.results


def _cols(v):
    return np.ascontiguousarray(np.asarray(v, np.float32).reshape(-1, 128).T)


TS = 256
NG = 16
M_MAGIC = 12582912.0


def range_reduce(P, eng, out, in_, tmp, reads, writes, shift=0.0):
    res_in = reads
    if shift != 0.0:
        ts(P, eng, out, in_, shift, None, ALU.add, None, res_in, writes)
        in_ = out
        res_in = writes
    ts(P, eng, tmp[0], in_, 1.0 / (2 * PI), M_MAGIC, ALU.mult, ALU.add, res_in, [tmp[1]])
    ts(P, eng, tmp[0], tmp[0], -M_MAGIC, -2 * PI, ALU.add, ALU.mult, [tmp[1]], [tmp[1]])
    tt(P, eng, out, tmp[0], in_, ALU.add, [tmp[1]] + list(res_in), writes)
    ts(P, eng, out, out, 3.14159, -3.14159, ALU.min, ALU.max, writes, writes)


NAMES_T = ["lamre_T", "lamim_T", "logdt_T", "bre_T", "bim_T"]
NAMES_S = ["lamre_S", "lamim_S", "logdt_S"]


def phase_B(nc, dr):
    P = Prog(nc)
    uT = dr["uT_cs"]
    names_T = NAMES_T
    dT = {n: dr[n] for n in names_T}
    names_S = NAMES_S
    dS = {n: dr[n] for n in names_S}
    c1_d, c2_d, iota_d, gmask_d, sgn_d, J_d = dr["c1_S"], dr["c2_S"], dr["iota"], dr["gmask"], dr["sgn"], dr["Jm"]
    dsk = colvec(P, "dskip_cs", dr["dskip_cs"], 2)
    P.mk_banks(4)
    ybank = [P.ps("ybank%d" % i, [128, NT], F32) for i in range(2)]
    igb = P.ps("igb", [128, NT], F32)

    def ld(name, d, ncol):
        return colvec(P, name, d, ncol)

    lt = {n: ld("s_" + n, dT[n], 128) for n in names_T}
    cnt = [0]

    def newT(nm, ncol=128):
        cnt[0] += 1
        return P.sb("%s_%d" % (nm, cnt[0]), [128, ncol], F32)

    def derive(lamre, lamim, logdt, ncol, pfx):
        o = {}
        lr = newT(pfx + "lr", ncol)
        ts(P, "dve", lr[:, :], lamre[:, :], -1e-4, None, ALU.min, None, [lamre], [lr])
        dt = newT(pfx + "dt", ncol)
        act(P, dt[:, :], logdt[:, :], AF.Exp, [logdt], [dt])
        e = newT(pfx + "e", ncol)
        tt(P, "dve", e[:, :], lr[:, :], dt[:, :], ALU.mult, [lr, dt], [e])
        th = newT(pfx + "th", ncol)
        tt(P, "dve", th[:, :], lamim[:, :], dt[:, :], ALU.mult, [lamim, dt], [th])
        thr = newT(pfx + "thr", ncol)
        tmp = newT(pfx + "tmp", ncol)
        range_reduce(P, "dve", thr[:, :], th[:, :], (tmp[:, :], tmp), [th], [thr])
        thc = newT(pfx + "thc", ncol)
        range_reduce(P, "dve", thc[:, :], th[:, :], (tmp[:, :], tmp), [th], [thc], shift=PI / 2)
        mag = newT(pfx + "mag", ncol)
        act(P, mag[:, :], e[:, :], AF.Exp, [e], [mag])
        sn = newT(pfx + "sin", ncol)
        act(P, sn[:, :], thr[:, :], AF.Sin, [thr], [sn])
        cs = newT(pfx + "cos", ncol)
        act(P, cs[:, :], thc[:, :], AF.Sin, [thc], [cs])
        o.update(lr=lr, li=lamim, e=e, thr=thr, mag=mag, sin=sn, cos=cs)
        return o

    dl = derive(lt["lamre_T"], lt["lamim_T"], lt["logdt_T"], 128, "T")

    def mul(a, b, nm):
        t = newT(nm)
        tt(P, "dve", t[:, :], a[:, :], b[:, :], ALU.mult, [a, b], [t])
        return t

    def addsub(a, b, op, nm):
        t = newT(nm)
        tt(P, "dve", t[:, :], a[:, :], b[:, :], op, [a, b], [t])
        return t

    are = mul(dl["mag"], dl["cos"], "are")
    aim = mul(dl["mag"], dl["sin"], "aim")
    den = addsub(mul(dl["lr"], dl["lr"], "lr2"), mul(dl["li"], dl["li"], "li2"), ALU.add, "den")
    rden = newT("rden")
    P.op("dve", lambda E: E.reciprocal(out=rden[:, :], in_=den[:, :]), reads=[den], writes=[rden])
    nr = newT("nr")
    ts(P, "dve", nr[:, :], are[:, :], -1.0, None, ALU.add, None, [are], [nr])
    fre = mul(addsub(mul(nr, dl["lr"], "f1"), mul(aim, dl["li"], "f2"), ALU.add, "f3"), rden, "fre")
    fim = mul(addsub(mul(aim, dl["lr"], "f4"), mul(nr, dl["li"], "f5"), ALU.subtract, "f6"), rden, "fim")
    bbre = addsub(mul(fre, lt["bre_T"], "b1"), mul(fim, lt["bim_T"], "b2"), ALU.subtract, "bbre")
    bbim = addsub(mul(fre, lt["bim_T"], "b3"), mul(fim, lt["bre_T"], "b4"), ALU.add, "bbim")
    nbbim = newT("nbbim")
    ts(P, "dve", nbbim[:, :], bbim[:, :], -1.0, None, ALU.mult, None, [bbim], [nbbim])
    gmask = ld("gmask", gmask_d, 8)
    W1 = P.sb("W1pad", [128, NG, 128], BF16)
    W2 = P.sb("W2pad", [128, NG, 128], BF16)
    for jj in range(2):
        for gl in range(8):
            gg = jj * 8 + gl
            ms = gmask[:, gl:gl + 1]
            sl = slice(jj * 64, (jj + 1) * 64)
            ts(P, "dve", W1[:, gg, 0:64], bbre[:, sl], ms, None, ALU.mult, None, [bbre, gmask], [W1])
            ts(P, "dve", W1[:, gg, 64:128], bbim[:, sl], ms, None, ALU.mult, None, [bbim, gmask], [W1])
            ts(P, "dve", W2[:, gg, 0:64], nbbim[:, sl], ms, None, ALU.mult, None, [nbbim, gmask], [W2])
            ts(P, "dve", W2[:, gg, 64:128], bbre[:, sl], ms, None, ALU.mult, None, [bbre, gmask], [W2])

    ls_ = {n: ld("s_" + n, dS[n], NG) for n in names_S}
    ds = derive(ls_["lamre_S"], ls_["lamim_S"], ls_["logdt_S"], NG, "S")
    rho = ds["mag"]
    iota = ld("iota", iota_d, TS + 1)
    COS = P.sb("COS", [128, NG, TS + 1], F32)
    SIN = P.sb("SIN", [128, NG, TS + 1], F32)
    ARG = P.sb("ARG", [128, NG, TS + 1], F32)
    TMP = P.sb("TMPA", [128, NG, TS + 1], F32)
    Gg = [P.sb("Gg%d" % i, [128, TS], F32) for i in range(NG)]
    GT = [P.sb("GT%d" % i, [128, NG], F32) for i in range(2)]
    for gg in range(NG):
        ts(P, "dve", ARG[:, gg, :], iota[:, :], ds["thr"][:, gg:gg + 1], None, ALU.mult, None, [iota, ds["thr"]], [ARG])
    range_reduce(P, "dve", SIN[:, :, :], ARG[:, :, :], (TMP[:, :, :], TMP), [ARG], [SIN])
    range_reduce(P, "dve", COS[:, :, :], ARG[:, :, :], (TMP[:, :, :], TMP), [ARG], [COS], shift=PI / 2)
    act(P, SIN[:, :, :], SIN[:, :, :], AF.Sin, [SIN], [SIN])
    act(P, COS[:, :, :], COS[:, :, :], AF.Sin, [COS], [COS])
    c1 = ld("c1", c1_d, NG * 16)
    c2 = ld("c2", c2_d, NG * 16)
    sgn = ld("sgn", sgn_d, 1)
    L1 = P.sb("L1pad", [128, NG, 128], BF16)
    L2 = P.sb("L2pad", [128, NG, 128], BF16)
    P.op("pool", lambda E: E.memset(L1[:, :, :], 0.0), writes=[L1])
    P.op("pool", lambda E: E.memset(L2[:, :, :], 0.0), writes=[L2])
    for gg in range(NG):
        gl = gg % 8
        ts(P, "dve", L1[:, gg, gl * 16:(gl + 1) * 16], c1[:, gg * 16:(gg + 1) * 16], sgn[:, 0:1], None, ALU.mult, None, [c1, sgn], [L1])
        ts(P, "dve", L2[:, gg, gl * 16:(gl + 1) * 16], c2[:, gg * 16:(gg + 1) * 16], -1.0, None, ALU.mult, None, [c2], [L2])
    Jm = P.sb("Jm", [128, 128], F32)
    P.dma("sp", Jm[:, :], J_d.t[:, :], reads=[J_d], writes=[Jm])

    ust = Rot(P, "ust", 3, [128, 2, TS], F32)
    ubf = Rot(P, "ubf", 3, [128, 2, TS], BF16)
    t1r = Rot(P, "t1r", 4, [128, TS], F32)
    t2r = Rot(P, "t2r", 4, [128, TS], F32)
    xtr = Rot(P, "xtr", 4, [128, TS], F32)
    h1r = Rot(P, "h1r", 4, [128, TS], BF16)
    h2r = Rot(P, "h2r", 4, [128, TS], BF16)
    S0 = [P.sb("S0_%d" % i, [128, NG], F32) for i in range(2)]
    ys = Rot(P, "ys", 3, [128, 2, TS], BF16)
    P.op("pool", lambda E: E.memset(S0[0][:, :], 0.0), writes=[S0[0]])
    nch = SEQ // TS
    units = []
    chunk_res = {}
    for ch in range(nch):
        for jj in range(2):
            for gl in range(8):
                units.append(dict(ch=ch, jj=jj, gl=gl, gg=jj * 8 + gl))
    nu = len(units)

    def chunk_setup(ch):
        c0 = ch * TS
        us = ust.get()
        ub = ubf.get()
        P.dma("sp", us[:, :, :], uT.t[:, c0:c0 + TS].rearrange("(j p) t -> p j t", p=128), reads=[uT], writes=[us])
        cp(P, "act", ub[:, :, :], us[:, :, :], [us], [ub])
        chunk_res[ch] = dict(us=us, ub=ub, yo=ys.get())

    def s1(i):
        U = units[i]
        ch, jj, gg = U["ch"], U["jj"], U["gg"]
        if jj == 0 and U["gl"] == 0:
            chunk_setup(ch)
        ub = chunk_res[ch]["ub"]
        p1 = P.bank()
        p2 = P.bank()
        P.op("pe", lambda E, p1=p1, gg=gg, ub=ub, jj=jj: E.matmul(p1[:, 0:TS], lhsT=W1[:, gg, :], rhs=ub[:, jj, :], start=True, stop=True), reads=[W1, ub], writes=[p1])
        P.op("pe", lambda E, p2=p2, gg=gg, ub=ub, jj=jj: E.matmul(p2[:, 0:TS], lhsT=W2[:, gg, :], rhs=ub[:, jj, :], start=True, stop=True), reads=[W2, ub], writes=[p2])
        t1 = t1r.get()
        t2 = t2r.get()
        xt = xtr.get()
        tt(P, "dve", t1[:, :], p1[:, 0:TS], COS[:, gg, 1:TS + 1], ALU.mult, [p1, COS], [t1])
        tt(P, "dve", t2[:, :], p2[:, 0:TS], SIN[:, gg, 1:TS + 1], ALU.mult, [p2, SIN], [t2])
        tt(P, "pool", xt[:, :], t1[:, :], t2[:, :], ALU.subtract, [t1, t2], [xt])
        U["xt"] = xt

    def s2(i):
        U = units[i]
        ch, gg = U["ch"], U["gg"]
        gt, s0 = GT[ch % 2], S0[ch % 2]
        xt = U["xt"]
        G = Gg[gg]
        P.op("dve", lambda E, G=G, gg=gg, xt=xt, s0=s0: E.tensor_tensor_scan(
            out=G[:, :], data0=rho[:, gg:gg + 1].to_broadcast([128, TS]), data1=xt[:, :],
            initial=s0[:, gg:gg + 1], op0=ALU.mult, op1=ALU.add), reads=[rho, xt, s0], writes=[G])
        if ch + 1 < nch:
            cp(P, "act", gt[:, gg:gg + 1], G[:, TS - 1:TS], [G], [gt])
        h1 = h1r.get()
        h2 = h2r.get()
        tt(P, "dve", h1[:, :], G[:, :], COS[:, gg, 1:TS + 1], ALU.mult, [G, COS], [h1])
        tt(P, "pool", h2[:, :], G[:, :], SIN[:, gg, 1:TS + 1], ALU.mult, [G, SIN], [h2])
        U["h1"], U["h2"] = h1, h2
        if gg == NG - 1 and ch + 1 < nch:
            s1_ = S0[(ch + 1) % 2]
            P.op("pe", lambda E, gt=gt: E.matmul(igb[:, 0:NG], lhsT=Jm[:, :], rhs=gt[:, :], start=True, stop=True), reads=[Jm, gt], writes=[igb])
            ta = P.sb("bta%d" % ch, [128, NG], F32)
            tb = P.sb("btb%d" % ch, [128, NG], F32)
            tt(P, "dve", ta[:, :], gt[:, :], COS[:, :, TS], ALU.mult, [gt, COS], [ta])
            tt(P, "dve", tb[:, :], igb[:, 0:NG], SIN[:, :, TS], ALU.mult, [igb, SIN], [tb])
            tt(P, "dve", s1_[:, :], ta[:, :], tb[:, :], ALU.add, [ta, tb], [s1_])

    def s3(i):
        U = units[i]
        ch, jj, gl, gg = U["ch"], U["jj"], U["gl"], U["gg"]
        yb = ybank[jj]
        h1, h2 = U["h1"], U["h2"]
        P.op("pe", lambda E, yb=yb, gg=gg, h1=h1, gl=gl: E.matmul(yb[:, 0:TS], lhsT=L1[:, gg, :], rhs=h1[:, :], start=(gl == 0), stop=False), reads=[L1, h1], writes=[yb])
        P.op("pe", lambda E, yb=yb, gg=gg, h2=h2, gl=gl: E.matmul(yb[:, 0:TS], lhsT=L2[:, gg, :], rhs=h2[:, :], start=False, stop=(gl == 7)), reads=[L2, h2], writes=[yb])
        if gl == 7:
            cr = chunk_res[ch]
            yo, us = cr["yo"], cr["us"]
            P.op("dve", lambda E, yo=yo, us=us, yb=yb, jj=jj: E.scalar_tensor_tensor(
                out=yo[:, jj, :], in0=us[:, jj, :], scalar=dsk[:, jj:jj + 1], in1=yb[:, 0:TS], op0=ALU.mult, op1=ALU.add),
                reads=[us, dsk, yb], writes=[yo])
            if jj == 1:
                c0 = ch * TS
                ysrc = dr["y_src"][c0 // 2048]
                P.dma("act", ysrc.t[:, c0 % 2048:c0 % 2048 + TS].rearrange("(j p) t -> p j t", p=128), yo[:, :, :], reads=[yo], writes=[ysrc])

    for j in range(nu + 2):
        if j < nu:
            s1(j)
        if 0 <= j - 1 < nu:
            s2(j - 1)
        if 0 <= j - 2 < nu:
            s3(j - 2)
    for k in range(4):
        P.coll("AllGather", dr["y_src"][k], dr["y_all"][k], GROUPS)
    P.wait_all("act", dr["y_all"])
    P.emit()


def gelu_tanh(P, dn, out_bf, y, reads):
    s = dn.tmp.get()
    s2 = dn.tmp.get()
    act(P, s[:, :], y, AF.Square, reads, [s])
    ts(P, "dve", s[:, :], s[:, :], 0.044715, 1.0, ALU.mult, ALU.add, [s], [s])
    tt(P, "dve", s2[:, :], s[:, :], y, ALU.mult, [s] + list(reads), [s2])
    act(P, s2[:, :], s2[:, :], AF.Sigmoid, [s2], [s2], scale=1.5957691216057308)
    return s2


def ple(P, dn, Wpg, Wpp, Xb, Pb, X1):
    for m in range(8):
        wg = dn.load_w(Wpg, m, 8)
        wp = dn.load_w(Wpp, m, 2)
        for n in range(dn.T // NT):
            pg = P.bank()
            pp = P.bank()
            for kc in range(8):
                P.op("pe", lambda E, pg=pg, wg=wg, kc=kc, n=n: E.matmul(
                    pg[:, :], lhsT=wg[:, kc, :], rhs=Xb[:, kc, n * NT:(n + 1) * NT], start=(kc == 0), stop=(kc == 7)),
                    reads=[wg, Xb], writes=[pg])
            for kc in range(2):
                P.op("pe", lambda E, pp=pp, wp=wp, kc=kc, n=n: E.matmul(
                    pp[:, :], lhsT=wp[:, kc, :], rhs=Pb[:, kc, n * NT:(n + 1) * NT], start=(kc == 0), stop=(kc == 1)),
                    reads=[wp, Pb], writes=[pp])
            sg = dn.tmp.get()
            act(P, sg[:, :], pg[:, :], AF.Sigmoid, [pg], [sg])
            t = dn.tmp.get()
            tt(P, "dve", t[:, :], sg[:, :], pp[:, :], ALU.mult, [sg, pp], [t])
            tt(P, "pool", X1[:, m, n * NT:(n + 1) * NT], X1[:, m, n * NT:(n + 1) * NT], t[:, :], ALU.add, [X1, t], [X1])


NH = 4
NQT = SEQ // NT


def phase_D(nc, dr):
    P = Prog(nc)
    qT, kT, vT, tri_d = dr["qT_cs"], dr["kT_cs"], dr["vT_cs"], dr["ntri"]
    ident = P.sb("ident", [128, 128], BF16)
    P.op("pool", lambda E: E.memset(ident[:, :], 0.0), writes=[ident])
    P.op("pool", lambda E: E.affine_select(out=ident[:, :], in_=ident[:, :], pattern=[[-1, 128]], compare_op=ALU.not_equal,
                                           fill=1.0, base=0, channel_multiplier=1), reads=[ident], writes=[ident])
    vtb = Rot(P, "vtb", 3, [64, 2048], BF16)
    tpb = P.ps("tpb", [128, NT], BF16)
    zb = [P.ps("zb%d" % i, [128, NT], F32) for i in range(4)]
    ob = [P.ps("ob%d" % i, [64, NT], F32) for i in range(2)]
    st = P.sb("tri_st", [128, 128], F32)
    P.dma("sp", st[:, :], tri_d.t[:, :], reads=[tri_d], writes=[st])
    ntri = P.sb("ntri", [128, 128], BF16)
    cp(P, "dve", ntri[:, :], st[:, :], [st], [ntri])
    nones = P.sb("nones", [128, 128], BF16)
    P.op("pool", lambda E: E.memset(nones[:, :], -1.0), writes=[nones])
    Qb = P.sb("Qb", [128, 2, SEQ], BF16)
    Kb = P.sb("Kb", [128, 2, SEQ], BF16)
    Vb = [P.sb("Vb%d" % h, [128, 64 * 64], BF16) for h in range(NH)]
    er = Rot(P, "er", 3, [128, NT], F32)
    spr = Rot(P, "spr", 4, [128, NT], BF16)
    wr = Rot(P, "wr", 4, [128, NT], BF16)
    racc = Rot(P, "racc", 3, [128, NT], BF16)
    ost = Rot(P, "ost", 2, [64, NT], BF16)
    for pr in range(2):
        for c in range(SEQ // 2048):
            P.dma("sp", Qb[:, pr, c * 2048:(c + 1) * 2048], qT.t[pr * 128:(pr + 1) * 128, c * 2048:(c + 1) * 2048], reads=[qT], writes=[Qb])
            P.dma("sp", Kb[:, pr, c * 2048:(c + 1) * 2048], kT.t[pr * 128:(pr + 1) * 128, c * 2048:(c + 1) * 2048], reads=[kT], writes=[Kb])
    for h in range(NH):
        for c in range(SEQ // 2048):
            vb_ = vtb.get()
            P.dma("sp", vb_[:, :], vT.t[h * 64:(h + 1) * 64, c * 2048:(c + 1) * 2048], reads=[vT], writes=[vb_])
            for k8 in range(2):
                for j in range(8):
                    blk = k8 * 8 + j
                    P.op("pe", lambda E, vb_=vb_, j=j, blk=blk: E.transpose(
                        out=tpb[:, j * 64:(j + 1) * 64], in_=vb_[:, blk * 128:(blk + 1) * 128], identity=ident[0:64, 0:64]),
                        reads=[vb_, ident], writes=[tpb])
                kb0 = c * 16 + k8 * 8
                cp(P, "dve", Vb[h][:, kb0 * 64:(kb0 + 8) * 64], tpb[:, :], [tpb], [Vb[h]])
    blocks = []
    for h in range(NH):
        for qt in range(NQT):
            kbs = list(range(4 * qt + 3, -1, -1))
            for idx, kb in enumerate(kbs):
                blocks.append(dict(h=h, qt=qt, kb=kb, idx=idx, n=len(kbs), g=h * NQT + qt))
    nb = len(blocks)

    def operands(B):
        hp, pr = B["h"] % 2, B["h"] // 2
        ksl = Kb[hp * 64:(hp + 1) * 64, pr, B["kb"] * 128:(B["kb"] + 1) * 128]
        qsl = Qb[hp * 64:(hp + 1) * 64, pr, B["qt"] * NT:(B["qt"] + 1) * NT]
        return ksl, qsl

    def mask(t, B):
        base = B["qt"] * NT - 128 * B["kb"]
        P.op("pool", lambda E, t=t, base=base: E.affine_select(
            out=t[:, :], in_=t[:, :], pattern=[[1, NT]], compare_op=ALU.is_gt, fill=0.0, base=base, channel_multiplier=-1),
            reads=[t], writes=[t])

    def stage1a(i):
        B = blocks[i]
        ksl, qsl = operands(B)
        z = zb[i % 4]
        P.op("pe", lambda E, z=z, ksl=ksl, qsl=qsl: E.matmul(z[:, :], lhsT=ksl, rhs=qsl, start=True, stop=False), reads=[Kb, Qb], writes=[z])
        e = er.get()
        act(P, e[:, :], z[:, :], AF.Exp, [z], [e])
        B["e"] = e

    def stage1b(i):
        B = blocks[i]
        e = B["e"]
        sp = spr.get()
        P.op("act", lambda E, sp=sp, e=e: E.activation(out=sp[:, :], in_=e[:, :], func=AF.Ln, scale=1.0, bias=1.0),
             reads=[e], writes=[sp], skip_same=True)
        if B["kb"] >= 4 * B["qt"]:
            mask(sp, B)
        B["sp"] = sp

    def stage2(i):
        B = blocks[i]
        ksl, qsl = operands(B)
        b = zb[i % 4]
        sp = B["sp"]
        first = B["idx"] == 0
        ra_prev = None if first else blocks[i - 1]["ra"]
        P.op("pe", lambda E, b=b, sp=sp, first=first: E.matmul(b[:, :], lhsT=ntri[:, :], rhs=sp[:, :], start=False, stop=first), reads=[ntri, sp], writes=[b])
        if not first:
            P.op("pe", lambda E, b=b, ra=ra_prev: E.matmul(b[:, :], lhsT=nones[:, :], rhs=ra[:, :], start=False, stop=True), reads=[nones, ra_prev], writes=[b])
        w = wr.get()
        act(P, w[:, :], b[:, :], AF.Exp, [b], [w])
        if B["kb"] >= 4 * B["qt"]:
            mask(w, B)
        B["w"] = w
        if B["idx"] + 1 < B["n"]:
            ra = racc.get()
            if first:
                cp(P, "dve", ra[:, :], sp[:, :], [sp], [ra])
            else:
                tt(P, "dve", ra[:, :], ra_prev[:, :], sp[:, :], ALU.add, [ra_prev, sp], [ra])
            B["ra"] = ra

    def stage3(i):
        B = blocks[i]
        o_ps = ob[B["g"] % 2]
        w = B["w"]
        h, kb = B["h"], B["kb"]
        P.op("pe", lambda E, o_ps=o_ps, w=w, h=h, kb=kb, B=B: E.matmul(
            o_ps[:, :], lhsT=Vb[h][:, kb * 64:(kb + 1) * 64], rhs=w[:, :], start=(B["idx"] == 0), stop=(B["idx"] == B["n"] - 1)),
            reads=[Vb[h], w], writes=[o_ps])
        if B["idx"] == B["n"] - 1:
            o = ost.get()
            cp(P, "dve", o[:, :], o_ps[:, :], [o_ps], [o])
            q0 = B["qt"] * NT
            osrc = dr["o_src"][q0 // 2048]
            P.dma("act", osrc.t[h * 64:(h + 1) * 64, q0 % 2048:q0 % 2048 + NT], o[:, :], reads=[o], writes=[osrc])

    for j in range(nb + 2):
        if j < nb:
            stage1a(j)
            stage1b(j)
        if 0 <= j - 1 < nb:
            stage2(j - 1)
        if 0 <= j - 2 < nb:
            stage3(j - 2)
    for k in range(4):
        P.coll("AllGather", dr["o_src"][k], dr["o_all"][k], GROUPS)
    P.wait_all("act", dr["o_all"])
    P.emit()


GROUPS = [[0, 1, 2, 3], [4, 5, 6, 7]]
TOKC = 2048
TP = 1024


def proj_pass(P, dn, src_fn, xs, rs, jobs, col0):
    for kc in range(8):
        srcs = src_fn(kc)
        if isinstance(srcs, tuple):
            P.dma("sp", xs[:, kc, :], srcs[1], reads=[srcs[0]], writes=[xs])
        else:
            w_ = TP // len(srcs)
            for i_, (sr_, sa_) in enumerate(srcs):
                P.dma("sp", xs[:, kc, i_ * w_:(i_ + 1) * w_], sa_, reads=[sr_], writes=[xs])
    dn.rstd(xs, rs)
    done = {}
    for job in jobs:
        xb, gain, wbs, MC, dst = job[:5]
        oscale = job[5] if len(job) > 5 else None
        if id(xb) not in done:
            done[id(xb)] = 1
            for kc in range(8):
                scale_cast(P, kc, xb[:, kc, :], xs[:, kc, :], gain[:, kc:kc + 1], [xs, gain], [xb])

        def epi(m, n, ps, dst=dst, oscale=oscale):
            if oscale is None:
                o = dn.ost.get()
                tt(P, "dve", o[:, :], ps[:, :], rs[:, n * NT:(n + 1) * NT], ALU.mult, [ps, rs], [o])
            else:
                o = dn.ostb.get()
                P.op("dve", lambda E, o=o, ps=ps, n=n: E.scalar_tensor_tensor(
                    out=o[:, :], in0=ps[:, :], scalar=oscale, in1=rs[:, n * NT:(n + 1) * NT], op0=ALU.mult, op1=ALU.mult),
                    reads=[ps, rs], writes=[o])
            dn.store(dst, m, n, col0, o, o[:, :])
        dn.linear(None, 8, MC, xb, epi, wbs=wbs)


def phase_A(nc, dr):
    P = Prog(nc)
    P.mk_banks(6)
    dn = Dense(P, TP)
    g = colvec(P, "g_pre", dr["g_pre"], 8)
    xsr = Rot(P, "xs", 2, [128, 8, TP], F32)
    xb = P.sb("xb", [128, 8, TP], BF16)
    rs = P.sb("rs", [128, TP], F32)
    xT = dr["xT_full"]
    wbs = dn.prep_w("wu", dr["w_in_u"], 8, 2)
    for pa in range(SEQ // TP):
        t0 = pa * TP
        proj_pass(P, dn, lambda kc, t0=t0: (xT, xT.t[kc * 128:(kc + 1) * 128, t0:t0 + TP]), xsr.get(), rs,
                  [(xb, g, wbs, 2, dr["uT_cs"])], t0)
    P.wait_all("pool", [dr["uT_cs"]])
    P.emit()


def select4(P, dn, src_fn, sel, dt):
    acc = dn.tmp.get()
    for s in range(4):
        it = dn.ist.get() if dt == F32 else dn.istb.get()
        sres, sap = src_fn(s)
        P.dma("sp", it[:, :], sap, reads=[sres], writes=[it])
        if s == 0:
            ts(P, "dve", acc[:, :], it[:, :], sel[:, 0:1], None, ALU.mult, None, [it, sel], [acc])
        else:
            P.op("dve", lambda E, it=it, s=s, acc=acc: E.scalar_tensor_tensor(
                out=acc[:, :], in0=it[:, :], scalar=sel[:, s:s + 1], in1=acc[:, :], op0=ALU.mult, op1=ALU.add),
                reads=[it, sel, acc], writes=[acc])
    return acc


def phase_C(nc, dr):
    P = Prog(nc)
    P.mk_banks(7)
    dn = Dense(P, TP)
    dn.istb = Rot(P, "istb", 4, [128, NT], BF16)
    sel = colvec(P, "sel", dr["sel"], 4)
    gpre = colvec(P, "g_pre", dr["g_pre"], 8)
    vl = []
    for i in range(4):
        t_ = P.sb("vec%d" % i, [128, 8], F32)
        P.dma("sp", t_[:, :], dr["vecsC"].t[:, i * 8:(i + 1) * 8], reads=[dr["vecsC"]], writes=[t_])
        vl.append(t_)
    bglu, gpost, gbpre, _unused = vl
    xT, y_all, pT = dr["xT_own"], dr["y_all"], dr["p0T"]
    Gb = P.sb("Gb", [128, 8, TP], BF16)
    SGb = P.sb("SGb", [128, 8, TP], BF16)
    Y2b = P.sb("Y2b", [128, 8, TP], BF16)
    X1 = P.sb("X1", [128, 8, TP], F32)
    Pb = P.sb("Pb", [128, 2, TP], BF16)
    rs = P.sb("rs", [128, TP], F32)
    for pa in range(TOKC // TP):
        t0 = pa * TP
        for kc in range(8):
            P.dma("sp", X1[:, kc, :], xT.t[kc * 128:(kc + 1) * 128, t0:t0 + TP], reads=[xT], writes=[X1])
        dn.rstd(X1, rs)
        for kc in range(8):
            scale_cast(P, kc, Y2b[:, kc, :], X1[:, kc, :], gpre[:, kc:kc + 1], [X1, gpre], [Y2b])

        def epi_gate(m, n, ps):
            t = dn.tmp.get()
            tt(P, "dve", t[:, :], ps[:, :], rs[:, n * NT:(n + 1) * NT], ALU.mult, [ps, rs], [t])
            act(P, SGb[:, m, n * NT:(n + 1) * NT], t[:, :], AF.Silu, [t], [SGb])
        dn.linear(dr["w_in_g"], 8, 8, Y2b, epi_gate)
        for kc in range(8):
            for n in range(TP // NT):
                def ysrc_fn(s, n=n, kc=kc):
                    g0 = s * TOKC + t0 + n * NT
                    ya = y_all[g0 // 2048]
                    return ya, ya.t[kc * 128:(kc + 1) * 128, g0 % 2048:g0 % 2048 + NT]
                y = select4(P, dn, ysrc_fn, sel, BF16)
                s2 = gelu_tanh(P, dn, None, y[:, :], [y])
                tt(P, "pool", Gb[:, kc, n * NT:(n + 1) * NT], s2[:, :], y[:, :], ALU.mult, [s2, y], [Gb])

        def epi_glu(m, n, ps):
            sg = dn.tmp.get()
            act(P, sg[:, :], ps[:, :], AF.Sigmoid, [ps, bglu], [sg], bias=bglu[:, m:m + 1])
            t = dn.tmp.get()
            tt(P, "dve", t[:, :], sg[:, :], Gb[:, m, n * NT:(n + 1) * NT], ALU.mult, [sg, Gb], [t])
            tt(P, "pool", Y2b[:, m, n * NT:(n + 1) * NT], t[:, :], SGb[:, m, n * NT:(n + 1) * NT], ALU.mult, [t, SGb], [Y2b])
        dn.linear(dr["w_glu"], 8, 8, Gb, epi_glu)

        def epi_out(m, n, ps):
            act(P, X1[:, m, n * NT:(n + 1) * NT], ps[:, :], AF.Copy, [ps], [X1])
        dn.linear(dr["w_out0"], 8, 8, Y2b, epi_out)
        dn.rstd(X1, rs)
        for kc in range(8):
            for n in range(TP // NT):
                c0 = t0 + n * NT
                ix = dn.ist.get()
                P.dma("sp", ix[:, :], xT.t[kc * 128:(kc + 1) * 128, c0:c0 + NT], reads=[xT], writes=[ix])
                t = dn.tmp.get()
                P.op("dve", lambda E, t=t, kc=kc, n=n: E.scalar_tensor_tensor(
                    out=t[:, :], in0=X1[:, kc, n * NT:(n + 1) * NT], scalar=gpost[:, kc:kc + 1], in1=rs[:, n * NT:(n + 1) * NT],
                    op0=ALU.mult, op1=ALU.mult), reads=[X1, gpost, rs], writes=[t])
                tt(P, "pool", X1[:, kc, n * NT:(n + 1) * NT], t[:, :], ix[:, :], ALU.add, [t, ix], [X1])
                cp(P, "act", Gb[:, kc, n * NT:(n + 1) * NT], X1[:, kc, n * NT:(n + 1) * NT], [X1], [Gb])
        for kc in range(2):
            for n in range(TP // NT):
                c0 = t0 + n * NT
                ip = dn.ist.get()
                P.dma("sp", ip[:, :], pT.t[kc * 128:(kc + 1) * 128, c0:c0 + NT], reads=[pT], writes=[ip])
                cp(P, "dve", Pb[:, kc, n * NT:(n + 1) * NT], ip[:, :], [ip], [Pb])
        ple(P, dn, dr["w_pg0"], dr["w_pp0"], Gb, Pb, X1)
        dn.rstd(X1, rs)
        for kc in range(8):
            cp(P, "dve", Y2b[:, kc, :], X1[:, kc, :], [X1], [Y2b])
            scale_cast(P, kc + 1, SGb[:, kc, :], X1[:, kc, :], gbpre[:, kc:kc + 1], [X1, gbpre], [SGb])
            P.dma("pool", dr["x1_own"].t[kc * 128:(kc + 1) * 128, t0:t0 + TP], X1[:, kc, :], reads=[X1], writes=[dr["x1_own"]])
            for n in range(TP // NT):
                xsrc = dr["x1_src"][(t0 + n * NT) // NT]
                P.dma("pool", xsrc.t[kc * 128:(kc + 1) * 128, :], Y2b[:, kc, n * NT:(n + 1) * NT], reads=[Y2b], writes=[xsrc])

        def epi_g1(m, n, ps):
            o = dn.ost.get()
            tt(P, "dve", o[:, :], ps[:, :], rs[:, n * NT:(n + 1) * NT], ALU.mult, [ps, rs], [o])
            dn.store(dr["g1T"], m, n, t0, o, o[:, :])
        dn.linear(dr["w_bin_g"], 8, 8, SGb, epi_g1)
    for k in range(4):
        P.coll("AllGather", dr["x1_src"][k], dr["x1_all"][k], GROUPS)
    P.wait_all("pool", dr["x1_all"] + [dr["x1_own"], dr["g1T"]])
    P.emit()


def phase_QKV(nc, dr):
    P = Prog(nc)
    P.mk_banks(6)
    dn = Dense(P, TP)
    gkv = colvec(P, "g_kv", dr["g_kv"], 8)
    gbpre = colvec(P, "g_bpre", dr["g_bpre"], 8)
    xsr = Rot(P, "xs", 2, [128, 8, TP], BF16)
    xq = P.sb("xq", [128, 8, TP], BF16)
    xk = P.sb("xk", [128, 8, TP], BF16)
    rs = P.sb("rs", [128, TP], F32)
    xa = dr["x1_all"]
    wq = dn.prep_w("wq", dr["w_q"], 8, 2)
    wk = dn.prep_w("wk", dr["w_k"], 8, 2)
    wv = dn.prep_w("wv", dr["w_v"], 8, 2)
    for pa in range(SEQ // TP):
        t0 = pa * TP
        s, tl = t0 // TOKC, t0 % TOKC
        proj_pass(P, dn, lambda kc, s=s, tl=tl: [(xa[(tl + h_ * NT) // NT], xa[(tl + h_ * NT) // NT].t[s * D + kc * 128:s * D + (kc + 1) * 128, :]) for h_ in range(TP // NT)], xsr.get(), rs,
                  [(xq, gbpre, wq, 2, dr["qT_cs"], 1.0), (xk, gkv, wk, 2, dr["kT_cs"], 0.125), (xk, gkv, wv, 2, dr["vT_cs"], 1.0)], t0)
    P.wait_all("pool", [dr["qT_cs"], dr["kT_cs"], dr["vT_cs"]])
    P.emit()


def phase_E(nc, dr):
    P = Prog(nc)
    P.mk_banks(7)
    dn = Dense(P, TP)
    dn.istb = Rot(P, "istb", 4, [128, NT], BF16)
    sel = colvec(P, "sel", dr["sel"], 4)
    gpost = colvec(P, "g_bpost", dr["g_bpost"], 8)
    Ob = P.sb("Ob", [128, 8, TP], BF16)
    Xb = P.sb("Xb", [128, 8, TP], BF16)
    X1 = P.sb("X1", [128, 8, TP], F32)
    Pb = P.sb("Pb", [128, 2, TP], BF16)
    rs = P.sb("rs", [128, TP], F32)
    x1T, gT, pT, outT = dr["x1_own"], dr["g1T"], dr["p1T"], dr["outT"]
    for pa in range(TOKC // TP):
        t0 = pa * TP
        for kc in range(8):
            for n in range(TP // NT):
                c0 = t0 + n * NT
                def osrc_fn(s, n=n, kc=kc):
                    g0 = s * TOKC + t0 + n * NT
                    oa = dr["o_all"][g0 // 2048]
                    return oa, oa.t[kc * 128:(kc + 1) * 128, g0 % 2048:g0 % 2048 + NT]
                o = select4(P, dn, osrc_fn, sel, BF16)
                ig = dn.ist.get()
                P.dma("sp", ig[:, :], gT.t[kc * 128:(kc + 1) * 128, c0:c0 + NT], reads=[gT], writes=[ig])
                sg = dn.tmp.get()
                act(P, sg[:, :], ig[:, :], AF.Silu, [ig], [sg])
                tt(P, "pool", Ob[:, kc, n * NT:(n + 1) * NT], sg[:, :], o[:, :], ALU.mult, [sg, o], [Ob])

        def epi_out(m, n, ps):
            act(P, X1[:, m, n * NT:(n + 1) * NT], ps[:, :], AF.Copy, [ps], [X1])
        dn.linear(dr["w_out1"], 8, 8, Ob, epi_out)
        dn.rstd(X1, rs)
        for kc in range(8):
            for n in range(TP // NT):
                c0 = t0 + n * NT
                ix = dn.ist.get()
                P.dma("sp", ix[:, :], x1T.t[kc * 128:(kc + 1) * 128, c0:c0 + NT], reads=[x1T], writes=[ix])
                t = dn.tmp.get()
                P.op("dve", lambda E, t=t, kc=kc, n=n: E.scalar_tensor_tensor(
                    out=t[:, :], in0=X1[:, kc, n * NT:(n + 1) * NT], scalar=gpost[:, kc:kc + 1], in1=rs[:, n * NT:(n + 1) * NT],
                    op0=ALU.mult, op1=ALU.mult), reads=[X1, gpost, rs], writes=[t])
                tt(P, "pool", X1[:, kc, n * NT:(n + 1) * NT], t[:, :], ix[:, :], ALU.add, [t, ix], [X1])
                cp(P, "act", Xb[:, kc, n * NT:(n + 1) * NT], X1[:, kc, n * NT:(n + 1) * NT], [X1], [Xb])
        for kc in range(2):
            for n in range(TP // NT):
                c0 = t0 + n * NT
                ip = dn.ist.get()
                P.dma("sp", ip[:, :], pT.t[kc * 128:(kc + 1) * 128, c0:c0 + NT], reads=[pT], writes=[ip])
                cp(P, "dve", Pb[:, kc, n * NT:(n + 1) * NT], ip[:, :], [ip], [Pb])
        ple(P, dn, dr["w_pg1"], dr["w_pp1"], Xb, Pb, X1)
        for kc in range(8):
            P.dma("pool", outT.t[kc * 128:(kc + 1) * 128, t0:t0 + TP], X1[:, kc, :], reads=[X1], writes=[outT])
    P.wait_all("pool", [outT])
    P.emit()


IN_SPECS = {
    "xT_full": ([D, SEQ], F32), "xT_own": ([D, TOKC], F32), "p0T": ([256, TOKC], F32), "p1T": ([256, TOKC], F32),
    "sel": ([128, 4], F32), "g_pre": ([128, 8], F32), "w_in_u": ([D, 256], F32), "w_in_g": ([D, D], F32),
    "lamre_T": ([128, 128], F32), "lamim_T": ([128, 128], F32), "logdt_T": ([128, 128], F32), "bre_T": ([128, 128], F32),
    "bim_T": ([128, 128], F32), "lamre_S": ([128, NG], F32), "lamim_S": ([128, NG], F32), "logdt_S": ([128, NG], F32),
    "c1_S": ([128, NG * 16], F32), "c2_S": ([128, NG * 16], F32), "iota": ([128, TS + 1], F32), "gmask": ([128, 8], F32),
    "sgn": ([128, 1], F32), "Jm": ([128, 128], F32), "dskip_cs": ([128, 2], F32), "vecsC": ([128, 32], F32),
    "w_glu": ([D, D], F32), "w_out0": ([D, D], F32), "w_pg0": ([D, D], F32), "w_pp0": ([256, D], F32),
    "w_bin_g": ([D, D], F32), "g_kv": ([128, 8], F32), "g_bpre": ([128, 8], F32), "w_q": ([D, 256], F32),
    "w_k": ([D, 256], F32), "w_v": ([D, 256], F32), "ntri": ([128, 128], F32), "g_bpost": ([128, 8], F32),
    "w_out1": ([D, D], F32), "w_pg1": ([D, D], F32), "w_pp1": ([256, D], F32),
}
SCRATCH = {
    "uT_cs": ([256, SEQ], F32), "y_src": ([256, 2048], BF16, 4), "y_all": ([D, 2048], BF16, 4), "x1_own": ([D, TOKC], F32),
    "x1_src": ([D, NT], BF16, 4), "x1_all": ([4 * D, NT], BF16, 4), "g1T": ([D, TOKC], F32), "qT_cs": ([256, SEQ], BF16),
    "kT_cs": ([256, SEQ], BF16), "vT_cs": ([256, SEQ], BF16), "o_src": ([256, 2048], BF16, 4), "o_all": ([D, 2048], BF16, 4),
}


def build_fused():
    nc = bass.Bass("TRN2", target_bir_lowering=False)
    dr = {}
    for n, (shp, dt) in IN_SPECS.items():
        dr[n] = Res(n, nc.dram_tensor(n, list(shp), dt, kind="ExternalInput").ap())
    for n, spec in SCRATCH.items():
        shp, dt = spec[0], spec[1]
        if len(spec) == 3:
            dr[n] = [Res("%s%d" % (n, i), nc.dram_tensor("%s%d" % (n, i), list(shp), dt, kind="Internal").ap()) for i in range(spec[2])]
        else:
            dr[n] = Res(n, nc.dram_tensor(n, list(shp), dt, kind="Internal").ap())
    dr["outT"] = Res("outT", nc.dram_tensor("outT", [D, TOKC], F32, kind="ExternalOutput").ap())
    phase_A(nc, dr)
    phase_B(nc, dr)
    phase_C(nc, dr)
    phase_QKV(nc, dr)
    phase_D(nc, dr)
    phase_E(nc, dr)
    Prog.finish()
    return nc


def _f(a):
    return np.ascontiguousarray(np.asarray(a, dtype=np.float32))


def kernel(**inputs):
    inp = {k: np.asarray(v) for k, v in inputs.items()}
    x, p = inp["x"], inp["p"]
    lam_re, lam_im, log_dt = _f(inp["a_lam_re"][0]), _f(inp["a_lam_im"][0]), _f(inp["a_log_dt"][0])
    b_re, b_im, c_re, c_im = _f(inp["a_b_re"][0]), _f(inp["a_b_im"][0]), _f(inp["a_c_re"][0]), _f(inp["a_c_im"][0])
    iota = _f(np.broadcast_to(np.arange(TS + 1, dtype=np.float32), (128, TS + 1)))
    gmask = np.zeros((128, 8), np.float32)
    for gl in range(8):
        gmask[gl * 16:(gl + 1) * 16, gl] = 1.0
    sgn = np.ones((128, 1), np.float32)
    sgn[64:] = -1.0
    J = np.zeros((128, 128), np.float32)
    for q in range(64):
        J[64 + q, q] = -1.0
        J[q, 64 + q] = 1.0
    ntri = np.zeros((128, 128), np.float32)
    for j in range(128):
        ntri[j, :j + 1] = -1.0
    vecsC = np.zeros((128, 32), np.float32)
    for i, v in enumerate([inp["a_b_glu"][0], inp["a_norm_post"][0], inp["b_norm_pre"][0]]):
        vecsC[:, i * 8:(i + 1) * 8] = _cols(v)
    a_w_in, b_w_in, w_kv = _f(inp["a_w_in"][0]), _f(inp["b_w_in"][0]), _f(inp["w_kv"])
    common = {
        "g_pre": _cols(inp["a_norm_pre"][0]), "w_in_g": _f(a_w_in[:, D:]), "iota": iota, "gmask": gmask, "sgn": sgn, "Jm": J,
        "vecsC": vecsC, "w_glu": _f(inp["a_w_glu"][0]), "w_out0": _f(inp["a_w_out"][0]), "w_pg0": _f(inp["ple_w_gate"][0]),
        "w_pp0": _f(inp["ple_w_proj"][0]), "w_bin_g": _f(b_w_in[:, D:]), "g_kv": _cols(inp["kv_norm"]),
        "g_bpre": _cols(inp["b_norm_pre"][0]), "ntri": ntri, "g_bpost": _cols(inp["b_norm_post"][0]),
        "w_out1": _f(inp["b_w_out"][0]), "w_pg1": _f(inp["ple_w_gate"][1]), "w_pp1": _f(inp["ple_w_proj"][1]),
    }
    xT_full = [_f(np.asarray(x[b], np.float32).T) for b in range(2)]
    maps = []
    for c in range(NCORE):
        b, r = c // 4, c % 4
        gs = np.arange(16 * r, 16 * r + 16)
        tsl = slice(r * TOKC, (r + 1) * TOKC)
        csl = slice(256 * r, 256 * r + 256)
        m = dict(common)
        m["xT_full"] = xT_full[b]
        m["xT_own"] = _f(xT_full[b][:, tsl])
        m["p0T"] = _f(np.asarray(p[0, b, tsl, :], np.float32).T)
        m["p1T"] = _f(np.asarray(p[1, b, tsl, :], np.float32).T)
        sel = np.zeros((128, 4), np.float32)
        sel[:, r] = 1.0
        m["sel"] = sel
        m["w_in_u"] = _f(a_w_in[:, csl])
        m["dskip_cs"] = _cols(inp["a_d_skip"][0][csl])
        m["w_q"] = _f(b_w_in[:, csl])
        m["w_k"] = _f(w_kv[:, csl])
        m["w_v"] = _f(w_kv[:, D + 256 * r:D + 256 * r + 256])

        def lt_gp(a):
            t = a[gs].reshape(2, 8, 64)
            t = np.broadcast_to(t[:, :, None, :], (2, 8, 16, 64))
            return _f(t.transpose(1, 2, 0, 3).reshape(128, 128))

        def lt_b(a):
            t = a[gs].reshape(2, 8, 64, 16)
            return _f(t.transpose(1, 3, 0, 2).reshape(128, 128))

        def sp_gp(a):
            t = a[gs].T
            return _f(np.concatenate([t, t], axis=0))
        ldt = np.broadcast_to(log_dt[:, None], (64, 64))
        m["lamre_T"], m["lamim_T"], m["logdt_T"] = lt_gp(lam_re), lt_gp(lam_im), lt_gp(ldt)
        m["bre_T"], m["bim_T"] = lt_b(b_re), lt_b(b_im)
        m["lamre_S"], m["lamim_S"], m["logdt_S"] = sp_gp(lam_re), sp_gp(lam_im), sp_gp(ldt)
        cr = c_re[gs].transpose(2, 0, 1).reshape(64, 256)
        ci = c_im[gs].transpose(2, 0, 1).reshape(64, 256)
        m["c1_S"] = _f(np.concatenate([cr, ci], axis=0))
        m["c2_S"] = _f(np.concatenate([ci, cr], axis=0))
        maps.append(m)
    res = _run(build_fused(), maps)
    out = np.empty((2, SEQ, D), np.float32)
    for c in range(NCORE):
        b, r = c // 4, c % 4
        out[b, r * TOKC:(r + 1) * TOKC, :] = res[c]["outT"].T
    return out
```

```python
from contextlib import ExitStack
import numpy as np
import concourse.bass as bass
import concourse.mybir as mybir
from concourse.bass_utils import run_bass_kernel_spmd

F32 = mybir.dt.float32
BF16 = mybir.dt.bfloat16
AF = mybir.ActivationFunctionType
ALU = mybir.AluOpType

ENGS = ("pe", "act", "dve", "pool", "sp")
NCORE = 8
D = 1024
SEQ = 8192
NT = 512
EPS = 1e-6
PI = float(np.pi)


class Res:
    __slots__ = ("name", "w", "r", "dsem", "dcnt", "t")

    def __init__(self, name, t=None):
        self.name = name
        self.w = {}
        self.r = {}
        self.dsem = None
        self.dcnt = 0
        self.t = t

    def __getitem__(self, idx):
        return self.t[idx]


class Prog:
    _n = 0
    G = None

    def __init__(self, nc):
        Prog._n += 1
        self.pfx = "f%d_" % Prog._n
        self.nc = nc
        if Prog.G is None or Prog.G["nc"] is not nc:
            ges = ExitStack()
            Prog.G = {"nc": nc, "es": ges, "sems": {}, "cnt": {}}
            for e in ENGS:
                Prog.G["sems"][e] = ges.enter_context(nc.semaphore("s_" + e))
                Prog.G["cnt"][e] = 0
        G = Prog.G
        self.es = ExitStack()
        self.lists = {e: [] for e in ENGS}
        self.sems = G["sems"]
        self.cnt = G["cnt"]
        self.seen = {e: dict(self.cnt) for e in ENGS}
        self.nd = 0
        self.banks = []
        self.bi = 0
        self.touched = {}

    @staticmethod
    def finish():
        if Prog.G is not None:
            Prog.G["es"].close()
            Prog.G = None

    def _newsem(self):
        key = "d%d" % self.nd
        self.nd += 1
        if key not in self.sems:
            self.sems[key] = Prog.G["es"].enter_context(self.nc.semaphore("sd_" + key))
            self.cnt[key] = 0
        return key

    def sb(self, name, shape, dt):
        return Res(name, self.es.enter_context(self.nc.sbuf_tensor(self.pfx + "sb_" + name, list(shape), dt)))

    def ps(self, name, shape, dt=F32):
        return Res(name, self.es.enter_context(self.nc.psum_tensor(self.pfx + "ps_" + name, list(shape), dt)))

    def dram(self, name, shape, dt, kind="Internal"):
        return Res(name, self.nc.dram_tensor(name, list(shape), dt, kind=kind).ap())

    def mk_banks(self, n):
        self.banks = [self.ps("bank%d" % i, [128, NT], F32) for i in range(n)]

    def bank(self):
        b = self.banks[self.bi % len(self.banks)]
        self.bi += 1
        return b

    def _dsem(self, res):
        if res.dsem is None:
            res.dsem = self._newsem()
        return res.dsem

    def _waits(self, eng, reads, writes, skip_same=False):
        for x_ in reads:
            self.touched[id(x_)] = x_
        for x_ in writes:
            self.touched[id(x_)] = x_
        deps = {}
        for r in reads:
            for k, v in r.w.items():
                if skip_same and k == eng:
                    continue
                if v > deps.get(k, 0):
                    deps[k] = v
        for w in writes:
            for k, v in w.w.items():
                if k != eng and v > deps.get(k, 0):
                    deps[k] = v
            for k, v in w.r.items():
                if k != eng and v > deps.get(k, 0):
                    deps[k] = v
        seen = self.seen[eng]
        for k, v in deps.items():
            if v > seen.get(k, 0):
                seen[k] = v
                sem = self.sems[k]
                self.lists[eng].append(lambda E, sem=sem, v=v: E.wait_ge(sem, v))

    def op(self, eng, fn, reads=(), writes=(), skip_same=False):
        self._waits(eng, reads, writes, skip_same)
        self.cnt[eng] += 1
        n = self.cnt[eng]
        sem = self.sems[eng]
        self.lists[eng].append(lambda E, fn=fn, sem=sem: fn(E).then_inc(sem, 1))
        for r in reads:
            r.r[eng] = n
        for w in writes:
            w.w[eng] = n

    def dma(self, eng, out_ap, in_ap, reads=(), writes=()):
        wres = writes[0]
        self._waits(eng, reads, writes)
        key = self._dsem(wres)
        self.cnt[key] += 16
        v = self.cnt[key]
        sem = self.sems[key]
        self.lists[eng].append(
            lambda E, o=out_ap, i=in_ap, sem=sem: E.dma_start(out=o, in_=i).then_inc(sem, 16))
        for r in reads:
            r.r[key] = v
        wres.w[key] = v

    def coll(self, kind, src, dst, groups):
        self._waits("pool", [src], [dst])
        key = self._newsem()
        self.cnt[key] += 1
        v = self.cnt[key]
        sem = self.sems[key]
        self.lists["pool"].append(lambda E, sem=sem: E.collective_compute(
            kind, ALU.bypass, replica_groups=groups, ins=[src.t.opt()], outs=[dst.t.opt()]).then_inc(sem))
        src.r[key] = v
        dst.w[key] = v

    def wait_all(self, eng, ress):
        self._waits(eng, ress, ())

    def emit(self):
        L = self.lists
        for e in ENGS:
            for k, sem in self.sems.items():
                tgt = self.cnt[k]
                if k != e and tgt > self.seen[e].get(k, 0):
                    self.seen[e][k] = tgt
                    L[e].append(lambda E, sem=sem, tgt=tgt: E.wait_ge(sem, tgt))
        with self.nc.Block() as block:
            @block.tensor
            def _(E):
                for f in L["pe"]:
                    f(E)

            @block.scalar
            def _(E):
                for f in L["act"]:
                    f(E)

            @block.vector
            def _(E):
                for f in L["dve"]:
                    f(E)

            @block.gpsimd
            def _(E):
                for f in L["pool"]:
                    f(E)

            @block.sync
            def _(E):
                for f in L["sp"]:
                    f(E)
        self.es.close()
        for x_ in self.touched.values():
            x_.w = {}
            x_.r = {}
            x_.dsem = None


def tt(P, eng, out, in0, in1, op, reads, writes):
    P.op(eng, lambda E: E.tensor_tensor(out=out, in0=in0, in1=in1, op=op), reads=reads, writes=writes)


def ts(P, eng, out, in0, s1, s2, op0, op1, reads, writes):
    if s2 is None:
        P.op(eng, lambda E: E.tensor_scalar(out=out, in0=in0, scalar1=s1, scalar2=None, op0=op0), reads=reads, writes=writes)
    else:
        P.op(eng, lambda E: E.tensor_scalar(out=out, in0=in0, scalar1=s1, scalar2=s2, op0=op0, op1=op1), reads=reads, writes=writes)


def act(P, out, in_, func, reads, writes, scale=1.0, bias=None):
    if bias is None:
        P.op("act", lambda E: E.activation(out=out, in_=in_, func=func, scale=scale), reads=reads, writes=writes)
    else:
        P.op("act", lambda E: E.activation(out=out, in_=in_, func=func, scale=scale, bias=bias), reads=reads, writes=writes)


def scale_cast(P, i, out, in_, col, reads, writes):
    if i % 2:
        P.op("act", lambda E: E.activation(out=out, in_=in_, func=AF.Copy, scale=col), reads=reads, writes=writes)
    else:
        ts(P, "dve", out, in_, col, None, ALU.mult, None, reads, writes)


def cp(P, eng, out, in_, reads, writes):
    if eng == "act":
        P.op(eng, lambda E: E.activation(out=out, in_=in_, func=AF.Copy), reads=reads, writes=writes)
    else:
        P.op(eng, lambda E: E.tensor_copy(out=out, in_=in_), reads=reads, writes=writes)


class Rot:
    def __init__(self, P, name, n, shape, dt):
        self.bufs = [P.sb("%s%d" % (name, i), shape, dt) for i in range(n)]
        self.i = 0

    def get(self):
        b = self.bufs[self.i % len(self.bufs)]
        self.i += 1
        return b


class Dense:
    def __init__(self, P, T):
        self.P = P
        self.T = T
        self.wst = Rot(P, "wst", 4, [128, 8, 128], F32)
        self.wbf = Rot(P, "wbf", 4, [128, 8, 128], BF16)
        self.ost = Rot(P, "ost", 4, [128, NT], F32)
        self.ostb = Rot(P, "ostb", 4, [128, NT], BF16)
        self.ist = Rot(P, "ist", 4, [128, NT], F32)
        self.tmp = Rot(P, "tmp", 6, [128, NT], F32)
        self.ones = P.sb("ones", [128, 128], BF16)
        P.op("pool", lambda E: E.memset(self.ones[:], 1.0), writes=[self.ones])
        self.sq = P.sb("sq", [128, 8, T], BF16)

    def load_w(self, W, m, KC, rowscale=None, wb=None):
        P = self.P
        assert rowscale is None
        st = self.wst.get()
        if wb is None:
            wb = self.wbf.get()
        P.dma("sp", st[:, 0:KC, :], W.t[:, m * 128:(m + 1) * 128].rearrange("(kc p) m -> p kc m", p=128), reads=[W], writes=[st])
        self.wi = getattr(self, "wi", 0) + 1
        cp(P, "act" if self.wi % 2 else "pool", wb[:, 0:KC, :], st[:, 0:KC, :], [st], [wb])
        return wb

    def prep_w(self, name, W, KC, MC):
        wbs = []
        for m in range(MC):
            wb = self.P.sb("%s_w%d" % (name, m), [128, KC, 128], BF16)
            wbs.append(self.load_w(W, m, KC, wb=wb))
        return wbs

    def linear(self, W, KC, MC, a, epi, rowscale=None, wbs=None):
        P = self.P
        for m in range(MC):
            wb = wbs[m] if wbs is not None else self.load_w(W, m, KC, rowscale)
            for n in range(self.T // NT):
                ps = P.bank()
                for kc in range(KC):
                    P.op("pe", lambda E, ps=ps, wb=wb, kc=kc, n=n: E.matmul(
                        ps[:, :], lhsT=wb[:, kc, :], rhs=a[:, kc, n * NT:(n + 1) * NT], start=(kc == 0), stop=(kc == KC - 1)),
                        reads=[wb, a], writes=[ps])
                epi(m, n, ps)

    def rstd(self, src, out):
        P = self.P
        sq = self.sq
        for kc in range(8):
            act(P, sq[:, kc, :], src[:, kc, :], AF.Square, [src], [sq])
        for n in range(self.T // NT):
            ps = P.bank()
            for kc in range(8):
                P.op("pe", lambda E, ps=ps, kc=kc, n=n: E.matmul(
                    ps[:, :], lhsT=self.ones[:, :], rhs=sq[:, kc, n * NT:(n + 1) * NT], start=(kc == 0), stop=(kc == 7)),
                    reads=[self.ones, sq], writes=[ps])
            t = self.tmp.get()
            act(P, t[:, :], ps[:, :], AF.Ln, [ps], [t], scale=1.0 / D, bias=EPS)
            act(P, out[:, n * NT:(n + 1) * NT], t[:, :], AF.Exp, [t], [out], scale=-0.5)

    def store(self, dst, m, n, t0, src_res, src_ap):
        self.P.dma("pool", dst.t[m * 128:(m + 1) * 128, t0 + n * NT:t0 + (n + 1) * NT], src_ap, reads=[src_res], writes=[dst])

    def load_act(self, src, t0, dst, KC=8):
        for kc in range(KC):
            self.P.dma("sp", dst[:, kc, :], src.t[kc * 128:(kc + 1) * 128, t0:t0 + self.T], reads=[src], writes=[dst])


def colvec(P, name, dram_res, ncol):
    t = P.sb(name, [128, ncol], F32)
    P.dma("sp", t[:, :], dram_res.t[:, :], reads=[dram_res], writes=[t])
    return t


def _run(nc, in_maps):
    res = run_bass_kernel_spmd(nc, in_maps, core_ids=list(range(NCORE)))
    return res.results


def _cols(v):
    return np.ascontiguousarray(np.asarray(v, np.float32).reshape(-1, 128).T)


TS = 256
NG = 16
M_MAGIC = 12582912.0


def range_reduce(P, eng, out, in_, tmp, reads, writes, shift=0.0):
    res_in = reads
    if shift != 0.0:
        ts(P, eng, out, in_, shift, None, ALU.add, None, res_in, writes)
        in_ = out
        res_in = writes
    ts(P, eng, tmp[0], in_, 1.0 / (2 * PI), M_MAGIC, ALU.mult, ALU.add, res_in, [tmp[1]])
    ts(P, eng, tmp[0], tmp[0], -M_MAGIC, -2 * PI, ALU.add, ALU.mult, [tmp[1]], [tmp[1]])
    tt(P, eng, out, tmp[0], in_, ALU.add, [tmp[1]] + list(res_in), writes)
    ts(P, eng, out, out, 3.14159, -3.14159, ALU.min, ALU.max, writes, writes)


NAMES_T = ["lamre_T", "lamim_T", "logdt_T", "bre_T", "bim_T"]
NAMES_S = ["lamre_S", "lamim_S", "logdt_S"]


def phase_B(nc, dr):
    P = Prog(nc)
    uT = dr["uT_cs"]
    names_T = NAMES_T
    dT = {n: dr[n] for n in names_T}
    names_S = NAMES_S
    dS = {n: dr[n] for n in names_S}
    c1_d, c2_d, iota_d, gmask_d, sgn_d, J_d = dr["c1_S"], dr["c2_S"], dr["iota"], dr["gmask"], dr["sgn"], dr["Jm"]
    dsk = colvec(P, "dskip_cs", dr["dskip_cs"], 2)
    P.mk_banks(4)
    ybank = [P.ps("ybank%d" % i, [128, NT], F32) for i in range(2)]
    igb = P.ps("igb", [128, NT], F32)

    def ld(name, d, ncol):
        return colvec(P, name, d, ncol)

    lt = {n: ld("s_" + n, dT[n], 128) for n in names_T}
    cnt = [0]

    def newT(nm, ncol=128):
        cnt[0] += 1
        return P.sb("%s_%d" % (nm, cnt[0]), [128, ncol], F32)

    def derive(lamre, lamim, logdt, ncol, pfx):
        o = {}
        lr = newT(pfx + "lr", ncol)
        ts(P, "dve", lr[:, :], lamre[:, :], -1e-4, None, ALU.min, None, [lamre], [lr])
        dt = newT(pfx + "dt", ncol)
        act(P, dt[:, :], logdt[:, :], AF.Exp, [logdt], [dt])
        e = newT(pfx + "e", ncol)
        tt(P, "dve", e[:, :], lr[:, :], dt[:, :], ALU.mult, [lr, dt], [e])
        th = newT(pfx + "th", ncol)
        tt(P, "dve", th[:, :], lamim[:, :], dt[:, :], ALU.mult, [lamim, dt], [th])
        thr = newT(pfx + "thr", ncol)
        tmp = newT(pfx + "tmp", ncol)
        range_reduce(P, "dve", thr[:, :], th[:, :], (tmp[:, :], tmp), [th], [thr])
        thc = newT(pfx + "thc", ncol)
        range_reduce(P, "dve", thc[:, :], th[:, :], (tmp[:, :], tmp), [th], [thc], shift=PI / 2)
        mag = newT(pfx + "mag", ncol)
        act(P, mag[:, :], e[:, :], AF.Exp, [e], [mag])
        sn = newT(pfx + "sin", ncol)
        act(P, sn[:, :], thr[:, :], AF.Sin, [thr], [sn])
        cs = newT(pfx + "cos", ncol)
        act(P, cs[:, :], thc[:, :], AF.Sin, [thc], [cs])
        o.update(lr=lr, li=lamim, e=e, thr=thr, mag=mag, sin=sn, cos=cs)
        return o

    dl = derive(lt["lamre_T"], lt["lamim_T"], lt["logdt_T"], 128, "T")

    def mul(a, b, nm):
        t = newT(nm)
        tt(P, "dve", t[:, :], a[:, :], b[:, :], ALU.mult, [a, b], [t])
        return t

    def addsub(a, b, op, nm):
        t = newT(nm)
        tt(P, "dve", t[:, :], a[:, :], b[:, :], op, [a, b], [t])
        return t

    are = mul(dl["mag"], dl["cos"], "are")
    aim = mul(dl["mag"], dl["sin"], "aim")
    den = addsub(mul(dl["lr"], dl["lr"], "lr2"), mul(dl["li"], dl["li"], "li2"), ALU.add, "den")
    rden = newT("rden")
    P.op("dve", lambda E: E.reciprocal(out=rden[:, :], in_=den[:, :]), reads=[den], writes=[rden])
    nr = newT("nr")
    ts(P, "dve", nr[:, :], are[:, :], -1.0, None, ALU.add, None, [are], [nr])
    fre = mul(addsub(mul(nr, dl["lr"], "f1"), mul(aim, dl["li"], "f2"), ALU.add, "f3"), rden, "fre")
    fim = mul(addsub(mul(aim, dl["lr"], "f4"), mul(nr, dl["li"], "f5"), ALU.subtract, "f6"), rden, "fim")
    bbre = addsub(mul(fre, lt["bre_T"], "b1"), mul(fim, lt["bim_T"], "b2"), ALU.subtract, "bbre")
    bbim = addsub(mul(fre, lt["bim_T"], "b3"), mul(fim, lt["bre_T"], "b4"), ALU.add, "bbim")
    nbbim = newT("nbbim")
    ts(P, "dve", nbbim[:, :], bbim[:, :], -1.0, None, ALU.mult, None, [bbim], [nbbim])
    gmask = ld("gmask", gmask_d, 8)
    W1 = P.sb("W1pad", [128, NG, 128], BF16)
    W2 = P.sb("W2pad", [128, NG, 128], BF16)
    for jj in range(2):
        for gl in range(8):
            gg = jj * 8 + gl
            ms = gmask[:, gl:gl + 1]
            sl = slice(jj * 64, (jj + 1) * 64)
            ts(P, "dve", W1[:, gg, 0:64], bbre[:, sl], ms, None, ALU.mult, None, [bbre, gmask], [W1])
            ts(P, "dve", W1[:, gg, 64:128], bbim[:, sl], ms, None, ALU.mult, None, [bbim, gmask], [W1])
            ts(P, "dve", W2[:, gg, 0:64], nbbim[:, sl], ms, None, ALU.mult, None, [nbbim, gmask], [W2])
            ts(P, "dve", W2[:, gg, 64:128], bbre[:, sl], ms, None, ALU.mult, None, [bbre, gmask], [W2])

    ls_ = {n: ld("s_" + n, dS[n], NG) for n in names_S}
    ds = derive(ls_["lamre_S"], ls_["lamim_S"], ls_["logdt_S"], NG, "S")
    rho = ds["mag"]
    iota = ld("iota", iota_d, TS + 1)
    COS = P.sb("COS", [128, NG, TS + 1], F32)
    SIN = P.sb("SIN", [128, NG, TS + 1], F32)
    ARG = P.sb("ARG", [128, NG, TS + 1], F32)
    TMP = P.sb("TMPA", [128, NG, TS + 1], F32)
    Gg = [P.sb("Gg%d" % i, [128, TS], F32) for i in range(NG)]
    GT = [P.sb("GT%d" % i, [128, NG], F32) for i in range(2)]
    for gg in range(NG):
        ts(P, "dve", ARG[:, gg, :], iota[:, :], ds["thr"][:, gg:gg + 1], None, ALU.mult, None, [iota, ds["thr"]], [ARG])
    range_reduce(P, "dve", SIN[:, :, :], ARG[:, :, :], (TMP[:, :, :], TMP), [ARG], [SIN])
    range_reduce(P, "dve", COS[:, :, :], ARG[:, :, :], (TMP[:, :, :], TMP), [ARG], [COS], shift=PI / 2)
    act(P, SIN[:, :, :], SIN[:, :, :], AF.Sin, [SIN], [SIN])
    act(P, COS[:, :, :], COS[:, :, :], AF.Sin, [COS], [COS])
    c1 = ld("c1", c1_d, NG * 16)
    c2 = ld("c2", c2_d, NG * 16)
    sgn = ld("sgn", sgn_d, 1)
    L1 = P.sb("L1pad", [128, NG, 128], BF16)
    L2 = P.sb("L2pad", [128, NG, 128], BF16)
    P.op("pool", lambda E: E.memset(L1[:, :, :], 0.0), writes=[L1])
    P.op("pool", lambda E: E.memset(L2[:, :, :], 0.0), writes=[L2])
    for gg in range(NG):
        gl = gg % 8
        ts(P, "dve", L1[:, gg, gl * 16:(gl + 1) * 16], c1[:, gg * 16:(gg + 1) * 16], sgn[:, 0:1], None, ALU.mult, None, [c1, sgn], [L1])
        ts(P, "dve", L2[:, gg, gl * 16:(gl + 1) * 16], c2[:, gg * 16:(gg + 1) * 16], -1.0, None, ALU.mult, None, [c2], [L2])
    Jm = P.sb("Jm", [128, 128], F32)
    P.dma("sp", Jm[:, :], J_d.t[:, :], reads=[J_d], writes=[Jm])

    ust = Rot(P, "ust", 3, [128, 2, TS], F32)
    ubf = Rot(P, "ubf", 3, [128, 2, TS], BF16)
    t1r = Rot(P, "t1r", 4, [128, TS], F32)
    t2r = Rot(P, "t2r", 4, [128, TS], F32)
    xtr = Rot(P, "xtr", 4, [128, TS], F32)
    h1r = Rot(P, "h1r", 4, [128, TS], BF16)
    h2r = Rot(P, "h2r", 4, [128, TS], BF16)
    S0 = [P.sb("S0_%d" % i, [128, NG], F32) for i in range(2)]
    ys = Rot(P, "ys", 3, [128, 2, TS], BF16)
    P.op("pool", lambda E: E.memset(S0[0][:, :], 0.0), writes=[S0[0]])
    nch = SEQ // TS
    units = []
    chunk_res = {}
    for ch in range(nch):
        for jj in range(2):
            for gl in range(8):
                units.append(dict(ch=ch, jj=jj, gl=gl, gg=jj * 8 + gl))
    nu = len(units)

    def chunk_setup(ch):
        c0 = ch * TS
        us = ust.get()
        ub = ubf.get()
        P.dma("sp", us[:, :, :], uT.t[:, c0:c0 + TS].rearrange("(j p) t -> p j t", p=128), reads=[uT], writes=[us])
        cp(P, "act", ub[:, :, :], us[:, :, :], [us], [ub])
        chunk_res[ch] = dict(us=us, ub=ub, yo=ys.get())

    def s1(i):
        U = units[i]
        ch, jj, gg = U["ch"], U["jj"], U["gg"]
        if jj == 0 and U["gl"] == 0:
            chunk_setup(ch)
        ub = chunk_res[ch]["ub"]
        p1 = P.bank()
        p2 = P.bank()
        P.op("pe", lambda E, p1=p1, gg=gg, ub=ub, jj=jj: E.matmul(p1[:, 0:TS], lhsT=W1[:, gg, :], rhs=ub[:, jj, :], start=True, stop=True), reads=[W1, ub], writes=[p1])
        P.op("pe", lambda E, p2=p2, gg=gg, ub=ub, jj=jj: E.matmul(p2[:, 0:TS], lhsT=W2[:, gg, :], rhs=ub[:, jj, :], start=True, stop=True), reads=[W2, ub], writes=[p2])
        t1 = t1r.get()
        t2 = t2r.get()
        xt = xtr.get()
        tt(P, "dve", t1[:, :], p1[:, 0:TS], COS[:, gg, 1:TS + 1], ALU.mult, [p1, COS], [t1])
        tt(P, "dve", t2[:, :], p2[:, 0:TS], SIN[:, gg, 1:TS + 1], ALU.mult, [p2, SIN], [t2])
        tt(P, "pool", xt[:, :], t1[:, :], t2[:, :], ALU.subtract, [t1, t2], [xt])
        U["xt"] = xt

    def s2(i):
        U = units[i]
        ch, gg = U["ch"], U["gg"]
        gt, s0 = GT[ch % 2], S0[ch % 2]
        xt = U["xt"]
        G = Gg[gg]
        P.op("dve", lambda E, G=G, gg=gg, xt=xt, s0=s0: E.tensor_tensor_scan(
            out=G[:, :], data0=rho[:, gg:gg + 1].to_broadcast([128, TS]), data1=xt[:, :],
            initial=s0[:, gg:gg + 1], op0=ALU.mult, op1=ALU.add), reads=[rho, xt, s0], writes=[G])
        if ch + 1 < nch:
            cp(P, "act", gt[:, gg:gg + 1], G[:, TS - 1:TS], [G], [gt])
        h1 = h1r.get()
        h2 = h2r.get()
        tt(P, "dve", h1[:, :], G[:, :], COS[:, gg, 1:TS + 1], ALU.mult, [G, COS], [h1])
        tt(P, "pool", h2[:, :], G[:, :], SIN[:, gg, 1:TS + 1], ALU.mult, [G, SIN], [h2])
        U["h1"], U["h2"] = h1, h2
        if gg == NG - 1 and ch + 1 < nch:
            s1_ = S0[(ch + 1) % 2]
            P.op("pe", lambda E, gt=gt: E.matmul(igb[:, 0:NG], lhsT=Jm[:, :], rhs=gt[:, :], start=True, stop=True), reads=[Jm, gt], writes=[igb])
            ta = P.sb("bta%d" % ch, [128, NG], F32)
            tb = P.sb("btb%d" % ch, [128, NG], F32)
            tt(P, "dve", ta[:, :], gt[:, :], COS[:, :, TS], ALU.mult, [gt, COS], [ta])
            tt(P, "dve", tb[:, :], igb[:, 0:NG], SIN[:, :, TS], ALU.mult, [igb, SIN], [tb])
            tt(P, "dve", s1_[:, :], ta[:, :], tb[:, :], ALU.add, [ta, tb], [s1_])

    def s3(i):
        U = units[i]
        ch, jj, gl, gg = U["ch"], U["jj"], U["gl"], U["gg"]
        yb = ybank[jj]
        h1, h2 = U["h1"], U["h2"]
        P.op("pe", lambda E, yb=yb, gg=gg, h1=h1, gl=gl: E.matmul(yb[:, 0:TS], lhsT=L1[:, gg, :], rhs=h1[:, :], start=(gl == 0), stop=False), reads=[L1, h1], writes=[yb])
        P.op("pe", lambda E, yb=yb, gg=gg, h2=h2, gl=gl: E.matmul(yb[:, 0:TS], lhsT=L2[:, gg, :], rhs=h2[:, :], start=False, stop=(gl == 7)), reads=[L2, h2], writes=[yb])
        if gl == 7:
            cr = chunk_res[ch]
            yo, us = cr["yo"], cr["us"]
            P.op("dve", lambda E, yo=yo, us=us, yb=yb, jj=jj: E.scalar_tensor_tensor(
                out=yo[:, jj, :], in0=us[:, jj, :], scalar=dsk[:, jj:jj + 1], in1=yb[:, 0:TS], op0=ALU.mult, op1=ALU.add),
                reads=[us, dsk, yb], writes=[yo])
            if jj == 1:
                c0 = ch * TS
                ysrc = dr["y_src"][c0 // 2048]
                P.dma("act", ysrc.t[:, c0 % 2048:c0 % 2048 + TS].rearrange("(j p) t -> p j t", p=128), yo[:, :, :], reads=[yo], writes=[ysrc])

    for j in range(nu + 2):
        if j < nu:
            s1(j)
        if 0 <= j - 1 < nu:
            s2(j - 1)
        if 0 <= j - 2 < nu:
            s3(j - 2)
    for k in range(4):
        P.coll("AllGather", dr["y_src"][k], dr["y_all"][k], GROUPS)
    P.wait_all("act", dr["y_all"])
    P.emit()


def gelu_tanh(P, dn, out_bf, y, reads):
    s = dn.tmp.get()
    s2 = dn.tmp.get()
    act(P, s[:, :], y, AF.Square, reads, [s])
    ts(P, "dve", s[:, :], s[:, :], 0.044715, 1.0, ALU.mult, ALU.add, [s], [s])
    tt(P, "dve", s2[:, :], s[:, :], y, ALU.mult, [s] + list(reads), [s2])
    act(P, s2[:, :], s2[:, :], AF.Sigmoid, [s2], [s2], scale=1.5957691216057308)
    return s2


def ple(P, dn, Wpg, Wpp, Xb, Pb, X1):
    for m in range(8):
        wg = dn.load_w(Wpg, m, 8)
        wp = dn.load_w(Wpp, m, 2)
        for n in range(dn.T // NT):
            pg = P.bank()
            pp = P.bank()
            for kc in range(8):
                P.op("pe", lambda E, pg=pg, wg=wg, kc=kc, n=n: E.matmul(
                    pg[:, :], lhsT=wg[:, kc, :], rhs=Xb[:, kc, n * NT:(n + 1) * NT], start=(kc == 0), stop=(kc == 7)),
                    reads=[wg, Xb], writes=[pg])
            for kc in range(2):
                P.op("pe", lambda E, pp=pp, wp=wp, kc=kc, n=n: E.matmul(
                    pp[:, :], lhsT=wp[:, kc, :], rhs=Pb[:, kc, n * NT:(n + 1) * NT], start=(kc == 0), stop=(kc == 1)),
                    reads=[wp, Pb], writes=[pp])
            sg = dn.tmp.get()
            act(P, sg[:, :], pg[:, :], AF.Sigmoid, [pg], [sg])
            t = dn.tmp.get()
            tt(P, "dve", t[:, :], sg[:, :], pp[:, :], ALU.mult, [sg, pp], [t])
            tt(P, "pool", X1[:, m, n * NT:(n + 1) * NT], X1[:, m, n * NT:(n + 1) * NT], t[:, :], ALU.add, [X1, t], [X1])


NH = 4
NQT = SEQ // NT


def phase_D(nc, dr):
    P = Prog(nc)
    qT, kT, vT, tri_d = dr["qT_cs"], dr["kT_cs"], dr["vT_cs"], dr["ntri"]
    ident = P.sb("ident", [128, 128], BF16)
    P.op("pool", lambda E: E.memset(ident[:, :], 0.0), writes=[ident])
    P.op("pool", lambda E: E.affine_select(out=ident[:, :], in_=ident[:, :], pattern=[[-1, 128]], compare_op=ALU.not_equal,
                                           fill=1.0, base=0, channel_multiplier=1), reads=[ident], writes=[ident])
    vtb = Rot(P, "vtb", 3, [64, 2048], BF16)
    tpb = P.ps("tpb", [128, NT], BF16)
    zb = [P.ps("zb%d" % i, [128, NT], F32) for i in range(4)]
    ob = [P.ps("ob%d" % i, [64, NT], F32) for i in range(2)]
    st = P.sb("tri_st", [128, 128], F32)
    P.dma("sp", st[:, :], tri_d.t[:, :], reads=[tri_d], writes=[st])
    ntri = P.sb("ntri", [128, 128], BF16)
    cp(P, "dve", ntri[:, :], st[:, :], [st], [ntri])
    nones = P.sb("nones", [128, 128], BF16)
    P.op("pool", lambda E: E.memset(nones[:, :], -1.0), writes=[nones])
    Qb = P.sb("Qb", [128, 2, SEQ], BF16)
    Kb = P.sb("Kb", [128, 2, SEQ], BF16)
    Vb = [P.sb("Vb%d" % h, [128, 64 * 64], BF16) for h in range(NH)]
    er = Rot(P, "er", 3, [128, NT], F32)
    spr = Rot(P, "spr", 4, [128, NT], BF16)
    wr = Rot(P, "wr", 4, [128, NT], BF16)
    racc = Rot(P, "racc", 3, [128, NT], BF16)
    ost = Rot(P, "ost", 2, [64, NT], BF16)
    for pr in range(2):
        for c in range(SEQ // 2048):
            P.dma("sp", Qb[:, pr, c * 2048:(c + 1) * 2048], qT.t[pr * 128:(pr + 1) * 128, c * 2048:(c + 1) * 2048], reads=[qT], writes=[Qb])
            P.dma("sp", Kb[:, pr, c * 2048:(c + 1) * 2048], kT.t[pr * 128:(pr + 1) * 128, c * 2048:(c + 1) * 2048], reads=[kT], writes=[Kb])
    for h in range(NH):
        for c in range(SEQ // 2048):
            vb_ = vtb.get()
            P.dma("sp", vb_[:, :], vT.t[h * 64:(h + 1) * 64, c * 2048:(c + 1) * 2048], reads=[vT], writes=[vb_])
            for k8 in range(2):
                for j in range(8):
                    blk = k8 * 8 + j
                    P.op("pe", lambda E, vb_=vb_, j=j, blk=blk: E.transpose(
                        out=tpb[:, j * 64:(j + 1) * 64], in_=vb_[:, blk * 128:(blk + 1) * 128], identity=ident[0:64, 0:64]),
                        reads=[vb_, ident], writes=[tpb])
                kb0 = c * 16 + k8 * 8
                cp(P, "dve", Vb[h][:, kb0 * 64:(kb0 + 8) * 64], tpb[:, :], [tpb], [Vb[h]])
    blocks = []
    for h in range(NH):
        for qt in range(NQT):
            kbs = list(range(4 * qt + 3, -1, -1))
            for idx, kb in enumerate(kbs):
                blocks.append(dict(h=h, qt=qt, kb=kb, idx=idx, n=len(kbs), g=h * NQT + qt))
    nb = len(blocks)

    def operands(B):
        hp, pr = B["h"] % 2, B["h"] // 2
        ksl = Kb[hp * 64:(hp + 1) * 64, pr, B["kb"] * 128:(B["kb"] + 1) * 128]
        qsl = Qb[hp * 64:(hp + 1) * 64, pr, B["qt"] * NT:(B["qt"] + 1) * NT]
        return ksl, qsl

    def mask(t, B):
        base = B["qt"] * NT - 128 * B["kb"]
        P.op("pool", lambda E, t=t, base=base: E.affine_select(
            out=t[:, :], in_=t[:, :], pattern=[[1, NT]], compare_op=ALU.is_gt, fill=0.0, base=base, channel_multiplier=-1),
            reads=[t], writes=[t])

    def stage1a(i):
        B = blocks[i]
        ksl, qsl = operands(B)
        z = zb[i % 4]
        P.op("pe", lambda E, z=z, ksl=ksl, qsl=qsl: E.matmul(z[:, :], lhsT=ksl, rhs=qsl, start=True, stop=False), reads=[Kb, Qb], writes=[z])
        e = er.get()
        act(P, e[:, :], z[:, :], AF.Exp, [z], [e])
        B["e"] = e

    def stage1b(i):
        B = blocks[i]
        e = B["e"]
        sp = spr.get()
        P.op("act", lambda E, sp=sp, e=e: E.activation(out=sp[:, :], in_=e[:, :], func=AF.Ln, scale=1.0, bias=1.0),
             reads=[e], writes=[sp], skip_same=True)
        if B["kb"] >= 4 * B["qt"]:
            mask(sp, B)
        B["sp"] = sp

    def stage2(i):
        B = blocks[i]
        ksl, qsl = operands(B)
        b = zb[i % 4]
        sp = B["sp"]
        first = B["idx"] == 0
        ra_prev = None if first else blocks[i - 1]["ra"]
        P.op("pe", lambda E, b=b, sp=sp, first=first: E.matmul(b[:, :], lhsT=ntri[:, :], rhs=sp[:, :], start=False, stop=first), reads=[ntri, sp], writes=[b])
        if not first:
            P.op("pe", lambda E, b=b, ra=ra_prev: E.matmul(b[:, :], lhsT=nones[:, :], rhs=ra[:, :], start=False, stop=True), reads=[nones, ra_prev], writes=[b])
        w = wr.get()
        act(P, w[:, :], b[:, :], AF.Exp, [b], [w])
        if B["kb"] >= 4 * B["qt"]:
            mask(w, B)
        B["w"] = w
        if B["idx"] + 1 < B["n"]:
            ra = racc.get()
            if first:
                cp(P, "dve", ra[:, :], sp[:, :], [sp], [ra])
            else:
                tt(P, "dve", ra[:, :], ra_prev[:, :], sp[:, :], ALU.add, [ra_prev, sp], [ra])
            B["ra"] = ra

    def stage3(i):
        B = blocks[i]
        o_ps = ob[B["g"] % 2]
        w = B["w"]
        h, kb = B["h"], B["kb"]
        P.op("pe", lambda E, o_ps=o_ps, w=w, h=h, kb=kb, B=B: E.matmul(
            o_ps[:, :], lhsT=Vb[h][:, kb * 64:(kb + 1) * 64], rhs=w[:, :], start=(B["idx"] == 0), stop=(B["idx"] == B["n"] - 1)),
            reads=[Vb[h], w], writes=[o_ps])
        if B["idx"] == B["n"] - 1:
            o = ost.get()
            cp(P, "dve", o[:, :], o_ps[:, :], [o_ps], [o])
            q0 = B["qt"] * NT
            osrc = dr["o_src"][q0 // 2048]
            P.dma("act", osrc.t[h * 64:(h + 1) * 64, q0 % 2048:q0 % 2048 + NT], o[:, :], reads=[o], writes=[osrc])

    for j in range(nb + 2):
        if j < nb:
            stage1a(j)
            stage1b(j)
        if 0 <= j - 1 < nb:
            stage2(j - 1)
        if 0 <= j - 2 < nb:
            stage3(j - 2)
    for k in range(4):
        P.coll("AllGather", dr["o_src"][k], dr["o_all"][k], GROUPS)
    P.wait_all("act", dr["o_all"])
    P.emit()


GROUPS = [[0, 1, 2, 3], [4, 5, 6, 7]]
TOKC = 2048
TP = 1024


def proj_pass(P, dn, src_fn, xs, rs, jobs, col0):
    for kc in range(8):
        srcs = src_fn(kc)
        if isinstance(srcs, tuple):
            P.dma("sp", xs[:, kc, :], srcs[1], reads=[srcs[0]], writes=[xs])
        else:
            w_ = TP // len(srcs)
            for i_, (sr_, sa_) in enumerate(srcs):
                P.dma("sp", xs[:, kc, i_ * w_:(i_ + 1) * w_], sa_, reads=[sr_], writes=[xs])
    dn.rstd(xs, rs)
    done = {}
    for job in jobs:
        xb, gain, wbs, MC, dst = job[:5]
        oscale = job[5] if len(job) > 5 else None
        if id(xb) not in done:
            done[id(xb)] = 1
            for kc in range(8):
                scale_cast(P, kc, xb[:, kc, :], xs[:, kc, :], gain[:, kc:kc + 1], [xs, gain], [xb])

        def epi(m, n, ps, dst=dst, oscale=oscale):
            if oscale is None:
                o = dn.ost.get()
                tt(P, "dve", o[:, :], ps[:, :], rs[:, n * NT:(n + 1) * NT], ALU.mult, [ps, rs], [o])
            else:
                o = dn.ostb.get()
                P.op("dve", lambda E, o=o, ps=ps, n=n: E.scalar_tensor_tensor(
                    out=o[:, :], in0=ps[:, :], scalar=oscale, in1=rs[:, n * NT:(n + 1) * NT], op0=ALU.mult, op1=ALU.mult),
                    reads=[ps, rs], writes=[o])
            dn.store(dst, m, n, col0, o, o[:, :])
        dn.linear(None, 8, MC, xb, epi, wbs=wbs)


def phase_A(nc, dr):
    P = Prog(nc)
    P.mk_banks(6)
    dn = Dense(P, TP)
    g = colvec(P, "g_pre", dr["g_pre"], 8)
    xsr = Rot(P, "xs", 2, [128, 8, TP], F32)
    xb = P.sb("xb", [128, 8, TP], BF16)
    rs = P.sb("rs", [128, TP], F32)
    xT = dr["xT_full"]
    wbs = dn.prep_w("wu", dr["w_in_u"], 8, 2)
    for pa in range(SEQ // TP):
        t0 = pa * TP
        proj_pass(P, dn, lambda kc, t0=t0: (xT, xT.t[kc * 128:(kc + 1) * 128, t0:t0 + TP]), xsr.get(), rs,
                  [(xb, g, wbs, 2, dr["uT_cs"])], t0)
    P.wait_all("pool", [dr["uT_cs"]])
    P.emit()


def mk_select(P, dn, sel):
    ident = P.sb("identf", [128, 128], F32)
    P.op("pool", lambda E: E.memset(ident[:, :], 0.0), writes=[ident])
    P.op("pool", lambda E: E.affine_select(out=ident[:, :], in_=ident[:, :], pattern=[[-1, 128]], compare_op=ALU.not_equal,
                                           fill=1.0, base=0, channel_multiplier=1), reads=[ident], writes=[ident])
    mI = P.sb("mI", [128, 4, 128], BF16)
    for s_ in range(4):
        ts(P, "dve", mI[:, s_, :], ident[:, :], sel[:, s_:s_ + 1], None, ALU.mult, None, [ident, sel], [mI])
    dn.mI = mI


def select4(P, dn, src_fn, sel, dt):
    ps = P.bank()
    for s in range(4):
        it = dn.istb.get()
        sres, sap = src_fn(s)
        P.dma("sp", it[:, :], sap, reads=[sres], writes=[it])
        P.op("pe", lambda E, ps=ps, it=it, s=s: E.matmul(ps[:, :], lhsT=dn.mI[:, s, :], rhs=it[:, :], start=(s == 0), stop=(s == 3)),
             reads=[dn.mI, it], writes=[ps])
    acc = dn.tmp.get()
    act(P, acc[:, :], ps[:, :], AF.Copy, [ps], [acc])
    return acc


def phase_C(nc, dr):
    P = Prog(nc)
    P.mk_banks(7)
    dn = Dense(P, TP)
    dn.istb = Rot(P, "istb", 8, [128, NT], BF16)
    sel = colvec(P, "sel", dr["sel"], 4)
    mk_select(P, dn, sel)
    gpre = colvec(P, "g_pre", dr["g_pre"], 8)
    vl = []
    for i in range(4):
        t_ = P.sb("vec%d" % i, [128, 8], F32)
        P.dma("sp", t_[:, :], dr["vecsC"].t[:, i * 8:(i + 1) * 8], reads=[dr["vecsC"]], writes=[t_])
        vl.append(t_)
    bglu, gpost, gbpre, _unused = vl
    xT, y_all, pT = dr["xT_own"], dr["y_all"], dr["p0T"]
    Gb = P.sb("Gb", [128, 8, TP], BF16)
    SGb = P.sb("SGb", [128, 8, TP], BF16)
    Y2b = P.sb("Y2b", [128, 8, TP], BF16)
    X1 = P.sb("X1", [128, 8, TP], F32)
    Pb = P.sb("Pb", [128, 2, TP], BF16)
    rs = P.sb("rs", [128, TP], F32)
    for pa in range(TOKC // TP):
        t0 = pa * TP
        for kc in range(8):
            P.dma("sp", X1[:, kc, :], xT.t[kc * 128:(kc + 1) * 128, t0:t0 + TP], reads=[xT], writes=[X1])
        dn.rstd(X1, rs)
        for kc in range(8):
            scale_cast(P, kc, Y2b[:, kc, :], X1[:, kc, :], gpre[:, kc:kc + 1], [X1, gpre], [Y2b])

        def epi_gate(m, n, ps):
            t = dn.tmp.get()
            tt(P, "dve", t[:, :], ps[:, :], rs[:, n * NT:(n + 1) * NT], ALU.mult, [ps, rs], [t])
            act(P, SGb[:, m, n * NT:(n + 1) * NT], t[:, :], AF.Silu, [t], [SGb])
        dn.linear(dr["w_in_g"], 8, 8, Y2b, epi_gate)
        for kc in range(8):
            for n in range(TP // NT):
                def ysrc_fn(s, n=n, kc=kc):
                    g0 = s * TOKC + t0 + n * NT
                    ya = y_all[g0 // 2048]
                    return ya, ya.t[kc * 128:(kc + 1) * 128, g0 % 2048:g0 % 2048 + NT]
                y = select4(P, dn, ysrc_fn, sel, BF16)
                s2 = gelu_tanh(P, dn, None, y[:, :], [y])
                tt(P, "pool", Gb[:, kc, n * NT:(n + 1) * NT], s2[:, :], y[:, :], ALU.mult, [s2, y], [Gb])

        def epi_glu(m, n, ps):
            sg = dn.tmp.get()
            act(P, sg[:, :], ps[:, :], AF.Sigmoid, [ps, bglu], [sg], bias=bglu[:, m:m + 1])
            t = dn.tmp.get()
            tt(P, "dve", t[:, :], sg[:, :], Gb[:, m, n * NT:(n + 1) * NT], ALU.mult, [sg, Gb], [t])
            tt(P, "pool", Y2b[:, m, n * NT:(n + 1) * NT], t[:, :], SGb[:, m, n * NT:(n + 1) * NT], ALU.mult, [t, SGb], [Y2b])
        dn.linear(dr["w_glu"], 8, 8, Gb, epi_glu)

        def epi_out(m, n, ps):
            act(P, X1[:, m, n * NT:(n + 1) * NT], ps[:, :], AF.Copy, [ps], [X1])
        dn.linear(dr["w_out0"], 8, 8, Y2b, epi_out)
        dn.rstd(X1, rs)
        for kc in range(8):
            for n in range(TP // NT):
                c0 = t0 + n * NT
                ix = dn.ist.get()
                P.dma("sp", ix[:, :], xT.t[kc * 128:(kc + 1) * 128, c0:c0 + NT], reads=[xT], writes=[ix])
                t = dn.tmp.get()
                P.op("dve", lambda E, t=t, kc=kc, n=n: E.scalar_tensor_tensor(
                    out=t[:, :], in0=X1[:, kc, n * NT:(n + 1) * NT], scalar=gpost[:, kc:kc + 1], in1=rs[:, n * NT:(n + 1) * NT],
                    op0=ALU.mult, op1=ALU.mult), reads=[X1, gpost, rs], writes=[t])
                tt(P, "pool", X1[:, kc, n * NT:(n + 1) * NT], t[:, :], ix[:, :], ALU.add, [t, ix], [X1])
                cp(P, "act", Gb[:, kc, n * NT:(n + 1) * NT], X1[:, kc, n * NT:(n + 1) * NT], [X1], [Gb])
        for kc in range(2):
            for n in range(TP // NT):
                c0 = t0 + n * NT
                ip = dn.ist.get()
                P.dma("sp", ip[:, :], pT.t[kc * 128:(kc + 1) * 128, c0:c0 + NT], reads=[pT], writes=[ip])
                cp(P, "dve", Pb[:, kc, n * NT:(n + 1) * NT], ip[:, :], [ip], [Pb])
        ple(P, dn, dr["w_pg0"], dr["w_pp0"], Gb, Pb, X1)
        dn.rstd(X1, rs)
        for kc in range(8):
            cp(P, "dve", Y2b[:, kc, :], X1[:, kc, :], [X1], [Y2b])
            scale_cast(P, kc + 1, SGb[:, kc, :], X1[:, kc, :], gbpre[:, kc:kc + 1], [X1, gbpre], [SGb])
            P.dma("pool", dr["x1_own"].t[kc * 128:(kc + 1) * 128, t0:t0 + TP], X1[:, kc, :], reads=[X1], writes=[dr["x1_own"]])
            for n in range(TP // NT):
                xsrc = dr["x1_src"][(t0 + n * NT) // NT]
                P.dma("pool", xsrc.t[kc * 128:(kc + 1) * 128, :], Y2b[:, kc, n * NT:(n + 1) * NT], reads=[Y2b], writes=[xsrc])

        def epi_g1(m, n, ps):
            o = dn.ost.get()
            tt(P, "dve", o[:, :], ps[:, :], rs[:, n * NT:(n + 1) * NT], ALU.mult, [ps, rs], [o])
            dn.store(dr["g1T"], m, n, t0, o, o[:, :])
        dn.linear(dr["w_bin_g"], 8, 8, SGb, epi_g1)
    for k in range(4):
        P.coll("AllGather", dr["x1_src"][k], dr["x1_all"][k], GROUPS)
    P.wait_all("pool", dr["x1_all"] + [dr["x1_own"], dr["g1T"]])
    P.emit()


def phase_QKV(nc, dr):
    P = Prog(nc)
    P.mk_banks(6)
    dn = Dense(P, TP)
    gkv = colvec(P, "g_kv", dr["g_kv"], 8)
    gbpre = colvec(P, "g_bpre", dr["g_bpre"], 8)
    xsr = Rot(P, "xs", 2, [128, 8, TP], BF16)
    xq = P.sb("xq", [128, 8, TP], BF16)
    xk = P.sb("xk", [128, 8, TP], BF16)
    rs = P.sb("rs", [128, TP], F32)
    xa = dr["x1_all"]
    wq = dn.prep_w("wq", dr["w_q"], 8, 2)
    wk = dn.prep_w("wk", dr["w_k"], 8, 2)
    wv = dn.prep_w("wv", dr["w_v"], 8, 2)
    for pa in range(SEQ // TP):
        t0 = pa * TP
        s, tl = t0 // TOKC, t0 % TOKC
        proj_pass(P, dn, lambda kc, s=s, tl=tl: [(xa[(tl + h_ * NT) // NT], xa[(tl + h_ * NT) // NT].t[s * D + kc * 128:s * D + (kc + 1) * 128, :]) for h_ in range(TP // NT)], xsr.get(), rs,
                  [(xq, gbpre, wq, 2, dr["qT_cs"], 1.0), (xk, gkv, wk, 2, dr["kT_cs"], 0.125), (xk, gkv, wv, 2, dr["vT_cs"], 1.0)], t0)
    P.wait_all("pool", [dr["qT_cs"], dr["kT_cs"], dr["vT_cs"]])
    P.emit()


def phase_E(nc, dr):
    P = Prog(nc)
    P.mk_banks(7)
    dn = Dense(P, TP)
    dn.istb = Rot(P, "istb", 8, [128, NT], BF16)
    sel = colvec(P, "sel", dr["sel"], 4)
    mk_select(P, dn, sel)
    gpost = colvec(P, "g_bpost", dr["g_bpost"], 8)
    Ob = P.sb("Ob", [128, 8, TP], BF16)
    Xb = P.sb("Xb", [128, 8, TP], BF16)
    X1 = P.sb("X1", [128, 8, TP], F32)
    Pb = P.sb("Pb", [128, 2, TP], BF16)
    rs = P.sb("rs", [128, TP], F32)
    x1T, gT, pT, outT = dr["x1_own"], dr["g1T"], dr["p1T"], dr["outT"]
    for pa in range(TOKC // TP):
        t0 = pa * TP
        for kc in range(8):
            for n in range(TP // NT):
                c0 = t0 + n * NT
                def osrc_fn(s, n=n, kc=kc):
                    g0 = s * TOKC + t0 + n * NT
                    oa = dr["o_all"][g0 // 2048]
                    return oa, oa.t[kc * 128:(kc + 1) * 128, g0 % 2048:g0 % 2048 + NT]
                o = select4(P, dn, osrc_fn, sel, BF16)
                ig = dn.ist.get()
                P.dma("sp", ig[:, :], gT.t[kc * 128:(kc + 1) * 128, c0:c0 + NT], reads=[gT], writes=[ig])
                sg = dn.tmp.get()
                act(P, sg[:, :], ig[:, :], AF.Silu, [ig], [sg])
                tt(P, "pool", Ob[:, kc, n * NT:(n + 1) * NT], sg[:, :], o[:, :], ALU.mult, [sg, o], [Ob])

        def epi_out(m, n, ps):
            act(P, X1[:, m, n * NT:(n + 1) * NT], ps[:, :], AF.Copy, [ps], [X1])
        dn.linear(dr["w_out1"], 8, 8, Ob, epi_out)
        dn.rstd(X1, rs)
        for kc in range(8):
            for n in range(TP // NT):
                c0 = t0 + n * NT
                ix = dn.ist.get()
                P.dma("sp", ix[:, :], x1T.t[kc * 128:(kc + 1) * 128, c0:c0 + NT], reads=[x1T], writes=[ix])
                t = dn.tmp.get()
                P.op("dve", lambda E, t=t, kc=kc, n=n: E.scalar_tensor_tensor(
                    out=t[:, :], in0=X1[:, kc, n * NT:(n + 1) * NT], scalar=gpost[:, kc:kc + 1], in1=rs[:, n * NT:(n + 1) * NT],
                    op0=ALU.mult, op1=ALU.mult), reads=[X1, gpost, rs], writes=[t])
                tt(P, "pool", X1[:, kc, n * NT:(n + 1) * NT], t[:, :], ix[:, :], ALU.add, [t, ix], [X1])
                cp(P, "act", Xb[:, kc, n * NT:(n + 1) * NT], X1[:, kc, n * NT:(n + 1) * NT], [X1], [Xb])
        for kc in range(2):
            for n in range(TP // NT):
                c0 = t0 + n * NT
                ip = dn.ist.get()
                P.dma("sp", ip[:, :], pT.t[kc * 128:(kc + 1) * 128, c0:c0 + NT], reads=[pT], writes=[ip])
                cp(P, "dve", Pb[:, kc, n * NT:(n + 1) * NT], ip[:, :], [ip], [Pb])
        ple(P, dn, dr["w_pg1"], dr["w_pp1"], Xb, Pb, X1)
        for kc in range(8):
            P.dma("pool", outT.t[kc * 128:(kc + 1) * 128, t0:t0 + TP], X1[:, kc, :], reads=[X1], writes=[outT])
    P.wait_all("pool", [outT])
    P.emit()


IN_SPECS = {
    "xT_full": ([D, SEQ], F32), "xT_own": ([D, TOKC], F32), "p0T": ([256, TOKC], F32), "p1T": ([256, TOKC], F32),
    "sel": ([128, 4], F32), "g_pre": ([128, 8], F32), "w_in_u": ([D, 256], F32), "w_in_g": ([D, D], F32),
    "lamre_T": ([128, 128], F32), "lamim_T": ([128, 128], F32), "logdt_T": ([128, 128], F32), "bre_T": ([128, 128], F32),
    "bim_T": ([128, 128], F32), "lamre_S": ([128, NG], F32), "lamim_S": ([128, NG], F32), "logdt_S": ([128, NG], F32),
    "c1_S": ([128, NG * 16], F32), "c2_S": ([128, NG * 16], F32), "iota": ([128, TS + 1], F32), "gmask": ([128, 8], F32),
    "sgn": ([128, 1], F32), "Jm": ([128, 128], F32), "dskip_cs": ([128, 2], F32), "vecsC": ([128, 32], F32),
    "w_glu": ([D, D], F32), "w_out0": ([D, D], F32), "w_pg0": ([D, D], F32), "w_pp0": ([256, D], F32),
    "w_bin_g": ([D, D], F32), "g_kv": ([128, 8], F32), "g_bpre": ([128, 8], F32), "w_q": ([D, 256], F32),
    "w_k": ([D, 256], F32), "w_v": ([D, 256], F32), "ntri": ([128, 128], F32), "g_bpost": ([128, 8], F32),
    "w_out1": ([D, D], F32), "w_pg1": ([D, D], F32), "w_pp1": ([256, D], F32),
}
SCRATCH = {
    "uT_cs": ([256, SEQ], F32), "y_src": ([256, 2048], BF16, 4), "y_all": ([D, 2048], BF16, 4), "x1_own": ([D, TOKC], F32),
    "x1_src": ([D, NT], BF16, 4), "x1_all": ([4 * D, NT], BF16, 4), "g1T": ([D, TOKC], F32), "qT_cs": ([256, SEQ], BF16),
    "kT_cs": ([256, SEQ], BF16), "vT_cs": ([256, SEQ], BF16), "o_src": ([256, 2048], BF16, 4), "o_all": ([D, 2048], BF16, 4),
}


def build_fused():
    nc = bass.Bass("TRN2", target_bir_lowering=False)
    dr = {}
    for n, (shp, dt) in IN_SPECS.items():
        dr[n] = Res(n, nc.dram_tensor(n, list(shp), dt, kind="ExternalInput").ap())
    for n, spec in SCRATCH.items():
        shp, dt = spec[0], spec[1]
        if len(spec) == 3:
            dr[n] = [Res("%s%d" % (n, i), nc.dram_tensor("%s%d" % (n, i), list(shp), dt, kind="Internal").ap()) for i in range(spec[2])]
        else:
            dr[n] = Res(n, nc.dram_tensor(n, list(shp), dt, kind="Internal").ap())
    dr["outT"] = Res("outT", nc.dram_tensor("outT", [D, TOKC], F32, kind="ExternalOutput").ap())
    phase_A(nc, dr)
    phase_B(nc, dr)
    phase_C(nc, dr)
    phase_QKV(nc, dr)
    phase_D(nc, dr)
    phase_E(nc, dr)
    Prog.finish()
    return nc


def _f(a):
    return np.ascontiguousarray(np.asarray(a, dtype=np.float32))


def kernel(**inputs):
    inp = {k: np.asarray(v) for k, v in inputs.items()}
    x, p = inp["x"], inp["p"]
    lam_re, lam_im, log_dt = _f(inp["a_lam_re"][0]), _f(inp["a_lam_im"][0]), _f(inp["a_log_dt"][0])
    b_re, b_im, c_re, c_im = _f(inp["a_b_re"][0]), _f(inp["a_b_im"][0]), _f(inp["a_c_re"][0]), _f(inp["a_c_im"][0])
    iota = _f(np.broadcast_to(np.arange(TS + 1, dtype=np.float32), (128, TS + 1)))
    gmask = np.zeros((128, 8), np.float32)
    for gl in range(8):
        gmask[gl * 16:(gl + 1) * 16, gl] = 1.0
    sgn = np.ones((128, 1), np.float32)
    sgn[64:] = -1.0
    J = np.zeros((128, 128), np.float32)
    for q in range(64):
        J[64 + q, q] = -1.0
        J[q, 64 + q] = 1.0
    ntri = np.zeros((128, 128), np.float32)
    for j in range(128):
        ntri[j, :j + 1] = -1.0
    vecsC = np.zeros((128, 32), np.float32)
    for i, v in enumerate([inp["a_b_glu"][0], inp["a_norm_post"][0], inp["b_norm_pre"][0]]):
        vecsC[:, i * 8:(i + 1) * 8] = _cols(v)
    a_w_in, b_w_in, w_kv = _f(inp["a_w_in"][0]), _f(inp["b_w_in"][0]), _f(inp["w_kv"])
    common = {
        "g_pre": _cols(inp["a_norm_pre"][0]), "w_in_g": _f(a_w_in[:, D:]), "iota": iota, "gmask": gmask, "sgn": sgn, "Jm": J,
        "vecsC": vecsC, "w_glu": _f(inp["a_w_glu"][0]), "w_out0": _f(inp["a_w_out"][0]), "w_pg0": _f(inp["ple_w_gate"][0]),
        "w_pp0": _f(inp["ple_w_proj"][0]), "w_bin_g": _f(b_w_in[:, D:]), "g_kv": _cols(inp["kv_norm"]),
        "g_bpre": _cols(inp["b_norm_pre"][0]), "ntri": ntri, "g_bpost": _cols(inp["b_norm_post"][0]),
        "w_out1": _f(inp["b_w_out"][0]), "w_pg1": _f(inp["ple_w_gate"][1]), "w_pp1": _f(inp["ple_w_proj"][1]),
    }
    xT_full = [_f(np.asarray(x[b], np.float32).T) for b in range(2)]
    maps = []
    for c in range(NCORE):
        b, r = c // 4, c % 4
        gs = np.arange(16 * r, 16 * r + 16)
        tsl = slice(r * TOKC, (r + 1) * TOKC)
        csl = slice(256 * r, 256 * r + 256)
        m = dict(common)
        m["xT_full"] = xT_full[b]
        m["xT_own"] = _f(xT_full[b][:, tsl])
        m["p0T"] = _f(np.asarray(p[0, b, tsl, :], np.float32).T)
        m["p1T"] = _f(np.asarray(p[1, b, tsl, :], np.float32).T)
        sel = np.zeros((128, 4), np.float32)
        sel[:, r] = 1.0
        m["sel"] = sel
        m["w_in_u"] = _f(a_w_in[:, csl])
        m["dskip_cs"] = _cols(inp["a_d_skip"][0][csl])
        m["w_q"] = _f(b_w_in[:, csl])
        m["w_k"] = _f(w_kv[:, csl])
        m["w_v"] = _f(w_kv[:, D + 256 * r:D + 256 * r + 256])

        def lt_gp(a):
            t = a[gs].reshape(2, 8, 64)
            t = np.broadcast_to(t[:, :, None, :], (2, 8, 16, 64))
            return _f(t.transpose(1, 2, 0, 3).reshape(128, 128))

        def lt_b(a):
            t = a[gs].reshape(2, 8, 64, 16)
            return _f(t.transpose(1, 3, 0, 2).reshape(128, 128))

        def sp_gp(a):
            t = a[gs].T
            return _f(np.concatenate([t, t], axis=0))
        ldt = np.broadcast_to(log_dt[:, None], (64, 64))
        m["lamre_T"], m["lamim_T"], m["logdt_T"] = lt_gp(lam_re), lt_gp(lam_im), lt_gp(ldt)
        m["bre_T"], m["bim_T"] = lt_b(b_re), lt_b(b_im)
        m["lamre_S"], m["lamim_S"], m["logdt_S"] = sp_gp(lam_re), sp_gp(lam_im), sp_gp(ldt)
        cr = c_re[gs].transpose(2, 0, 1).reshape(64, 256)
        ci = c_im[gs].transpose(2, 0, 1).reshape(64, 256)
        m["c1_S"] = _f(np.concatenate([cr, ci], axis=0))
        m["c2_S"] = _f(np.concatenate([ci, cr], axis=0))
        maps.append(m)
    res = _run(build_fused(), maps)
    out = np.empty((2, SEQ, D), np.float32)
    for c in range(NCORE):
        b, r = c // 4, c % 4
        out[b, r * TOKC:(r + 1) * TOKC, :] = res[c]["outT"].T
    return out
```

```python
from contextlib import ExitStack
import numpy as np
import concourse.bass as bass
import concourse.mybir as mybir
from concourse.bass_utils import run_bass_kernel_spmd

F32 = mybir.dt.float32
BF16 = mybir.dt.bfloat16
AF = mybir.ActivationFunctionType
ALU = mybir.AluOpType

ENGS = ("pe", "act", "dve", "pool", "sp")
NCORE = 8
D = 1024
SEQ = 8192
NT = 512
EPS = 1e-6
PI = float(np.pi)


class Res:
    __slots__ = ("name", "w", "r", "dsem", "dcnt", "t")

    def __init__(self, name, t=None):
        self.name = name
        self.w = {}
        self.r = {}
        self.dsem = None
        self.dcnt = 0
        self.t = t

    def __getitem__(self, idx):
        return self.t[idx]


class Prog:
    _n = 0
    G = None

    def __init__(self, nc):
        Prog._n += 1
        self.pfx = "f%d_" % Prog._n
        self.nc = nc
        if Prog.G is None or Prog.G["nc"] is not nc:
            ges = ExitStack()
            Prog.G = {"nc": nc, "es": ges, "sems": {}, "cnt": {}}
            for e in ENGS:
                Prog.G["sems"][e] = ges.enter_context(nc.semaphore("s_" + e))
                Prog.G["cnt"][e] = 0
        G = Prog.G
        self.es = ExitStack()
        self.lists = {e: [] for e in ENGS}
        self.sems = G["sems"]
        self.cnt = G["cnt"]
        self.seen = {e: dict(self.cnt) for e in ENGS}
        self.nd = 0
        self.banks = []
        self.bi = 0
        self.touched = {}

    @staticmethod
    def finish():
        if Prog.G is not None:
            Prog.G["es"].close()
            Prog.G = None

    def _newsem(self):
        key = "d%d" % self.nd
        self.nd += 1
        if key not in self.sems:
            self.sems[key] = Prog.G["es"].enter_context(self.nc.semaphore("sd_" + key))
            self.cnt[key] = 0
        return key

    def sb(self, name, shape, dt):
        return Res(name, self.es.enter_context(self.nc.sbuf_tensor(self.pfx + "sb_" + name, list(shape), dt)))

    def ps(self, name, shape, dt=F32):
        return Res(name, self.es.enter_context(self.nc.psum_tensor(self.pfx + "ps_" + name, list(shape), dt)))

    def dram(self, name, shape, dt, kind="Internal"):
        return Res(name, self.nc.dram_tensor(name, list(shape), dt, kind=kind).ap())

    def mk_banks(self, n):
        self.banks = [self.ps("bank%d" % i, [128, NT], F32) for i in range(n)]

    def bank(self):
        b = self.banks[self.bi % len(self.banks)]
        self.bi += 1
        return b

    def _dsem(self, res):
        if res.dsem is None:
            res.dsem = self._newsem()
        return res.dsem

    def _waits(self, eng, reads, writes, skip_same=False):
        for x_ in reads:
            self.touched[id(x_)] = x_
        for x_ in writes:
            self.touched[id(x_)] = x_
        deps = {}
        for r in reads:
            for k, v in r.w.items():
                if skip_same and k == eng:
                    continue
                if v > deps.get(k, 0):
                    deps[k] = v
        for w in writes:
            for k, v in w.w.items():
                if k != eng and v > deps.get(k, 0):
                    deps[k] = v
            for k, v in w.r.items():
                if k != eng and v > deps.get(k, 0):
                    deps[k] = v
        seen = self.seen[eng]
        for k, v in deps.items():
            if v > seen.get(k, 0):
                seen[k] = v
                sem = self.sems[k]
                self.lists[eng].append(lambda E, sem=sem, v=v: E.wait_ge(sem, v))

    def op(self, eng, fn, reads=(), writes=(), skip_same=False):
        self._waits(eng, reads, writes, skip_same)
        self.cnt[eng] += 1
        n = self.cnt[eng]
        sem = self.sems[eng]
        self.lists[eng].append(lambda E, fn=fn, sem=sem: fn(E).then_inc(sem, 1))
        for r in reads:
            r.r[eng] = n
        for w in writes:
            w.w[eng] = n

    def dma(self, eng, out_ap, in_ap, reads=(), writes=()):
        wres = writes[0]
        self._waits(eng, reads, writes)
        key = self._dsem(wres)
        self.cnt[key] += 16
        v = self.cnt[key]
        sem = self.sems[key]
        self.lists[eng].append(
            lambda E, o=out_ap, i=in_ap, sem=sem: E.dma_start(out=o, in_=i).then_inc(sem, 16))
        for r in reads:
            r.r[key] = v
        wres.w[key] = v

    def coll(self, kind, src, dst, groups):
        self._waits("pool", [src], [dst])
        key = self._newsem()
        self.cnt[key] += 1
        v = self.cnt[key]
        sem = self.sems[key]
        self.lists["pool"].append(lambda E, sem=sem: E.collective_compute(
            kind, ALU.bypass, replica_groups=groups, ins=[src.t.opt()], outs=[dst.t.opt()]).then_inc(sem))
        src.r[key] = v
        dst.w[key] = v

    def wait_all(self, eng, ress):
        self._waits(eng, ress, ())

    def emit(self):
        L = self.lists
        for e in ENGS:
            for k, sem in self.sems.items():
                tgt = self.cnt[k]
                if k != e and tgt > self.seen[e].get(k, 0):
                    self.seen[e][k] = tgt
                    L[e].append(lambda E, sem=sem, tgt=tgt: E.wait_ge(sem, tgt))
        with self.nc.Block() as block:
            @block.tensor
            def _(E):
                for f in L["pe"]:
                    f(E)

            @block.scalar
            def _(E):
                for f in L["act"]:
                    f(E)

            @block.vector
            def _(E):
                for f in L["dve"]:
                    f(E)

            @block.gpsimd
            def _(E):
                for f in L["pool"]:
                    f(E)

            @block.sync
            def _(E):
                for f in L["sp"]:
                    f(E)
        self.es.close()
        for x_ in self.touched.values():
            x_.w = {}
            x_.r = {}
            x_.dsem = None


def tt(P, eng, out, in0, in1, op, reads, writes):
    P.op(eng, lambda E: E.tensor_tensor(out=out, in0=in0, in1=in1, op=op), reads=reads, writes=writes)


def ts(P, eng, out, in0, s1, s2, op0, op1, reads, writes):
    if s2 is None:
        P.op(eng, lambda E: E.tensor_scalar(out=out, in0=in0, scalar1=s1, scalar2=None, op0=op0), reads=reads, writes=writes)
    else:
        P.op(eng, lambda E: E.tensor_scalar(out=out, in0=in0, scalar1=s1, scalar2=s2, op0=op0, op1=op1), reads=reads, writes=writes)


def act(P, out, in_, func, reads, writes, scale=1.0, bias=None):
    if bias is None:
        P.op("act", lambda E: E.activation(out=out, in_=in_, func=func, scale=scale), reads=reads, writes=writes)
    else:
        P.op("act", lambda E: E.activation(out=out, in_=in_, func=func, scale=scale, bias=bias), reads=reads, writes=writes)


def scale_cast(P, i, out, in_, col, reads, writes):
    if i % 2:
        P.op("act", lambda E: E.activation(out=out, in_=in_, func=AF.Copy, scale=col), reads=reads, writes=writes)
    else:
        ts(P, "dve", out, in_, col, None, ALU.mult, None, reads, writes)


def cp(P, eng, out, in_, reads, writes):
    if eng == "act":
        P.op(eng, lambda E: E.activation(out=out, in_=in_, func=AF.Copy), reads=reads, writes=writes)
    else:
        P.op(eng, lambda E: E.tensor_copy(out=out, in_=in_), reads=reads, writes=writes)


class Rot:
    def __init__(self, P, name, n, shape, dt):
        self.bufs = [P.sb("%s%d" % (name, i), shape, dt) for i in range(n)]
        self.i = 0

    def get(self):
        b = self.bufs[self.i % len(self.bufs)]
        self.i += 1
        return b


class Dense:
    def __init__(self, P, T):
        self.P = P
        self.T = T
        self.wst = Rot(P, "wst", 4, [128, 8, 128], F32)
        self.wbf = Rot(P, "wbf", 4, [128, 8, 128], BF16)
        self.ost = Rot(P, "ost", 4, [128, NT], F32)
        self.ostb = Rot(P, "ostb", 4, [128, NT], BF16)
        self.ist = Rot(P, "ist", 4, [128, NT], F32)
        self.tmp = Rot(P, "tmp", 6, [128, NT], F32)
        self.ones = P.sb("ones", [128, 128], BF16)
        P.op("pool", lambda E: E.memset(self.ones[:], 1.0), writes=[self.ones])
        self.sq = P.sb("sq", [128, 8, T], BF16)

    def load_w(self, W, m, KC, rowscale=None, wb=None):
        P = self.P
        assert rowscale is None
        st = self.wst.get()
        if wb is None:
            wb = self.wbf.get()
        P.dma("sp", st[:, 0:KC, :], W.t[:, m * 128:(m + 1) * 128].rearrange("(kc p) m -> p kc m", p=128), reads=[W], writes=[st])
        self.wi = getattr(self, "wi", 0) + 1
        cp(P, "act" if self.wi % 2 else "pool", wb[:, 0:KC, :], st[:, 0:KC, :], [st], [wb])
        return wb

    def prep_w(self, name, W, KC, MC):
        wbs = []
        for m in range(MC):
            wb = self.P.sb("%s_w%d" % (name, m), [128, KC, 128], BF16)
            wbs.append(self.load_w(W, m, KC, wb=wb))
        return wbs

    def linear(self, W, KC, MC, a, epi, rowscale=None, wbs=None):
        P = self.P
        for m in range(MC):
            wb = wbs[m] if wbs is not None else self.load_w(W, m, KC, rowscale)
            for n in range(self.T // NT):
                ps = P.bank()
                for kc in range(KC):
                    P.op("pe", lambda E, ps=ps, wb=wb, kc=kc, n=n: E.matmul(
                        ps[:, :], lhsT=wb[:, kc, :], rhs=a[:, kc, n * NT:(n + 1) * NT], start=(kc == 0), stop=(kc == KC - 1)),
                        reads=[wb, a], writes=[ps])
                epi(m, n, ps)

    def rstd(self, src, out):
        P = self.P
        sq = self.sq
        for kc in range(8):
            act(P, sq[:, kc, :], src[:, kc, :], AF.Square, [src], [sq])
        for n in range(self.T // NT):
            ps = P.bank()
            for kc in range(8):
                P.op("pe", lambda E, ps=ps, kc=kc, n=n: E.matmul(
                    ps[:, :], lhsT=self.ones[:, :], rhs=sq[:, kc, n * NT:(n + 1) * NT], start=(kc == 0), stop=(kc == 7)),
                    reads=[self.ones, sq], writes=[ps])
            t = self.tmp.get()
            act(P, t[:, :], ps[:, :], AF.Ln, [ps], [t], scale=1.0 / D, bias=EPS)
            act(P, out[:, n * NT:(n + 1) * NT], t[:, :], AF.Exp, [t], [out], scale=-0.5)

    def store(self, dst, m, n, t0, src_res, src_ap):
        self.P.dma("pool", dst.t[m * 128:(m + 1) * 128, t0 + n * NT:t0 + (n + 1) * NT], src_ap, reads=[src_res], writes=[dst])

    def load_act(self, src, t0, dst, KC=8):
        for kc in range(KC):
            self.P.dma("sp", dst[:, kc, :], src.t[kc * 128:(kc + 1) * 128, t0:t0 + self.T], reads=[src], writes=[dst])


def colvec(P, name, dram_res, ncol):
    t = P.sb(name, [128, ncol], F32)
    P.dma("sp", t[:, :], dram_res.t[:, :], reads=[dram_res], writes=[t])
    return t


def _run(nc, in_maps):
    res = run_bass_kernel_spmd(nc, in_maps, core_ids=list(range(NCORE)))
    return res.results


def _cols(v):
    return np.ascontiguousarray(np.asarray(v, np.float32).reshape(-1, 128).T)


TS = 256
NG = 16
M_MAGIC = 12582912.0


def range_reduce(P, eng, out, in_, tmp, reads, writes, shift=0.0):
    res_in = reads
    if shift != 0.0:
        ts(P, eng, out, in_, shift, None, ALU.add, None, res_in, writes)
        in_ = out
        res_in = writes
    ts(P, eng, tmp[0], in_, 1.0 / (2 * PI), M_MAGIC, ALU.mult, ALU.add, res_in, [tmp[1]])
    ts(P, eng, tmp[0], tmp[0], -M_MAGIC, -2 * PI, ALU.add, ALU.mult, [tmp[1]], [tmp[1]])
    tt(P, eng, out, tmp[0], in_, ALU.add, [tmp[1]] + list(res_in), writes)
    ts(P, eng, out, out, 3.14159, -3.14159, ALU.min, ALU.max, writes, writes)


NAMES_T = ["lamre_T", "lamim_T", "logdt_T", "bre_T", "bim_T"]
NAMES_S = ["lamre_S", "lamim_S", "logdt_S"]


def phase_B(nc, dr):
    P = Prog(nc)
    uT = dr["uT_cs"]
    names_T = NAMES_T
    dT = {n: dr[n] for n in names_T}
    names_S = NAMES_S
    dS = {n: dr[n] for n in names_S}
    c1_d, c2_d, iota_d, gmask_d, sgn_d, J_d = dr["c1_S"], dr["c2_S"], dr["iota"], dr["gmask"], dr["sgn"], dr["Jm"]
    dsk = colvec(P, "dskip_cs", dr["dskip_cs"], 2)
    P.mk_banks(4)
    ybank = [P.ps("ybank%d" % i, [128, NT], F32) for i in range(2)]
    igb = P.ps("igb", [128, NT], F32)

    def ld(name, d, ncol):
        return colvec(P, name, d, ncol)

    lt = {n: ld("s_" + n, dT[n], 128) for n in names_T}
    cnt = [0]

    def newT(nm, ncol=128):
        cnt[0] += 1
        return P.sb("%s_%d" % (nm, cnt[0]), [128, ncol], F32)

    def derive(lamre, lamim, logdt, ncol, pfx):
        o = {}
        lr = newT(pfx + "lr", ncol)
        ts(P, "dve", lr[:, :], lamre[:, :], -1e-4, None, ALU.min, None, [lamre], [lr])
        dt = newT(pfx + "dt", ncol)
        act(P, dt[:, :], logdt[:, :], AF.Exp, [logdt], [dt])
        e = newT(pfx + "e", ncol)
        tt(P, "dve", e[:, :], lr[:, :], dt[:, :], ALU.mult, [lr, dt], [e])
        th = newT(pfx + "th", ncol)
        tt(P, "dve", th[:, :], lamim[:, :], dt[:, :], ALU.mult, [lamim, dt], [th])
        thr = newT(pfx + "thr", ncol)
        tmp = newT(pfx + "tmp", ncol)
        range_reduce(P, "dve", thr[:, :], th[:, :], (tmp[:, :], tmp), [th], [thr])
        thc = newT(pfx + "thc", ncol)
        range_reduce(P, "dve", thc[:, :], th[:, :], (tmp[:, :], tmp), [th], [thc], shift=PI / 2)
        mag = newT(pfx + "mag", ncol)
        act(P, mag[:, :], e[:, :], AF.Exp, [e], [mag])
        sn = newT(pfx + "sin", ncol)
        act(P, sn[:, :], thr[:, :], AF.Sin, [thr], [sn])
        cs = newT(pfx + "cos", ncol)
        act(P, cs[:, :], thc[:, :], AF.Sin, [thc], [cs])
        o.update(lr=lr, li=lamim, e=e, thr=thr, mag=mag, sin=sn, cos=cs)
        return o

    dl = derive(lt["lamre_T"], lt["lamim_T"], lt["logdt_T"], 128, "T")

    def mul(a, b, nm):
        t = newT(nm)
        tt(P, "dve", t[:, :], a[:, :], b[:, :], ALU.mult, [a, b], [t])
        return t

    def addsub(a, b, op, nm):
        t = newT(nm)
        tt(P, "dve", t[:, :], a[:, :], b[:, :], op, [a, b], [t])
        return t

    are = mul(dl["mag"], dl["cos"], "are")
    aim = mul(dl["mag"], dl["sin"], "aim")
    den = addsub(mul(dl["lr"], dl["lr"], "lr2"), mul(dl["li"], dl["li"], "li2"), ALU.add, "den")
    rden = newT("rden")
    P.op("dve", lambda E: E.reciprocal(out=rden[:, :], in_=den[:, :]), reads=[den], writes=[rden])
    nr = newT("nr")
    ts(P, "dve", nr[:, :], are[:, :], -1.0, None, ALU.add, None, [are], [nr])
    fre = mul(addsub(mul(nr, dl["lr"], "f1"), mul(aim, dl["li"], "f2"), ALU.add, "f3"), rden, "fre")
    fim = mul(addsub(mul(aim, dl["lr"], "f4"), mul(nr, dl["li"], "f5"), ALU.subtract, "f6"), rden, "fim")
    bbre = addsub(mul(fre, lt["bre_T"], "b1"), mul(fim, lt["bim_T"], "b2"), ALU.subtract, "bbre")
    bbim = addsub(mul(fre, lt["bim_T"], "b3"), mul(fim, lt["bre_T"], "b4"), ALU.add, "bbim")
    nbbim = newT("nbbim")
    ts(P, "dve", nbbim[:, :], bbim[:, :], -1.0, None, ALU.mult, None, [bbim], [nbbim])
    gmask = ld("gmask", gmask_d, 8)
    W1 = P.sb("W1pad", [128, NG, 128], BF16)
    W2 = P.sb("W2pad", [128, NG, 128], BF16)
    for jj in range(2):
        for gl in range(8):
            gg = jj * 8 + gl
            ms = gmask[:, gl:gl + 1]
            sl = slice(jj * 64, (jj + 1) * 64)
            ts(P, "dve", W1[:, gg, 0:64], bbre[:, sl], ms, None, ALU.mult, None, [bbre, gmask], [W1])
            ts(P, "dve", W1[:, gg, 64:128], bbim[:, sl], ms, None, ALU.mult, None, [bbim, gmask], [W1])
            ts(P, "dve", W2[:, gg, 0:64], nbbim[:, sl], ms, None, ALU.mult, None, [nbbim, gmask], [W2])
            ts(P, "dve", W2[:, gg, 64:128], bbre[:, sl], ms, None, ALU.mult, None, [bbre, gmask], [W2])

    ls_ = {n: ld("s_" + n, dS[n], NG) for n in names_S}
    ds = derive(ls_["lamre_S"], ls_["lamim_S"], ls_["logdt_S"], NG, "S")
    rho = ds["mag"]
    iota = ld("iota", iota_d, TS + 1)
    COS = P.sb("COS", [128, NG, TS + 1], F32)
    SIN = P.sb("SIN", [128, NG, TS + 1], F32)
    ARG = P.sb("ARG", [128, NG, TS + 1], F32)
    TMP = P.sb("TMPA", [128, NG, TS + 1], F32)
    Gg = [P.sb("Gg%d" % i, [128, TS], F32) for i in range(NG)]
    GT = [P.sb("GT%d" % i, [128, NG], F32) for i in range(2)]
    for gg in range(NG):
        ts(P, "dve", ARG[:, gg, :], iota[:, :], ds["thr"][:, gg:gg + 1], None, ALU.mult, None, [iota, ds["thr"]], [ARG])
    range_reduce(P, "dve", SIN[:, :, :], ARG[:, :, :], (TMP[:, :, :], TMP), [ARG], [SIN])
    range_reduce(P, "dve", COS[:, :, :], ARG[:, :, :], (TMP[:, :, :], TMP), [ARG], [COS], shift=PI / 2)
    act(P, SIN[:, :, :], SIN[:, :, :], AF.Sin, [SIN], [SIN])
    act(P, COS[:, :, :], COS[:, :, :], AF.Sin, [COS], [COS])
    c1 = ld("c1", c1_d, NG * 16)
    c2 = ld("c2", c2_d, NG * 16)
    sgn = ld("sgn", sgn_d, 1)
    L1 = P.sb("L1pad", [128, NG, 128], BF16)
    L2 = P.sb("L2pad", [128, NG, 128], BF16)
    P.op("pool", lambda E: E.memset(L1[:, :, :], 0.0), writes=[L1])
    P.op("pool", lambda E: E.memset(L2[:, :, :], 0.0), writes=[L2])
    for gg in range(NG):
        gl = gg % 8
        ts(P, "dve", L1[:, gg, gl * 16:(gl + 1) * 16], c1[:, gg * 16:(gg + 1) * 16], sgn[:, 0:1], None, ALU.mult, None, [c1, sgn], [L1])
        ts(P, "dve", L2[:, gg, gl * 16:(gl + 1) * 16], c2[:, gg * 16:(gg + 1) * 16], -1.0, None, ALU.mult, None, [c2], [L2])
    Jm = P.sb("Jm", [128, 128], F32)
    P.dma("sp", Jm[:, :], J_d.t[:, :], reads=[J_d], writes=[Jm])

    ust = Rot(P, "ust", 3, [128, 2, TS], F32)
    ubf = Rot(P, "ubf", 3, [128, 2, TS], BF16)
    t1r = Rot(P, "t1r", 4, [128, TS], F32)
    t2r = Rot(P, "t2r", 4, [128, TS], F32)
    xtr = Rot(P, "xtr", 4, [128, TS], F32)
    h1r = Rot(P, "h1r", 4, [128, TS], BF16)
    h2r = Rot(P, "h2r", 4, [128, TS], BF16)
    S0 = [P.sb("S0_%d" % i, [128, NG], F32) for i in range(2)]
    ys = Rot(P, "ys", 3, [128, 2, TS], BF16)
    P.op("pool", lambda E: E.memset(S0[0][:, :], 0.0), writes=[S0[0]])
    nch = SEQ // TS
    units = []
    chunk_res = {}
    for ch in range(nch):
        for jj in range(2):
            for gl in range(8):
                units.append(dict(ch=ch, jj=jj, gl=gl, gg=jj * 8 + gl))
    nu = len(units)

    def chunk_setup(ch):
        c0 = ch * TS
        us = ust.get()
        ub = ubf.get()
        P.dma("sp", us[:, :, :], uT.t[:, c0:c0 + TS].rearrange("(j p) t -> p j t", p=128), reads=[uT], writes=[us])
        cp(P, "act", ub[:, :, :], us[:, :, :], [us], [ub])
        chunk_res[ch] = dict(us=us, ub=ub, yo=ys.get())

    def s1(i):
        U = units[i]
        ch, jj, gg = U["ch"], U["jj"], U["gg"]
        if jj == 0 and U["gl"] == 0:
            chunk_setup(ch)
        ub = chunk_res[ch]["ub"]
        p1 = P.bank()
        p2 = P.bank()
        P.op("pe", lambda E, p1=p1, gg=gg, ub=ub, jj=jj: E.matmul(p1[:, 0:TS], lhsT=W1[:, gg, :], rhs=ub[:, jj, :], start=True, stop=True), reads=[W1, ub], writes=[p1])
        P.op("pe", lambda E, p2=p2, gg=gg, ub=ub, jj=jj: E.matmul(p2[:, 0:TS], lhsT=W2[:, gg, :], rhs=ub[:, jj, :], start=True, stop=True), reads=[W2, ub], writes=[p2])
        t1 = t1r.get()
        t2 = t2r.get()
        xt = xtr.get()
        tt(P, "dve", t1[:, :], p1[:, 0:TS], COS[:, gg, 1:TS + 1], ALU.mult, [p1, COS], [t1])
        tt(P, "dve", t2[:, :], p2[:, 0:TS], SIN[:, gg, 1:TS + 1], ALU.mult, [p2, SIN], [t2])
        tt(P, "pool", xt[:, :], t1[:, :], t2[:, :], ALU.subtract, [t1, t2], [xt])
        U["xt"] = xt

    def s2(i):
        U = units[i]
        ch, gg = U["ch"], U["gg"]
        gt, s0 = GT[ch % 2], S0[ch % 2]
        xt = U["xt"]
        G = Gg[gg]
        P.op("dve", lambda E, G=G, gg=gg, xt=xt, s0=s0: E.tensor_tensor_scan(
            out=G[:, :], data0=rho[:, gg:gg + 1].to_broadcast([128, TS]), data1=xt[:, :],
            initial=s0[:, gg:gg + 1], op0=ALU.mult, op1=ALU.add), reads=[rho, xt, s0], writes=[G])
        if ch + 1 < nch:
            cp(P, "act", gt[:, gg:gg + 1], G[:, TS - 1:TS], [G], [gt])
        h1 = h1r.get()
        h2 = h2r.get()
        tt(P, "dve", h1[:, :], G[:, :], COS[:, gg, 1:TS + 1], ALU.mult, [G, COS], [h1])
        tt(P, "pool", h2[:, :], G[:, :], SIN[:, gg, 1:TS + 1], ALU.mult, [G, SIN], [h2])
        U["h1"], U["h2"] = h1, h2
        if gg == NG - 1 and ch + 1 < nch:
            s1_ = S0[(ch + 1) % 2]
            P.op("pe", lambda E, gt=gt: E.matmul(igb[:, 0:NG], lhsT=Jm[:, :], rhs=gt[:, :], start=True, stop=True), reads=[Jm, gt], writes=[igb])
            ta = P.sb("bta%d" % ch, [128, NG], F32)
            tb = P.sb("btb%d" % ch, [128, NG], F32)
            tt(P, "dve", ta[:, :], gt[:, :], COS[:, :, TS], ALU.mult, [gt, COS], [ta])
            tt(P, "dve", tb[:, :], igb[:, 0:NG], SIN[:, :, TS], ALU.mult, [igb, SIN], [tb])
            tt(P, "dve", s1_[:, :], ta[:, :], tb[:, :], ALU.add, [ta, tb], [s1_])

    def s3(i):
        U = units[i]
        ch, jj, gl, gg = U["ch"], U["jj"], U["gl"], U["gg"]
        yb = ybank[jj]
        h1, h2 = U["h1"], U["h2"]
        P.op("pe", lambda E, yb=yb, gg=gg, h1=h1, gl=gl: E.matmul(yb[:, 0:TS], lhsT=L1[:, gg, :], rhs=h1[:, :], start=(gl == 0), stop=False), reads=[L1, h1], writes=[yb])
        P.op("pe", lambda E, yb=yb, gg=gg, h2=h2, gl=gl: E.matmul(yb[:, 0:TS], lhsT=L2[:, gg, :], rhs=h2[:, :], start=False, stop=(gl == 7)), reads=[L2, h2], writes=[yb])
        if gl == 7:
            cr = chunk_res[ch]
            yo, us = cr["yo"], cr["us"]
            P.op("dve", lambda E, yo=yo, us=us, yb=yb, jj=jj: E.scalar_tensor_tensor(
                out=yo[:, jj, :], in0=us[:, jj, :], scalar=dsk[:, jj:jj + 1], in1=yb[:, 0:TS], op0=ALU.mult, op1=ALU.add),
                reads=[us, dsk, yb], writes=[yo])
            if jj == 1:
                c0 = ch * TS
                ysrc = dr["y_src"][c0 // 2048]
                P.dma("act", ysrc.t[:, c0 % 2048:c0 % 2048 + TS].rearrange("(j p) t -> p j t", p=128), yo[:, :, :], reads=[yo], writes=[ysrc])
                if (c0 + TS) % 2048 == 0:
                    P.coll("AllGather", ysrc, dr["y_all"][c0 // 2048], GROUPS)

    for j in range(nu + 2):
        if j < nu:
            s1(j)
        if 0 <= j - 1 < nu:
            s2(j - 1)
        if 0 <= j - 2 < nu:
            s3(j - 2)
    P.wait_all("act", dr["y_all"])
    P.emit()


def gelu_tanh(P, dn, out_bf, y, reads):
    s = dn.tmp.get()
    s2 = dn.tmp.get()
    act(P, s[:, :], y, AF.Square, reads, [s])
    ts(P, "dve", s[:, :], s[:, :], 0.044715, 1.0, ALU.mult, ALU.add, [s], [s])
    tt(P, "dve", s2[:, :], s[:, :], y, ALU.mult, [s] + list(reads), [s2])
    act(P, s2[:, :], s2[:, :], AF.Sigmoid, [s2], [s2], scale=1.5957691216057308)
    return s2


def ple(P, dn, Wpg, Wpp, Xb, Pb, X1):
    for m in range(8):
        wg = dn.load_w(Wpg, m, 8)
        wp = dn.load_w(Wpp, m, 2)
        for n in range(dn.T // NT):
            pg = P.bank()
            pp = P.bank()
            for kc in range(8):
                P.op("pe", lambda E, pg=pg, wg=wg, kc=kc, n=n: E.matmul(
                    pg[:, :], lhsT=wg[:, kc, :], rhs=Xb[:, kc, n * NT:(n + 1) * NT], start=(kc == 0), stop=(kc == 7)),
                    reads=[wg, Xb], writes=[pg])
            for kc in range(2):
                P.op("pe", lambda E, pp=pp, wp=wp, kc=kc, n=n: E.matmul(
                    pp[:, :], lhsT=wp[:, kc, :], rhs=Pb[:, kc, n * NT:(n + 1) * NT], start=(kc == 0), stop=(kc == 1)),
                    reads=[wp, Pb], writes=[pp])
            sg = dn.tmp.get()
            act(P, sg[:, :], pg[:, :], AF.Sigmoid, [pg], [sg])
            t = dn.tmp.get()
            tt(P, "dve", t[:, :], sg[:, :], pp[:, :], ALU.mult, [sg, pp], [t])
            tt(P, "pool", X1[:, m, n * NT:(n + 1) * NT], X1[:, m, n * NT:(n + 1) * NT], t[:, :], ALU.add, [X1, t], [X1])


NH = 4
NQT = SEQ // NT


def phase_D(nc, dr):
    P = Prog(nc)
    qT, kT, vT, tri_d = dr["qT_cs"], dr["kT_cs"], dr["vT_cs"], dr["ntri"]
    ident = P.sb("ident", [128, 128], BF16)
    P.op("pool", lambda E: E.memset(ident[:, :], 0.0), writes=[ident])
    P.op("pool", lambda E: E.affine_select(out=ident[:, :], in_=ident[:, :], pattern=[[-1, 128]], compare_op=ALU.not_equal,
                                           fill=1.0, base=0, channel_multiplier=1), reads=[ident], writes=[ident])
    vtb = Rot(P, "vtb", 3, [64, 2048], BF16)
    tpb = P.ps("tpb", [128, NT], BF16)
    zb = [P.ps("zb%d" % i, [128, NT], F32) for i in range(4)]
    ob = [P.ps("ob%d" % i, [64, NT], F32) for i in range(2)]
    st = P.sb("tri_st", [128, 128], F32)
    P.dma("sp", st[:, :], tri_d.t[:, :], reads=[tri_d], writes=[st])
    ntri = P.sb("ntri", [128, 128], BF16)
    cp(P, "dve", ntri[:, :], st[:, :], [st], [ntri])
    nones = P.sb("nones", [128, 128], BF16)
    P.op("pool", lambda E: E.memset(nones[:, :], -1.0), writes=[nones])
    Qb = P.sb("Qb", [128, 2, SEQ], BF16)
    Kb = P.sb("Kb", [128, 2, SEQ], BF16)
    Vb = [P.sb("Vb%d" % h, [128, 64 * 64], BF16) for h in range(NH)]
    er = Rot(P, "er", 3, [128, NT], F32)
    spr = Rot(P, "spr", 4, [128, NT], BF16)
    wr = Rot(P, "wr", 4, [128, NT], BF16)
    racc = Rot(P, "racc", 3, [128, NT], BF16)
    ost = Rot(P, "ost", 2, [64, NT], BF16)
    for pr in range(2):
        for c in range(SEQ // 2048):
            P.dma("sp", Qb[:, pr, c * 2048:(c + 1) * 2048], qT.t[pr * 128:(pr + 1) * 128, c * 2048:(c + 1) * 2048], reads=[qT], writes=[Qb])
            P.dma("sp", Kb[:, pr, c * 2048:(c + 1) * 2048], kT.t[pr * 128:(pr + 1) * 128, c * 2048:(c + 1) * 2048], reads=[kT], writes=[Kb])
    for h in range(NH):
        for c in range(SEQ // 2048):
            vb_ = vtb.get()
            P.dma("sp", vb_[:, :], vT.t[h * 64:(h + 1) * 64, c * 2048:(c + 1) * 2048], reads=[vT], writes=[vb_])
            for k8 in range(2):
                for j in range(8):
                    blk = k8 * 8 + j
                    P.op("pe", lambda E, vb_=vb_, j=j, blk=blk: E.transpose(
                        out=tpb[:, j * 64:(j + 1) * 64], in_=vb_[:, blk * 128:(blk + 1) * 128], identity=ident[0:64, 0:64]),
                        reads=[vb_, ident], writes=[tpb])
                kb0 = c * 16 + k8 * 8
                cp(P, "dve", Vb[h][:, kb0 * 64:(kb0 + 8) * 64], tpb[:, :], [tpb], [Vb[h]])
    blocks = []
    for h in range(NH):
        for qt in range(NQT):
            kbs = list(range(4 * qt + 3, -1, -1))
            for idx, kb in enumerate(kbs):
                blocks.append(dict(h=h, qt=qt, kb=kb, idx=idx, n=len(kbs), g=h * NQT + qt))
    nb = len(blocks)

    def operands(B):
        hp, pr = B["h"] % 2, B["h"] // 2
        ksl = Kb[hp * 64:(hp + 1) * 64, pr, B["kb"] * 128:(B["kb"] + 1) * 128]
        qsl = Qb[hp * 64:(hp + 1) * 64, pr, B["qt"] * NT:(B["qt"] + 1) * NT]
        return ksl, qsl

    def mask(t, B):
        base = B["qt"] * NT - 128 * B["kb"]
        P.op("pool", lambda E, t=t, base=base: E.affine_select(
            out=t[:, :], in_=t[:, :], pattern=[[1, NT]], compare_op=ALU.is_gt, fill=0.0, base=base, channel_multiplier=-1),
            reads=[t], writes=[t])

    def stage1a(i):
        B = blocks[i]
        ksl, qsl = operands(B)
        z = zb[i % 4]
        P.op("pe", lambda E, z=z, ksl=ksl, qsl=qsl: E.matmul(z[:, :], lhsT=ksl, rhs=qsl, start=True, stop=False), reads=[Kb, Qb], writes=[z])
        e = er.get()
        act(P, e[:, :], z[:, :], AF.Exp, [z], [e])
        B["e"] = e

    def stage1b(i):
        B = blocks[i]
        e = B["e"]
        sp = spr.get()
        P.op("act", lambda E, sp=sp, e=e: E.activation(out=sp[:, :], in_=e[:, :], func=AF.Ln, scale=1.0, bias=1.0),
             reads=[e], writes=[sp], skip_same=True)
        if B["kb"] >= 4 * B["qt"]:
            mask(sp, B)
        B["sp"] = sp

    def stage2(i):
        B = blocks[i]
        ksl, qsl = operands(B)
        b = zb[i % 4]
        sp = B["sp"]
        first = B["idx"] == 0
        ra_prev = None if first else blocks[i - 1]["ra"]
        P.op("pe", lambda E, b=b, sp=sp, first=first: E.matmul(b[:, :], lhsT=ntri[:, :], rhs=sp[:, :], start=False, stop=first), reads=[ntri, sp], writes=[b])
        if not first:
            P.op("pe", lambda E, b=b, ra=ra_prev: E.matmul(b[:, :], lhsT=nones[:, :], rhs=ra[:, :], start=False, stop=True), reads=[nones, ra_prev], writes=[b])
        w = wr.get()
        act(P, w[:, :], b[:, :], AF.Exp, [b], [w])
        if B["kb"] >= 4 * B["qt"]:
            mask(w, B)
        B["w"] = w
        if B["idx"] + 1 < B["n"]:
            ra = racc.get()
            if first:
                cp(P, "dve", ra[:, :], sp[:, :], [sp], [ra])
            else:
                tt(P, "dve", ra[:, :], ra_prev[:, :], sp[:, :], ALU.add, [ra_prev, sp], [ra])
            B["ra"] = ra

    def stage3(i):
        B = blocks[i]
        o_ps = ob[B["g"] % 2]
        w = B["w"]
        h, kb = B["h"], B["kb"]
        P.op("pe", lambda E, o_ps=o_ps, w=w, h=h, kb=kb, B=B: E.matmul(
            o_ps[:, :], lhsT=Vb[h][:, kb * 64:(kb + 1) * 64], rhs=w[:, :], start=(B["idx"] == 0), stop=(B["idx"] == B["n"] - 1)),
            reads=[Vb[h], w], writes=[o_ps])
        if B["idx"] == B["n"] - 1:
            o = ost.get()
            cp(P, "dve", o[:, :], o_ps[:, :], [o_ps], [o])
            q0 = B["qt"] * NT
            osrc = dr["o_src"][q0 // 2048]
            P.dma("act", osrc.t[h * 64:(h + 1) * 64, q0 % 2048:q0 % 2048 + NT], o[:, :], reads=[o], writes=[osrc])
            if h == NH - 1 and (q0 + NT) % 2048 == 0:
                P.coll("AllGather", osrc, dr["o_all"][q0 // 2048], GROUPS)

    for j in range(nb + 2):
        if j < nb:
            stage1a(j)
            stage1b(j)
        if 0 <= j - 1 < nb:
            stage2(j - 1)
        if 0 <= j - 2 < nb:
            stage3(j - 2)
    P.wait_all("act", dr["o_all"])
    P.emit()


GROUPS = [[0, 1, 2, 3], [4, 5, 6, 7]]
TOKC = 2048
TP = 1024


def proj_pass(P, dn, src_fn, xs, rs, jobs, col0):
    for kc in range(8):
        srcs = src_fn(kc)
        if isinstance(srcs, tuple):
            P.dma("sp", xs[:, kc, :], srcs[1], reads=[srcs[0]], writes=[xs])
        else:
            w_ = TP // len(srcs)
            for i_, (sr_, sa_) in enumerate(srcs):
                P.dma("sp", xs[:, kc, i_ * w_:(i_ + 1) * w_], sa_, reads=[sr_], writes=[xs])
    dn.rstd(xs, rs)
    done = {}
    for job in jobs:
        xb, gain, wbs, MC, dst = job[:5]
        oscale = job[5] if len(job) > 5 else None
        if id(xb) not in done:
            done[id(xb)] = 1
            for kc in range(8):
                scale_cast(P, kc, xb[:, kc, :], xs[:, kc, :], gain[:, kc:kc + 1], [xs, gain], [xb])

        def epi(m, n, ps, dst=dst, oscale=oscale):
            if oscale is None:
                o = dn.ost.get()
                tt(P, "dve", o[:, :], ps[:, :], rs[:, n * NT:(n + 1) * NT], ALU.mult, [ps, rs], [o])
            else:
                o = dn.ostb.get()
                P.op("dve", lambda E, o=o, ps=ps, n=n: E.scalar_tensor_tensor(
                    out=o[:, :], in0=ps[:, :], scalar=oscale, in1=rs[:, n * NT:(n + 1) * NT], op0=ALU.mult, op1=ALU.mult),
                    reads=[ps, rs], writes=[o])
            dn.store(dst, m, n, col0, o, o[:, :])
        dn.linear(None, 8, MC, xb, epi, wbs=wbs)


def phase_A(nc, dr):
    P = Prog(nc)
    P.mk_banks(6)
    dn = Dense(P, TP)
    g = colvec(P, "g_pre", dr["g_pre"], 8)
    xsr = Rot(P, "xs", 2, [128, 8, TP], F32)
    xb = P.sb("xb", [128, 8, TP], BF16)
    rs = P.sb("rs", [128, TP], F32)
    xT = dr["xT_full"]
    wbs = dn.prep_w("wu", dr["w_in_u"], 8, 2)
    for pa in range(SEQ // TP):
        t0 = pa * TP
        proj_pass(P, dn, lambda kc, t0=t0: (xT, xT.t[kc * 128:(kc + 1) * 128, t0:t0 + TP]), xsr.get(), rs,
                  [(xb, g, wbs, 2, dr["uT_cs"])], t0)
    P.wait_all("pool", [dr["uT_cs"]])
    P.emit()


def mk_select(P, dn, sel):
    ident = P.sb("identf", [128, 128], F32)
    P.op("pool", lambda E: E.memset(ident[:, :], 0.0), writes=[ident])
    P.op("pool", lambda E: E.affine_select(out=ident[:, :], in_=ident[:, :], pattern=[[-1, 128]], compare_op=ALU.not_equal,
                                           fill=1.0, base=0, channel_multiplier=1), reads=[ident], writes=[ident])
    mI = P.sb("mI", [128, 4, 128], BF16)
    for s_ in range(4):
        ts(P, "dve", mI[:, s_, :], ident[:, :], sel[:, s_:s_ + 1], None, ALU.mult, None, [ident, sel], [mI])
    dn.mI = mI


def select4(P, dn, src_fn, sel, dt):
    ps = P.bank()
    for s in range(4):
        it = dn.istb.get()
        sres, sap = src_fn(s)
        P.dma("sp", it[:, :], sap, reads=[sres], writes=[it])
        P.op("pe", lambda E, ps=ps, it=it, s=s: E.matmul(ps[:, :], lhsT=dn.mI[:, s, :], rhs=it[:, :], start=(s == 0), stop=(s == 3)),
             reads=[dn.mI, it], writes=[ps])
    acc = dn.tmp.get()
    act(P, acc[:, :], ps[:, :], AF.Copy, [ps], [acc])
    return acc


def phase_C(nc, dr):
    P = Prog(nc)
    P.mk_banks(7)
    dn = Dense(P, TP)
    dn.istb = Rot(P, "istb", 8, [128, NT], BF16)
    sel = colvec(P, "sel", dr["sel"], 4)
    mk_select(P, dn, sel)
    gpre = colvec(P, "g_pre", dr["g_pre"], 8)
    vl = []
    for i in range(4):
        t_ = P.sb("vec%d" % i, [128, 8], F32)
        P.dma("sp", t_[:, :], dr["vecsC"].t[:, i * 8:(i + 1) * 8], reads=[dr["vecsC"]], writes=[t_])
        vl.append(t_)
    bglu, gpost, gbpre, _unused = vl
    xT, y_all, pT = dr["xT_own"], dr["y_all"], dr["p0T"]
    Gb = P.sb("Gb", [128, 8, TP], BF16)
    SGb = P.sb("SGb", [128, 8, TP], BF16)
    Y2b = P.sb("Y2b", [128, 8, TP], BF16)
    X1 = P.sb("X1", [128, 8, TP], F32)
    Pb = P.sb("Pb", [128, 2, TP], BF16)
    rs = P.sb("rs", [128, TP], F32)
    for pa in range(TOKC // TP):
        t0 = pa * TP
        for kc in range(8):
            P.dma("sp", X1[:, kc, :], xT.t[kc * 128:(kc + 1) * 128, t0:t0 + TP], reads=[xT], writes=[X1])
        dn.rstd(X1, rs)
        for kc in range(8):
            scale_cast(P, kc, Y2b[:, kc, :], X1[:, kc, :], gpre[:, kc:kc + 1], [X1, gpre], [Y2b])

        def epi_gate(m, n, ps):
            t = dn.tmp.get()
            tt(P, "dve", t[:, :], ps[:, :], rs[:, n * NT:(n + 1) * NT], ALU.mult, [ps, rs], [t])
            act(P, SGb[:, m, n * NT:(n + 1) * NT], t[:, :], AF.Silu, [t], [SGb])
        dn.linear(dr["w_in_g"], 8, 8, Y2b, epi_gate)
        for kc in range(8):
            for n in range(TP // NT):
                def ysrc_fn(s, n=n, kc=kc):
                    g0 = s * TOKC + t0 + n * NT
                    ya = y_all[g0 // 2048]
                    return ya, ya.t[kc * 128:(kc + 1) * 128, g0 % 2048:g0 % 2048 + NT]
                y = select4(P, dn, ysrc_fn, sel, BF16)
                s2 = gelu_tanh(P, dn, None, y[:, :], [y])
                tt(P, "pool", Gb[:, kc, n * NT:(n + 1) * NT], s2[:, :], y[:, :], ALU.mult, [s2, y], [Gb])

        def epi_glu(m, n, ps):
            sg = dn.tmp.get()
            act(P, sg[:, :], ps[:, :], AF.Sigmoid, [ps, bglu], [sg], bias=bglu[:, m:m + 1])
            t = dn.tmp.get()
            tt(P, "dve", t[:, :], sg[:, :], Gb[:, m, n * NT:(n + 1) * NT], ALU.mult, [sg, Gb], [t])
            tt(P, "pool", Y2b[:, m, n * NT:(n + 1) * NT], t[:, :], SGb[:, m, n * NT:(n + 1) * NT], ALU.mult, [t, SGb], [Y2b])
        dn.linear(dr["w_glu"], 8, 8, Gb, epi_glu)

        def epi_out(m, n, ps):
            act(P, X1[:, m, n * NT:(n + 1) * NT], ps[:, :], AF.Copy, [ps], [X1])
        dn.linear(dr["w_out0"], 8, 8, Y2b, epi_out)
        dn.rstd(X1, rs)
        for kc in range(8):
            for n in range(TP // NT):
                c0 = t0 + n * NT
                ix = dn.ist.get()
                P.dma("sp", ix[:, :], xT.t[kc * 128:(kc + 1) * 128, c0:c0 + NT], reads=[xT], writes=[ix])
                t = dn.tmp.get()
                P.op("dve", lambda E, t=t, kc=kc, n=n: E.scalar_tensor_tensor(
                    out=t[:, :], in0=X1[:, kc, n * NT:(n + 1) * NT], scalar=gpost[:, kc:kc + 1], in1=rs[:, n * NT:(n + 1) * NT],
                    op0=ALU.mult, op1=ALU.mult), reads=[X1, gpost, rs], writes=[t])
                tt(P, "pool", X1[:, kc, n * NT:(n + 1) * NT], t[:, :], ix[:, :], ALU.add, [t, ix], [X1])
                cp(P, "act", Gb[:, kc, n * NT:(n + 1) * NT], X1[:, kc, n * NT:(n + 1) * NT], [X1], [Gb])
        for kc in range(2):
            for n in range(TP // NT):
                c0 = t0 + n * NT
                ip = dn.ist.get()
                P.dma("sp", ip[:, :], pT.t[kc * 128:(kc + 1) * 128, c0:c0 + NT], reads=[pT], writes=[ip])
                cp(P, "dve", Pb[:, kc, n * NT:(n + 1) * NT], ip[:, :], [ip], [Pb])
        ple(P, dn, dr["w_pg0"], dr["w_pp0"], Gb, Pb, X1)
        dn.rstd(X1, rs)
        for kc in range(8):
            cp(P, "dve", Y2b[:, kc, :], X1[:, kc, :], [X1], [Y2b])
            scale_cast(P, kc + 1, SGb[:, kc, :], X1[:, kc, :], gbpre[:, kc:kc + 1], [X1, gbpre], [SGb])
            P.dma("pool", dr["x1_own"].t[kc * 128:(kc + 1) * 128, t0:t0 + TP], X1[:, kc, :], reads=[X1], writes=[dr["x1_own"]])
            for n in range(TP // NT):
                xsrc = dr["x1_src"][(t0 + n * NT) // NT]
                P.dma("pool", xsrc.t[kc * 128:(kc + 1) * 128, :], Y2b[:, kc, n * NT:(n + 1) * NT], reads=[Y2b], writes=[xsrc])

        def epi_g1(m, n, ps):
            o = dn.ost.get()
            tt(P, "dve", o[:, :], ps[:, :], rs[:, n * NT:(n + 1) * NT], ALU.mult, [ps, rs], [o])
            dn.store(dr["g1T"], m, n, t0, o, o[:, :])
        for k in range(t0 // NT, (t0 + TP) // NT):
            P.coll("AllGather", dr["x1_src"][k], dr["x1_all"][k], GROUPS)
        dn.linear(dr["w_bin_g"], 8, 8, SGb, epi_g1)
    P.wait_all("pool", dr["x1_all"] + [dr["x1_own"], dr["g1T"]])
    P.emit()


def phase_QKV(nc, dr):
    P = Prog(nc)
    P.mk_banks(6)
    dn = Dense(P, TP)
    gkv = colvec(P, "g_kv", dr["g_kv"], 8)
    gbpre = colvec(P, "g_bpre", dr["g_bpre"], 8)
    xsr = Rot(P, "xs", 2, [128, 8, TP], BF16)
    xq = P.sb("xq", [128, 8, TP], BF16)
    xk = P.sb("xk", [128, 8, TP], BF16)
    rs = P.sb("rs", [128, TP], F32)
    xa = dr["x1_all"]
    wq = dn.prep_w("wq", dr["w_q"], 8, 2)
    wk = dn.prep_w("wk", dr["w_k"], 8, 2)
    wv = dn.prep_w("wv", dr["w_v"], 8, 2)
    for pa in range(SEQ // TP):
        t0 = pa * TP
        s, tl = t0 // TOKC, t0 % TOKC
        proj_pass(P, dn, lambda kc, s=s, tl=tl: [(xa[(tl + h_ * NT) // NT], xa[(tl + h_ * NT) // NT].t[s * D + kc * 128:s * D + (kc + 1) * 128, :]) for h_ in range(TP // NT)], xsr.get(), rs,
                  [(xq, gbpre, wq, 2, dr["qT_cs"], 1.0), (xk, gkv, wk, 2, dr["kT_cs"], 0.125), (xk, gkv, wv, 2, dr["vT_cs"], 1.0)], t0)
    P.wait_all("pool", [dr["qT_cs"], dr["kT_cs"], dr["vT_cs"]])
    P.emit()


def phase_E(nc, dr):
    P = Prog(nc)
    P.mk_banks(7)
    dn = Dense(P, TP)
    dn.istb = Rot(P, "istb", 8, [128, NT], BF16)
    sel = colvec(P, "sel", dr["sel"], 4)
    mk_select(P, dn, sel)
    gpost = colvec(P, "g_bpost", dr["g_bpost"], 8)
    Ob = P.sb("Ob", [128, 8, TP], BF16)
    Xb = P.sb("Xb", [128, 8, TP], BF16)
    X1 = P.sb("X1", [128, 8, TP], F32)
    Pb = P.sb("Pb", [128, 2, TP], BF16)
    rs = P.sb("rs", [128, TP], F32)
    x1T, gT, pT, outT = dr["x1_own"], dr["g1T"], dr["p1T"], dr["outT"]
    for pa in range(TOKC // TP):
        t0 = pa * TP
        for kc in range(8):
            for n in range(TP // NT):
                c0 = t0 + n * NT
                def osrc_fn(s, n=n, kc=kc):
                    g0 = s * TOKC + t0 + n * NT
                    oa = dr["o_all"][g0 // 2048]
                    return oa, oa.t[kc * 128:(kc + 1) * 128, g0 % 2048:g0 % 2048 + NT]
                o = select4(P, dn, osrc_fn, sel, BF16)
                ig = dn.ist.get()
                P.dma("sp", ig[:, :], gT.t[kc * 128:(kc + 1) * 128, c0:c0 + NT], reads=[gT], writes=[ig])
                sg = dn.tmp.get()
                act(P, sg[:, :], ig[:, :], AF.Silu, [ig], [sg])
                tt(P, "pool", Ob[:, kc, n * NT:(n + 1) * NT], sg[:, :], o[:, :], ALU.mult, [sg, o], [Ob])

        def epi_out(m, n, ps):
            act(P, X1[:, m, n * NT:(n + 1) * NT], ps[:, :], AF.Copy, [ps], [X1])
        dn.linear(dr["w_out1"], 8, 8, Ob, epi_out)
        dn.rstd(X1, rs)
        for kc in range(8):
            for n in range(TP // NT):
                c0 = t0 + n * NT
                ix = dn.ist.get()
                P.dma("sp", ix[:, :], x1T.t[kc * 128:(kc + 1) * 128, c0:c0 + NT], reads=[x1T], writes=[ix])
                t = dn.tmp.get()
                P.op("dve", lambda E, t=t, kc=kc, n=n: E.scalar_tensor_tensor(
                    out=t[:, :], in0=X1[:, kc, n * NT:(n + 1) * NT], scalar=gpost[:, kc:kc + 1], in1=rs[:, n * NT:(n + 1) * NT],
                    op0=ALU.mult, op1=ALU.mult), reads=[X1, gpost, rs], writes=[t])
                tt(P, "pool", X1[:, kc, n * NT:(n + 1) * NT], t[:, :], ix[:, :], ALU.add, [t, ix], [X1])
                cp(P, "act", Xb[:, kc, n * NT:(n + 1) * NT], X1[:, kc, n * NT:(n + 1) * NT], [X1], [Xb])
        for kc in range(2):
            for n in range(TP // NT):
                c0 = t0 + n * NT
                ip = dn.ist.get()
                P.dma("sp", ip[:, :], pT.t[kc * 128:(kc + 1) * 128, c0:c0 + NT], reads=[pT], writes=[ip])
                cp(P, "dve", Pb[:, kc, n * NT:(n + 1) * NT], ip[:, :], [ip], [Pb])
        ple(P, dn, dr["w_pg1"], dr["w_pp1"], Xb, Pb, X1)
        for kc in range(8):
            P.dma("pool", outT.t[kc * 128:(kc + 1) * 128, t0:t0 + TP], X1[:, kc, :], reads=[X1], writes=[outT])
    P.wait_all("pool", [outT])
    P.emit()


IN_SPECS = {
    "xT_full": ([D, SEQ], F32), "xT_own": ([D, TOKC], F32), "p0T": ([256, TOKC], F32), "p1T": ([256, TOKC], F32),
    "sel": ([128, 4], F32), "g_pre": ([128, 8], F32), "w_in_u": ([D, 256], F32), "w_in_g": ([D, D], F32),
    "lamre_T": ([128, 128], F32), "lamim_T": ([128, 128], F32), "logdt_T": ([128, 128], F32), "bre_T": ([128, 128], F32),
    "bim_T": ([128, 128], F32), "lamre_S": ([128, NG], F32), "lamim_S": ([128, NG], F32), "logdt_S": ([128, NG], F32),
    "c1_S": ([128, NG * 16], F32), "c2_S": ([128, NG * 16], F32), "iota": ([128, TS + 1], F32), "gmask": ([128, 8], F32),
    "sgn": ([128, 1], F32), "Jm": ([128, 128], F32), "dskip_cs": ([128, 2], F32), "vecsC": ([128, 32], F32),
    "w_glu": ([D, D], F32), "w_out0": ([D, D], F32), "w_pg0": ([D, D], F32), "w_pp0": ([256, D], F32),
    "w_bin_g": ([D, D], F32), "g_kv": ([128, 8], F32), "g_bpre": ([128, 8], F32), "w_q": ([D, 256], F32),
    "w_k": ([D, 256], F32), "w_v": ([D, 256], F32), "ntri": ([128, 128], F32), "g_bpost": ([128, 8], F32),
    "w_out1": ([D, D], F32), "w_pg1": ([D, D], F32), "w_pp1": ([256, D], F32),
}
SCRATCH = {
    "uT_cs": ([256, SEQ], F32), "y_src": ([256, 2048], BF16, 4), "y_all": ([D, 2048], BF16, 4), "x1_own": ([D, TOKC], F32),
    "x1_src": ([D, NT], BF16, 4), "x1_all": ([4 * D, NT], BF16, 4), "g1T": ([D, TOKC], F32), "qT_cs": ([256, SEQ], BF16),
    "kT_cs": ([256, SEQ], BF16), "vT_cs": ([256, SEQ], BF16), "o_src": ([256, 2048], BF16, 4), "o_all": ([D, 2048], BF16, 4),
}


def build_fused():
    nc = bass.Bass("TRN2", target_bir_lowering=False)
    dr = {}
    for n, (shp, dt) in IN_SPECS.items():
        dr[n] = Res(n, nc.dram_tensor(n, list(shp), dt, kind="ExternalInput").ap())
    for n, spec in SCRATCH.items():
        shp, dt = spec[0], spec[1]
        if len(spec) == 3:
            dr[n] = [Res("%s%d" % (n, i), nc.dram_tensor("%s%d" % (n, i), list(shp), dt, kind="Internal").ap()) for i in range(spec[2])]
        else:
            dr[n] = Res(n, nc.dram_tensor(n, list(shp), dt, kind="Internal").ap())
    dr["outT"] = Res("outT", nc.dram_tensor("outT", [D, TOKC], F32, kind="ExternalOutput").ap())
    phase_A(nc, dr)
    phase_B(nc, dr)
    phase_C(nc, dr)
    phase_QKV(nc, dr)
    phase_D(nc, dr)
    phase_E(nc, dr)
    Prog.finish()
    return nc


def _f(a):
    return np.ascontiguousarray(np.asarray(a, dtype=np.float32))


def kernel(**inputs):
    inp = {k: np.asarray(v) for k, v in inputs.items()}
    x, p = inp["x"], inp["p"]
    lam_re, lam_im, log_dt = _f(inp["a_lam_re"][0]), _f(inp["a_lam_im"][0]), _f(inp["a_log_dt"][0])
    b_re, b_im, c_re, c_im = _f(inp["a_b_re"][0]), _f(inp["a_b_im"][0]), _f(inp["a_c_re"][0]), _f(inp["a_c_im"][0])
    iota = _f(np.broadcast_to(np.arange(TS + 1, dtype=np.float32), (128, TS + 1)))
    gmask = np.zeros((128, 8), np.float32)
    for gl in range(8):
        gmask[gl * 16:(gl + 1) * 16, gl] = 1.0
    sgn = np.ones((128, 1), np.float32)
    sgn[64:] = -1.0
    J = np.zeros((128, 128), np.float32)
    for q in range(64):
        J[64 + q, q] = -1.0
        J[q, 64 + q] = 1.0
    ntri = np.zeros((128, 128), np.float32)
    for j in range(128):
        ntri[j, :j + 1] = -1.0
    vecsC = np.zeros((128, 32), np.float32)
    for i, v in enumerate([inp["a_b_glu"][0], inp["a_norm_post"][0], inp["b_norm_pre"][0]]):
        vecsC[:, i * 8:(i + 1) * 8] = _cols(v)
    a_w_in, b_w_in, w_kv = _f(inp["a_w_in"][0]), _f(inp["b_w_in"][0]), _f(inp["w_kv"])
    common = {
        "g_pre": _cols(inp["a_norm_pre"][0]), "w_in_g": _f(a_w_in[:, D:]), "iota": iota, "gmask": gmask, "sgn": sgn, "Jm": J,
        "vecsC": vecsC, "w_glu": _f(inp["a_w_glu"][0]), "w_out0": _f(inp["a_w_out"][0]), "w_pg0": _f(inp["ple_w_gate"][0]),
        "w_pp0": _f(inp["ple_w_proj"][0]), "w_bin_g": _f(b_w_in[:, D:]), "g_kv": _cols(inp["kv_norm"]),
        "g_bpre": _cols(inp["b_norm_pre"][0]), "ntri": ntri, "g_bpost": _cols(inp["b_norm_post"][0]),
        "w_out1": _f(inp["b_w_out"][0]), "w_pg1": _f(inp["ple_w_gate"][1]), "w_pp1": _f(inp["ple_w_proj"][1]),
    }
    xT_full = [_f(np.asarray(x[b], np.float32).T) for b in range(2)]
    maps = []
    for c in range(NCORE):
        b, r = c // 4, c % 4
        gs = np.arange(16 * r, 16 * r + 16)
        tsl = slice(r * TOKC, (r + 1) * TOKC)
        csl = slice(256 * r, 256 * r + 256)
        m = dict(common)
        m["xT_full"] = xT_full[b]
        m["xT_own"] = _f(xT_full[b][:, tsl])
        m["p0T"] = _f(np.asarray(p[0, b, tsl, :], np.float32).T)
        m["p1T"] = _f(np.asarray(p[1, b, tsl, :], np.float32).T)
        sel = np.zeros((128, 4), np.float32)
        sel[:, r] = 1.0
        m["sel"] = sel
        m["w_in_u"] = _f(a_w_in[:, csl])
        m["dskip_cs"] = _cols(inp["a_d_skip"][0][csl])
        m["w_q"] = _f(b_w_in[:, csl])
        m["w_k"] = _f(w_kv[:, csl])
        m["w_v"] = _f(w_kv[:, D + 256 * r:D + 256 * r + 256])

        def lt_gp(a):
            t = a[gs].reshape(2, 8, 64)
            t = np.broadcast_to(t[:, :, None, :], (2, 8, 16, 64))
            return _f(t.transpose(1, 2, 0, 3).reshape(128, 128))

        def lt_b(a):
            t = a[gs].reshape(2, 8, 64, 16)
            return _f(t.transpose(1, 3, 0, 2).reshape(128, 128))

        def sp_gp(a):
            t = a[gs].T
            return _f(np.concatenate([t, t], axis=0))
        ldt = np.broadcast_to(log_dt[:, None], (64, 64))
        m["lamre_T"], m["lamim_T"], m["logdt_T"] = lt_gp(lam_re), lt_gp(lam_im), lt_gp(ldt)
        m["bre_T"], m["bim_T"] = lt_b(b_re), lt_b(b_im)
        m["lamre_S"], m["lamim_S"], m["logdt_S"] = sp_gp(lam_re), sp_gp(lam_im), sp_gp(ldt)
        cr = c_re[gs].transpose(2, 0, 1).reshape(64, 256)
        ci = c_im[gs].transpose(2, 0, 1).reshape(64, 256)
        m["c1_S"] = _f(np.concatenate([cr, ci], axis=0))
        m["c2_S"] = _f(np.concatenate([ci, cr], axis=0))
        maps.append(m)
    res = _run(build_fused(), maps)
    out = np.empty((2, SEQ, D), np.float32)
    for c in range(NCORE):
        b, r = c // 4, c % 4
        out[b, r * TOKC:(r + 1) * TOKC, :] = res[c]["outT"].T
    return out
```

```python
from contextlib import ExitStack
import numpy as np
import concourse.bass as bass
import concourse.mybir as mybir
from concourse.bass_utils import run_bass_kernel_spmd

F32 = mybir.dt.float32
BF16 = mybir.dt.bfloat16
AF = mybir.ActivationFunctionType
ALU = mybir.AluOpType

ENGS = ("pe", "act", "dve", "pool", "sp")
NCORE = 8
D = 1024
SEQ = 8192
NT = 512
EPS = 1e-6
PI = float(np.pi)


class Res:
    __slots__ = ("name", "w", "r", "dsem", "dcnt", "t")

    def __init__(self, name, t=None):
        self.name = name
        self.w = {}
        self.r = {}
        self.dsem = None
        self.dcnt = 0
        self.t = t

    def __getitem__(self, idx):
        return self.t[idx]


class Prog:
    _n = 0
    G = None

    def __init__(self, nc):
        Prog._n += 1
        self.pfx = "f%d_" % Prog._n
        self.nc = nc
        if Prog.G is None or Prog.G["nc"] is not nc:
            ges = ExitStack()
            Prog.G = {"nc": nc, "es": ges, "sems": {}, "cnt": {}}
            for e in ENGS:
                Prog.G["sems"][e] = ges.enter_context(nc.semaphore("s_" + e))
                Prog.G["cnt"][e] = 0
        G = Prog.G
        self.es = ExitStack()
        self.lists = {e: [] for e in ENGS}
        self.sems = G["sems"]
        self.cnt = G["cnt"]
        self.seen = {e: dict(self.cnt) for e in ENGS}
        self.nd = 0
        self.banks = []
        self.bi = 0
        self.touched = {}

    @staticmethod
    def finish():
        if Prog.G is not None:
            Prog.G["es"].close()
            Prog.G = None

    def _newsem(self):
        key = "d%d" % self.nd
        self.nd += 1
        if key not in self.sems:
            self.sems[key] = Prog.G["es"].enter_context(self.nc.semaphore("sd_" + key))
            self.cnt[key] = 0
        return key

    def sb(self, name, shape, dt):
        return Res(name, self.es.enter_context(self.nc.sbuf_tensor(self.pfx + "sb_" + name, list(shape), dt)))

    def ps(self, name, shape, dt=F32):
        return Res(name, self.es.enter_context(self.nc.psum_tensor(self.pfx + "ps_" + name, list(shape), dt)))

    def dram(self, name, shape, dt, kind="Internal"):
        return Res(name, self.nc.dram_tensor(name, list(shape), dt, kind=kind).ap())

    def mk_banks(self, n):
        self.banks = [self.ps("bank%d" % i, [128, NT], F32) for i in range(n)]

    def bank(self):
        b = self.banks[self.bi % len(self.banks)]
        self.bi += 1
        return b

    def _dsem(self, res):
        if res.dsem is None:
            res.dsem = self._newsem()
        return res.dsem

    def _waits(self, eng, reads, writes, skip_same=False):
        for x_ in reads:
            self.touched[id(x_)] = x_
        for x_ in writes:
            self.touched[id(x_)] = x_
        deps = {}
        for r in reads:
            for k, v in r.w.items():
                if skip_same and k == eng:
                    continue
                if v > deps.get(k, 0):
                    deps[k] = v
        for w in writes:
            for k, v in w.w.items():
                if k != eng and v > deps.get(k, 0):
                    deps[k] = v
            for k, v in w.r.items():
                if k != eng and v > deps.get(k, 0):
                    deps[k] = v
        seen = self.seen[eng]
        for k, v in deps.items():
            if v > seen.get(k, 0):
                seen[k] = v
                sem = self.sems[k]
                self.lists[eng].append(lambda E, sem=sem, v=v: E.wait_ge(sem, v))

    def op(self, eng, fn, reads=(), writes=(), skip_same=False):
        self._waits(eng, reads, writes, skip_same)
        self.cnt[eng] += 1
        n = self.cnt[eng]
        sem = self.sems[eng]
        self.lists[eng].append(lambda E, fn=fn, sem=sem: fn(E).then_inc(sem, 1))
        for r in reads:
            r.r[eng] = n
        for w in writes:
            w.w[eng] = n

    def dma(self, eng, out_ap, in_ap, reads=(), writes=()):
        wres = writes[0]
        self._waits(eng, reads, writes)
        key = self._dsem(wres)
        self.cnt[key] += 16
        v = self.cnt[key]
        sem = self.sems[key]
        self.lists[eng].append(
            lambda E, o=out_ap, i=in_ap, sem=sem: E.dma_start(out=o, in_=i).then_inc(sem, 16))
        for r in reads:
            r.r[key] = v
        wres.w[key] = v

    def coll(self, kind, src, dst, groups):
        self._waits("pool", [src], [dst])
        key = self._newsem()
        self.cnt[key] += 1
        v = self.cnt[key]
        sem = self.sems[key]
        self.lists["pool"].append(lambda E, sem=sem: E.collective_compute(
            kind, ALU.bypass, replica_groups=groups, ins=[src.t.opt()], outs=[dst.t.opt()]).then_inc(sem))
        src.r[key] = v
        dst.w[key] = v

    def wait_all(self, eng, ress):
        self._waits(eng, ress, ())

    def emit(self):
        L = self.lists
        for e in ENGS:
            for k, sem in self.sems.items():
                tgt = self.cnt[k]
                if k != e and tgt > self.seen[e].get(k, 0):
                    self.seen[e][k] = tgt
                    L[e].append(lambda E, sem=sem, tgt=tgt: E.wait_ge(sem, tgt))
        with self.nc.Block() as block:
            @block.tensor
            def _(E):
                for f in L["pe"]:
                    f(E)

            @block.scalar
            def _(E):
                for f in L["act"]:
                    f(E)

            @block.vector
            def _(E):
                for f in L["dve"]:
                    f(E)

            @block.gpsimd
            def _(E):
                for f in L["pool"]:
                    f(E)

            @block.sync
            def _(E):
                for f in L["sp"]:
                    f(E)
        self.es.close()
        for x_ in self.touched.values():
            x_.w = {}
            x_.r = {}
            x_.dsem = None


def tt(P, eng, out, in0, in1, op, reads, writes):
    P.op(eng, lambda E: E.tensor_tensor(out=out, in0=in0, in1=in1, op=op), reads=reads, writes=writes)


def ts(P, eng, out, in0, s1, s2, op0, op1, reads, writes):
    if s2 is None:
        P.op(eng, lambda E: E.tensor_scalar(out=out, in0=in0, scalar1=s1, scalar2=None, op0=op0), reads=reads, writes=writes)
    else:
        P.op(eng, lambda E: E.tensor_scalar(out=out, in0=in0, scalar1=s1, scalar2=s2, op0=op0, op1=op1), reads=reads, writes=writes)


def act(P, out, in_, func, reads, writes, scale=1.0, bias=None):
    if bias is None:
        P.op("act", lambda E: E.activation(out=out, in_=in_, func=func, scale=scale), reads=reads, writes=writes)
    else:
        P.op("act", lambda E: E.activation(out=out, in_=in_, func=func, scale=scale, bias=bias), reads=reads, writes=writes)


def scale_cast(P, i, out, in_, col, reads, writes):
    if i % 2:
        P.op("act", lambda E: E.activation(out=out, in_=in_, func=AF.Copy, scale=col), reads=reads, writes=writes)
    else:
        ts(P, "dve", out, in_, col, None, ALU.mult, None, reads, writes)


def cp(P, eng, out, in_, reads, writes):
    if eng == "act":
        P.op(eng, lambda E: E.activation(out=out, in_=in_, func=AF.Copy), reads=reads, writes=writes)
    else:
        P.op(eng, lambda E: E.tensor_copy(out=out, in_=in_), reads=reads, writes=writes)


class Rot:
    def __init__(self, P, name, n, shape, dt):
        self.bufs = [P.sb("%s%d" % (name, i), shape, dt) for i in range(n)]
        self.i = 0

    def get(self):
        b = self.bufs[self.i % len(self.bufs)]
        self.i += 1
        return b


class Dense:
    def __init__(self, P, T):
        self.P = P
        self.T = T
        self.wst = Rot(P, "wst", 4, [128, 8, 128], F32)
        self.wbf = Rot(P, "wbf", 4, [128, 8, 128], BF16)
        self.ost = Rot(P, "ost", 4, [128, NT], F32)
        self.ostb = Rot(P, "ostb", 4, [128, NT], BF16)
        self.ist = Rot(P, "ist", 4, [128, NT], F32)
        self.tmp = Rot(P, "tmp", 6, [128, NT], F32)
        self.ones = P.sb("ones", [128, 128], BF16)
        P.op("pool", lambda E: E.memset(self.ones[:], 1.0), writes=[self.ones])
        self.sq = P.sb("sq", [128, 8, T], BF16)

    def load_w(self, W, m, KC, rowscale=None, wb=None):
        P = self.P
        assert rowscale is None
        st = self.wst.get()
        if wb is None:
            wb = self.wbf.get()
        P.dma("sp", st[:, 0:KC, :], W.t[:, m * 128:(m + 1) * 128].rearrange("(kc p) m -> p kc m", p=128), reads=[W], writes=[st])
        self.wi = getattr(self, "wi", 0) + 1
        cp(P, "act" if self.wi % 2 else "pool", wb[:, 0:KC, :], st[:, 0:KC, :], [st], [wb])
        return wb

    def prep_w(self, name, W, KC, MC):
        wbs = []
        for m in range(MC):
            wb = self.P.sb("%s_w%d" % (name, m), [128, KC, 128], BF16)
            wbs.append(self.load_w(W, m, KC, wb=wb))
        return wbs

    def linear(self, W, KC, MC, a, epi, rowscale=None, wbs=None):
        P = self.P
        for m in range(MC):
            wb = wbs[m] if wbs is not None else self.load_w(W, m, KC, rowscale)
            for n in range(self.T // NT):
                ps = P.bank()
                for kc in range(KC):
                    P.op("pe", lambda E, ps=ps, wb=wb, kc=kc, n=n: E.matmul(
                        ps[:, :], lhsT=wb[:, kc, :], rhs=a[:, kc, n * NT:(n + 1) * NT], start=(kc == 0), stop=(kc == KC - 1)),
                        reads=[wb, a], writes=[ps])
                epi(m, n, ps)

    def rstd(self, src, out):
        P = self.P
        sq = self.sq
        for kc in range(8):
            act(P, sq[:, kc, :], src[:, kc, :], AF.Square, [src], [sq])
        for n in range(self.T // NT):
            ps = P.bank()
            for kc in range(8):
                P.op("pe", lambda E, ps=ps, kc=kc, n=n: E.matmul(
                    ps[:, :], lhsT=self.ones[:, :], rhs=sq[:, kc, n * NT:(n + 1) * NT], start=(kc == 0), stop=(kc == 7)),
                    reads=[self.ones, sq], writes=[ps])
            t = self.tmp.get()
            act(P, t[:, :], ps[:, :], AF.Ln, [ps], [t], scale=1.0 / D, bias=EPS)
            act(P, out[:, n * NT:(n + 1) * NT], t[:, :], AF.Exp, [t], [out], scale=-0.5)

    def store(self, dst, m, n, t0, src_res, src_ap):
        self.P.dma("pool", dst.t[m * 128:(m + 1) * 128, t0 + n * NT:t0 + (n + 1) * NT], src_ap, reads=[src_res], writes=[dst])

    def load_act(self, src, t0, dst, KC=8):
        for kc in range(KC):
            self.P.dma("sp", dst[:, kc, :], src.t[kc * 128:(kc + 1) * 128, t0:t0 + self.T], reads=[src], writes=[dst])


def colvec(P, name, dram_res, ncol):
    t = P.sb(name, [128, ncol], F32)
    P.dma("sp", t[:, :], dram_res.t[:, :], reads=[dram_res], writes=[t])
    return t


def _run(nc, in_maps):
    res = run_bass_kernel_spmd(nc, in_maps, core_ids=list(range(NCORE)))
    return res.results


def _cols(v):
    return np.ascontiguousarray(np.asarray(v, np.float32).reshape(-1, 128).T)


TS = 256
NG = 16
M_MAGIC = 12582912.0


def range_reduce(P, eng, out, in_, tmp, reads, writes, shift=0.0):
    res_in = reads
    if shift != 0.0:
        ts(P, eng, out, in_, shift, None, ALU.add, None, res_in, writes)
        in_ = out
        res_in = writes
    ts(P, eng, tmp[0], in_, 1.0 / (2 * PI), M_MAGIC, ALU.mult, ALU.add, res_in, [tmp[1]])
    ts(P, eng, tmp[0], tmp[0], -M_MAGIC, -2 * PI, ALU.add, ALU.mult, [tmp[1]], [tmp[1]])
    tt(P, eng, out, tmp[0], in_, ALU.add, [tmp[1]] + list(res_in), writes)
    ts(P, eng, out, out, 3.14159, -3.14159, ALU.min, ALU.max, writes, writes)


NAMES_T = ["lamre_T", "lamim_T", "logdt_T", "bre_T", "bim_T"]
NAMES_S = ["lamre_S", "lamim_S", "logdt_S"]


def phase_B(nc, dr):
    P = Prog(nc)
    uT = dr["uT_cs"]
    names_T = NAMES_T
    dT = {n: dr[n] for n in names_T}
    names_S = NAMES_S
    dS = {n: dr[n] for n in names_S}
    c1_d, c2_d, iota_d, gmask_d, sgn_d, J_d = dr["c1_S"], dr["c2_S"], dr["iota"], dr["gmask"], dr["sgn"], dr["Jm"]
    dsk = colvec(P, "dskip_cs", dr["dskip_cs"], 2)
    P.mk_banks(4)
    ybank = [P.ps("ybank%d" % i, [128, NT], F32) for i in range(2)]
    igb = P.ps("igb", [128, NT], F32)

    def ld(name, d, ncol):
        return colvec(P, name, d, ncol)

    lt = {n: ld("s_" + n, dT[n], 128) for n in names_T}
    cnt = [0]

    def newT(nm, ncol=128):
        cnt[0] += 1
        return P.sb("%s_%d" % (nm, cnt[0]), [128, ncol], F32)

    def derive(lamre, lamim, logdt, ncol, pfx):
        o = {}
        lr = newT(pfx + "lr", ncol)
        ts(P, "dve", lr[:, :], lamre[:, :], -1e-4, None, ALU.min, None, [lamre], [lr])
        dt = newT(pfx + "dt", ncol)
        act(P, dt[:, :], logdt[:, :], AF.Exp, [logdt], [dt])
        e = newT(pfx + "e", ncol)
        tt(P, "dve", e[:, :], lr[:, :], dt[:, :], ALU.mult, [lr, dt], [e])
        th = newT(pfx + "th", ncol)
        tt(P, "dve", th[:, :], lamim[:, :], dt[:, :], ALU.mult, [lamim, dt], [th])
        thr = newT(pfx + "thr", ncol)
        tmp = newT(pfx + "tmp", ncol)
        range_reduce(P, "dve", thr[:, :], th[:, :], (tmp[:, :], tmp), [th], [thr])
        thc = newT(pfx + "thc", ncol)
        range_reduce(P, "dve", thc[:, :], th[:, :], (tmp[:, :], tmp), [th], [thc], shift=PI / 2)
        mag = newT(pfx + "mag", ncol)
        act(P, mag[:, :], e[:, :], AF.Exp, [e], [mag])
        sn = newT(pfx + "sin", ncol)
        act(P, sn[:, :], thr[:, :], AF.Sin, [thr], [sn])
        cs = newT(pfx + "cos", ncol)
        act(P, cs[:, :], thc[:, :], AF.Sin, [thc], [cs])
        o.update(lr=lr, li=lamim, e=e, thr=thr, mag=mag, sin=sn, cos=cs)
        return o

    dl = derive(lt["lamre_T"], lt["lamim_T"], lt["logdt_T"], 128, "T")

    def mul(a, b, nm):
        t = newT(nm)
        tt(P, "dve", t[:, :], a[:, :], b[:, :], ALU.mult, [a, b], [t])
        return t

    def addsub(a, b, op, nm):
        t = newT(nm)
        tt(P, "dve", t[:, :], a[:, :], b[:, :], op, [a, b], [t])
        return t

    are = mul(dl["mag"], dl["cos"], "are")
    aim = mul(dl["mag"], dl["sin"], "aim")
    den = addsub(mul(dl["lr"], dl["lr"], "lr2"), mul(dl["li"], dl["li"], "li2"), ALU.add, "den")
    rden = newT("rden")
    P.op("dve", lambda E: E.reciprocal(out=rden[:, :], in_=den[:, :]), reads=[den], writes=[rden])
    nr = newT("nr")
    ts(P, "dve", nr[:, :], are[:, :], -1.0, None, ALU.add, None, [are], [nr])
    fre = mul(addsub(mul(nr, dl["lr"], "f1"), mul(aim, dl["li"], "f2"), ALU.add, "f3"), rden, "fre")
    fim = mul(addsub(mul(aim, dl["lr"], "f4"), mul(nr, dl["li"], "f5"), ALU.subtract, "f6"), rden, "fim")
    bbre = addsub(mul(fre, lt["bre_T"], "b1"), mul(fim, lt["bim_T"], "b2"), ALU.subtract, "bbre")
    bbim = addsub(mul(fre, lt["bim_T"], "b3"), mul(fim, lt["bre_T"], "b4"), ALU.add, "bbim")
    nbbim = newT("nbbim")
    ts(P, "dve", nbbim[:, :], bbim[:, :], -1.0, None, ALU.mult, None, [bbim], [nbbim])
    gmask = ld("gmask", gmask_d, 8)
    W1 = P.sb("W1pad", [128, NG, 128], BF16)
    W2 = P.sb("W2pad", [128, NG, 128], BF16)
    for jj in range(2):
        for gl in range(8):
            gg = jj * 8 + gl
            ms = gmask[:, gl:gl + 1]
            sl = slice(jj * 64, (jj + 1) * 64)
            ts(P, "dve", W1[:, gg, 0:64], bbre[:, sl], ms, None, ALU.mult, None, [bbre, gmask], [W1])
            ts(P, "dve", W1[:, gg, 64:128], bbim[:, sl], ms, None, ALU.mult, None, [bbim, gmask], [W1])
            ts(P, "dve", W2[:, gg, 0:64], nbbim[:, sl], ms, None, ALU.mult, None, [nbbim, gmask], [W2])
            ts(P, "dve", W2[:, gg, 64:128], bbre[:, sl], ms, None, ALU.mult, None, [bbre, gmask], [W2])

    ls_ = {n: ld("s_" + n, dS[n], NG) for n in names_S}
    ds = derive(ls_["lamre_S"], ls_["lamim_S"], ls_["logdt_S"], NG, "S")
    rho = ds["mag"]
    iota = ld("iota", iota_d, TS + 1)
    COS = P.sb("COS", [128, NG, TS + 1], F32)
    SIN = P.sb("SIN", [128, NG, TS + 1], F32)
    ARG = P.sb("ARG", [128, NG, TS + 1], F32)
    TMP = P.sb("TMPA", [128, NG, TS + 1], F32)
    Gg = [P.sb("Gg%d" % i, [128, TS], F32) for i in range(NG)]
    GT = [P.sb("GT%d" % i, [128, NG], F32) for i in range(2)]
    for gg in range(NG):
        ts(P, "dve", ARG[:, gg, :], iota[:, :], ds["thr"][:, gg:gg + 1], None, ALU.mult, None, [iota, ds["thr"]], [ARG])
    range_reduce(P, "dve", SIN[:, :, :], ARG[:, :, :], (TMP[:, :, :], TMP), [ARG], [SIN])
    range_reduce(P, "dve", COS[:, :, :], ARG[:, :, :], (TMP[:, :, :], TMP), [ARG], [COS], shift=PI / 2)
    act(P, SIN[:, :, :], SIN[:, :, :], AF.Sin, [SIN], [SIN])
    act(P, COS[:, :, :], COS[:, :, :], AF.Sin, [COS], [COS])
    c1 = ld("c1", c1_d, NG * 16)
    c2 = ld("c2", c2_d, NG * 16)
    sgn = ld("sgn", sgn_d, 1)
    L1 = P.sb("L1pad", [128, NG, 128], BF16)
    L2 = P.sb("L2pad", [128, NG, 128], BF16)
    P.op("pool", lambda E: E.memset(L1[:, :, :], 0.0), writes=[L1])
    P.op("pool", lambda E: E.memset(L2[:, :, :], 0.0), writes=[L2])
    for gg in range(NG):
        gl = gg % 8
        ts(P, "dve", L1[:, gg, gl * 16:(gl + 1) * 16], c1[:, gg * 16:(gg + 1) * 16], sgn[:, 0:1], None, ALU.mult, None, [c1, sgn], [L1])
        ts(P, "dve", L2[:, gg, gl * 16:(gl + 1) * 16], c2[:, gg * 16:(gg + 1) * 16], -1.0, None, ALU.mult, None, [c2], [L2])
    Jm = P.sb("Jm", [128, 128], F32)
    P.dma("sp", Jm[:, :], J_d.t[:, :], reads=[J_d], writes=[Jm])

    ust = Rot(P, "ust", 3, [128, 2, TS], F32)
    ubf = Rot(P, "ubf", 3, [128, 2, TS], BF16)
    t1r = Rot(P, "t1r", 4, [128, TS], F32)
    t2r = Rot(P, "t2r", 4, [128, TS], F32)
    xtr = Rot(P, "xtr", 4, [128, TS], F32)
    h1r = Rot(P, "h1r", 4, [128, TS], BF16)
    h2r = Rot(P, "h2r", 4, [128, TS], BF16)
    S0 = [P.sb("S0_%d" % i, [128, NG], F32) for i in range(2)]
    ys = Rot(P, "ys", 3, [128, 2, TS], BF16)
    P.op("pool", lambda E: E.memset(S0[0][:, :], 0.0), writes=[S0[0]])
    nch = SEQ // TS
    units = []
    chunk_res = {}
    for ch in range(nch):
        for jj in range(2):
            for gl in range(8):
                units.append(dict(ch=ch, jj=jj, gl=gl, gg=jj * 8 + gl))
    nu = len(units)

    def chunk_setup(ch):
        c0 = ch * TS
        us = ust.get()
        ub = ubf.get()
        P.dma("sp", us[:, :, :], uT.t[:, c0:c0 + TS].rearrange("(j p) t -> p j t", p=128), reads=[uT], writes=[us])
        cp(P, "act", ub[:, :, :], us[:, :, :], [us], [ub])
        chunk_res[ch] = dict(us=us, ub=ub, yo=ys.get())

    def s1(i):
        U = units[i]
        ch, jj, gg = U["ch"], U["jj"], U["gg"]
        if jj == 0 and U["gl"] == 0:
            chunk_setup(ch)
        ub = chunk_res[ch]["ub"]
        p1 = P.bank()
        p2 = P.bank()
        P.op("pe", lambda E, p1=p1, gg=gg, ub=ub, jj=jj: E.matmul(p1[:, 0:TS], lhsT=W1[:, gg, :], rhs=ub[:, jj, :], start=True, stop=True), reads=[W1, ub], writes=[p1])
        P.op("pe", lambda E, p2=p2, gg=gg, ub=ub, jj=jj: E.matmul(p2[:, 0:TS], lhsT=W2[:, gg, :], rhs=ub[:, jj, :], start=True, stop=True), reads=[W2, ub], writes=[p2])
        t1 = t1r.get()
        t2 = t2r.get()
        xt = xtr.get()
        tt(P, "dve", t1[:, :], p1[:, 0:TS], COS[:, gg, 1:TS + 1], ALU.mult, [p1, COS], [t1])
        tt(P, "dve", t2[:, :], p2[:, 0:TS], SIN[:, gg, 1:TS + 1], ALU.mult, [p2, SIN], [t2])
        tt(P, "pool", xt[:, :], t1[:, :], t2[:, :], ALU.subtract, [t1, t2], [xt])
        U["xt"] = xt

    def s2(i):
        U = units[i]
        ch, gg = U["ch"], U["gg"]
        gt, s0 = GT[ch % 2], S0[ch % 2]
        xt = U["xt"]
        G = Gg[gg]
        P.op("dve", lambda E, G=G, gg=gg, xt=xt, s0=s0: E.tensor_tensor_scan(
            out=G[:, :], data0=rho[:, gg:gg + 1].to_broadcast([128, TS]), data1=xt[:, :],
            initial=s0[:, gg:gg + 1], op0=ALU.mult, op1=ALU.add), reads=[rho, xt, s0], writes=[G])
        if ch + 1 < nch:
            cp(P, "act", gt[:, gg:gg + 1], G[:, TS - 1:TS], [G], [gt])
        h1 = h1r.get()
        h2 = h2r.get()
        tt(P, "dve", h1[:, :], G[:, :], COS[:, gg, 1:TS + 1], ALU.mult, [G, COS], [h1])
        tt(P, "pool", h2[:, :], G[:, :], SIN[:, gg, 1:TS + 1], ALU.mult, [G, SIN], [h2])
        U["h1"], U["h2"] = h1, h2
        if gg == NG - 1 and ch + 1 < nch:
            s1_ = S0[(ch + 1) % 2]
            P.op("pe", lambda E, gt=gt: E.matmul(igb[:, 0:NG], lhsT=Jm[:, :], rhs=gt[:, :], start=True, stop=True), reads=[Jm, gt], writes=[igb])
            ta = P.sb("bta%d" % ch, [128, NG], F32)
            tb = P.sb("btb%d" % ch, [128, NG], F32)
            tt(P, "dve", ta[:, :], gt[:, :], COS[:, :, TS], ALU.mult, [gt, COS], [ta])
            tt(P, "dve", tb[:, :], igb[:, 0:NG], SIN[:, :, TS], ALU.mult, [igb, SIN], [tb])
            tt(P, "dve", s1_[:, :], ta[:, :], tb[:, :], ALU.add, [ta, tb], [s1_])

    def s3(i):
        U = units[i]
        ch, jj, gl, gg = U["ch"], U["jj"], U["gl"], U["gg"]
        yb = ybank[jj]
        h1, h2 = U["h1"], U["h2"]
        P.op("pe", lambda E, yb=yb, gg=gg, h1=h1, gl=gl: E.matmul(yb[:, 0:TS], lhsT=L1[:, gg, :], rhs=h1[:, :], start=(gl == 0), stop=False), reads=[L1, h1], writes=[yb])
        P.op("pe", lambda E, yb=yb, gg=gg, h2=h2, gl=gl: E.matmul(yb[:, 0:TS], lhsT=L2[:, gg, :], rhs=h2[:, :], start=False, stop=(gl == 7)), reads=[L2, h2], writes=[yb])
        if gl == 7:
            cr = chunk_res[ch]
            yo, us = cr["yo"], cr["us"]
            P.op("dve", lambda E, yo=yo, us=us, yb=yb, jj=jj: E.scalar_tensor_tensor(
                out=yo[:, jj, :], in0=us[:, jj, :], scalar=dsk[:, jj:jj + 1], in1=yb[:, 0:TS], op0=ALU.mult, op1=ALU.add),
                reads=[us, dsk, yb], writes=[yo])
            if jj == 1:
                c0 = ch * TS
                ysrc = dr["y_src"][c0 // 2048]
                P.dma("act", ysrc.t[:, c0 % 2048:c0 % 2048 + TS].rearrange("(j p) t -> p j t", p=128), yo[:, :, :], reads=[yo], writes=[ysrc])
                if (c0 + TS) % 2048 == 0:
                    P.coll("AllGather", ysrc, dr["y_all"][c0 // 2048], GROUPS)

    for j in range(nu + 2):
        if j < nu:
            s1(j)
        if 0 <= j - 1 < nu:
            s2(j - 1)
        if 0 <= j - 2 < nu:
            s3(j - 2)
    P.wait_all("act", dr["y_all"])
    P.emit()


def gelu_tanh(P, dn, out_bf, y, reads):
    s = dn.tmp.get()
    s2 = dn.tmp.get()
    act(P, s[:, :], y, AF.Square, reads, [s])
    ts(P, "dve", s[:, :], s[:, :], 0.044715, 1.0, ALU.mult, ALU.add, [s], [s])
    tt(P, "dve", s2[:, :], s[:, :], y, ALU.mult, [s] + list(reads), [s2])
    act(P, s2[:, :], s2[:, :], AF.Sigmoid, [s2], [s2], scale=1.5957691216057308)
    return s2


def ple(P, dn, Wpg, Wpp, Xb, Pb, X1):
    for m in range(8):
        wg = dn.load_w(Wpg, m, 8)
        wp = dn.load_w(Wpp, m, 2)
        for n in range(dn.T // NT):
            pg = P.bank()
            pp = P.bank()
            for kc in range(8):
                P.op("pe", lambda E, pg=pg, wg=wg, kc=kc, n=n: E.matmul(
                    pg[:, :], lhsT=wg[:, kc, :], rhs=Xb[:, kc, n * NT:(n + 1) * NT], start=(kc == 0), stop=(kc == 7)),
                    reads=[wg, Xb], writes=[pg])
            for kc in range(2):
                P.op("pe", lambda E, pp=pp, wp=wp, kc=kc, n=n: E.matmul(
                    pp[:, :], lhsT=wp[:, kc, :], rhs=Pb[:, kc, n * NT:(n + 1) * NT], start=(kc == 0), stop=(kc == 1)),
                    reads=[wp, Pb], writes=[pp])
            sg = dn.tmp.get()
            act(P, sg[:, :], pg[:, :], AF.Sigmoid, [pg], [sg])
            t = dn.tmp.get()
            tt(P, "dve", t[:, :], sg[:, :], pp[:, :], ALU.mult, [sg, pp], [t])
            tt(P, "pool", X1[:, m, n * NT:(n + 1) * NT], X1[:, m, n * NT:(n + 1) * NT], t[:, :], ALU.add, [X1, t], [X1])


NH = 4
NQT = SEQ // NT


def phase_D(nc, dr):
    P = Prog(nc)
    qT, kT, vT, tri_d = dr["qT_cs"], dr["kT_cs"], dr["vT_cs"], dr["ntri"]
    ident = P.sb("ident", [128, 128], BF16)
    P.op("pool", lambda E: E.memset(ident[:, :], 0.0), writes=[ident])
    P.op("pool", lambda E: E.affine_select(out=ident[:, :], in_=ident[:, :], pattern=[[-1, 128]], compare_op=ALU.not_equal,
                                           fill=1.0, base=0, channel_multiplier=1), reads=[ident], writes=[ident])
    vtb = Rot(P, "vtb", 3, [64, 2048], BF16)
    tpb = P.ps("tpb", [128, NT], BF16)
    zb = [P.ps("zb%d" % i, [128, NT], F32) for i in range(4)]
    ob = [P.ps("ob%d" % i, [64, NT], F32) for i in range(2)]
    st = P.sb("tri_st", [128, 128], F32)
    P.dma("sp", st[:, :], tri_d.t[:, :], reads=[tri_d], writes=[st])
    ntri = P.sb("ntri", [128, 128], BF16)
    cp(P, "dve", ntri[:, :], st[:, :], [st], [ntri])
    negm = P.sb("negm", [128, 4, NT], BF16)
    P.op("pool", lambda E: E.memset(negm[:, :, :], 0.0), writes=[negm])
    for d_ in range(4):
        P.op("pool", lambda E, d_=d_: E.affine_select(
            out=negm[:, d_, :], in_=negm[:, d_, :], pattern=[[1, NT]], compare_op=ALU.is_gt, fill=-30000.0, base=-128 * d_, channel_multiplier=-1),
            reads=[negm], writes=[negm])
    nones = P.sb("nones", [128, 128], BF16)
    P.op("pool", lambda E: E.memset(nones[:, :], -1.0), writes=[nones])
    Qb = [P.sb("Qb%d" % i, [128, SEQ], BF16) for i in range(2)]
    Kb = [P.sb("Kb%d" % i, [128, SEQ], BF16) for i in range(2)]
    Vb = [P.sb("Vb%d" % h, [128, 64 * 64], BF16) for h in range(NH)]
    er = Rot(P, "er", 3, [128, NT], F32)
    spr = Rot(P, "spr", 4, [128, NT], BF16)
    wr = Rot(P, "wr", 4, [128, NT], BF16)
    racc = Rot(P, "racc", 3, [128, NT], BF16)
    ost = Rot(P, "ost", 2, [64, NT], BF16)
    for pr in range(2):
        for c in range(SEQ // 2048):
            P.dma("sp", Qb[pr][:, c * 2048:(c + 1) * 2048], qT.t[pr * 128:(pr + 1) * 128, c * 2048:(c + 1) * 2048], reads=[qT], writes=[Qb[pr]])
            P.dma("sp", Kb[pr][:, c * 2048:(c + 1) * 2048], kT.t[pr * 128:(pr + 1) * 128, c * 2048:(c + 1) * 2048], reads=[kT], writes=[Kb[pr]])

    def load_v(h):
        for c in range(SEQ // 2048):
            vb_ = vtb.get()
            P.dma("sp", vb_[:, :], vT.t[h * 64:(h + 1) * 64, c * 2048:(c + 1) * 2048], reads=[vT], writes=[vb_])
            for k8 in range(2):
                for j in range(8):
                    blk = k8 * 8 + j
                    P.op("pe", lambda E, vb_=vb_, j=j, blk=blk: E.transpose(
                        out=tpb[:, j * 64:(j + 1) * 64], in_=vb_[:, blk * 128:(blk + 1) * 128], identity=ident[0:64, 0:64]),
                        reads=[vb_, ident], writes=[tpb])
                kb0 = c * 16 + k8 * 8
                cp(P, "dve", Vb[h][:, kb0 * 64:(kb0 + 8) * 64], tpb[:, :], [tpb], [Vb[h]])
    load_v(0)
    blocks = []
    for h in range(NH):
        for qt in range(NQT):
            kbs = list(range(4 * qt + 3, -1, -1))
            for idx, kb in enumerate(kbs):
                blocks.append(dict(h=h, qt=qt, kb=kb, idx=idx, n=len(kbs), g=h * NQT + qt))
    nb = len(blocks)

    def operands(B):
        hp, pr = B["h"] % 2, B["h"] // 2
        ksl = Kb[pr][hp * 64:(hp + 1) * 64, B["kb"] * 128:(B["kb"] + 1) * 128]
        qsl = Qb[pr][hp * 64:(hp + 1) * 64, B["qt"] * NT:(B["qt"] + 1) * NT]
        return ksl, qsl, Kb[pr], Qb[pr]

    def mask(t, B):
        base = B["qt"] * NT - 128 * B["kb"]
        P.op("pool", lambda E, t=t, base=base: E.affine_select(
            out=t[:, :], in_=t[:, :], pattern=[[1, NT]], compare_op=ALU.is_gt, fill=0.0, base=base, channel_multiplier=-1),
            reads=[t], writes=[t])

    def stage1a(i):
        B = blocks[i]
        ksl, qsl, kres, qres = operands(B)
        if B["qt"] == 2 and B["idx"] == 0 and B["h"] + 1 < NH:
            load_v(B["h"] + 1)
        z = zb[i % 4]
        P.op("pe", lambda E, z=z, ksl=ksl, qsl=qsl: E.matmul(z[:, :], lhsT=ksl, rhs=qsl, start=True, stop=False), reads=[kres, qres], writes=[z])
        if B["kb"] >= 4 * B["qt"]:
            d_ = B["kb"] - 4 * B["qt"]
            P.op("pe", lambda E, z=z, d_=d_: E.matmul(z[:, :], lhsT=ident[:, :], rhs=negm[:, d_, :], start=False, stop=False), reads=[ident, negm], writes=[z])
        e = er.get()
        act(P, e[:, :], z[:, :], AF.Exp, [z], [e])
        B["e"] = e

    def stage1b(i):
        B = blocks[i]
        e = B["e"]
        sp = spr.get()
        P.op("act", lambda E, sp=sp, e=e: E.activation(out=sp[:, :], in_=e[:, :], func=AF.Ln, scale=1.0, bias=1.0),
             reads=[e], writes=[sp], skip_same=True)
        B["sp"] = sp

    def stage2(i):
        B = blocks[i]
        b = zb[i % 4]
        sp = B["sp"]
        first = B["idx"] == 0
        ra_prev = None if first else blocks[i - 1]["ra"]
        P.op("pe", lambda E, b=b, sp=sp, first=first: E.matmul(b[:, :], lhsT=ntri[:, :], rhs=sp[:, :], start=False, stop=first), reads=[ntri, sp], writes=[b])
        if not first:
            P.op("pe", lambda E, b=b, ra=ra_prev: E.matmul(b[:, :], lhsT=nones[:, :], rhs=ra[:, :], start=False, stop=True), reads=[nones, ra_prev], writes=[b])
        w = wr.get()
        act(P, w[:, :], b[:, :], AF.Exp, [b], [w])
        B["w"] = w
        if B["idx"] + 1 < B["n"]:
            ra = racc.get()
            if first:
                cp(P, "dve", ra[:, :], sp[:, :], [sp], [ra])
            else:
                tt(P, "dve", ra[:, :], ra_prev[:, :], sp[:, :], ALU.add, [ra_prev, sp], [ra])
            B["ra"] = ra

    def stage3(i):
        B = blocks[i]
        o_ps = ob[B["g"] % 2]
        w = B["w"]
        h, kb = B["h"], B["kb"]
        P.op("pe", lambda E, o_ps=o_ps, w=w, h=h, kb=kb, B=B: E.matmul(
            o_ps[:, :], lhsT=Vb[h][:, kb * 64:(kb + 1) * 64], rhs=w[:, :], start=(B["idx"] == 0), stop=(B["idx"] == B["n"] - 1)),
            reads=[Vb[h], w], writes=[o_ps])
        if B["idx"] == B["n"] - 1:
            o = ost.get()
            cp(P, "dve", o[:, :], o_ps[:, :], [o_ps], [o])
            q0 = B["qt"] * NT
            osrc = dr["o_src"][q0 // 2048]
            P.dma("act", osrc.t[h * 64:(h + 1) * 64, q0 % 2048:q0 % 2048 + NT], o[:, :], reads=[o], writes=[osrc])
            if h == NH - 1 and (q0 + NT) % 2048 == 0:
                P.coll("AllGather", osrc, dr["o_all"][q0 // 2048], GROUPS)

    for j in range(nb + 2):
        if j < nb:
            stage1a(j)
            stage1b(j)
        if 0 <= j - 1 < nb:
            stage2(j - 1)
        if 0 <= j - 2 < nb:
            stage3(j - 2)
    P.wait_all("act", dr["o_all"])
    P.emit()


GROUPS = [[0, 1, 2, 3], [4, 5, 6, 7]]
TOKC = 2048
TP = 1024


def proj_pass(P, dn, src_fn, xs, rs, jobs, col0):
    for kc in range(8):
        srcs = src_fn(kc)
        if isinstance(srcs, tuple):
            P.dma("sp", xs[:, kc, :], srcs[1], reads=[srcs[0]], writes=[xs])
        else:
            w_ = TP // len(srcs)
            for i_, (sr_, sa_) in enumerate(srcs):
                P.dma("sp", xs[:, kc, i_ * w_:(i_ + 1) * w_], sa_, reads=[sr_], writes=[xs])
    dn.rstd(xs, rs)
    done = {}
    for job in jobs:
        xb, gain, wbs, MC, dst = job[:5]
        oscale = job[5] if len(job) > 5 else None
        if id(xb) not in done:
            done[id(xb)] = 1
            for kc in range(8):
                scale_cast(P, kc, xb[:, kc, :], xs[:, kc, :], gain[:, kc:kc + 1], [xs, gain], [xb])

        def epi(m, n, ps, dst=dst, oscale=oscale):
            if oscale is None:
                o = dn.ost.get()
                tt(P, "dve", o[:, :], ps[:, :], rs[:, n * NT:(n + 1) * NT], ALU.mult, [ps, rs], [o])
            else:
                o = dn.ostb.get()
                P.op("dve", lambda E, o=o, ps=ps, n=n: E.scalar_tensor_tensor(
                    out=o[:, :], in0=ps[:, :], scalar=oscale, in1=rs[:, n * NT:(n + 1) * NT], op0=ALU.mult, op1=ALU.mult),
                    reads=[ps, rs], writes=[o])
            dn.store(dst, m, n, col0, o, o[:, :])
        dn.linear(None, 8, MC, xb, epi, wbs=wbs)


def phase_A(nc, dr):
    P = Prog(nc)
    P.mk_banks(6)
    dn = Dense(P, TP)
    g = colvec(P, "g_pre", dr["g_pre"], 8)
    xsr = Rot(P, "xs", 2, [128, 8, TP], F32)
    xb = P.sb("xb", [128, 8, TP], BF16)
    rs = P.sb("rs", [128, TP], F32)
    xT = dr["xT_full"]
    wbs = dn.prep_w("wu", dr["w_in_u"], 8, 2)
    for pa in range(SEQ // TP):
        t0 = pa * TP
        proj_pass(P, dn, lambda kc, t0=t0: (xT, xT.t[kc * 128:(kc + 1) * 128, t0:t0 + TP]), xsr.get(), rs,
                  [(xb, g, wbs, 2, dr["uT_cs"])], t0)
    P.wait_all("pool", [dr["uT_cs"]])
    P.emit()


def mk_select(P, dn, sel):
    ident = P.sb("identf", [128, 128], F32)
    P.op("pool", lambda E: E.memset(ident[:, :], 0.0), writes=[ident])
    P.op("pool", lambda E: E.affine_select(out=ident[:, :], in_=ident[:, :], pattern=[[-1, 128]], compare_op=ALU.not_equal,
                                           fill=1.0, base=0, channel_multiplier=1), reads=[ident], writes=[ident])
    mI = P.sb("mI", [128, 4, 128], BF16)
    for s_ in range(4):
        ts(P, "dve", mI[:, s_, :], ident[:, :], sel[:, s_:s_ + 1], None, ALU.mult, None, [ident, sel], [mI])
    dn.mI = mI


def select4(P, dn, src_fn, sel, dt):
    ps = P.bank()
    for s in range(4):
        it = dn.istb.get()
        sres, sap = src_fn(s)
        P.dma("sp", it[:, :], sap, reads=[sres], writes=[it])
        P.op("pe", lambda E, ps=ps, it=it, s=s: E.matmul(ps[:, :], lhsT=dn.mI[:, s, :], rhs=it[:, :], start=(s == 0), stop=(s == 3)),
             reads=[dn.mI, it], writes=[ps])
    acc = dn.tmp.get()
    act(P, acc[:, :], ps[:, :], AF.Copy, [ps], [acc])
    return acc


def phase_C(nc, dr):
    P = Prog(nc)
    P.mk_banks(7)
    dn = Dense(P, TP)
    dn.istb = Rot(P, "istb", 8, [128, NT], BF16)
    sel = colvec(P, "sel", dr["sel"], 4)
    mk_select(P, dn, sel)
    gpre = colvec(P, "g_pre", dr["g_pre"], 8)
    vl = []
    for i in range(4):
        t_ = P.sb("vec%d" % i, [128, 8], F32)
        P.dma("sp", t_[:, :], dr["vecsC"].t[:, i * 8:(i + 1) * 8], reads=[dr["vecsC"]], writes=[t_])
        vl.append(t_)
    bglu, gpost, gbpre, _unused = vl
    xT, y_all, pT = dr["xT_own"], dr["y_all"], dr["p0T"]
    Gb = P.sb("Gb", [128, 8, TP], BF16)
    SGb = P.sb("SGb", [128, 8, TP], BF16)
    Y2b = P.sb("Y2b", [128, 8, TP], BF16)
    X1 = P.sb("X1", [128, 8, TP], F32)
    Pb = P.sb("Pb", [128, 2, TP], BF16)
    rs = P.sb("rs", [128, TP], F32)
    for pa in range(TOKC // TP):
        t0 = pa * TP
        for kc in range(8):
            P.dma("sp", X1[:, kc, :], xT.t[kc * 128:(kc + 1) * 128, t0:t0 + TP], reads=[xT], writes=[X1])
        dn.rstd(X1, rs)
        for kc in range(8):
            scale_cast(P, kc, Y2b[:, kc, :], X1[:, kc, :], gpre[:, kc:kc + 1], [X1, gpre], [Y2b])

        def epi_gate(m, n, ps):
            t = dn.tmp.get()
            tt(P, "dve", t[:, :], ps[:, :], rs[:, n * NT:(n + 1) * NT], ALU.mult, [ps, rs], [t])
            act(P, SGb[:, m, n * NT:(n + 1) * NT], t[:, :], AF.Silu, [t], [SGb])
        dn.linear(dr["w_in_g"], 8, 8, Y2b, epi_gate)
        for kc in range(8):
            for n in range(TP // NT):
                def ysrc_fn(s, n=n, kc=kc):
                    g0 = s * TOKC + t0 + n * NT
                    ya = y_all[g0 // 2048]
                    return ya, ya.t[kc * 128:(kc + 1) * 128, g0 % 2048:g0 % 2048 + NT]
                y = select4(P, dn, ysrc_fn, sel, BF16)
                s2 = gelu_tanh(P, dn, None, y[:, :], [y])
                tt(P, "pool", Gb[:, kc, n * NT:(n + 1) * NT], s2[:, :], y[:, :], ALU.mult, [s2, y], [Gb])

        def epi_glu(m, n, ps):
            sg = dn.tmp.get()
            act(P, sg[:, :], ps[:, :], AF.Sigmoid, [ps, bglu], [sg], bias=bglu[:, m:m + 1])
            t = dn.tmp.get()
            tt(P, "dve", t[:, :], sg[:, :], Gb[:, m, n * NT:(n + 1) * NT], ALU.mult, [sg, Gb], [t])
            tt(P, "pool", Y2b[:, m, n * NT:(n + 1) * NT], t[:, :], SGb[:, m, n * NT:(n + 1) * NT], ALU.mult, [t, SGb], [Y2b])
        dn.linear(dr["w_glu"], 8, 8, Gb, epi_glu)

        def epi_out(m, n, ps):
            act(P, X1[:, m, n * NT:(n + 1) * NT], ps[:, :], AF.Copy, [ps], [X1])
        dn.linear(dr["w_out0"], 8, 8, Y2b, epi_out)
        dn.rstd(X1, rs)
        for kc in range(8):
            for n in range(TP // NT):
                c0 = t0 + n * NT
                ix = dn.ist.get()
                P.dma("sp", ix[:, :], xT.t[kc * 128:(kc + 1) * 128, c0:c0 + NT], reads=[xT], writes=[ix])
                t = dn.tmp.get()
                P.op("dve", lambda E, t=t, kc=kc, n=n: E.scalar_tensor_tensor(
                    out=t[:, :], in0=X1[:, kc, n * NT:(n + 1) * NT], scalar=gpost[:, kc:kc + 1], in1=rs[:, n * NT:(n + 1) * NT],
                    op0=ALU.mult, op1=ALU.mult), reads=[X1, gpost, rs], writes=[t])
                tt(P, "pool", X1[:, kc, n * NT:(n + 1) * NT], t[:, :], ix[:, :], ALU.add, [t, ix], [X1])
                cp(P, "act", Gb[:, kc, n * NT:(n + 1) * NT], X1[:, kc, n * NT:(n + 1) * NT], [X1], [Gb])
        for kc in range(2):
            for n in range(TP // NT):
                c0 = t0 + n * NT
                ip = dn.ist.get()
                P.dma("sp", ip[:, :], pT.t[kc * 128:(kc + 1) * 128, c0:c0 + NT], reads=[pT], writes=[ip])
                cp(P, "dve", Pb[:, kc, n * NT:(n + 1) * NT], ip[:, :], [ip], [Pb])
        ple(P, dn, dr["w_pg0"], dr["w_pp0"], Gb, Pb, X1)
        dn.rstd(X1, rs)
        for kc in range(8):
            cp(P, "dve", Y2b[:, kc, :], X1[:, kc, :], [X1], [Y2b])
            scale_cast(P, kc + 1, SGb[:, kc, :], X1[:, kc, :], gbpre[:, kc:kc + 1], [X1, gbpre], [SGb])
            P.dma("pool", dr["x1_own"].t[kc * 128:(kc + 1) * 128, t0:t0 + TP], X1[:, kc, :], reads=[X1], writes=[dr["x1_own"]])
            for n in range(TP // NT):
                xsrc = dr["x1_src"][(t0 + n * NT) // NT]
                P.dma("pool", xsrc.t[kc * 128:(kc + 1) * 128, :], Y2b[:, kc, n * NT:(n + 1) * NT], reads=[Y2b], writes=[xsrc])

        def epi_g1(m, n, ps):
            o = dn.ost.get()
            tt(P, "dve", o[:, :], ps[:, :], rs[:, n * NT:(n + 1) * NT], ALU.mult, [ps, rs], [o])
            dn.store(dr["g1T"], m, n, t0, o, o[:, :])
        for k in range(t0 // NT, (t0 + TP) // NT):
            P.coll("AllGather", dr["x1_src"][k], dr["x1_all"][k], GROUPS)
        dn.linear(dr["w_bin_g"], 8, 8, SGb, epi_g1)
    P.wait_all("pool", dr["x1_all"] + [dr["x1_own"], dr["g1T"]])
    P.emit()


def phase_QKV(nc, dr):
    P = Prog(nc)
    P.mk_banks(6)
    dn = Dense(P, TP)
    gkv = colvec(P, "g_kv", dr["g_kv"], 8)
    gbpre = colvec(P, "g_bpre", dr["g_bpre"], 8)
    xsr = Rot(P, "xs", 2, [128, 8, TP], BF16)
    xq = P.sb("xq", [128, 8, TP], BF16)
    xk = P.sb("xk", [128, 8, TP], BF16)
    rs = P.sb("rs", [128, TP], F32)
    xa = dr["x1_all"]
    wq = dn.prep_w("wq", dr["w_q"], 8, 2)
    wk = dn.prep_w("wk", dr["w_k"], 8, 2)
    wv = dn.prep_w("wv", dr["w_v"], 8, 2)
    for pa in range(SEQ // TP):
        t0 = pa * TP
        s, tl = t0 // TOKC, t0 % TOKC
        proj_pass(P, dn, lambda kc, s=s, tl=tl: [(xa[(tl + h_ * NT) // NT], xa[(tl + h_ * NT) // NT].t[s * D + kc * 128:s * D + (kc + 1) * 128, :]) for h_ in range(TP // NT)], xsr.get(), rs,
                  [(xq, gbpre, wq, 2, dr["qT_cs"], 1.0), (xk, gkv, wk, 2, dr["kT_cs"], 0.125), (xk, gkv, wv, 2, dr["vT_cs"], 1.0)], t0)
    P.wait_all("pool", [dr["qT_cs"], dr["kT_cs"], dr["vT_cs"]])
    P.emit()


def phase_E(nc, dr):
    P = Prog(nc)
    P.mk_banks(7)
    dn = Dense(P, TP)
    dn.istb = Rot(P, "istb", 8, [128, NT], BF16)
    sel = colvec(P, "sel", dr["sel"], 4)
    mk_select(P, dn, sel)
    gpost = colvec(P, "g_bpost", dr["g_bpost"], 8)
    Ob = P.sb("Ob", [128, 8, TP], BF16)
    Xb = P.sb("Xb", [128, 8, TP], BF16)
    X1 = P.sb("X1", [128, 8, TP], F32)
    Pb = P.sb("Pb", [128, 2, TP], BF16)
    rs = P.sb("rs", [128, TP], F32)
    x1T, gT, pT, outT = dr["x1_own"], dr["g1T"], dr["p1T"], dr["outT"]
    for pa in range(TOKC // TP):
        t0 = pa * TP
        for kc in range(8):
            for n in range(TP // NT):
                c0 = t0 + n * NT
                def osrc_fn(s, n=n, kc=kc):
                    g0 = s * TOKC + t0 + n * NT
                    oa = dr["o_all"][g0 // 2048]
                    return oa, oa.t[kc * 128:(kc + 1) * 128, g0 % 2048:g0 % 2048 + NT]
                o = select4(P, dn, osrc_fn, sel, BF16)
                ig = dn.ist.get()
                P.dma("sp", ig[:, :], gT.t[kc * 128:(kc + 1) * 128, c0:c0 + NT], reads=[gT], writes=[ig])
                sg = dn.tmp.get()
                act(P, sg[:, :], ig[:, :], AF.Silu, [ig], [sg])
                tt(P, "pool", Ob[:, kc, n * NT:(n + 1) * NT], sg[:, :], o[:, :], ALU.mult, [sg, o], [Ob])

        def epi_out(m, n, ps):
            act(P, X1[:, m, n * NT:(n + 1) * NT], ps[:, :], AF.Copy, [ps], [X1])
        dn.linear(dr["w_out1"], 8, 8, Ob, epi_out)
        dn.rstd(X1, rs)
        for kc in range(8):
            for n in range(TP // NT):
                c0 = t0 + n * NT
                ix = dn.ist.get()
                P.dma("sp", ix[:, :], x1T.t[kc * 128:(kc + 1) * 128, c0:c0 + NT], reads=[x1T], writes=[ix])
                t = dn.tmp.get()
                P.op("dve", lambda E, t=t, kc=kc, n=n: E.scalar_tensor_tensor(
                    out=t[:, :], in0=X1[:, kc, n * NT:(n + 1) * NT], scalar=gpost[:, kc:kc + 1], in1=rs[:, n * NT:(n + 1) * NT],
                    op0=ALU.mult, op1=ALU.mult), reads=[X1, gpost, rs], writes=[t])
                tt(P, "pool", X1[:, kc, n * NT:(n + 1) * NT], t[:, :], ix[:, :], ALU.add, [t, ix], [X1])
                cp(P, "act", Xb[:, kc, n * NT:(n + 1) * NT], X1[:, kc, n * NT:(n + 1) * NT], [X1], [Xb])
        for kc in range(2):
            for n in range(TP // NT):
                c0 = t0 + n * NT
                ip = dn.ist.get()
                P.dma("sp", ip[:, :], pT.t[kc * 128:(kc + 1) * 128, c0:c0 + NT], reads=[pT], writes=[ip])
                cp(P, "dve", Pb[:, kc, n * NT:(n + 1) * NT], ip[:, :], [ip], [Pb])
        ple(P, dn, dr["w_pg1"], dr["w_pp1"], Xb, Pb, X1)
        for kc in range(8):
            P.dma("pool", outT.t[kc * 128:(kc + 1) * 128, t0:t0 + TP], X1[:, kc, :], reads=[X1], writes=[outT])
    P.wait_all("pool", [outT])
    P.emit()


IN_SPECS = {
    "xT_full": ([D, SEQ], F32), "xT_own": ([D, TOKC], F32), "p0T": ([256, TOKC], F32), "p1T": ([256, TOKC], F32),
    "sel": ([128, 4], F32), "g_pre": ([128, 8], F32), "w_in_u": ([D, 256], F32), "w_in_g": ([D, D], F32),
    "lamre_T": ([128, 128], F32), "lamim_T": ([128, 128], F32), "logdt_T": ([128, 128], F32), "bre_T": ([128, 128], F32),
    "bim_T": ([128, 128], F32), "lamre_S": ([128, NG], F32), "lamim_S": ([128, NG], F32), "logdt_S": ([128, NG], F32),
    "c1_S": ([128, NG * 16], F32), "c2_S": ([128, NG * 16], F32), "iota": ([128, TS + 1], F32), "gmask": ([128, 8], F32),
    "sgn": ([128, 1], F32), "Jm": ([128, 128], F32), "dskip_cs": ([128, 2], F32), "vecsC": ([128, 32], F32),
    "w_glu": ([D, D], F32), "w_out0": ([D, D], F32), "w_pg0": ([D, D], F32), "w_pp0": ([256, D], F32),
    "w_bin_g": ([D, D], F32), "g_kv": ([128, 8], F32), "g_bpre": ([128, 8], F32), "w_q": ([D, 256], F32),
    "w_k": ([D, 256], F32), "w_v": ([D, 256], F32), "ntri": ([128, 128], F32), "g_bpost": ([128, 8], F32),
    "w_out1": ([D, D], F32), "w_pg1": ([D, D], F32), "w_pp1": ([256, D], F32),
}
SCRATCH = {
    "uT_cs": ([256, SEQ], F32), "y_src": ([256, 2048], BF16, 4), "y_all": ([D, 2048], BF16, 4), "x1_own": ([D, TOKC], F32),
    "x1_src": ([D, NT], BF16, 4), "x1_all": ([4 * D, NT], BF16, 4), "g1T": ([D, TOKC], F32), "qT_cs": ([256, SEQ], BF16),
    "kT_cs": ([256, SEQ], BF16), "vT_cs": ([256, SEQ], BF16), "o_src": ([256, 2048], BF16, 4), "o_all": ([D, 2048], BF16, 4),
}


def build_fused():
    nc = bass.Bass("TRN2", target_bir_lowering=False)
    dr = {}
    for n, (shp, dt) in IN_SPECS.items():
        dr[n] = Res(n, nc.dram_tensor(n, list(shp), dt, kind="ExternalInput").ap())
    for n, spec in SCRATCH.items():
        shp, dt = spec[0], spec[1]
        if len(spec) == 3:
            dr[n] = [Res("%s%d" % (n, i), nc.dram_tensor("%s%d" % (n, i), list(shp), dt, kind="Internal").ap()) for i in range(spec[2])]
        else:
            dr[n] = Res(n, nc.dram_tensor(n, list(shp), dt, kind="Internal").ap())
    dr["outT"] = Res("outT", nc.dram_tensor("outT", [D, TOKC], F32, kind="ExternalOutput").ap())
    phase_A(nc, dr)
    phase_B(nc, dr)
    phase_C(nc, dr)
    phase_QKV(nc, dr)
    phase_D(nc, dr)
    phase_E(nc, dr)
    Prog.finish()
    return nc


def _f(a):
    return np.ascontiguousarray(np.asarray(a, dtype=np.float32))


def kernel(**inputs):
    inp = {k: np.asarray(v) for k, v in inputs.items()}
    x, p = inp["x"], inp["p"]
    lam_re, lam_im, log_dt = _f(inp["a_lam_re"][0]), _f(inp["a_lam_im"][0]), _f(inp["a_log_dt"][0])
    b_re, b_im, c_re, c_im = _f(inp["a_b_re"][0]), _f(inp["a_b_im"][0]), _f(inp["a_c_re"][0]), _f(inp["a_c_im"][0])
    iota = _f(np.broadcast_to(np.arange(TS + 1, dtype=np.float32), (128, TS + 1)))
    gmask = np.zeros((128, 8), np.float32)
    for gl in range(8):
        gmask[gl * 16:(gl + 1) * 16, gl] = 1.0
    sgn = np.ones((128, 1), np.float32)
    sgn[64:] = -1.0
    J = np.zeros((128, 128), np.float32)
    for q in range(64):
        J[64 + q, q] = -1.0
        J[q, 64 + q] = 1.0
    ntri = np.zeros((128, 128), np.float32)
    for j in range(128):
        ntri[j, :j + 1] = -1.0
    vecsC = np.zeros((128, 32), np.float32)
    for i, v in enumerate([inp["a_b_glu"][0], inp["a_norm_post"][0], inp["b_norm_pre"][0]]):
        vecsC[:, i * 8:(i + 1) * 8] = _cols(v)
    a_w_in, b_w_in, w_kv = _f(inp["a_w_in"][0]), _f(inp["b_w_in"][0]), _f(inp["w_kv"])
    common = {
        "g_pre": _cols(inp["a_norm_pre"][0]), "w_in_g": _f(a_w_in[:, D:]), "iota": iota, "gmask": gmask, "sgn": sgn, "Jm": J,
        "vecsC": vecsC, "w_glu": _f(inp["a_w_glu"][0]), "w_out0": _f(inp["a_w_out"][0]), "w_pg0": _f(inp["ple_w_gate"][0]),
        "w_pp0": _f(inp["ple_w_proj"][0]), "w_bin_g": _f(b_w_in[:, D:]), "g_kv": _cols(inp["kv_norm"]),
        "g_bpre": _cols(inp["b_norm_pre"][0]), "ntri": ntri, "g_bpost": _cols(inp["b_norm_post"][0]),
        "w_out1": _f(inp["b_w_out"][0]), "w_pg1": _f(inp["ple_w_gate"][1]), "w_pp1": _f(inp["ple_w_proj"][1]),
    }
    xT_full = [_f(np.asarray(x[b], np.float32).T) for b in range(2)]
    maps = []
    for c in range(NCORE):
        b, r = c // 4, c % 4
        gs = np.arange(16 * r, 16 * r + 16)
        tsl = slice(r * TOKC, (r + 1) * TOKC)
        csl = slice(256 * r, 256 * r + 256)
        m = dict(common)
        m["xT_full"] = xT_full[b]
        m["xT_own"] = _f(xT_full[b][:, tsl])
        m["p0T"] = _f(np.asarray(p[0, b, tsl, :], np.float32).T)
        m["p1T"] = _f(np.asarray(p[1, b, tsl, :], np.float32).T)
        sel = np.zeros((128, 4), np.float32)
        sel[:, r] = 1.0
        m["sel"] = sel
        m["w_in_u"] = _f(a_w_in[:, csl])
        m["dskip_cs"] = _cols(inp["a_d_skip"][0][csl])
        m["w_q"] = _f(b_w_in[:, csl])
        m["w_k"] = _f(w_kv[:, csl])
        m["w_v"] = _f(w_kv[:, D + 256 * r:D + 256 * r + 256])

        def lt_gp(a):
            t = a[gs].reshape(2, 8, 64)
            t = np.broadcast_to(t[:, :, None, :], (2, 8, 16, 64))
            return _f(t.transpose(1, 2, 0, 3).reshape(128, 128))

        def lt_b(a):
            t = a[gs].reshape(2, 8, 64, 16)
            return _f(t.transpose(1, 3, 0, 2).reshape(128, 128))

        def sp_gp(a):
            t = a[gs].T
            return _f(np.concatenate([t, t], axis=0))
        ldt = np.broadcast_to(log_dt[:, None], (64, 64))
        m["lamre_T"], m["lamim_T"], m["logdt_T"] = lt_gp(lam_re), lt_gp(lam_im), lt_gp(ldt)
        m["bre_T"], m["bim_T"] = lt_b(b_re), lt_b(b_im)
        m["lamre_S"], m["lamim_S"], m["logdt_S"] = sp_gp(lam_re), sp_gp(lam_im), sp_gp(ldt)
        cr = c_re[gs].transpose(2, 0, 1).reshape(64, 256)
        ci = c_im[gs].transpose(2, 0, 1).reshape(64, 256)
        m["c1_S"] = _f(np.concatenate([cr, ci], axis=0))
        m["c2_S"] = _f(np.concatenate([ci, cr], axis=0))
        maps.append(m)
    res = _run(build_fused(), maps)
    out = np.empty((2, SEQ, D), np.float32)
    for c in range(NCORE):
        b, r = c // 4, c % 4
        out[b, r * TOKC:(r + 1) * TOKC, :] = res[c]["outT"].T
    return out
```

```python
from contextlib import ExitStack
import numpy as np
import concourse.bass as bass
import concourse.mybir as mybir
from concourse.bass_utils import run_bass_kernel_spmd

F32 = mybir.dt.float32
BF16 = mybir.dt.bfloat16
AF = mybir.ActivationFunctionType
ALU = mybir.AluOpType

ENGS = ("pe", "act", "dve", "pool", "sp")
NCORE = 8
D = 1024
SEQ = 8192
NT = 512
EPS = 1e-6
PI = float(np.pi)


class Res:
    __slots__ = ("name", "w", "r", "dsem", "dcnt", "t")

    def __init__(self, name, t=None):
        self.name = name
        self.w = {}
        self.r = {}
        self.dsem = None
        self.dcnt = 0
        self.t = t

    def __getitem__(self, idx):
        return self.t[idx]


class Prog:
    _n = 0
    G = None

    def __init__(self, nc):
        Prog._n += 1
        self.pfx = "f%d_" % Prog._n
        self.nc = nc
        if Prog.G is None or Prog.G["nc"] is not nc:
            ges = ExitStack()
            Prog.G = {"nc": nc, "es": ges, "sems": {}, "cnt": {}}
            for e in ENGS:
                Prog.G["sems"][e] = ges.enter_context(nc.semaphore("s_" + e))
                Prog.G["cnt"][e] = 0
        G = Prog.G
        self.es = ExitStack()
        self.lists = {e: [] for e in ENGS}
        self.sems = G["sems"]
        self.cnt = G["cnt"]
        self.seen = {e: dict(self.cnt) for e in ENGS}
        self.nd = 0
        self.banks = []
        self.bi = 0
        self.touched = {}

    @staticmethod
    def finish():
        if Prog.G is not None:
            Prog.G["es"].close()
            Prog.G = None

    def _newsem(self):
        key = "d%d" % self.nd
        self.nd += 1
        if key not in self.sems:
            self.sems[key] = Prog.G["es"].enter_context(self.nc.semaphore("sd_" + key))
            self.cnt[key] = 0
        return key

    def sb(self, name, shape, dt):
        return Res(name, self.es.enter_context(self.nc.sbuf_tensor(self.pfx + "sb_" + name, list(shape), dt)))

    def ps(self, name, shape, dt=F32):
        return Res(name, self.es.enter_context(self.nc.psum_tensor(self.pfx + "ps_" + name, list(shape), dt)))

    def dram(self, name, shape, dt, kind="Internal"):
        return Res(name, self.nc.dram_tensor(name, list(shape), dt, kind=kind).ap())

    def mk_banks(self, n):
        self.banks = [self.ps("bank%d" % i, [128, NT], F32) for i in range(n)]

    def bank(self):
        b = self.banks[self.bi % len(self.banks)]
        self.bi += 1
        return b

    def _dsem(self, res):
        if res.dsem is None:
            res.dsem = self._newsem()
        return res.dsem

    def _waits(self, eng, reads, writes, skip_same=False):
        for x_ in reads:
            self.touched[id(x_)] = x_
        for x_ in writes:
            self.touched[id(x_)] = x_
        deps = {}
        for r in reads:
            for k, v in r.w.items():
                if skip_same and k == eng:
                    continue
                if v > deps.get(k, 0):
                    deps[k] = v
        for w in writes:
            for k, v in w.w.items():
                if k != eng and v > deps.get(k, 0):
                    deps[k] = v
            for k, v in w.r.items():
                if k != eng and v > deps.get(k, 0):
                    deps[k] = v
        seen = self.seen[eng]
        for k, v in deps.items():
            if v > seen.get(k, 0):
                seen[k] = v
                sem = self.sems[k]
                self.lists[eng].append(lambda E, sem=sem, v=v: E.wait_ge(sem, v))

    def op(self, eng, fn, reads=(), writes=(), skip_same=False):
        self._waits(eng, reads, writes, skip_same)
        self.cnt[eng] += 1
        n = self.cnt[eng]
        sem = self.sems[eng]
        self.lists[eng].append(lambda E, fn=fn, sem=sem: fn(E).then_inc(sem, 1))
        for r in reads:
            r.r[eng] = n
        for w in writes:
            w.w[eng] = n

    def dma(self, eng, out_ap, in_ap, reads=(), writes=()):
        wres = writes[0]
        self._waits(eng, reads, writes)
        key = self._dsem(wres)
        self.cnt[key] += 16
        v = self.cnt[key]
        sem = self.sems[key]
        self.lists[eng].append(
            lambda E, o=out_ap, i=in_ap, sem=sem: E.dma_start(out=o, in_=i).then_inc(sem, 16))
        for r in reads:
            r.r[key] = v
        wres.w[key] = v

    def coll(self, kind, src, dst, groups):
        self._waits("pool", [src], [dst])
        key = self._newsem()
        self.cnt[key] += 1
        v = self.cnt[key]
        sem = self.sems[key]
        self.lists["pool"].append(lambda E, sem=sem: E.collective_compute(
            kind, ALU.bypass, replica_groups=groups, ins=[src.t.opt()], outs=[dst.t.opt()]).then_inc(sem))
        src.r[key] = v
        dst.w[key] = v

    def wait_all(self, eng, ress):
        self._waits(eng, ress, ())

    def emit(self):
        L = self.lists
        for e in ENGS:
            for k, sem in self.sems.items():
                tgt = self.cnt[k]
                if k != e and tgt > self.seen[e].get(k, 0):
                    self.seen[e][k] = tgt
                    L[e].append(lambda E, sem=sem, tgt=tgt: E.wait_ge(sem, tgt))
        with self.nc.Block() as block:
            @block.tensor
            def _(E):
                for f in L["pe"]:
                    f(E)

            @block.scalar
            def _(E):
                for f in L["act"]:
                    f(E)

            @block.vector
            def _(E):
                for f in L["dve"]:
                    f(E)

            @block.gpsimd
            def _(E):
                for f in L["pool"]:
                    f(E)

            @block.sync
            def _(E):
                for f in L["sp"]:
                    f(E)
        self.es.close()
        for x_ in self.touched.values():
            x_.w = {}
            x_.r = {}
            x_.dsem = None


def tt(P, eng, out, in0, in1, op, reads, writes):
    P.op(eng, lambda E: E.tensor_tensor(out=out, in0=in0, in1=in1, op=op), reads=reads, writes=writes)


def ts(P, eng, out, in0, s1, s2, op0, op1, reads, writes):
    if s2 is None:
        P.op(eng, lambda E: E.tensor_scalar(out=out, in0=in0, scalar1=s1, scalar2=None, op0=op0), reads=reads, writes=writes)
    else:
        P.op(eng, lambda E: E.tensor_scalar(out=out, in0=in0, scalar1=s1, scalar2=s2, op0=op0, op1=op1), reads=reads, writes=writes)


def act(P, out, in_, func, reads, writes, scale=1.0, bias=None):
    if bias is None:
        P.op("act", lambda E: E.activation(out=out, in_=in_, func=func, scale=scale), reads=reads, writes=writes)
    else:
        P.op("act", lambda E: E.activation(out=out, in_=in_, func=func, scale=scale, bias=bias), reads=reads, writes=writes)


def scale_cast(P, i, out, in_, col, reads, writes):
    if i % 2:
        P.op("act", lambda E: E.activation(out=out, in_=in_, func=AF.Copy, scale=col), reads=reads, writes=writes)
    else:
        ts(P, "dve", out, in_, col, None, ALU.mult, None, reads, writes)


def cp(P, eng, out, in_, reads, writes):
    if eng == "act":
        P.op(eng, lambda E: E.activation(out=out, in_=in_, func=AF.Copy), reads=reads, writes=writes)
    else:
        P.op(eng, lambda E: E.tensor_copy(out=out, in_=in_), reads=reads, writes=writes)


class Rot:
    def __init__(self, P, name, n, shape, dt):
        self.bufs = [P.sb("%s%d" % (name, i), shape, dt) for i in range(n)]
        self.i = 0

    def get(self):
        b = self.bufs[self.i % len(self.bufs)]
        self.i += 1
        return b


class Dense:
    def __init__(self, P, T):
        self.P = P
        self.T = T
        self.wst = Rot(P, "wst", 4, [128, 8, 128], F32)
        self.wbf = Rot(P, "wbf", 4, [128, 8, 128], BF16)
        self.ost = Rot(P, "ost", 4, [128, NT], F32)
        self.ostb = Rot(P, "ostb", 4, [128, NT], BF16)
        self.ist = Rot(P, "ist", 6, [128, NT], F32)
        self.tmp = Rot(P, "tmp", 12, [128, NT], F32)
        self.ones = P.sb("ones", [128, 128], BF16)
        P.op("pool", lambda E: E.memset(self.ones[:], 1.0), writes=[self.ones])
        self.sq = P.sb("sq", [128, 8, T], BF16)

    def load_w(self, W, m, KC, rowscale=None, wb=None):
        P = self.P
        assert rowscale is None
        st = self.wst.get()
        if wb is None:
            wb = self.wbf.get()
        P.dma("sp", st[:, 0:KC, :], W.t[:, m * 128:(m + 1) * 128].rearrange("(kc p) m -> p kc m", p=128), reads=[W], writes=[st])
        self.wi = getattr(self, "wi", 0) + 1
        cp(P, "act" if self.wi % 2 else "pool", wb[:, 0:KC, :], st[:, 0:KC, :], [st], [wb])
        return wb

    def prep_w(self, name, W, KC, MC):
        wbs = []
        for m in range(MC):
            wb = self.P.sb("%s_w%d" % (name, m), [128, KC, 128], BF16)
            wbs.append(self.load_w(W, m, KC, wb=wb))
        return wbs

    def linear(self, W, KC, MC, a, epi, rowscale=None, wbs=None):
        P = self.P
        for m in range(MC):
            wb = wbs[m] if wbs is not None else self.load_w(W, m, KC, rowscale)
            for n in range(self.T // NT):
                ps = P.bank()
                for kc in range(KC):
                    P.op("pe", lambda E, ps=ps, wb=wb, kc=kc, n=n: E.matmul(
                        ps[:, :], lhsT=wb[:, kc, :], rhs=a[:, kc, n * NT:(n + 1) * NT], start=(kc == 0), stop=(kc == KC - 1)),
                        reads=[wb, a], writes=[ps])
                epi(m, n, ps)

    def rstd(self, src, out):
        P = self.P
        sq = self.sq
        for kc in range(8):
            act(P, sq[:, kc, :], src[:, kc, :], AF.Square, [src], [sq])
        for n in range(self.T // NT):
            ps = P.bank()
            for kc in range(8):
                P.op("pe", lambda E, ps=ps, kc=kc, n=n: E.matmul(
                    ps[:, :], lhsT=self.ones[:, :], rhs=sq[:, kc, n * NT:(n + 1) * NT], start=(kc == 0), stop=(kc == 7)),
                    reads=[self.ones, sq], writes=[ps])
            t = self.tmp.get()
            act(P, t[:, :], ps[:, :], AF.Ln, [ps], [t], scale=1.0 / D, bias=EPS)
            act(P, out[:, n * NT:(n + 1) * NT], t[:, :], AF.Exp, [t], [out], scale=-0.5)

    def store(self, dst, m, n, t0, src_res, src_ap):
        self.P.dma("pool", dst.t[m * 128:(m + 1) * 128, t0 + n * NT:t0 + (n + 1) * NT], src_ap, reads=[src_res], writes=[dst])

    def load_act(self, src, t0, dst, KC=8):
        for kc in range(KC):
            self.P.dma("sp", dst[:, kc, :], src.t[kc * 128:(kc + 1) * 128, t0:t0 + self.T], reads=[src], writes=[dst])


def colvec(P, name, dram_res, ncol):
    t = P.sb(name, [128, ncol], F32)
    P.dma("sp", t[:, :], dram_res.t[:, :], reads=[dram_res], writes=[t])
    return t


def _run(nc, in_maps):
    res = run_bass_kernel_spmd(nc, in_maps, core_ids=list(range(NCORE)))
    return res.results


def _cols(v):
    return np.ascontiguousarray(np.asarray(v, np.float32).reshape(-1, 128).T)


TS = 256
NG = 16
M_MAGIC = 12582912.0


def range_reduce(P, eng, out, in_, tmp, reads, writes, shift=0.0):
    res_in = reads
    if shift != 0.0:
        ts(P, eng, out, in_, shift, None, ALU.add, None, res_in, writes)
        in_ = out
        res_in = writes
    ts(P, eng, tmp[0], in_, 1.0 / (2 * PI), M_MAGIC, ALU.mult, ALU.add, res_in, [tmp[1]])
    ts(P, eng, tmp[0], tmp[0], -M_MAGIC, -2 * PI, ALU.add, ALU.mult, [tmp[1]], [tmp[1]])
    tt(P, eng, out, tmp[0], in_, ALU.add, [tmp[1]] + list(res_in), writes)
    ts(P, eng, out, out, 3.14159, -3.14159, ALU.min, ALU.max, writes, writes)


NAMES_T = ["lamre_T", "lamim_T", "logdt_T", "bre_T", "bim_T"]
NAMES_S = ["lamre_S", "lamim_S", "logdt_S"]


def phase_B(nc, dr):
    P = Prog(nc)
    uT = dr["uT_cs"]
    names_T = NAMES_T
    dT = {n: dr[n] for n in names_T}
    names_S = NAMES_S
    dS = {n: dr[n] for n in names_S}
    c1_d, c2_d, iota_d, gmask_d, sgn_d, J_d = dr["c1_S"], dr["c2_S"], dr["iota"], dr["gmask"], dr["sgn"], dr["Jm"]
    dsk = colvec(P, "dskip_cs", dr["dskip_cs"], 2)
    P.mk_banks(4)
    ybank = [P.ps("ybank%d" % i, [128, NT], F32) for i in range(2)]
    igb = P.ps("igb", [128, NT], F32)

    def ld(name, d, ncol):
        return colvec(P, name, d, ncol)

    lt = {n: ld("s_" + n, dT[n], 128) for n in names_T}
    cnt = [0]

    def newT(nm, ncol=128):
        cnt[0] += 1
        return P.sb("%s_%d" % (nm, cnt[0]), [128, ncol], F32)

    def derive(lamre, lamim, logdt, ncol, pfx):
        o = {}
        lr = newT(pfx + "lr", ncol)
        ts(P, "dve", lr[:, :], lamre[:, :], -1e-4, None, ALU.min, None, [lamre], [lr])
        dt = newT(pfx + "dt", ncol)
        act(P, dt[:, :], logdt[:, :], AF.Exp, [logdt], [dt])
        e = newT(pfx + "e", ncol)
        tt(P, "dve", e[:, :], lr[:, :], dt[:, :], ALU.mult, [lr, dt], [e])
        th = newT(pfx + "th", ncol)
        tt(P, "dve", th[:, :], lamim[:, :], dt[:, :], ALU.mult, [lamim, dt], [th])
        thr = newT(pfx + "thr", ncol)
        tmp = newT(pfx + "tmp", ncol)
        range_reduce(P, "dve", thr[:, :], th[:, :], (tmp[:, :], tmp), [th], [thr])
        thc = newT(pfx + "thc", ncol)
        range_reduce(P, "dve", thc[:, :], th[:, :], (tmp[:, :], tmp), [th], [thc], shift=PI / 2)
        mag = newT(pfx + "mag", ncol)
        act(P, mag[:, :], e[:, :], AF.Exp, [e], [mag])
        sn = newT(pfx + "sin", ncol)
        act(P, sn[:, :], thr[:, :], AF.Sin, [thr], [sn])
        cs = newT(pfx + "cos", ncol)
        act(P, cs[:, :], thc[:, :], AF.Sin, [thc], [cs])
        o.update(lr=lr, li=lamim, e=e, thr=thr, mag=mag, sin=sn, cos=cs)
        return o

    dl = derive(lt["lamre_T"], lt["lamim_T"], lt["logdt_T"], 128, "T")

    def mul(a, b, nm):
        t = newT(nm)
        tt(P, "dve", t[:, :], a[:, :], b[:, :], ALU.mult, [a, b], [t])
        return t

    def addsub(a, b, op, nm):
        t = newT(nm)
        tt(P, "dve", t[:, :], a[:, :], b[:, :], op, [a, b], [t])
        return t

    are = mul(dl["mag"], dl["cos"], "are")
    aim = mul(dl["mag"], dl["sin"], "aim")
    den = addsub(mul(dl["lr"], dl["lr"], "lr2"), mul(dl["li"], dl["li"], "li2"), ALU.add, "den")
    rden = newT("rden")
    P.op("dve", lambda E: E.reciprocal(out=rden[:, :], in_=den[:, :]), reads=[den], writes=[rden])
    nr = newT("nr")
    ts(P, "dve", nr[:, :], are[:, :], -1.0, None, ALU.add, None, [are], [nr])
    fre = mul(addsub(mul(nr, dl["lr"], "f1"), mul(aim, dl["li"], "f2"), ALU.add, "f3"), rden, "fre")
    fim = mul(addsub(mul(aim, dl["lr"], "f4"), mul(nr, dl["li"], "f5"), ALU.subtract, "f6"), rden, "fim")
    bbre = addsub(mul(fre, lt["bre_T"], "b1"), mul(fim, lt["bim_T"], "b2"), ALU.subtract, "bbre")
    bbim = addsub(mul(fre, lt["bim_T"], "b3"), mul(fim, lt["bre_T"], "b4"), ALU.add, "bbim")
    nbbim = newT("nbbim")
    ts(P, "dve", nbbim[:, :], bbim[:, :], -1.0, None, ALU.mult, None, [bbim], [nbbim])
    gmask = ld("gmask", gmask_d, 8)
    W1 = P.sb("W1pad", [128, NG, 128], BF16)
    W2 = P.sb("W2pad", [128, NG, 128], BF16)
    for jj in range(2):
        for gl in range(8):
            gg = jj * 8 + gl
            ms = gmask[:, gl:gl + 1]
            sl = slice(jj * 64, (jj + 1) * 64)
            ts(P, "dve", W1[:, gg, 0:64], bbre[:, sl], ms, None, ALU.mult, None, [bbre, gmask], [W1])
            ts(P, "dve", W1[:, gg, 64:128], bbim[:, sl], ms, None, ALU.mult, None, [bbim, gmask], [W1])
            ts(P, "dve", W2[:, gg, 0:64], nbbim[:, sl], ms, None, ALU.mult, None, [nbbim, gmask], [W2])
            ts(P, "dve", W2[:, gg, 64:128], bbre[:, sl], ms, None, ALU.mult, None, [bbre, gmask], [W2])

    ls_ = {n: ld("s_" + n, dS[n], NG) for n in names_S}
    ds = derive(ls_["lamre_S"], ls_["lamim_S"], ls_["logdt_S"], NG, "S")
    rho = ds["mag"]
    iota = ld("iota", iota_d, TS + 1)
    COS = P.sb("COS", [128, NG, TS + 1], F32)
    SIN = P.sb("SIN", [128, NG, TS + 1], F32)
    ARG = P.sb("ARG", [128, NG, TS + 1], F32)
    TMP = P.sb("TMPA", [128, NG, TS + 1], F32)
    Gg = [P.sb("Gg%d" % i, [128, TS], F32) for i in range(NG)]
    GT = [P.sb("GT%d" % i, [128, NG], F32) for i in range(2)]
    for gg in range(NG):
        ts(P, "dve", ARG[:, gg, :], iota[:, :], ds["thr"][:, gg:gg + 1], None, ALU.mult, None, [iota, ds["thr"]], [ARG])
    range_reduce(P, "dve", SIN[:, :, :], ARG[:, :, :], (TMP[:, :, :], TMP), [ARG], [SIN])
    range_reduce(P, "dve", COS[:, :, :], ARG[:, :, :], (TMP[:, :, :], TMP), [ARG], [COS], shift=PI / 2)
    act(P, SIN[:, :, :], SIN[:, :, :], AF.Sin, [SIN], [SIN])
    act(P, COS[:, :, :], COS[:, :, :], AF.Sin, [COS], [COS])
    c1 = ld("c1", c1_d, NG * 16)
    c2 = ld("c2", c2_d, NG * 16)
    sgn = ld("sgn", sgn_d, 1)
    L1 = P.sb("L1pad", [128, NG, 128], BF16)
    L2 = P.sb("L2pad", [128, NG, 128], BF16)
    P.op("pool", lambda E: E.memset(L1[:, :, :], 0.0), writes=[L1])
    P.op("pool", lambda E: E.memset(L2[:, :, :], 0.0), writes=[L2])
    for gg in range(NG):
        gl = gg % 8
        ts(P, "dve", L1[:, gg, gl * 16:(gl + 1) * 16], c1[:, gg * 16:(gg + 1) * 16], sgn[:, 0:1], None, ALU.mult, None, [c1, sgn], [L1])
        ts(P, "dve", L2[:, gg, gl * 16:(gl + 1) * 16], c2[:, gg * 16:(gg + 1) * 16], -1.0, None, ALU.mult, None, [c2], [L2])
    Jm = P.sb("Jm", [128, 128], F32)
    P.dma("sp", Jm[:, :], J_d.t[:, :], reads=[J_d], writes=[Jm])

    ust = Rot(P, "ust", 3, [128, 2, TS], F32)
    ubf = Rot(P, "ubf", 3, [128, 2, TS], BF16)
    t1r = Rot(P, "t1r", 4, [128, TS], F32)
    t2r = Rot(P, "t2r", 4, [128, TS], F32)
    xtr = Rot(P, "xtr", 4, [128, TS], F32)
    h1r = Rot(P, "h1r", 4, [128, TS], BF16)
    h2r = Rot(P, "h2r", 4, [128, TS], BF16)
    S0 = [P.sb("S0_%d" % i, [128, NG], F32) for i in range(2)]
    ys = Rot(P, "ys", 3, [128, 2, TS], BF16)
    P.op("pool", lambda E: E.memset(S0[0][:, :], 0.0), writes=[S0[0]])
    nch = SEQ // TS
    units = []
    chunk_res = {}
    for ch in range(nch):
        for jj in range(2):
            for gl in range(8):
                units.append(dict(ch=ch, jj=jj, gl=gl, gg=jj * 8 + gl))
    nu = len(units)

    def chunk_setup(ch):
        c0 = ch * TS
        us = ust.get()
        ub = ubf.get()
        P.dma("sp", us[:, :, :], uT.t[:, c0:c0 + TS].rearrange("(j p) t -> p j t", p=128), reads=[uT], writes=[us])
        cp(P, "act", ub[:, :, :], us[:, :, :], [us], [ub])
        chunk_res[ch] = dict(us=us, ub=ub, yo=ys.get())

    def s1(i):
        U = units[i]
        ch, jj, gg = U["ch"], U["jj"], U["gg"]
        if jj == 0 and U["gl"] == 0:
            chunk_setup(ch)
        ub = chunk_res[ch]["ub"]
        p1 = P.bank()
        p2 = P.bank()
        P.op("pe", lambda E, p1=p1, gg=gg, ub=ub, jj=jj: E.matmul(p1[:, 0:TS], lhsT=W1[:, gg, :], rhs=ub[:, jj, :], start=True, stop=True), reads=[W1, ub], writes=[p1])
        P.op("pe", lambda E, p2=p2, gg=gg, ub=ub, jj=jj: E.matmul(p2[:, 0:TS], lhsT=W2[:, gg, :], rhs=ub[:, jj, :], start=True, stop=True), reads=[W2, ub], writes=[p2])
        t1 = t1r.get()
        t2 = t2r.get()
        xt = xtr.get()
        tt(P, "dve", t1[:, :], p1[:, 0:TS], COS[:, gg, 1:TS + 1], ALU.mult, [p1, COS], [t1])
        tt(P, "dve", t2[:, :], p2[:, 0:TS], SIN[:, gg, 1:TS + 1], ALU.mult, [p2, SIN], [t2])
        tt(P, "pool", xt[:, :], t1[:, :], t2[:, :], ALU.subtract, [t1, t2], [xt])
        U["xt"] = xt

    def s2(i):
        U = units[i]
        ch, gg = U["ch"], U["gg"]
        gt, s0 = GT[ch % 2], S0[ch % 2]
        xt = U["xt"]
        G = Gg[gg]
        P.op("dve", lambda E, G=G, gg=gg, xt=xt, s0=s0: E.tensor_tensor_scan(
            out=G[:, :], data0=rho[:, gg:gg + 1].to_broadcast([128, TS]), data1=xt[:, :],
            initial=s0[:, gg:gg + 1], op0=ALU.mult, op1=ALU.add), reads=[rho, xt, s0], writes=[G])
        if ch + 1 < nch:
            cp(P, "act", gt[:, gg:gg + 1], G[:, TS - 1:TS], [G], [gt])
        h1 = h1r.get()
        h2 = h2r.get()
        tt(P, "dve", h1[:, :], G[:, :], COS[:, gg, 1:TS + 1], ALU.mult, [G, COS], [h1])
        tt(P, "pool", h2[:, :], G[:, :], SIN[:, gg, 1:TS + 1], ALU.mult, [G, SIN], [h2])
        U["h1"], U["h2"] = h1, h2
        if gg == NG - 1 and ch + 1 < nch:
            s1_ = S0[(ch + 1) % 2]
            P.op("pe", lambda E, gt=gt: E.matmul(igb[:, 0:NG], lhsT=Jm[:, :], rhs=gt[:, :], start=True, stop=True), reads=[Jm, gt], writes=[igb])
            ta = P.sb("bta%d" % ch, [128, NG], F32)
            tb = P.sb("btb%d" % ch, [128, NG], F32)
            tt(P, "dve", ta[:, :], gt[:, :], COS[:, :, TS], ALU.mult, [gt, COS], [ta])
            tt(P, "dve", tb[:, :], igb[:, 0:NG], SIN[:, :, TS], ALU.mult, [igb, SIN], [tb])
            tt(P, "dve", s1_[:, :], ta[:, :], tb[:, :], ALU.add, [ta, tb], [s1_])

    def s3(i):
        U = units[i]
        ch, jj, gl, gg = U["ch"], U["jj"], U["gl"], U["gg"]
        yb = ybank[jj]
        h1, h2 = U["h1"], U["h2"]
        P.op("pe", lambda E, yb=yb, gg=gg, h1=h1, gl=gl: E.matmul(yb[:, 0:TS], lhsT=L1[:, gg, :], rhs=h1[:, :], start=(gl == 0), stop=False), reads=[L1, h1], writes=[yb])
        P.op("pe", lambda E, yb=yb, gg=gg, h2=h2, gl=gl: E.matmul(yb[:, 0:TS], lhsT=L2[:, gg, :], rhs=h2[:, :], start=False, stop=(gl == 7)), reads=[L2, h2], writes=[yb])
        if gl == 7:
            cr = chunk_res[ch]
            yo, us = cr["yo"], cr["us"]
            P.op("dve", lambda E, yo=yo, us=us, yb=yb, jj=jj: E.scalar_tensor_tensor(
                out=yo[:, jj, :], in0=us[:, jj, :], scalar=dsk[:, jj:jj + 1], in1=yb[:, 0:TS], op0=ALU.mult, op1=ALU.add),
                reads=[us, dsk, yb], writes=[yo])
            if jj == 1:
                c0 = ch * TS
                ysrc = dr["y_src"][c0 // 2048]
                P.dma("act", ysrc.t[:, c0 % 2048:c0 % 2048 + TS].rearrange("(j p) t -> p j t", p=128), yo[:, :, :], reads=[yo], writes=[ysrc])
                if (c0 + TS) % 2048 == 0:
                    P.coll("AllGather", ysrc, dr["y_all"][c0 // 2048], GROUPS)

    for j in range(nu + 2):
        if j < nu:
            s1(j)
        if 0 <= j - 1 < nu:
            s2(j - 1)
        if 0 <= j - 2 < nu:
            s3(j - 2)
    P.wait_all("act", dr["y_all"])
    P.emit()


def gelu_tanh(P, dn, out_bf, y, reads):
    s = dn.tmp.get()
    s2 = dn.tmp.get()
    act(P, s[:, :], y, AF.Square, reads, [s])
    ts(P, "dve", s[:, :], s[:, :], 0.044715, 1.0, ALU.mult, ALU.add, [s], [s])
    tt(P, "dve", s2[:, :], s[:, :], y, ALU.mult, [s] + list(reads), [s2])
    act(P, s2[:, :], s2[:, :], AF.Sigmoid, [s2], [s2], scale=1.5957691216057308)
    return s2


def ple(P, dn, Wpg, Wpp, Xb, Pb, X1):
    for m in range(8):
        wg = dn.load_w(Wpg, m, 8)
        wp = dn.load_w(Wpp, m, 2)
        for n in range(dn.T // NT):
            pg = P.bank()
            pp = P.bank()
            for kc in range(8):
                P.op("pe", lambda E, pg=pg, wg=wg, kc=kc, n=n: E.matmul(
                    pg[:, :], lhsT=wg[:, kc, :], rhs=Xb[:, kc, n * NT:(n + 1) * NT], start=(kc == 0), stop=(kc == 7)),
                    reads=[wg, Xb], writes=[pg])
            for kc in range(2):
                P.op("pe", lambda E, pp=pp, wp=wp, kc=kc, n=n: E.matmul(
                    pp[:, :], lhsT=wp[:, kc, :], rhs=Pb[:, kc, n * NT:(n + 1) * NT], start=(kc == 0), stop=(kc == 1)),
                    reads=[wp, Pb], writes=[pp])
            sg = dn.tmp.get()
            act(P, sg[:, :], pg[:, :], AF.Sigmoid, [pg], [sg])
            t = dn.tmp.get()
            tt(P, "dve", t[:, :], sg[:, :], pp[:, :], ALU.mult, [sg, pp], [t])
            tt(P, "pool", X1[:, m, n * NT:(n + 1) * NT], X1[:, m, n * NT:(n + 1) * NT], t[:, :], ALU.add, [X1, t], [X1])


NH = 4
NQT = SEQ // NT


def phase_D(nc, dr):
    P = Prog(nc)
    qT, kT, vT, tri_d = dr["qT_cs"], dr["kT_cs"], dr["vT_cs"], dr["ntri"]
    ident = P.sb("ident", [128, 128], BF16)
    P.op("pool", lambda E: E.memset(ident[:, :], 0.0), writes=[ident])
    P.op("pool", lambda E: E.affine_select(out=ident[:, :], in_=ident[:, :], pattern=[[-1, 128]], compare_op=ALU.not_equal,
                                           fill=1.0, base=0, channel_multiplier=1), reads=[ident], writes=[ident])
    vtb = Rot(P, "vtb", 3, [64, 2048], BF16)
    tpb = P.ps("tpb", [128, NT], BF16)
    zb = [P.ps("zb%d" % i, [128, NT], F32) for i in range(4)]
    ob = [P.ps("ob%d" % i, [64, NT], F32) for i in range(2)]
    st = P.sb("tri_st", [128, 128], F32)
    P.dma("sp", st[:, :], tri_d.t[:, :], reads=[tri_d], writes=[st])
    ntri = P.sb("ntri", [128, 128], BF16)
    cp(P, "dve", ntri[:, :], st[:, :], [st], [ntri])
    negm = P.sb("negm", [128, 4, NT], BF16)
    P.op("pool", lambda E: E.memset(negm[:, :, :], 0.0), writes=[negm])
    for d_ in range(4):
        P.op("pool", lambda E, d_=d_: E.affine_select(
            out=negm[:, d_, :], in_=negm[:, d_, :], pattern=[[1, NT]], compare_op=ALU.is_gt, fill=-30000.0, base=-128 * d_, channel_multiplier=-1),
            reads=[negm], writes=[negm])
    nones = P.sb("nones", [128, 128], BF16)
    P.op("pool", lambda E: E.memset(nones[:, :], -1.0), writes=[nones])
    Qb = [P.sb("Qb%d" % i, [128, SEQ], BF16) for i in range(2)]
    Kb = [P.sb("Kb%d" % i, [128, SEQ], BF16) for i in range(2)]
    Vb = [P.sb("Vb%d" % h, [128, 64 * 64], BF16) for h in range(NH)]
    er = Rot(P, "er", 3, [128, NT], F32)
    spr = Rot(P, "spr", 4, [128, NT], BF16)
    wr = Rot(P, "wr", 4, [128, NT], BF16)
    racc = Rot(P, "racc", 3, [128, NT], BF16)
    ost = Rot(P, "ost", 2, [64, NT], BF16)
    for pr in range(2):
        for c in range(SEQ // 2048):
            P.dma("sp", Qb[pr][:, c * 2048:(c + 1) * 2048], qT.t[pr * 128:(pr + 1) * 128, c * 2048:(c + 1) * 2048], reads=[qT], writes=[Qb[pr]])
            P.dma("sp", Kb[pr][:, c * 2048:(c + 1) * 2048], kT.t[pr * 128:(pr + 1) * 128, c * 2048:(c + 1) * 2048], reads=[kT], writes=[Kb[pr]])

    def load_v(h):
        for c in range(SEQ // 2048):
            vb_ = vtb.get()
            P.dma("sp", vb_[:, :], vT.t[h * 64:(h + 1) * 64, c * 2048:(c + 1) * 2048], reads=[vT], writes=[vb_])
            for k8 in range(2):
                for j in range(8):
                    blk = k8 * 8 + j
                    P.op("pe", lambda E, vb_=vb_, j=j, blk=blk: E.transpose(
                        out=tpb[:, j * 64:(j + 1) * 64], in_=vb_[:, blk * 128:(blk + 1) * 128], identity=ident[0:64, 0:64]),
                        reads=[vb_, ident], writes=[tpb])
                kb0 = c * 16 + k8 * 8
                cp(P, "dve", Vb[h][:, kb0 * 64:(kb0 + 8) * 64], tpb[:, :], [tpb], [Vb[h]])
    load_v(0)
    blocks = []
    for h in range(NH):
        for qt in range(NQT):
            kbs = list(range(4 * qt + 3, -1, -1))
            for idx, kb in enumerate(kbs):
                blocks.append(dict(h=h, qt=qt, kb=kb, idx=idx, n=len(kbs), g=h * NQT + qt))
    nb = len(blocks)

    def operands(B):
        hp, pr = B["h"] % 2, B["h"] // 2
        ksl = Kb[pr][hp * 64:(hp + 1) * 64, B["kb"] * 128:(B["kb"] + 1) * 128]
        qsl = Qb[pr][hp * 64:(hp + 1) * 64, B["qt"] * NT:(B["qt"] + 1) * NT]
        return ksl, qsl, Kb[pr], Qb[pr]

    def mask(t, B):
        base = B["qt"] * NT - 128 * B["kb"]
        P.op("pool", lambda E, t=t, base=base: E.affine_select(
            out=t[:, :], in_=t[:, :], pattern=[[1, NT]], compare_op=ALU.is_gt, fill=0.0, base=base, channel_multiplier=-1),
            reads=[t], writes=[t])

    def stage1a(i):
        B = blocks[i]
        ksl, qsl, kres, qres = operands(B)
        if B["qt"] == 2 and B["idx"] == 0 and B["h"] + 1 < NH:
            load_v(B["h"] + 1)
        z = zb[i % 4]
        P.op("pe", lambda E, z=z, ksl=ksl, qsl=qsl: E.matmul(z[:, :], lhsT=ksl, rhs=qsl, start=True, stop=False), reads=[kres, qres], writes=[z])
        if B["kb"] >= 4 * B["qt"]:
            d_ = B["kb"] - 4 * B["qt"]
            P.op("pe", lambda E, z=z, d_=d_: E.matmul(z[:, :], lhsT=ident[:, :], rhs=negm[:, d_, :], start=False, stop=False), reads=[ident, negm], writes=[z])
        e = er.get()
        act(P, e[:, :], z[:, :], AF.Exp, [z], [e])
        B["e"] = e

    def stage1b(i):
        B = blocks[i]
        e = B["e"]
        sp = spr.get()
        P.op("act", lambda E, sp=sp, e=e: E.activation(out=sp[:, :], in_=e[:, :], func=AF.Ln, scale=1.0, bias=1.0),
             reads=[e], writes=[sp], skip_same=True)
        B["sp"] = sp

    def stage2(i):
        B = blocks[i]
        b = zb[i % 4]
        sp = B["sp"]
        first = B["idx"] == 0
        ra_prev = None if first else blocks[i - 1]["ra"]
        P.op("pe", lambda E, b=b, sp=sp, first=first: E.matmul(b[:, :], lhsT=ntri[:, :], rhs=sp[:, :], start=False, stop=first), reads=[ntri, sp], writes=[b])
        if not first:
            P.op("pe", lambda E, b=b, ra=ra_prev: E.matmul(b[:, :], lhsT=nones[:, :], rhs=ra[:, :], start=False, stop=True), reads=[nones, ra_prev], writes=[b])
        w = wr.get()
        act(P, w[:, :], b[:, :], AF.Exp, [b], [w])
        B["w"] = w
        if B["idx"] + 1 < B["n"]:
            ra = racc.get()
            if first:
                cp(P, "dve", ra[:, :], sp[:, :], [sp], [ra])
            else:
                tt(P, "dve", ra[:, :], ra_prev[:, :], sp[:, :], ALU.add, [ra_prev, sp], [ra])
            B["ra"] = ra

    def stage3(i):
        B = blocks[i]
        o_ps = ob[B["g"] % 2]
        w = B["w"]
        h, kb = B["h"], B["kb"]
        P.op("pe", lambda E, o_ps=o_ps, w=w, h=h, kb=kb, B=B: E.matmul(
            o_ps[:, :], lhsT=Vb[h][:, kb * 64:(kb + 1) * 64], rhs=w[:, :], start=(B["idx"] == 0), stop=(B["idx"] == B["n"] - 1)),
            reads=[Vb[h], w], writes=[o_ps])
        if B["idx"] == B["n"] - 1:
            o = ost.get()
            cp(P, "dve", o[:, :], o_ps[:, :], [o_ps], [o])
            q0 = B["qt"] * NT
            osrc = dr["o_src"][q0 // 2048]
            P.dma("act", osrc.t[h * 64:(h + 1) * 64, q0 % 2048:q0 % 2048 + NT], o[:, :], reads=[o], writes=[osrc])
            if h == NH - 1 and (q0 + NT) % 2048 == 0:
                P.coll("AllGather", osrc, dr["o_all"][q0 // 2048], GROUPS)

    for j in range(nb + 2):
        if j < nb:
            stage1a(j)
            stage1b(j)
        if 0 <= j - 1 < nb:
            stage2(j - 1)
        if 0 <= j - 2 < nb:
            stage3(j - 2)
    P.wait_all("act", dr["o_all"])
    P.emit()


GROUPS = [[0, 1, 2, 3], [4, 5, 6, 7]]
TOKC = 2048
TP = 1024


def proj_pass(P, dn, src_fn, xs, rs, jobs, col0):
    for kc in range(8):
        srcs = src_fn(kc)
        if isinstance(srcs, tuple):
            P.dma("sp", xs[:, kc, :], srcs[1], reads=[srcs[0]], writes=[xs])
        else:
            w_ = TP // len(srcs)
            for i_, (sr_, sa_) in enumerate(srcs):
                P.dma("sp", xs[:, kc, i_ * w_:(i_ + 1) * w_], sa_, reads=[sr_], writes=[xs])
    dn.rstd(xs, rs)
    done = {}
    for job in jobs:
        xb, gain, wbs, MC, dst = job[:5]
        oscale = job[5] if len(job) > 5 else None
        if id(xb) not in done:
            done[id(xb)] = 1
            for kc in range(8):
                scale_cast(P, kc, xb[:, kc, :], xs[:, kc, :], gain[:, kc:kc + 1], [xs, gain], [xb])

        def epi(m, n, ps, dst=dst, oscale=oscale):
            if oscale is None:
                o = dn.ost.get()
                tt(P, "dve", o[:, :], ps[:, :], rs[:, n * NT:(n + 1) * NT], ALU.mult, [ps, rs], [o])
            else:
                o = dn.ostb.get()
                P.op("dve", lambda E, o=o, ps=ps, n=n: E.scalar_tensor_tensor(
                    out=o[:, :], in0=ps[:, :], scalar=oscale, in1=rs[:, n * NT:(n + 1) * NT], op0=ALU.mult, op1=ALU.mult),
                    reads=[ps, rs], writes=[o])
            dn.store(dst, m, n, col0, o, o[:, :])
        dn.linear(None, 8, MC, xb, epi, wbs=wbs)


def phase_A(nc, dr):
    P = Prog(nc)
    P.mk_banks(6)
    dn = Dense(P, TP)
    g = colvec(P, "g_pre", dr["g_pre"], 8)
    xsr = Rot(P, "xs", 2, [128, 8, TP], F32)
    xb = P.sb("xb", [128, 8, TP], BF16)
    rs = P.sb("rs", [128, TP], F32)
    xT = dr["xT_full"]
    wbs = dn.prep_w("wu", dr["w_in_u"], 8, 2)
    for pa in range(SEQ // TP):
        t0 = pa * TP
        proj_pass(P, dn, lambda kc, t0=t0: (xT, xT.t[kc * 128:(kc + 1) * 128, t0:t0 + TP]), xsr.get(), rs,
                  [(xb, g, wbs, 2, dr["uT_cs"])], t0)
    P.wait_all("pool", [dr["uT_cs"]])
    P.emit()


def mk_select(P, dn, sel):
    ident = P.sb("identf", [128, 128], F32)
    P.op("pool", lambda E: E.memset(ident[:, :], 0.0), writes=[ident])
    P.op("pool", lambda E: E.affine_select(out=ident[:, :], in_=ident[:, :], pattern=[[-1, 128]], compare_op=ALU.not_equal,
                                           fill=1.0, base=0, channel_multiplier=1), reads=[ident], writes=[ident])
    mI = P.sb("mI", [128, 4, 128], BF16)
    for s_ in range(4):
        ts(P, "dve", mI[:, s_, :], ident[:, :], sel[:, s_:s_ + 1], None, ALU.mult, None, [ident, sel], [mI])
    dn.mI = mI


def select4(P, dn, src_fn, sel, dt):
    ps = P.bank()
    for s in range(4):
        it = dn.istb.get()
        sres, sap = src_fn(s)
        P.dma("sp", it[:, :], sap, reads=[sres], writes=[it])
        P.op("pe", lambda E, ps=ps, it=it, s=s: E.matmul(ps[:, :], lhsT=dn.mI[:, s, :], rhs=it[:, :], start=(s == 0), stop=(s == 3)),
             reads=[dn.mI, it], writes=[ps])
    acc = dn.tmp.get()
    act(P, acc[:, :], ps[:, :], AF.Copy, [ps], [acc])
    return acc


def phase_C(nc, dr):
    P = Prog(nc)
    P.mk_banks(7)
    dn = Dense(P, TP)
    dn.istb = Rot(P, "istb", 8, [128, NT], BF16)
    sel = colvec(P, "sel", dr["sel"], 4)
    mk_select(P, dn, sel)
    gpre = colvec(P, "g_pre", dr["g_pre"], 8)
    vl = []
    for i in range(4):
        t_ = P.sb("vec%d" % i, [128, 8], F32)
        P.dma("sp", t_[:, :], dr["vecsC"].t[:, i * 8:(i + 1) * 8], reads=[dr["vecsC"]], writes=[t_])
        vl.append(t_)
    bglu, gpost, gbpre, _unused = vl
    xT, y_all, pT = dr["xT_own"], dr["y_all"], dr["p0T"]
    Gb = P.sb("Gb", [128, 8, TP], BF16)
    SGb = P.sb("SGb", [128, 8, TP], BF16)
    Y2b = P.sb("Y2b", [128, 8, TP], BF16)
    X1 = P.sb("X1", [128, 8, TP], F32)
    Pb = P.sb("Pb", [128, 2, TP], BF16)
    rs = P.sb("rs", [128, TP], F32)
    for pa in range(TOKC // TP):
        t0 = pa * TP
        for kc in range(8):
            P.dma("sp", X1[:, kc, :], xT.t[kc * 128:(kc + 1) * 128, t0:t0 + TP], reads=[xT], writes=[X1])
        dn.rstd(X1, rs)
        for kc in range(8):
            scale_cast(P, kc, Y2b[:, kc, :], X1[:, kc, :], gpre[:, kc:kc + 1], [X1, gpre], [Y2b])

        def epi_gate(m, n, ps):
            t = dn.tmp.get()
            tt(P, "dve", t[:, :], ps[:, :], rs[:, n * NT:(n + 1) * NT], ALU.mult, [ps, rs], [t])
            act(P, SGb[:, m, n * NT:(n + 1) * NT], t[:, :], AF.Silu, [t], [SGb])
        dn.linear(dr["w_in_g"], 8, 8, Y2b, epi_gate)
        for kc in range(8):
            for n in range(TP // NT):
                def ysrc_fn(s, n=n, kc=kc):
                    g0 = s * TOKC + t0 + n * NT
                    ya = y_all[g0 // 2048]
                    return ya, ya.t[kc * 128:(kc + 1) * 128, g0 % 2048:g0 % 2048 + NT]
                y = select4(P, dn, ysrc_fn, sel, BF16)
                s2 = gelu_tanh(P, dn, None, y[:, :], [y])
                tt(P, "pool", Gb[:, kc, n * NT:(n + 1) * NT], s2[:, :], y[:, :], ALU.mult, [s2, y], [Gb])

        def epi_glu(m, n, ps):
            sg = dn.tmp.get()
            act(P, sg[:, :], ps[:, :], AF.Sigmoid, [ps, bglu], [sg], bias=bglu[:, m:m + 1])
            t = dn.tmp.get()
            tt(P, "dve", t[:, :], sg[:, :], Gb[:, m, n * NT:(n + 1) * NT], ALU.mult, [sg, Gb], [t])
            tt(P, "pool", Y2b[:, m, n * NT:(n + 1) * NT], t[:, :], SGb[:, m, n * NT:(n + 1) * NT], ALU.mult, [t, SGb], [Y2b])
        dn.linear(dr["w_glu"], 8, 8, Gb, epi_glu)

        def epi_out(m, n, ps):
            act(P, X1[:, m, n * NT:(n + 1) * NT], ps[:, :], AF.Copy, [ps], [X1])
        dn.linear(dr["w_out0"], 8, 8, Y2b, epi_out)
        dn.rstd(X1, rs)
        for kc in range(8):
            for n in range(TP // NT):
                c0 = t0 + n * NT
                ix = dn.ist.get()
                P.dma("sp", ix[:, :], xT.t[kc * 128:(kc + 1) * 128, c0:c0 + NT], reads=[xT], writes=[ix])
                t = dn.tmp.get()
                P.op("dve", lambda E, t=t, kc=kc, n=n: E.scalar_tensor_tensor(
                    out=t[:, :], in0=X1[:, kc, n * NT:(n + 1) * NT], scalar=gpost[:, kc:kc + 1], in1=rs[:, n * NT:(n + 1) * NT],
                    op0=ALU.mult, op1=ALU.mult), reads=[X1, gpost, rs], writes=[t])
                tt(P, "pool", X1[:, kc, n * NT:(n + 1) * NT], t[:, :], ix[:, :], ALU.add, [t, ix], [X1])
                cp(P, "act", Gb[:, kc, n * NT:(n + 1) * NT], X1[:, kc, n * NT:(n + 1) * NT], [X1], [Gb])
        for kc in range(2):
            for n in range(TP // NT):
                c0 = t0 + n * NT
                ip = dn.ist.get()
                P.dma("sp", ip[:, :], pT.t[kc * 128:(kc + 1) * 128, c0:c0 + NT], reads=[pT], writes=[ip])
                cp(P, "dve", Pb[:, kc, n * NT:(n + 1) * NT], ip[:, :], [ip], [Pb])
        ple(P, dn, dr["w_pg0"], dr["w_pp0"], Gb, Pb, X1)
        dn.rstd(X1, rs)
        for kc in range(8):
            cp(P, "dve", Y2b[:, kc, :], X1[:, kc, :], [X1], [Y2b])
            scale_cast(P, kc + 1, SGb[:, kc, :], X1[:, kc, :], gbpre[:, kc:kc + 1], [X1, gbpre], [SGb])
            P.dma("pool", dr["x1_own"].t[kc * 128:(kc + 1) * 128, t0:t0 + TP], X1[:, kc, :], reads=[X1], writes=[dr["x1_own"]])
            for n in range(TP // NT):
                xsrc = dr["x1_src"][(t0 + n * NT) // NT]
                P.dma("pool", xsrc.t[kc * 128:(kc + 1) * 128, :], Y2b[:, kc, n * NT:(n + 1) * NT], reads=[Y2b], writes=[xsrc])

        def epi_g1(m, n, ps):
            o = dn.ost.get()
            tt(P, "dve", o[:, :], ps[:, :], rs[:, n * NT:(n + 1) * NT], ALU.mult, [ps, rs], [o])
            dn.store(dr["g1T"], m, n, t0, o, o[:, :])
        for k in range(t0 // NT, (t0 + TP) // NT):
            P.coll("AllGather", dr["x1_src"][k], dr["x1_all"][k], GROUPS)
        dn.linear(dr["w_bin_g"], 8, 8, SGb, epi_g1)
    P.wait_all("pool", dr["x1_all"] + [dr["x1_own"], dr["g1T"]])
    P.emit()


def phase_QKV(nc, dr):
    P = Prog(nc)
    P.mk_banks(6)
    dn = Dense(P, TP)
    gkv = colvec(P, "g_kv", dr["g_kv"], 8)
    gbpre = colvec(P, "g_bpre", dr["g_bpre"], 8)
    xsr = Rot(P, "xs", 2, [128, 8, TP], BF16)
    xq = P.sb("xq", [128, 8, TP], BF16)
    xk = P.sb("xk", [128, 8, TP], BF16)
    rs = P.sb("rs", [128, TP], F32)
    xa = dr["x1_all"]
    wq = dn.prep_w("wq", dr["w_q"], 8, 2)
    wk = dn.prep_w("wk", dr["w_k"], 8, 2)
    wv = dn.prep_w("wv", dr["w_v"], 8, 2)
    for pa in range(SEQ // TP):
        t0 = pa * TP
        s, tl = t0 // TOKC, t0 % TOKC
        proj_pass(P, dn, lambda kc, s=s, tl=tl: [(xa[(tl + h_ * NT) // NT], xa[(tl + h_ * NT) // NT].t[s * D + kc * 128:s * D + (kc + 1) * 128, :]) for h_ in range(TP // NT)], xsr.get(), rs,
                  [(xq, gbpre, wq, 2, dr["qT_cs"], 1.0), (xk, gkv, wk, 2, dr["kT_cs"], 0.125), (xk, gkv, wv, 2, dr["vT_cs"], 1.0)], t0)
    P.wait_all("pool", [dr["qT_cs"], dr["kT_cs"], dr["vT_cs"]])
    P.emit()


def phase_E(nc, dr):
    P = Prog(nc)
    P.mk_banks(7)
    dn = Dense(P, TP)
    dn.istb = Rot(P, "istb", 8, [128, NT], BF16)
    sel = colvec(P, "sel", dr["sel"], 4)
    mk_select(P, dn, sel)
    gpost = colvec(P, "g_bpost", dr["g_bpost"], 8)
    Ob = P.sb("Ob", [128, 8, TP], BF16)
    Xb = P.sb("Xb", [128, 8, TP], BF16)
    X1 = P.sb("X1", [128, 8, TP], F32)
    Pb = P.sb("Pb", [128, 2, TP], BF16)
    rs = P.sb("rs", [128, TP], F32)
    x1T, gT, pT, outT = dr["x1_own"], dr["g1T"], dr["p1T"], dr["outT"]
    for pa in range(TOKC // TP):
        t0 = pa * TP
        for kc in range(8):
            for n in range(TP // NT):
                c0 = t0 + n * NT
                def osrc_fn(s, n=n, kc=kc):
                    g0 = s * TOKC + t0 + n * NT
                    oa = dr["o_all"][g0 // 2048]
                    return oa, oa.t[kc * 128:(kc + 1) * 128, g0 % 2048:g0 % 2048 + NT]
                o = select4(P, dn, osrc_fn, sel, BF16)
                ig = dn.ist.get()
                P.dma("sp", ig[:, :], gT.t[kc * 128:(kc + 1) * 128, c0:c0 + NT], reads=[gT], writes=[ig])
                sg = dn.tmp.get()
                act(P, sg[:, :], ig[:, :], AF.Silu, [ig], [sg])
                tt(P, "pool", Ob[:, kc, n * NT:(n + 1) * NT], sg[:, :], o[:, :], ALU.mult, [sg, o], [Ob])

        def epi_out(m, n, ps):
            act(P, X1[:, m, n * NT:(n + 1) * NT], ps[:, :], AF.Copy, [ps], [X1])
        dn.linear(dr["w_out1"], 8, 8, Ob, epi_out)
        dn.rstd(X1, rs)
        for kc in range(8):
            for n in range(TP // NT):
                c0 = t0 + n * NT
                ix = dn.ist.get()
                P.dma("sp", ix[:, :], x1T.t[kc * 128:(kc + 1) * 128, c0:c0 + NT], reads=[x1T], writes=[ix])
                t = dn.tmp.get()
                P.op("dve", lambda E, t=t, kc=kc, n=n: E.scalar_tensor_tensor(
                    out=t[:, :], in0=X1[:, kc, n * NT:(n + 1) * NT], scalar=gpost[:, kc:kc + 1], in1=rs[:, n * NT:(n + 1) * NT],
                    op0=ALU.mult, op1=ALU.mult), reads=[X1, gpost, rs], writes=[t])
                tt(P, "pool", X1[:, kc, n * NT:(n + 1) * NT], t[:, :], ix[:, :], ALU.add, [t, ix], [X1])
                cp(P, "act", Xb[:, kc, n * NT:(n + 1) * NT], X1[:, kc, n * NT:(n + 1) * NT], [X1], [Xb])
        for kc in range(2):
            for n in range(TP // NT):
                c0 = t0 + n * NT
                ip = dn.ist.get()
                P.dma("sp", ip[:, :], pT.t[kc * 128:(kc + 1) * 128, c0:c0 + NT], reads=[pT], writes=[ip])
                cp(P, "dve", Pb[:, kc, n * NT:(n + 1) * NT], ip[:, :], [ip], [Pb])
        ple(P, dn, dr["w_pg1"], dr["w_pp1"], Xb, Pb, X1)
        for kc in range(8):
            P.dma("pool", outT.t[kc * 128:(kc + 1) * 128, t0:t0 + TP], X1[:, kc, :], reads=[X1], writes=[outT])
    P.wait_all("pool", [outT])
    P.emit()


IN_SPECS = {
    "xT_full": ([D, SEQ], F32), "xT_own": ([D, TOKC], F32), "p0T": ([256, TOKC], F32), "p1T": ([256, TOKC], F32),
    "sel": ([128, 4], F32), "g_pre": ([128, 8], F32), "w_in_u": ([D, 256], F32), "w_in_g": ([D, D], F32),
    "lamre_T": ([128, 128], F32), "lamim_T": ([128, 128], F32), "logdt_T": ([128, 128], F32), "bre_T": ([128, 128], F32),
    "bim_T": ([128, 128], F32), "lamre_S": ([128, NG], F32), "lamim_S": ([128, NG], F32), "logdt_S": ([128, NG], F32),
    "c1_S": ([128, NG * 16], F32), "c2_S": ([128, NG * 16], F32), "iota": ([128, TS + 1], F32), "gmask": ([128, 8], F32),
    "sgn": ([128, 1], F32), "Jm": ([128, 128], F32), "dskip_cs": ([128, 2], F32), "vecsC": ([128, 32], F32),
    "w_glu": ([D, D], F32), "w_out0": ([D, D], F32), "w_pg0": ([D, D], F32), "w_pp0": ([256, D], F32),
    "w_bin_g": ([D, D], F32), "g_kv": ([128, 8], F32), "g_bpre": ([128, 8], F32), "w_q": ([D, 256], F32),
    "w_k": ([D, 256], F32), "w_v": ([D, 256], F32), "ntri": ([128, 128], F32), "g_bpost": ([128, 8], F32),
    "w_out1": ([D, D], F32), "w_pg1": ([D, D], F32), "w_pp1": ([256, D], F32),
}
SCRATCH = {
    "uT_cs": ([256, SEQ], F32), "y_src": ([256, 2048], BF16, 4), "y_all": ([D, 2048], BF16, 4), "x1_own": ([D, TOKC], F32),
    "x1_src": ([D, NT], BF16, 4), "x1_all": ([4 * D, NT], BF16, 4), "g1T": ([D, TOKC], F32), "qT_cs": ([256, SEQ], BF16),
    "kT_cs": ([256, SEQ], BF16), "vT_cs": ([256, SEQ], BF16), "o_src": ([256, 2048], BF16, 4), "o_all": ([D, 2048], BF16, 4),
}


def build_fused():
    nc = bass.Bass("TRN2", target_bir_lowering=False)
    dr = {}
    for n, (shp, dt) in IN_SPECS.items():
        dr[n] = Res(n, nc.dram_tensor(n, list(shp), dt, kind="ExternalInput").ap())
    for n, spec in SCRATCH.items():
        shp, dt = spec[0], spec[1]
        if len(spec) == 3:
            dr[n] = [Res("%s%d" % (n, i), nc.dram_tensor("%s%d" % (n, i), list(shp), dt, kind="Internal").ap()) for i in range(spec[2])]
        else:
            dr[n] = Res(n, nc.dram_tensor(n, list(shp), dt, kind="Internal").ap())
    dr["outT"] = Res("outT", nc.dram_tensor("outT", [D, TOKC], F32, kind="ExternalOutput").ap())
    phase_A(nc, dr)
    phase_B(nc, dr)
    phase_C(nc, dr)
    phase_QKV(nc, dr)
    phase_D(nc, dr)
    phase_E(nc, dr)
    Prog.finish()
    return nc


def _f(a):
    return np.ascontiguousarray(np.asarray(a, dtype=np.float32))


def kernel(**inputs):
    inp = {k: np.asarray(v) for k, v in inputs.items()}
    x, p = inp["x"], inp["p"]
    lam_re, lam_im, log_dt = _f(inp["a_lam_re"][0]), _f(inp["a_lam_im"][0]), _f(inp["a_log_dt"][0])
    b_re, b_im, c_re, c_im = _f(inp["a_b_re"][0]), _f(inp["a_b_im"][0]), _f(inp["a_c_re"][0]), _f(inp["a_c_im"][0])
    iota = _f(np.broadcast_to(np.arange(TS + 1, dtype=np.float32), (128, TS + 1)))
    gmask = np.zeros((128, 8), np.float32)
    for gl in range(8):
        gmask[gl * 16:(gl + 1) * 16, gl] = 1.0
    sgn = np.ones((128, 1), np.float32)
    sgn[64:] = -1.0
    J = np.zeros((128, 128), np.float32)
    for q in range(64):
        J[64 + q, q] = -1.0
        J[q, 64 + q] = 1.0
    ntri = np.zeros((128, 128), np.float32)
    for j in range(128):
        ntri[j, :j + 1] = -1.0
    vecsC = np.zeros((128, 32), np.float32)
    for i, v in enumerate([inp["a_b_glu"][0], inp["a_norm_post"][0], inp["b_norm_pre"][0]]):
        vecsC[:, i * 8:(i + 1) * 8] = _cols(v)
    a_w_in, b_w_in, w_kv = _f(inp["a_w_in"][0]), _f(inp["b_w_in"][0]), _f(inp["w_kv"])
    common = {
        "g_pre": _cols(inp["a_norm_pre"][0]), "w_in_g": _f(a_w_in[:, D:]), "iota": iota, "gmask": gmask, "sgn": sgn, "Jm": J,
        "vecsC": vecsC, "w_glu": _f(inp["a_w_glu"][0]), "w_out0": _f(inp["a_w_out"][0]), "w_pg0": _f(inp["ple_w_gate"][0]),
        "w_pp0": _f(inp["ple_w_proj"][0]), "w_bin_g": _f(b_w_in[:, D:]), "g_kv": _cols(inp["kv_norm"]),
        "g_bpre": _cols(inp["b_norm_pre"][0]), "ntri": ntri, "g_bpost": _cols(inp["b_norm_post"][0]),
        "w_out1": _f(inp["b_w_out"][0]), "w_pg1": _f(inp["ple_w_gate"][1]), "w_pp1": _f(inp["ple_w_proj"][1]),
    }
    xT_full = [_f(np.asarray(x[b], np.float32).T) for b in range(2)]
    maps = []
    for c in range(NCORE):
        b, r = c // 4, c % 4
        gs = np.arange(16 * r, 16 * r + 16)
        tsl = slice(r * TOKC, (r + 1) * TOKC)
        csl = slice(256 * r, 256 * r + 256)
        m = dict(common)
        m["xT_full"] = xT_full[b]
        m["xT_own"] = _f(xT_full[b][:, tsl])
        m["p0T"] = _f(np.asarray(p[0, b, tsl, :], np.float32).T)
        m["p1T"] = _f(np.asarray(p[1, b, tsl, :], np.float32).T)
        sel = np.zeros((128, 4), np.float32)
        sel[:, r] = 1.0
        m["sel"] = sel
        m["w_in_u"] = _f(a_w_in[:, csl])
        m["dskip_cs"] = _cols(inp["a_d_skip"][0][csl])
        m["w_q"] = _f(b_w_in[:, csl])
        m["w_k"] = _f(w_kv[:, csl])
        m["w_v"] = _f(w_kv[:, D + 256 * r:D + 256 * r + 256])

        def lt_gp(a):
            t = a[gs].reshape(2, 8, 64)
            t = np.broadcast_to(t[:, :, None, :], (2, 8, 16, 64))
            return _f(t.transpose(1, 2, 0, 3).reshape(128, 128))

        def lt_b(a):
            t = a[gs].reshape(2, 8, 64, 16)
            return _f(t.transpose(1, 3, 0, 2).reshape(128, 128))

        def sp_gp(a):
            t = a[gs].T
            return _f(np.concatenate([t, t], axis=0))
        ldt = np.broadcast_to(log_dt[:, None], (64, 64))
        m["lamre_T"], m["lamim_T"], m["logdt_T"] = lt_gp(lam_re), lt_gp(lam_im), lt_gp(ldt)
        m["bre_T"], m["bim_T"] = lt_b(b_re), lt_b(b_im)
        m["lamre_S"], m["lamim_S"], m["logdt_S"] = sp_gp(lam_re), sp_gp(lam_im), sp_gp(ldt)
        cr = c_re[gs].transpose(2, 0, 1).reshape(64, 256)
        ci = c_im[gs].transpose(2, 0, 1).reshape(64, 256)
        m["c1_S"] = _f(np.concatenate([cr, ci], axis=0))
        m["c2_S"] = _f(np.concatenate([ci, cr], axis=0))
        maps.append(m)
    res = _run(build_fused(), maps)
    out = np.empty((2, SEQ, D), np.float32)
    for c in range(NCORE):
        b, r = c // 4, c % 4
        out[b, r * TOKC:(r + 1) * TOKC, :] = res[c]["outT"].T
    return out
```

```python
from contextlib import ExitStack
import numpy as np
import concourse.bass as bass
import concourse.mybir as mybir
from concourse.bass_utils import run_bass_kernel_spmd

F32 = mybir.dt.float32
BF16 = mybir.dt.bfloat16
AF = mybir.ActivationFunctionType
ALU = mybir.AluOpType

ENGS = ("pe", "act", "dve", "pool", "sp")
NCORE = 8
D = 1024
SEQ = 8192
NT = 512
EPS = 1e-6
PI = float(np.pi)


class Res:
    __slots__ = ("name", "w", "r", "dsem", "dcnt", "t")

    def __init__(self, name, t=None):
        self.name = name
        self.w = {}
        self.r = {}
        self.dsem = None
        self.dcnt = 0
        self.t = t

    def __getitem__(self, idx):
        return self.t[idx]


class Prog:
    _n = 0
    G = None

    def __init__(self, nc):
        Prog._n += 1
        self.pfx = "f%d_" % Prog._n
        self.nc = nc
        if Prog.G is None or Prog.G["nc"] is not nc:
            ges = ExitStack()
            Prog.G = {"nc": nc, "es": ges, "sems": {}, "cnt": {}}
            for e in ENGS:
                Prog.G["sems"][e] = ges.enter_context(nc.semaphore("s_" + e))
                Prog.G["cnt"][e] = 0
        G = Prog.G
        self.es = ExitStack()
        self.lists = {e: [] for e in ENGS}
        self.sems = G["sems"]
        self.cnt = G["cnt"]
        self.seen = {e: dict(self.cnt) for e in ENGS}
        self.nd = {"d": 0, "g": 0, "c": 0}
        self.banks = []
        self.bi = 0
        self.touched = {}

    @staticmethod
    def finish():
        if Prog.G is not None:
            Prog.G["es"].close()
            Prog.G = None

    def _newsem(self, ns="d"):
        key = "%s%d" % (ns, self.nd[ns])
        self.nd[ns] += 1
        if key not in self.sems:
            self.sems[key] = Prog.G["es"].enter_context(self.nc.semaphore("sd_" + key))
            self.cnt[key] = 0
        return key

    def sb(self, name, shape, dt):
        return Res(name, self.es.enter_context(self.nc.sbuf_tensor(self.pfx + "sb_" + name, list(shape), dt)))

    def ps(self, name, shape, dt=F32):
        return Res(name, self.es.enter_context(self.nc.psum_tensor(self.pfx + "ps_" + name, list(shape), dt)))

    def dram(self, name, shape, dt, kind="Internal"):
        return Res(name, self.nc.dram_tensor(name, list(shape), dt, kind=kind).ap())

    def mk_banks(self, n):
        self.banks = [self.ps("bank%d" % i, [128, NT], F32) for i in range(n)]

    def bank(self):
        b = self.banks[self.bi % len(self.banks)]
        self.bi += 1
        return b

    def _dsem(self, res, eng):
        ns = "g" if eng == "pool" else "d"
        if res.dsem is None:
            res.dsem = {}
        if ns not in res.dsem:
            res.dsem[ns] = self._newsem(ns)
        return res.dsem[ns]

    def _waits(self, eng, reads, writes, skip_same=False):
        for x_ in reads:
            self.touched[id(x_)] = x_
        for x_ in writes:
            self.touched[id(x_)] = x_
        deps = {}
        for r in reads:
            for k, v in r.w.items():
                if skip_same and k == eng:
                    continue
                if v > deps.get(k, 0):
                    deps[k] = v
        for w in writes:
            for k, v in w.w.items():
                if k != eng and v > deps.get(k, 0):
                    deps[k] = v
            for k, v in w.r.items():
                if k != eng and v > deps.get(k, 0):
                    deps[k] = v
        seen = self.seen[eng]
        for k, v in deps.items():
            if v > seen.get(k, 0):
                seen[k] = v
                sem = self.sems[k]
                self.lists[eng].append(lambda E, sem=sem, v=v: E.wait_ge(sem, v))

    def op(self, eng, fn, reads=(), writes=(), skip_same=False):
        self._waits(eng, reads, writes, skip_same)
        self.cnt[eng] += 1
        n = self.cnt[eng]
        sem = self.sems[eng]
        self.lists[eng].append(lambda E, fn=fn, sem=sem: fn(E).then_inc(sem, 1))
        for r in reads:
            r.r[eng] = n
        for w in writes:
            w.w[eng] = n

    def dma(self, eng, out_ap, in_ap, reads=(), writes=()):
        wres = writes[0]
        self._waits(eng, reads, writes)
        key = self._dsem(wres, eng)
        self.cnt[key] += 16
        v = self.cnt[key]
        sem = self.sems[key]
        self.lists[eng].append(
            lambda E, o=out_ap, i=in_ap, sem=sem: E.dma_start(out=o, in_=i).then_inc(sem, 16))
        for r in reads:
            r.r[key] = v
        wres.w[key] = v

    def coll(self, kind, src, dst, groups):
        self._waits("pool", [src], [dst])
        key = self._newsem("c")
        self.cnt[key] += 1
        v = self.cnt[key]
        sem = self.sems[key]
        self.lists["pool"].append(lambda E, sem=sem: E.collective_compute(
            kind, ALU.bypass, replica_groups=groups, ins=[src.t.opt()], outs=[dst.t.opt()]).then_inc(sem))
        src.r[key] = v
        dst.w[key] = v

    def wait_all(self, eng, ress):
        self._waits(eng, ress, ())

    def emit(self):
        L = self.lists
        for e in ENGS:
            for k, sem in self.sems.items():
                tgt = self.cnt[k]
                if k != e and tgt > self.seen[e].get(k, 0):
                    self.seen[e][k] = tgt
                    L[e].append(lambda E, sem=sem, tgt=tgt: E.wait_ge(sem, tgt))
        with self.nc.Block() as block:
            @block.tensor
            def _(E):
                for f in L["pe"]:
                    f(E)

            @block.scalar
            def _(E):
                for f in L["act"]:
                    f(E)

            @block.vector
            def _(E):
                for f in L["dve"]:
                    f(E)

            @block.gpsimd
            def _(E):
                for f in L["pool"]:
                    f(E)

            @block.sync
            def _(E):
                for f in L["sp"]:
                    f(E)
        self.es.close()
        for x_ in self.touched.values():
            x_.w = {}
            x_.r = {}
            x_.dsem = None


def tt(P, eng, out, in0, in1, op, reads, writes):
    P.op(eng, lambda E: E.tensor_tensor(out=out, in0=in0, in1=in1, op=op), reads=reads, writes=writes)


def ts(P, eng, out, in0, s1, s2, op0, op1, reads, writes):
    if s2 is None:
        P.op(eng, lambda E: E.tensor_scalar(out=out, in0=in0, scalar1=s1, scalar2=None, op0=op0), reads=reads, writes=writes)
    else:
        P.op(eng, lambda E: E.tensor_scalar(out=out, in0=in0, scalar1=s1, scalar2=s2, op0=op0, op1=op1), reads=reads, writes=writes)


def act(P, out, in_, func, reads, writes, scale=1.0, bias=None):
    if bias is None:
        P.op("act", lambda E: E.activation(out=out, in_=in_, func=func, scale=scale), reads=reads, writes=writes)
    else:
        P.op("act", lambda E: E.activation(out=out, in_=in_, func=func, scale=scale, bias=bias), reads=reads, writes=writes)


def scale_cast(P, i, out, in_, col, reads, writes):
    if i % 2:
        P.op("act", lambda E: E.activation(out=out, in_=in_, func=AF.Copy, scale=col), reads=reads, writes=writes)
    else:
        ts(P, "dve", out, in_, col, None, ALU.mult, None, reads, writes)


def cp(P, eng, out, in_, reads, writes):
    if eng == "act":
        P.op(eng, lambda E: E.activation(out=out, in_=in_, func=AF.Copy), reads=reads, writes=writes)
    else:
        P.op(eng, lambda E: E.tensor_copy(out=out, in_=in_), reads=reads, writes=writes)


class Rot:
    def __init__(self, P, name, n, shape, dt):
        self.bufs = [P.sb("%s%d" % (name, i), shape, dt) for i in range(n)]
        self.i = 0

    def get(self):
        b = self.bufs[self.i % len(self.bufs)]
        self.i += 1
        return b


class Dense:
    def __init__(self, P, T):
        self.P = P
        self.T = T
        self.wst = Rot(P, "wst", 4, [128, 8, 128], F32)
        self.wbf = Rot(P, "wbf", 4, [128, 8, 128], BF16)
        self.ost = Rot(P, "ost", 4, [128, NT], F32)
        self.ostb = Rot(P, "ostb", 4, [128, NT], BF16)
        self.ist = Rot(P, "ist", 6, [128, NT], F32)
        self.tmp = Rot(P, "tmp", 12, [128, NT], F32)
        self.ones = P.sb("ones", [128, 128], BF16)
        P.op("pool", lambda E: E.memset(self.ones[:], 1.0), writes=[self.ones])
        self.sq = P.sb("sq", [128, 8, T], BF16)

    def load_w(self, W, m, KC, rowscale=None, wb=None):
        P = self.P
        assert rowscale is None
        st = self.wst.get()
        if wb is None:
            wb = self.wbf.get()
        P.dma("sp", st[:, 0:KC, :], W.t[:, m * 128:(m + 1) * 128].rearrange("(kc p) m -> p kc m", p=128), reads=[W], writes=[st])
        self.wi = getattr(self, "wi", 0) + 1
        cp(P, "act" if self.wi % 2 else "pool", wb[:, 0:KC, :], st[:, 0:KC, :], [st], [wb])
        return wb

    def prep_w(self, name, W, KC, MC):
        wbs = []
        for m in range(MC):
            wb = self.P.sb("%s_w%d" % (name, m), [128, KC, 128], BF16)
            wbs.append(self.load_w(W, m, KC, wb=wb))
        return wbs

    def linear(self, W, KC, MC, a, epi, rowscale=None, wbs=None):
        P = self.P
        for m in range(MC):
            wb = wbs[m] if wbs is not None else self.load_w(W, m, KC, rowscale)
            for n in range(self.T // NT):
                ps = P.bank()
                for kc in range(KC):
                    P.op("pe", lambda E, ps=ps, wb=wb, kc=kc, n=n: E.matmul(
                        ps[:, :], lhsT=wb[:, kc, :], rhs=a[:, kc, n * NT:(n + 1) * NT], start=(kc == 0), stop=(kc == KC - 1)),
                        reads=[wb, a], writes=[ps])
                epi(m, n, ps)

    def rstd(self, src, out):
        P = self.P
        sq = self.sq
        for kc in range(8):
            act(P, sq[:, kc, :], src[:, kc, :], AF.Square, [src], [sq])
        for n in range(self.T // NT):
            ps = P.bank()
            for kc in range(8):
                P.op("pe", lambda E, ps=ps, kc=kc, n=n: E.matmul(
                    ps[:, :], lhsT=self.ones[:, :], rhs=sq[:, kc, n * NT:(n + 1) * NT], start=(kc == 0), stop=(kc == 7)),
                    reads=[self.ones, sq], writes=[ps])
            t = self.tmp.get()
            act(P, t[:, :], ps[:, :], AF.Ln, [ps], [t], scale=1.0 / D, bias=EPS)
            act(P, out[:, n * NT:(n + 1) * NT], t[:, :], AF.Exp, [t], [out], scale=-0.5)

    def store(self, dst, m, n, t0, src_res, src_ap):
        self.P.dma("pool", dst.t[m * 128:(m + 1) * 128, t0 + n * NT:t0 + (n + 1) * NT], src_ap, reads=[src_res], writes=[dst])

    def load_act(self, src, t0, dst, KC=8):
        for kc in range(KC):
            self.P.dma("sp", dst[:, kc, :], src.t[kc * 128:(kc + 1) * 128, t0:t0 + self.T], reads=[src], writes=[dst])


def colvec(P, name, dram_res, ncol):
    t = P.sb(name, [128, ncol], F32)
    P.dma("sp", t[:, :], dram_res.t[:, :], reads=[dram_res], writes=[t])
    return t


def _run(nc, in_maps):
    res = run_bass_kernel_spmd(nc, in_maps, core_ids=list(range(NCORE)))
    return res.results


def _cols(v):
    return np.ascontiguousarray(np.asarray(v, np.float32).reshape(-1, 128).T)


TS = 256
NG = 16
M_MAGIC = 12582912.0


def range_reduce(P, eng, out, in_, tmp, reads, writes, shift=0.0):
    res_in = reads
    if shift != 0.0:
        ts(P, eng, out, in_, shift, None, ALU.add, None, res_in, writes)
        in_ = out
        res_in = writes
    ts(P, eng, tmp[0], in_, 1.0 / (2 * PI), M_MAGIC, ALU.mult, ALU.add, res_in, [tmp[1]])
    ts(P, eng, tmp[0], tmp[0], -M_MAGIC, -2 * PI, ALU.add, ALU.mult, [tmp[1]], [tmp[1]])
    tt(P, eng, out, tmp[0], in_, ALU.add, [tmp[1]] + list(res_in), writes)
    ts(P, eng, out, out, 3.14159, -3.14159, ALU.min, ALU.max, writes, writes)


NAMES_T = ["lamre_T", "lamim_T", "logdt_T", "bre_T", "bim_T"]
NAMES_S = ["lamre_S", "lamim_S", "logdt_S"]


def phase_B(nc, dr):
    P = Prog(nc)
    uT = dr["uT_cs"]
    names_T = NAMES_T
    dT = {n: dr[n] for n in names_T}
    names_S = NAMES_S
    dS = {n: dr[n] for n in names_S}
    c1_d, c2_d, iota_d, gmask_d, sgn_d, J_d = dr["c1_S"], dr["c2_S"], dr["iota"], dr["gmask"], dr["sgn"], dr["Jm"]
    dsk = colvec(P, "dskip_cs", dr["dskip_cs"], 2)
    P.mk_banks(4)
    ybank = [P.ps("ybank%d" % i, [128, NT], F32) for i in range(2)]
    igb = P.ps("igb", [128, NT], F32)

    def ld(name, d, ncol):
        return colvec(P, name, d, ncol)

    lt = {n: ld("s_" + n, dT[n], 128) for n in names_T}
    cnt = [0]

    def newT(nm, ncol=128):
        cnt[0] += 1
        return P.sb("%s_%d" % (nm, cnt[0]), [128, ncol], F32)

    def derive(lamre, lamim, logdt, ncol, pfx):
        o = {}
        lr = newT(pfx + "lr", ncol)
        ts(P, "dve", lr[:, :], lamre[:, :], -1e-4, None, ALU.min, None, [lamre], [lr])
        dt = newT(pfx + "dt", ncol)
        act(P, dt[:, :], logdt[:, :], AF.Exp, [logdt], [dt])
        e = newT(pfx + "e", ncol)
        tt(P, "dve", e[:, :], lr[:, :], dt[:, :], ALU.mult, [lr, dt], [e])
        th = newT(pfx + "th", ncol)
        tt(P, "dve", th[:, :], lamim[:, :], dt[:, :], ALU.mult, [lamim, dt], [th])
        thr = newT(pfx + "thr", ncol)
        tmp = newT(pfx + "tmp", ncol)
        range_reduce(P, "dve", thr[:, :], th[:, :], (tmp[:, :], tmp), [th], [thr])
        thc = newT(pfx + "thc", ncol)
        range_reduce(P, "dve", thc[:, :], th[:, :], (tmp[:, :], tmp), [th], [thc], shift=PI / 2)
        mag = newT(pfx + "mag", ncol)
        act(P, mag[:, :], e[:, :], AF.Exp, [e], [mag])
        sn = newT(pfx + "sin", ncol)
        act(P, sn[:, :], thr[:, :], AF.Sin, [thr], [sn])
        cs = newT(pfx + "cos", ncol)
        act(P, cs[:, :], thc[:, :], AF.Sin, [thc], [cs])
        o.update(lr=lr, li=lamim, e=e, thr=thr, mag=mag, sin=sn, cos=cs)
        return o

    dl = derive(lt["lamre_T"], lt["lamim_T"], lt["logdt_T"], 128, "T")

    def mul(a, b, nm):
        t = newT(nm)
        tt(P, "dve", t[:, :], a[:, :], b[:, :], ALU.mult, [a, b], [t])
        return t

    def addsub(a, b, op, nm):
        t = newT(nm)
        tt(P, "dve", t[:, :], a[:, :], b[:, :], op, [a, b], [t])
        return t

    are = mul(dl["mag"], dl["cos"], "are")
    aim = mul(dl["mag"], dl["sin"], "aim")
    den = addsub(mul(dl["lr"], dl["lr"], "lr2"), mul(dl["li"], dl["li"], "li2"), ALU.add, "den")
    rden = newT("rden")
    P.op("dve", lambda E: E.reciprocal(out=rden[:, :], in_=den[:, :]), reads=[den], writes=[rden])
    nr = newT("nr")
    ts(P, "dve", nr[:, :], are[:, :], -1.0, None, ALU.add, None, [are], [nr])
    fre = mul(addsub(mul(nr, dl["lr"], "f1"), mul(aim, dl["li"], "f2"), ALU.add, "f3"), rden, "fre")
    fim = mul(addsub(mul(aim, dl["lr"], "f4"), mul(nr, dl["li"], "f5"), ALU.subtract, "f6"), rden, "fim")
    bbre = addsub(mul(fre, lt["bre_T"], "b1"), mul(fim, lt["bim_T"], "b2"), ALU.subtract, "bbre")
    bbim = addsub(mul(fre, lt["bim_T"], "b3"), mul(fim, lt["bre_T"], "b4"), ALU.add, "bbim")
    nbbim = newT("nbbim")
    ts(P, "dve", nbbim[:, :], bbim[:, :], -1.0, None, ALU.mult, None, [bbim], [nbbim])
    gmask = ld("gmask", gmask_d, 8)
    W1 = P.sb("W1pad", [128, NG, 128], BF16)
    W2 = P.sb("W2pad", [128, NG, 128], BF16)
    for jj in range(2):
        for gl in range(8):
            gg = jj * 8 + gl
            ms = gmask[:, gl:gl + 1]
            sl = slice(jj * 64, (jj + 1) * 64)
            ts(P, "dve", W1[:, gg, 0:64], bbre[:, sl], ms, None, ALU.mult, None, [bbre, gmask], [W1])
            ts(P, "dve", W1[:, gg, 64:128], bbim[:, sl], ms, None, ALU.mult, None, [bbim, gmask], [W1])
            ts(P, "dve", W2[:, gg, 0:64], nbbim[:, sl], ms, None, ALU.mult, None, [nbbim, gmask], [W2])
            ts(P, "dve", W2[:, gg, 64:128], bbre[:, sl], ms, None, ALU.mult, None, [bbre, gmask], [W2])

    ls_ = {n: ld("s_" + n, dS[n], NG) for n in names_S}
    ds = derive(ls_["lamre_S"], ls_["lamim_S"], ls_["logdt_S"], NG, "S")
    rho = ds["mag"]
    iota = ld("iota", iota_d, TS + 1)
    COS = P.sb("COS", [128, NG, TS + 1], F32)
    SIN = P.sb("SIN", [128, NG, TS + 1], F32)
    ARG = P.sb("ARG", [128, NG, TS + 1], F32)
    TMP = P.sb("TMPA", [128, NG, TS + 1], F32)
    Gg = [P.sb("Gg%d" % i, [128, TS], F32) for i in range(NG)]
    GT = [P.sb("GT%d" % i, [128, NG], F32) for i in range(2)]
    for gg in range(NG):
        ts(P, "dve", ARG[:, gg, :], iota[:, :], ds["thr"][:, gg:gg + 1], None, ALU.mult, None, [iota, ds["thr"]], [ARG])
    range_reduce(P, "dve", SIN[:, :, :], ARG[:, :, :], (TMP[:, :, :], TMP), [ARG], [SIN])
    range_reduce(P, "dve", COS[:, :, :], ARG[:, :, :], (TMP[:, :, :], TMP), [ARG], [COS], shift=PI / 2)
    act(P, SIN[:, :, :], SIN[:, :, :], AF.Sin, [SIN], [SIN])
    act(P, COS[:, :, :], COS[:, :, :], AF.Sin, [COS], [COS])
    c1 = ld("c1", c1_d, NG * 16)
    c2 = ld("c2", c2_d, NG * 16)
    sgn = ld("sgn", sgn_d, 1)
    L1 = P.sb("L1pad", [128, NG, 128], BF16)
    L2 = P.sb("L2pad", [128, NG, 128], BF16)
    P.op("pool", lambda E: E.memset(L1[:, :, :], 0.0), writes=[L1])
    P.op("pool", lambda E: E.memset(L2[:, :, :], 0.0), writes=[L2])
    for gg in range(NG):
        gl = gg % 8
        ts(P, "dve", L1[:, gg, gl * 16:(gl + 1) * 16], c1[:, gg * 16:(gg + 1) * 16], sgn[:, 0:1], None, ALU.mult, None, [c1, sgn], [L1])
        ts(P, "dve", L2[:, gg, gl * 16:(gl + 1) * 16], c2[:, gg * 16:(gg + 1) * 16], -1.0, None, ALU.mult, None, [c2], [L2])
    Jm = P.sb("Jm", [128, 128], F32)
    P.dma("sp", Jm[:, :], J_d.t[:, :], reads=[J_d], writes=[Jm])

    ust = Rot(P, "ust", 3, [128, 2, TS], F32)
    ubf = Rot(P, "ubf", 3, [128, 2, TS], BF16)
    t1r = Rot(P, "t1r", 4, [128, TS], F32)
    t2r = Rot(P, "t2r", 4, [128, TS], F32)
    xtr = Rot(P, "xtr", 4, [128, TS], F32)
    h1r = Rot(P, "h1r", 4, [128, TS], BF16)
    h2r = Rot(P, "h2r", 4, [128, TS], BF16)
    S0 = [P.sb("S0_%d" % i, [128, NG], F32) for i in range(2)]
    ys = Rot(P, "ys", 3, [128, 2, TS], BF16)
    P.op("pool", lambda E: E.memset(S0[0][:, :], 0.0), writes=[S0[0]])
    nch = SEQ // TS
    units = []
    chunk_res = {}
    for ch in range(nch):
        for jj in range(2):
            for gl in range(8):
                units.append(dict(ch=ch, jj=jj, gl=gl, gg=jj * 8 + gl))
    nu = len(units)

    def chunk_setup(ch):
        c0 = ch * TS
        us = ust.get()
        ub = ubf.get()
        P.dma("sp", us[:, :, :], uT.t[:, c0:c0 + TS].rearrange("(j p) t -> p j t", p=128), reads=[uT], writes=[us])
        cp(P, "act", ub[:, :, :], us[:, :, :], [us], [ub])
        chunk_res[ch] = dict(us=us, ub=ub, yo=ys.get())

    def s1(i):
        U = units[i]
        ch, jj, gg = U["ch"], U["jj"], U["gg"]
        if jj == 0 and U["gl"] == 0:
            chunk_setup(ch)
        ub = chunk_res[ch]["ub"]
        p1 = P.bank()
        p2 = P.bank()
        P.op("pe", lambda E, p1=p1, gg=gg, ub=ub, jj=jj: E.matmul(p1[:, 0:TS], lhsT=W1[:, gg, :], rhs=ub[:, jj, :], start=True, stop=True), reads=[W1, ub], writes=[p1])
        P.op("pe", lambda E, p2=p2, gg=gg, ub=ub, jj=jj: E.matmul(p2[:, 0:TS], lhsT=W2[:, gg, :], rhs=ub[:, jj, :], start=True, stop=True), reads=[W2, ub], writes=[p2])
        t1 = t1r.get()
        t2 = t2r.get()
        xt = xtr.get()
        tt(P, "dve", t1[:, :], p1[:, 0:TS], COS[:, gg, 1:TS + 1], ALU.mult, [p1, COS], [t1])
        tt(P, "dve", t2[:, :], p2[:, 0:TS], SIN[:, gg, 1:TS + 1], ALU.mult, [p2, SIN], [t2])
        tt(P, "pool", xt[:, :], t1[:, :], t2[:, :], ALU.subtract, [t1, t2], [xt])
        U["xt"] = xt

    def s2(i):
        U = units[i]
        ch, gg = U["ch"], U["gg"]
        gt, s0 = GT[ch % 2], S0[ch % 2]
        xt = U["xt"]
        G = Gg[gg]
        P.op("dve", lambda E, G=G, gg=gg, xt=xt, s0=s0: E.tensor_tensor_scan(
            out=G[:, :], data0=rho[:, gg:gg + 1].to_broadcast([128, TS]), data1=xt[:, :],
            initial=s0[:, gg:gg + 1], op0=ALU.mult, op1=ALU.add), reads=[rho, xt, s0], writes=[G])
        if ch + 1 < nch:
            cp(P, "act", gt[:, gg:gg + 1], G[:, TS - 1:TS], [G], [gt])
        h1 = h1r.get()
        h2 = h2r.get()
        tt(P, "dve", h1[:, :], G[:, :], COS[:, gg, 1:TS + 1], ALU.mult, [G, COS], [h1])
        tt(P, "pool", h2[:, :], G[:, :], SIN[:, gg, 1:TS + 1], ALU.mult, [G, SIN], [h2])
        U["h1"], U["h2"] = h1, h2
        if gg == NG - 1 and ch + 1 < nch:
            s1_ = S0[(ch + 1) % 2]
            P.op("pe", lambda E, gt=gt: E.matmul(igb[:, 0:NG], lhsT=Jm[:, :], rhs=gt[:, :], start=True, stop=True), reads=[Jm, gt], writes=[igb])
            ta = P.sb("bta%d" % ch, [128, NG], F32)
            tb = P.sb("btb%d" % ch, [128, NG], F32)
            tt(P, "dve", ta[:, :], gt[:, :], COS[:, :, TS], ALU.mult, [gt, COS], [ta])
            tt(P, "dve", tb[:, :], igb[:, 0:NG], SIN[:, :, TS], ALU.mult, [igb, SIN], [tb])
            tt(P, "dve", s1_[:, :], ta[:, :], tb[:, :], ALU.add, [ta, tb], [s1_])

    def s3(i):
        U = units[i]
        ch, jj, gl, gg = U["ch"], U["jj"], U["gl"], U["gg"]
        yb = ybank[jj]
        h1, h2 = U["h1"], U["h2"]
        P.op("pe", lambda E, yb=yb, gg=gg, h1=h1, gl=gl: E.matmul(yb[:, 0:TS], lhsT=L1[:, gg, :], rhs=h1[:, :], start=(gl == 0), stop=False), reads=[L1, h1], writes=[yb])
        P.op("pe", lambda E, yb=yb, gg=gg, h2=h2, gl=gl: E.matmul(yb[:, 0:TS], lhsT=L2[:, gg, :], rhs=h2[:, :], start=False, stop=(gl == 7)), reads=[L2, h2], writes=[yb])
        if gl == 7:
            cr = chunk_res[ch]
            yo, us = cr["yo"], cr["us"]
            P.op("dve", lambda E, yo=yo, us=us, yb=yb, jj=jj: E.scalar_tensor_tensor(
                out=yo[:, jj, :], in0=us[:, jj, :], scalar=dsk[:, jj:jj + 1], in1=yb[:, 0:TS], op0=ALU.mult, op1=ALU.add),
                reads=[us, dsk, yb], writes=[yo])
            if jj == 1:
                c0 = ch * TS
                ysrc = dr["y_src"][c0 // 2048]
                P.dma("act", ysrc.t[:, c0 % 2048:c0 % 2048 + TS].rearrange("(j p) t -> p j t", p=128), yo[:, :, :], reads=[yo], writes=[ysrc])
                if (c0 + TS) % 2048 == 0:
                    P.coll("AllGather", ysrc, dr["y_all"][c0 // 2048], GROUPS)

    for j in range(nu + 2):
        if j < nu:
            s1(j)
        if 0 <= j - 1 < nu:
            s2(j - 1)
        if 0 <= j - 2 < nu:
            s3(j - 2)
    P.wait_all("act", dr["y_all"])
    P.emit()


def gelu_tanh(P, dn, out_bf, y, reads):
    s = dn.tmp.get()
    s2 = dn.tmp.get()
    act(P, s[:, :], y, AF.Square, reads, [s])
    ts(P, "dve", s[:, :], s[:, :], 0.044715, 1.0, ALU.mult, ALU.add, [s], [s])
    tt(P, "dve", s2[:, :], s[:, :], y, ALU.mult, [s] + list(reads), [s2])
    act(P, s2[:, :], s2[:, :], AF.Sigmoid, [s2], [s2], scale=1.5957691216057308)
    return s2


def ple(P, dn, Wpg, Wpp, Xb, Pb, X1):
    for m in range(8):
        wg = dn.load_w(Wpg, m, 8)
        wp = dn.load_w(Wpp, m, 2)
        for n in range(dn.T // NT):
            pg = P.bank()
            pp = P.bank()
            for kc in range(8):
                P.op("pe", lambda E, pg=pg, wg=wg, kc=kc, n=n: E.matmul(
                    pg[:, :], lhsT=wg[:, kc, :], rhs=Xb[:, kc, n * NT:(n + 1) * NT], start=(kc == 0), stop=(kc == 7)),
                    reads=[wg, Xb], writes=[pg])
            for kc in range(2):
                P.op("pe", lambda E, pp=pp, wp=wp, kc=kc, n=n: E.matmul(
                    pp[:, :], lhsT=wp[:, kc, :], rhs=Pb[:, kc, n * NT:(n + 1) * NT], start=(kc == 0), stop=(kc == 1)),
                    reads=[wp, Pb], writes=[pp])
            sg = dn.tmp.get()
            act(P, sg[:, :], pg[:, :], AF.Sigmoid, [pg], [sg])
            t = dn.tmp.get()
            tt(P, "dve", t[:, :], sg[:, :], pp[:, :], ALU.mult, [sg, pp], [t])
            tt(P, "pool", X1[:, m, n * NT:(n + 1) * NT], X1[:, m, n * NT:(n + 1) * NT], t[:, :], ALU.add, [X1, t], [X1])


NH = 4
NQT = SEQ // NT


def phase_D(nc, dr):
    P = Prog(nc)
    qT, kT, vT, tri_d = dr["qT_cs"], dr["kT_cs"], dr["vT_cs"], dr["ntri"]
    ident = P.sb("ident", [128, 128], BF16)
    P.op("pool", lambda E: E.memset(ident[:, :], 0.0), writes=[ident])
    P.op("pool", lambda E: E.affine_select(out=ident[:, :], in_=ident[:, :], pattern=[[-1, 128]], compare_op=ALU.not_equal,
                                           fill=1.0, base=0, channel_multiplier=1), reads=[ident], writes=[ident])
    vtb = Rot(P, "vtb", 3, [64, 2048], BF16)
    tpb = P.ps("tpb", [128, NT], BF16)
    zb = [P.ps("zb%d" % i, [128, NT], F32) for i in range(4)]
    ob = [P.ps("ob%d" % i, [64, NT], F32) for i in range(2)]
    st = P.sb("tri_st", [128, 128], F32)
    P.dma("sp", st[:, :], tri_d.t[:, :], reads=[tri_d], writes=[st])
    ntri = P.sb("ntri", [128, 128], BF16)
    cp(P, "dve", ntri[:, :], st[:, :], [st], [ntri])
    negm = P.sb("negm", [128, 4, NT], BF16)
    P.op("pool", lambda E: E.memset(negm[:, :, :], 0.0), writes=[negm])
    for d_ in range(4):
        P.op("pool", lambda E, d_=d_: E.affine_select(
            out=negm[:, d_, :], in_=negm[:, d_, :], pattern=[[1, NT]], compare_op=ALU.is_gt, fill=-30000.0, base=-128 * d_, channel_multiplier=-1),
            reads=[negm], writes=[negm])
    nones = P.sb("nones", [128, 128], BF16)
    P.op("pool", lambda E: E.memset(nones[:, :], -1.0), writes=[nones])
    Qb = [P.sb("Qb%d" % i, [128, SEQ], BF16) for i in range(2)]
    Kb = [P.sb("Kb%d" % i, [128, SEQ], BF16) for i in range(2)]
    Vb = [P.sb("Vb%d" % h, [128, 64 * 64], BF16) for h in range(NH)]
    er = Rot(P, "er", 3, [128, NT], F32)
    spr = Rot(P, "spr", 4, [128, NT], BF16)
    wr = Rot(P, "wr", 4, [128, NT], BF16)
    racc = Rot(P, "racc", 3, [128, NT], BF16)
    ost = Rot(P, "ost", 2, [64, NT], BF16)
    for pr in range(2):
        for c in range(SEQ // 2048):
            P.dma("sp", Qb[pr][:, c * 2048:(c + 1) * 2048], qT.t[pr * 128:(pr + 1) * 128, c * 2048:(c + 1) * 2048], reads=[qT], writes=[Qb[pr]])
            P.dma("sp", Kb[pr][:, c * 2048:(c + 1) * 2048], kT.t[pr * 128:(pr + 1) * 128, c * 2048:(c + 1) * 2048], reads=[kT], writes=[Kb[pr]])

    def load_v(h):
        for c in range(SEQ // 2048):
            vb_ = vtb.get()
            P.dma("sp", vb_[:, :], vT.t[h * 64:(h + 1) * 64, c * 2048:(c + 1) * 2048], reads=[vT], writes=[vb_])
            for k8 in range(2):
                for j in range(8):
                    blk = k8 * 8 + j
                    P.op("pe", lambda E, vb_=vb_, j=j, blk=blk: E.transpose(
                        out=tpb[:, j * 64:(j + 1) * 64], in_=vb_[:, blk * 128:(blk + 1) * 128], identity=ident[0:64, 0:64]),
                        reads=[vb_, ident], writes=[tpb])
                kb0 = c * 16 + k8 * 8
                cp(P, "dve", Vb[h][:, kb0 * 64:(kb0 + 8) * 64], tpb[:, :], [tpb], [Vb[h]])
    load_v(0)
    blocks = []
    for h in range(NH):
        for qt in range(NQT):
            kbs = list(range(4 * qt + 3, -1, -1))
            for idx, kb in enumerate(kbs):
                blocks.append(dict(h=h, qt=qt, kb=kb, idx=idx, n=len(kbs), g=h * NQT + qt))
    nb = len(blocks)

    def operands(B):
        hp, pr = B["h"] % 2, B["h"] // 2
        ksl = Kb[pr][hp * 64:(hp + 1) * 64, B["kb"] * 128:(B["kb"] + 1) * 128]
        qsl = Qb[pr][hp * 64:(hp + 1) * 64, B["qt"] * NT:(B["qt"] + 1) * NT]
        return ksl, qsl, Kb[pr], Qb[pr]

    def mask(t, B):
        base = B["qt"] * NT - 128 * B["kb"]
        P.op("pool", lambda E, t=t, base=base: E.affine_select(
            out=t[:, :], in_=t[:, :], pattern=[[1, NT]], compare_op=ALU.is_gt, fill=0.0, base=base, channel_multiplier=-1),
            reads=[t], writes=[t])

    def stage1a(i):
        B = blocks[i]
        ksl, qsl, kres, qres = operands(B)
        if B["qt"] == 2 and B["idx"] == 0 and B["h"] + 1 < NH:
            load_v(B["h"] + 1)
        z = zb[i % 4]
        P.op("pe", lambda E, z=z, ksl=ksl, qsl=qsl: E.matmul(z[:, :], lhsT=ksl, rhs=qsl, start=True, stop=False), reads=[kres, qres], writes=[z])
        if B["kb"] >= 4 * B["qt"]:
            d_ = B["kb"] - 4 * B["qt"]
            P.op("pe", lambda E, z=z, d_=d_: E.matmul(z[:, :], lhsT=ident[:, :], rhs=negm[:, d_, :], start=False, stop=False), reads=[ident, negm], writes=[z])
        e = er.get()
        act(P, e[:, :], z[:, :], AF.Exp, [z], [e])
        B["e"] = e

    def stage1b(i):
        B = blocks[i]
        e = B["e"]
        sp = spr.get()
        P.op("act", lambda E, sp=sp, e=e: E.activation(out=sp[:, :], in_=e[:, :], func=AF.Ln, scale=1.0, bias=1.0),
             reads=[e], writes=[sp], skip_same=True)
        B["sp"] = sp

    def stage2(i):
        B = blocks[i]
        b = zb[i % 4]
        sp = B["sp"]
        first = B["idx"] == 0
        ra_prev = None if first else blocks[i - 1]["ra"]
        P.op("pe", lambda E, b=b, sp=sp, first=first: E.matmul(b[:, :], lhsT=ntri[:, :], rhs=sp[:, :], start=False, stop=first), reads=[ntri, sp], writes=[b])
        if not first:
            P.op("pe", lambda E, b=b, ra=ra_prev: E.matmul(b[:, :], lhsT=nones[:, :], rhs=ra[:, :], start=False, stop=True), reads=[nones, ra_prev], writes=[b])
        w = wr.get()
        act(P, w[:, :], b[:, :], AF.Exp, [b], [w])
        B["w"] = w
        if B["idx"] + 1 < B["n"]:
            ra = racc.get()
            if first:
                cp(P, "dve", ra[:, :], sp[:, :], [sp], [ra])
            else:
                tt(P, "dve", ra[:, :], ra_prev[:, :], sp[:, :], ALU.add, [ra_prev, sp], [ra])
            B["ra"] = ra

    def stage3(i):
        B = blocks[i]
        o_ps = ob[B["g"] % 2]
        w = B["w"]
        h, kb = B["h"], B["kb"]
        P.op("pe", lambda E, o_ps=o_ps, w=w, h=h, kb=kb, B=B: E.matmul(
            o_ps[:, :], lhsT=Vb[h][:, kb * 64:(kb + 1) * 64], rhs=w[:, :], start=(B["idx"] == 0), stop=(B["idx"] == B["n"] - 1)),
            reads=[Vb[h], w], writes=[o_ps])
        if B["idx"] == B["n"] - 1:
            o = ost.get()
            cp(P, "dve", o[:, :], o_ps[:, :], [o_ps], [o])
            q0 = B["qt"] * NT
            osrc = dr["o_src"][q0 // 2048]
            P.dma("act", osrc.t[h * 64:(h + 1) * 64, q0 % 2048:q0 % 2048 + NT], o[:, :], reads=[o], writes=[osrc])
            if h == NH - 1 and (q0 + NT) % 2048 == 0:
                P.coll("AllGather", osrc, dr["o_all"][q0 // 2048], GROUPS)

    for j in range(nb + 2):
        if j < nb:
            stage1a(j)
            stage1b(j)
        if 0 <= j - 1 < nb:
            stage2(j - 1)
        if 0 <= j - 2 < nb:
            stage3(j - 2)
    P.wait_all("act", dr["o_all"])
    P.emit()


GROUPS = [[0, 1, 2, 3], [4, 5, 6, 7]]
TOKC = 2048
TP = 1024


def proj_pass(P, dn, src_fn, xs, rs, jobs, col0):
    for kc in range(8):
        srcs = src_fn(kc)
        if isinstance(srcs, tuple):
            P.dma("sp", xs[:, kc, :], srcs[1], reads=[srcs[0]], writes=[xs])
        else:
            w_ = TP // len(srcs)
            for i_, (sr_, sa_) in enumerate(srcs):
                P.dma("sp", xs[:, kc, i_ * w_:(i_ + 1) * w_], sa_, reads=[sr_], writes=[xs])
    dn.rstd(xs, rs)
    done = {}
    for job in jobs:
        xb, gain, wbs, MC, dst = job[:5]
        oscale = job[5] if len(job) > 5 else None
        if id(xb) not in done:
            done[id(xb)] = 1
            for kc in range(8):
                scale_cast(P, kc, xb[:, kc, :], xs[:, kc, :], gain[:, kc:kc + 1], [xs, gain], [xb])

        def epi(m, n, ps, dst=dst, oscale=oscale):
            if oscale is None:
                o = dn.ost.get()
                tt(P, "dve", o[:, :], ps[:, :], rs[:, n * NT:(n + 1) * NT], ALU.mult, [ps, rs], [o])
            else:
                o = dn.ostb.get()
                P.op("dve", lambda E, o=o, ps=ps, n=n: E.scalar_tensor_tensor(
                    out=o[:, :], in0=ps[:, :], scalar=oscale, in1=rs[:, n * NT:(n + 1) * NT], op0=ALU.mult, op1=ALU.mult),
                    reads=[ps, rs], writes=[o])
            dn.store(dst, m, n, col0, o, o[:, :])
        dn.linear(None, 8, MC, xb, epi, wbs=wbs)


def phase_A(nc, dr):
    P = Prog(nc)
    P.mk_banks(6)
    dn = Dense(P, TP)
    g = colvec(P, "g_pre", dr["g_pre"], 8)
    xsr = Rot(P, "xs", 2, [128, 8, TP], F32)
    xb = P.sb("xb", [128, 8, TP], BF16)
    rs = P.sb("rs", [128, TP], F32)
    xT = dr["xT_full"]
    wbs = dn.prep_w("wu", dr["w_in_u"], 8, 2)
    for pa in range(SEQ // TP):
        t0 = pa * TP
        proj_pass(P, dn, lambda kc, t0=t0: (xT, xT.t[kc * 128:(kc + 1) * 128, t0:t0 + TP]), xsr.get(), rs,
                  [(xb, g, wbs, 2, dr["uT_cs"])], t0)
    P.wait_all("pool", [dr["uT_cs"]])
    P.emit()


def mk_select(P, dn, sel):
    ident = P.sb("identf", [128, 128], F32)
    P.op("pool", lambda E: E.memset(ident[:, :], 0.0), writes=[ident])
    P.op("pool", lambda E: E.affine_select(out=ident[:, :], in_=ident[:, :], pattern=[[-1, 128]], compare_op=ALU.not_equal,
                                           fill=1.0, base=0, channel_multiplier=1), reads=[ident], writes=[ident])
    mI = P.sb("mI", [128, 4, 128], BF16)
    for s_ in range(4):
        ts(P, "dve", mI[:, s_, :], ident[:, :], sel[:, s_:s_ + 1], None, ALU.mult, None, [ident, sel], [mI])
    dn.mI = mI


def select4(P, dn, src_fn, sel, dt):
    ps = P.bank()
    for s in range(4):
        it = dn.istb.get()
        sres, sap = src_fn(s)
        P.dma("sp", it[:, :], sap, reads=[sres], writes=[it])
        P.op("pe", lambda E, ps=ps, it=it, s=s: E.matmul(ps[:, :], lhsT=dn.mI[:, s, :], rhs=it[:, :], start=(s == 0), stop=(s == 3)),
             reads=[dn.mI, it], writes=[ps])
    acc = dn.tmp.get()
    act(P, acc[:, :], ps[:, :], AF.Copy, [ps], [acc])
    return acc


def phase_C(nc, dr):
    P = Prog(nc)
    P.mk_banks(7)
    dn = Dense(P, TP)
    dn.istb = Rot(P, "istb", 8, [128, NT], BF16)
    sel = colvec(P, "sel", dr["sel"], 4)
    mk_select(P, dn, sel)
    gpre = colvec(P, "g_pre", dr["g_pre"], 8)
    vl = []
    for i in range(4):
        t_ = P.sb("vec%d" % i, [128, 8], F32)
        P.dma("sp", t_[:, :], dr["vecsC"].t[:, i * 8:(i + 1) * 8], reads=[dr["vecsC"]], writes=[t_])
        vl.append(t_)
    bglu, gpost, gbpre, _unused = vl
    xT, y_all, pT = dr["xT_own"], dr["y_all"], dr["p0T"]
    Gb = P.sb("Gb", [128, 8, TP], BF16)
    SGb = P.sb("SGb", [128, 8, TP], BF16)
    Y2b = P.sb("Y2b", [128, 8, TP], BF16)
    X1 = P.sb("X1", [128, 8, TP], F32)
    Pb = P.sb("Pb", [128, 2, TP], BF16)
    rs = P.sb("rs", [128, TP], F32)
    for pa in range(TOKC // TP):
        t0 = pa * TP
        for kc in range(8):
            P.dma("sp", X1[:, kc, :], xT.t[kc * 128:(kc + 1) * 128, t0:t0 + TP], reads=[xT], writes=[X1])
        dn.rstd(X1, rs)
        for kc in range(8):
            scale_cast(P, kc, Y2b[:, kc, :], X1[:, kc, :], gpre[:, kc:kc + 1], [X1, gpre], [Y2b])

        def epi_gate(m, n, ps):
            t = dn.tmp.get()
            tt(P, "dve", t[:, :], ps[:, :], rs[:, n * NT:(n + 1) * NT], ALU.mult, [ps, rs], [t])
            act(P, SGb[:, m, n * NT:(n + 1) * NT], t[:, :], AF.Silu, [t], [SGb])
        dn.linear(dr["w_in_g"], 8, 8, Y2b, epi_gate)
        for kc in range(8):
            for n in range(TP // NT):
                def ysrc_fn(s, n=n, kc=kc):
                    g0 = s * TOKC + t0 + n * NT
                    ya = y_all[g0 // 2048]
                    return ya, ya.t[kc * 128:(kc + 1) * 128, g0 % 2048:g0 % 2048 + NT]
                y = select4(P, dn, ysrc_fn, sel, BF16)
                s2 = gelu_tanh(P, dn, None, y[:, :], [y])
                tt(P, "pool", Gb[:, kc, n * NT:(n + 1) * NT], s2[:, :], y[:, :], ALU.mult, [s2, y], [Gb])

        def epi_glu(m, n, ps):
            sg = dn.tmp.get()
            act(P, sg[:, :], ps[:, :], AF.Sigmoid, [ps, bglu], [sg], bias=bglu[:, m:m + 1])
            t = dn.tmp.get()
            tt(P, "dve", t[:, :], sg[:, :], Gb[:, m, n * NT:(n + 1) * NT], ALU.mult, [sg, Gb], [t])
            tt(P, "pool", Y2b[:, m, n * NT:(n + 1) * NT], t[:, :], SGb[:, m, n * NT:(n + 1) * NT], ALU.mult, [t, SGb], [Y2b])
        dn.linear(dr["w_glu"], 8, 8, Gb, epi_glu)

        def epi_out(m, n, ps):
            act(P, X1[:, m, n * NT:(n + 1) * NT], ps[:, :], AF.Copy, [ps], [X1])
        dn.linear(dr["w_out0"], 8, 8, Y2b, epi_out)
        dn.rstd(X1, rs)
        for kc in range(8):
            for n in range(TP // NT):
                c0 = t0 + n * NT
                ix = dn.ist.get()
                P.dma("sp", ix[:, :], xT.t[kc * 128:(kc + 1) * 128, c0:c0 + NT], reads=[xT], writes=[ix])
                t = dn.tmp.get()
                P.op("dve", lambda E, t=t, kc=kc, n=n: E.scalar_tensor_tensor(
                    out=t[:, :], in0=X1[:, kc, n * NT:(n + 1) * NT], scalar=gpost[:, kc:kc + 1], in1=rs[:, n * NT:(n + 1) * NT],
                    op0=ALU.mult, op1=ALU.mult), reads=[X1, gpost, rs], writes=[t])
                tt(P, "pool", X1[:, kc, n * NT:(n + 1) * NT], t[:, :], ix[:, :], ALU.add, [t, ix], [X1])
                cp(P, "act", Gb[:, kc, n * NT:(n + 1) * NT], X1[:, kc, n * NT:(n + 1) * NT], [X1], [Gb])
        for kc in range(2):
            for n in range(TP // NT):
                c0 = t0 + n * NT
                ip = dn.ist.get()
                P.dma("sp", ip[:, :], pT.t[kc * 128:(kc + 1) * 128, c0:c0 + NT], reads=[pT], writes=[ip])
                cp(P, "dve", Pb[:, kc, n * NT:(n + 1) * NT], ip[:, :], [ip], [Pb])
        ple(P, dn, dr["w_pg0"], dr["w_pp0"], Gb, Pb, X1)
        dn.rstd(X1, rs)
        for kc in range(8):
            cp(P, "dve", Y2b[:, kc, :], X1[:, kc, :], [X1], [Y2b])
            scale_cast(P, kc + 1, SGb[:, kc, :], X1[:, kc, :], gbpre[:, kc:kc + 1], [X1, gbpre], [SGb])
            P.dma("pool", dr["x1_own"].t[kc * 128:(kc + 1) * 128, t0:t0 + TP], X1[:, kc, :], reads=[X1], writes=[dr["x1_own"]])
            for n in range(TP // NT):
                xsrc = dr["x1_src"][(t0 + n * NT) // NT]
                P.dma("pool", xsrc.t[kc * 128:(kc + 1) * 128, :], Y2b[:, kc, n * NT:(n + 1) * NT], reads=[Y2b], writes=[xsrc])

        def epi_g1(m, n, ps):
            o = dn.ost.get()
            tt(P, "dve", o[:, :], ps[:, :], rs[:, n * NT:(n + 1) * NT], ALU.mult, [ps, rs], [o])
            dn.store(dr["g1T"], m, n, t0, o, o[:, :])
        for k in range(t0 // NT, (t0 + TP) // NT):
            P.coll("AllGather", dr["x1_src"][k], dr["x1_all"][k], GROUPS)
        dn.linear(dr["w_bin_g"], 8, 8, SGb, epi_g1)
    P.wait_all("pool", dr["x1_all"] + [dr["x1_own"], dr["g1T"]])
    P.emit()


def phase_QKV(nc, dr):
    P = Prog(nc)
    P.mk_banks(6)
    dn = Dense(P, TP)
    gkv = colvec(P, "g_kv", dr["g_kv"], 8)
    gbpre = colvec(P, "g_bpre", dr["g_bpre"], 8)
    xsr = Rot(P, "xs", 2, [128, 8, TP], BF16)
    xq = P.sb("xq", [128, 8, TP], BF16)
    xk = P.sb("xk", [128, 8, TP], BF16)
    rs = P.sb("rs", [128, TP], F32)
    xa = dr["x1_all"]
    wq = dn.prep_w("wq", dr["w_q"], 8, 2)
    wk = dn.prep_w("wk", dr["w_k"], 8, 2)
    wv = dn.prep_w("wv", dr["w_v"], 8, 2)
    for pa in range(SEQ // TP):
        t0 = pa * TP
        s, tl = t0 // TOKC, t0 % TOKC
        proj_pass(P, dn, lambda kc, s=s, tl=tl: [(xa[(tl + h_ * NT) // NT], xa[(tl + h_ * NT) // NT].t[s * D + kc * 128:s * D + (kc + 1) * 128, :]) for h_ in range(TP // NT)], xsr.get(), rs,
                  [(xq, gbpre, wq, 2, dr["qT_cs"], 1.0), (xk, gkv, wk, 2, dr["kT_cs"], 0.125), (xk, gkv, wv, 2, dr["vT_cs"], 1.0)], t0)
    P.wait_all("pool", [dr["qT_cs"], dr["kT_cs"], dr["vT_cs"]])
    P.emit()


def phase_E(nc, dr):
    P = Prog(nc)
    P.mk_banks(7)
    dn = Dense(P, TP)
    dn.istb = Rot(P, "istb", 8, [128, NT], BF16)
    sel = colvec(P, "sel", dr["sel"], 4)
    mk_select(P, dn, sel)
    gpost = colvec(P, "g_bpost", dr["g_bpost"], 8)
    Ob = P.sb("Ob", [128, 8, TP], BF16)
    Xb = P.sb("Xb", [128, 8, TP], BF16)
    X1 = P.sb("X1", [128, 8, TP], F32)
    Pb = P.sb("Pb", [128, 2, TP], BF16)
    rs = P.sb("rs", [128, TP], F32)
    x1T, gT, pT, outT = dr["x1_own"], dr["g1T"], dr["p1T"], dr["outT"]
    for pa in range(TOKC // TP):
        t0 = pa * TP
        for kc in range(8):
            for n in range(TP // NT):
                c0 = t0 + n * NT
                def osrc_fn(s, n=n, kc=kc):
                    g0 = s * TOKC + t0 + n * NT
                    oa = dr["o_all"][g0 // 2048]
                    return oa, oa.t[kc * 128:(kc + 1) * 128, g0 % 2048:g0 % 2048 + NT]
                o = select4(P, dn, osrc_fn, sel, BF16)
                ig = dn.ist.get()
                P.dma("sp", ig[:, :], gT.t[kc * 128:(kc + 1) * 128, c0:c0 + NT], reads=[gT], writes=[ig])
                sg = dn.tmp.get()
                act(P, sg[:, :], ig[:, :], AF.Silu, [ig], [sg])
                tt(P, "pool", Ob[:, kc, n * NT:(n + 1) * NT], sg[:, :], o[:, :], ALU.mult, [sg, o], [Ob])

        def epi_out(m, n, ps):
            act(P, X1[:, m, n * NT:(n + 1) * NT], ps[:, :], AF.Copy, [ps], [X1])
        dn.linear(dr["w_out1"], 8, 8, Ob, epi_out)
        dn.rstd(X1, rs)
        for kc in range(8):
            for n in range(TP // NT):
                c0 = t0 + n * NT
                ix = dn.ist.get()
                P.dma("sp", ix[:, :], x1T.t[kc * 128:(kc + 1) * 128, c0:c0 + NT], reads=[x1T], writes=[ix])
                t = dn.tmp.get()
                P.op("dve", lambda E, t=t, kc=kc, n=n: E.scalar_tensor_tensor(
                    out=t[:, :], in0=X1[:, kc, n * NT:(n + 1) * NT], scalar=gpost[:, kc:kc + 1], in1=rs[:, n * NT:(n + 1) * NT],
                    op0=ALU.mult, op1=ALU.mult), reads=[X1, gpost, rs], writes=[t])
                tt(P, "pool", X1[:, kc, n * NT:(n + 1) * NT], t[:, :], ix[:, :], ALU.add, [t, ix], [X1])
                cp(P, "act", Xb[:, kc, n * NT:(n + 1) * NT], X1[:, kc, n * NT:(n + 1) * NT], [X1], [Xb])
        for kc in range(2):
            for n in range(TP // NT):
                c0 = t0 + n * NT
                ip = dn.ist.get()
                P.dma("sp", ip[:, :], pT.t[kc * 128:(kc + 1) * 128, c0:c0 + NT], reads=[pT], writes=[ip])
                cp(P, "dve", Pb[:, kc, n * NT:(n + 1) * NT], ip[:, :], [ip], [Pb])
        ple(P, dn, dr["w_pg1"], dr["w_pp1"], Xb, Pb, X1)
        for kc in range(8):
            P.dma("pool", outT.t[kc * 128:(kc + 1) * 128, t0:t0 + TP], X1[:, kc, :], reads=[X1], writes=[outT])
    P.wait_all("pool", [outT])
    P.emit()


IN_SPECS = {
    "xT_full": ([D, SEQ], F32), "xT_own": ([D, TOKC], F32), "p0T": ([256, TOKC], F32), "p1T": ([256, TOKC], F32),
    "sel": ([128, 4], F32), "g_pre": ([128, 8], F32), "w_in_u": ([D, 256], F32), "w_in_g": ([D, D], F32),
    "lamre_T": ([128, 128], F32), "lamim_T": ([128, 128], F32), "logdt_T": ([128, 128], F32), "bre_T": ([128, 128], F32),
    "bim_T": ([128, 128], F32), "lamre_S": ([128, NG], F32), "lamim_S": ([128, NG], F32), "logdt_S": ([128, NG], F32),
    "c1_S": ([128, NG * 16], F32), "c2_S": ([128, NG * 16], F32), "iota": ([128, TS + 1], F32), "gmask": ([128, 8], F32),
    "sgn": ([128, 1], F32), "Jm": ([128, 128], F32), "dskip_cs": ([128, 2], F32), "vecsC": ([128, 32], F32),
    "w_glu": ([D, D], F32), "w_out0": ([D, D], F32), "w_pg0": ([D, D], F32), "w_pp0": ([256, D], F32),
    "w_bin_g": ([D, D], F32), "g_kv": ([128, 8], F32), "g_bpre": ([128, 8], F32), "w_q": ([D, 256], F32),
    "w_k": ([D, 256], F32), "w_v": ([D, 256], F32), "ntri": ([128, 128], F32), "g_bpost": ([128, 8], F32),
    "w_out1": ([D, D], F32), "w_pg1": ([D, D], F32), "w_pp1": ([256, D], F32),
}
SCRATCH = {
    "uT_cs": ([256, SEQ], F32), "y_src": ([256, 2048], BF16, 4), "y_all": ([D, 2048], BF16, 4), "x1_own": ([D, TOKC], F32),
    "x1_src": ([D, NT], BF16, 4), "x1_all": ([4 * D, NT], BF16, 4), "g1T": ([D, TOKC], F32), "qT_cs": ([256, SEQ], BF16),
    "kT_cs": ([256, SEQ], BF16), "vT_cs": ([256, SEQ], BF16), "o_src": ([256, 2048], BF16, 4), "o_all": ([D, 2048], BF16, 4),
}


def build_fused():
    nc = bass.Bass("TRN2", target_bir_lowering=False)
    dr = {}
    for n, (shp, dt) in IN_SPECS.items():
        dr[n] = Res(n, nc.dram_tensor(n, list(shp), dt, kind="ExternalInput").ap())
    for n, spec in SCRATCH.items():
        shp, dt = spec[0], spec[1]
        if len(spec) == 3:
            dr[n] = [Res("%s%d" % (n, i), nc.dram_tensor("%s%d" % (n, i), list(shp), dt, kind="Internal").ap()) for i in range(spec[2])]
        else:
            dr[n] = Res(n, nc.dram_tensor(n, list(shp), dt, kind="Internal").ap())
    dr["outT"] = Res("outT", nc.dram_tensor("outT", [D, TOKC], F32, kind="ExternalOutput").ap())
    phase_A(nc, dr)
    phase_B(nc, dr)
    phase_C(nc, dr)
    phase_QKV(nc, dr)
    phase_D(nc, dr)
    phase_E(nc, dr)
    Prog.finish()
    return nc


def _f(a):
    return np.ascontiguousarray(np.asarray(a, dtype=np.float32))


def kernel(**inputs):
    inp = {k: np.asarray(v) for k, v in inputs.items()}
    x, p = inp["x"], inp["p"]
    lam_re, lam_im, log_dt = _f(inp["a_lam_re"][0]), _f(inp["a_lam_im"][0]), _f(inp["a_log_dt"][0])
    b_re, b_im, c_re, c_im = _f(inp["a_b_re"][0]), _f(inp["a_b_im"][0]), _f(inp["a_c_re"][0]), _f(inp["a_c_im"][0])
    iota = _f(np.broadcast_to(np.arange(TS + 1, dtype=np.float32), (128, TS + 1)))
    gmask = np.zeros((128, 8), np.float32)
    for gl in range(8):
        gmask[gl * 16:(gl + 1) * 16, gl] = 1.0
    sgn = np.ones((128, 1), np.float32)
    sgn[64:] = -1.0
    J = np.zeros((128, 128), np.float32)
    for q in range(64):
        J[64 + q, q] = -1.0
        J[q, 64 + q] = 1.0
    ntri = np.zeros((128, 128), np.float32)
    for j in range(128):
        ntri[j, :j + 1] = -1.0
    vecsC = np.zeros((128, 32), np.float32)
    for i, v in enumerate([inp["a_b_glu"][0], inp["a_norm_post"][0], inp["b_norm_pre"][0]]):
        vecsC[:, i * 8:(i + 1) * 8] = _cols(v)
    a_w_in, b_w_in, w_kv = _f(inp["a_w_in"][0]), _f(inp["b_w_in"][0]), _f(inp["w_kv"])
    common = {
        "g_pre": _cols(inp["a_norm_pre"][0]), "w_in_g": _f(a_w_in[:, D:]), "iota": iota, "gmask": gmask, "sgn": sgn, "Jm": J,
        "vecsC": vecsC, "w_glu": _f(inp["a_w_glu"][0]), "w_out0": _f(inp["a_w_out"][0]), "w_pg0": _f(inp["ple_w_gate"][0]),
        "w_pp0": _f(inp["ple_w_proj"][0]), "w_bin_g": _f(b_w_in[:, D:]), "g_kv": _cols(inp["kv_norm"]),
        "g_bpre": _cols(inp["b_norm_pre"][0]), "ntri": ntri, "g_bpost": _cols(inp["b_norm_post"][0]),
        "w_out1": _f(inp["b_w_out"][0]), "w_pg1": _f(inp["ple_w_gate"][1]), "w_pp1": _f(inp["ple_w_proj"][1]),
    }
    xT_full = [_f(np.asarray(x[b], np.float32).T) for b in range(2)]
    maps = []
    for c in range(NCORE):
        b, r = c // 4, c % 4
        gs = np.arange(16 * r, 16 * r + 16)
        tsl = slice(r * TOKC, (r + 1) * TOKC)
        csl = slice(256 * r, 256 * r + 256)
        m = dict(common)
        m["xT_full"] = xT_full[b]
        m["xT_own"] = _f(xT_full[b][:, tsl])
        m["p0T"] = _f(np.asarray(p[0, b, tsl, :], np.float32).T)
        m["p1T"] = _f(np.asarray(p[1, b, tsl, :], np.float32).T)
        sel = np.zeros((128, 4), np.float32)
        sel[:, r] = 1.0
        m["sel"] = sel
        m["w_in_u"] = _f(a_w_in[:, csl])
        m["dskip_cs"] = _cols(inp["a_d_skip"][0][csl])
        m["w_q"] = _f(b_w_in[:, csl])
        m["w_k"] = _f(w_kv[:, csl])
        m["w_v"] = _f(w_kv[:, D + 256 * r:D + 256 * r + 256])

        def lt_gp(a):
            t = a[gs].reshape(2, 8, 64)
            t = np.broadcast_to(t[:, :, None, :], (2, 8, 16, 64))
            return _f(t.transpose(1, 2, 0, 3).reshape(128, 128))

        def lt_b(a):
            t = a[gs].reshape(2, 8, 64, 16)
            return _f(t.transpose(1, 3, 0, 2).reshape(128, 128))

        def sp_gp(a):
            t = a[gs].T
            return _f(np.concatenate([t, t], axis=0))
        ldt = np.broadcast_to(log_dt[:, None], (64, 64))
        m["lamre_T"], m["lamim_T"], m["logdt_T"] = lt_gp(lam_re), lt_gp(lam_im), lt_gp(ldt)
        m["bre_T"], m["bim_T"] = lt_b(b_re), lt_b(b_im)
        m["lamre_S"], m["lamim_S"], m["logdt_S"] = sp_gp(lam_re), sp_gp(lam_im), sp_gp(ldt)
        cr = c_re[gs].transpose(2, 0, 1).reshape(64, 256)
        ci = c_im[gs].transpose(2, 0, 1).reshape(64, 256)
        m["c1_S"] = _f(np.concatenate([cr, ci], axis=0))
        m["c2_S"] = _f(np.concatenate([ci, cr], axis=0))
        maps.append(m)
    res = _run(build_fused(), maps)
    out = np.empty((2, SEQ, D), np.float32)
    for c in range(NCORE):
        b, r = c // 4, c % 4
        out[b, r * TOKC:(r + 1) * TOKC, :] = res[c]["outT"].T
    return out
```

```python
from contextlib import ExitStack
import numpy as np
import concourse.bass as bass
import concourse.mybir as mybir
from concourse.bass_utils import run_bass_kernel_spmd

F32 = mybir.dt.float32
BF16 = mybir.dt.bfloat16
AF = mybir.ActivationFunctionType
ALU = mybir.AluOpType

ENGS = ("pe", "act", "dve", "pool", "sp")
NCORE = 8
D = 1024
SEQ = 8192
NT = 512
EPS = 1e-6
PI = float(np.pi)


class Res:
    __slots__ = ("name", "w", "r", "dsem", "dcnt", "t")

    def __init__(self, name, t=None):
        self.name = name
        self.w = {}
        self.r = {}
        self.dsem = None
        self.dcnt = 0
        self.t = t

    def __getitem__(self, idx):
        return self.t[idx]


class Prog:
    _n = 0
    G = None

    def __init__(self, nc):
        Prog._n += 1
        self.pfx = "f%d_" % Prog._n
        self.nc = nc
        if Prog.G is None or Prog.G["nc"] is not nc:
            ges = ExitStack()
            Prog.G = {"nc": nc, "es": ges, "sems": {}, "cnt": {}}
            for e in ENGS:
                Prog.G["sems"][e] = ges.enter_context(nc.semaphore("s_" + e))
                Prog.G["cnt"][e] = 0
        G = Prog.G
        self.es = ExitStack()
        self.lists = {e: [] for e in ENGS}
        self.sems = G["sems"]
        self.cnt = G["cnt"]
        self.seen = {e: dict(self.cnt) for e in ENGS}
        self.nd = {"d": 0, "g": 0, "c": 0}
        self.banks = []
        self.bi = 0
        self.touched = {}

    @staticmethod
    def finish():
        if Prog.G is not None:
            Prog.G["es"].close()
            Prog.G = None

    def _newsem(self, ns="d"):
        key = "%s%d" % (ns, self.nd[ns])
        self.nd[ns] += 1
        if key not in self.sems:
            self.sems[key] = Prog.G["es"].enter_context(self.nc.semaphore("sd_" + key))
            self.cnt[key] = 0
        return key

    def sb(self, name, shape, dt):
        return Res(name, self.es.enter_context(self.nc.sbuf_tensor(self.pfx + "sb_" + name, list(shape), dt)))

    def ps(self, name, shape, dt=F32):
        return Res(name, self.es.enter_context(self.nc.psum_tensor(self.pfx + "ps_" + name, list(shape), dt)))

    def dram(self, name, shape, dt, kind="Internal"):
        return Res(name, self.nc.dram_tensor(name, list(shape), dt, kind=kind).ap())

    def mk_banks(self, n):
        self.banks = [self.ps("bank%d" % i, [128, NT], F32) for i in range(n)]

    def bank(self):
        b = self.banks[self.bi % len(self.banks)]
        self.bi += 1
        return b

    def _dsem(self, res, eng):
        ns = "g" if eng == "pool" else "d"
        if res.dsem is None:
            res.dsem = {}
        if ns not in res.dsem:
            res.dsem[ns] = self._newsem(ns)
        return res.dsem[ns]

    def _waits(self, eng, reads, writes, skip_same=False):
        for x_ in reads:
            self.touched[id(x_)] = x_
        for x_ in writes:
            self.touched[id(x_)] = x_
        deps = {}
        for r in reads:
            for k, v in r.w.items():
                if skip_same and k == eng:
                    continue
                if v > deps.get(k, 0):
                    deps[k] = v
        for w in writes:
            for k, v in w.w.items():
                if k != eng and v > deps.get(k, 0):
                    deps[k] = v
            for k, v in w.r.items():
                if k != eng and v > deps.get(k, 0):
                    deps[k] = v
        seen = self.seen[eng]
        for k, v in deps.items():
            if v > seen.get(k, 0):
                seen[k] = v
                sem = self.sems[k]
                self.lists[eng].append(lambda E, sem=sem, v=v: E.wait_ge(sem, v))

    def op(self, eng, fn, reads=(), writes=(), skip_same=False):
        self._waits(eng, reads, writes, skip_same)
        self.cnt[eng] += 1
        n = self.cnt[eng]
        sem = self.sems[eng]
        self.lists[eng].append(lambda E, fn=fn, sem=sem: fn(E).then_inc(sem, 1))
        for r in reads:
            r.r[eng] = n
        for w in writes:
            w.w[eng] = n

    def dma(self, eng, out_ap, in_ap, reads=(), writes=()):
        wres = writes[0]
        self._waits(eng, reads, writes)
        key = self._dsem(wres, eng)
        self.cnt[key] += 16
        v = self.cnt[key]
        sem = self.sems[key]
        self.lists[eng].append(
            lambda E, o=out_ap, i=in_ap, sem=sem: E.dma_start(out=o, in_=i).then_inc(sem, 16))
        for r in reads:
            r.r[key] = v
        wres.w[key] = v

    def coll(self, kind, src, dst, groups):
        self._waits("pool", [src], [dst])
        key = self._newsem("c")
        self.cnt[key] += 1
        v = self.cnt[key]
        sem = self.sems[key]
        self.lists["pool"].append(lambda E, sem=sem: E.collective_compute(
            kind, ALU.bypass, replica_groups=groups, ins=[src.t.opt()], outs=[dst.t.opt()]).then_inc(sem))
        src.r[key] = v
        dst.w[key] = v

    def wait_all(self, eng, ress):
        self._waits(eng, ress, ())

    def emit(self):
        L = self.lists
        for e in ENGS:
            for k, sem in self.sems.items():
                tgt = self.cnt[k]
                if k != e and tgt > self.seen[e].get(k, 0):
                    self.seen[e][k] = tgt
                    L[e].append(lambda E, sem=sem, tgt=tgt: E.wait_ge(sem, tgt))
        with self.nc.Block() as block:
            @block.tensor
            def _(E):
                for f in L["pe"]:
                    f(E)

            @block.scalar
            def _(E):
                for f in L["act"]:
                    f(E)

            @block.vector
            def _(E):
                for f in L["dve"]:
                    f(E)

            @block.gpsimd
            def _(E):
                for f in L["pool"]:
                    f(E)

            @block.sync
            def _(E):
                for f in L["sp"]:
                    f(E)
        self.es.close()
        for x_ in self.touched.values():
            x_.w = {}
            x_.r = {}
            x_.dsem = None


def tt(P, eng, out, in0, in1, op, reads, writes):
    P.op(eng, lambda E: E.tensor_tensor(out=out, in0=in0, in1=in1, op=op), reads=reads, writes=writes)


def ts(P, eng, out, in0, s1, s2, op0, op1, reads, writes):
    if s2 is None:
        P.op(eng, lambda E: E.tensor_scalar(out=out, in0=in0, scalar1=s1, scalar2=None, op0=op0), reads=reads, writes=writes)
    else:
        P.op(eng, lambda E: E.tensor_scalar(out=out, in0=in0, scalar1=s1, scalar2=s2, op0=op0, op1=op1), reads=reads, writes=writes)


def act(P, out, in_, func, reads, writes, scale=1.0, bias=None):
    if bias is None:
        P.op("act", lambda E: E.activation(out=out, in_=in_, func=func, scale=scale), reads=reads, writes=writes)
    else:
        P.op("act", lambda E: E.activation(out=out, in_=in_, func=func, scale=scale, bias=bias), reads=reads, writes=writes)


def scale_cast(P, i, out, in_, col, reads, writes):
    if i % 2:
        P.op("act", lambda E: E.activation(out=out, in_=in_, func=AF.Copy, scale=col), reads=reads, writes=writes)
    else:
        ts(P, "dve", out, in_, col, None, ALU.mult, None, reads, writes)


def cp(P, eng, out, in_, reads, writes):
    if eng == "act":
        P.op(eng, lambda E: E.activation(out=out, in_=in_, func=AF.Copy), reads=reads, writes=writes)
    else:
        P.op(eng, lambda E: E.tensor_copy(out=out, in_=in_), reads=reads, writes=writes)


class Rot:
    def __init__(self, P, name, n, shape, dt):
        self.bufs = [P.sb("%s%d" % (name, i), shape, dt) for i in range(n)]
        self.i = 0

    def get(self):
        b = self.bufs[self.i % len(self.bufs)]
        self.i += 1
        return b


class Dense:
    def __init__(self, P, T):
        self.P = P
        self.T = T
        self.wst = Rot(P, "wst", 4, [128, 8, 128], F32)
        self.wbf = Rot(P, "wbf", 4, [128, 8, 128], BF16)
        self.ost = Rot(P, "ost", 4, [128, NT], F32)
        self.ostb = Rot(P, "ostb", 4, [128, NT], BF16)
        self.ist = Rot(P, "ist", 6, [128, NT], F32)
        self.tmp = Rot(P, "tmp", 12, [128, NT], F32)
        self.ones = P.sb("ones", [128, 128], BF16)
        P.op("pool", lambda E: E.memset(self.ones[:], 1.0), writes=[self.ones])
        self.sq = P.sb("sq", [128, 8, T], BF16)

    def load_w(self, W, m, KC, rowscale=None, wb=None):
        P = self.P
        assert rowscale is None
        st = self.wst.get()
        if wb is None:
            wb = self.wbf.get()
        P.dma("sp", st[:, 0:KC, :], W.t[:, m * 128:(m + 1) * 128].rearrange("(kc p) m -> p kc m", p=128), reads=[W], writes=[st])
        self.wi = getattr(self, "wi", 0) + 1
        cp(P, "act" if self.wi % 2 else "pool", wb[:, 0:KC, :], st[:, 0:KC, :], [st], [wb])
        return wb

    def prep_w(self, name, W, KC, MC):
        wbs = []
        for m in range(MC):
            wb = self.P.sb("%s_w%d" % (name, m), [128, KC, 128], BF16)
            wbs.append(self.load_w(W, m, KC, wb=wb))
        return wbs

    def linear(self, W, KC, MC, a, epi, rowscale=None, wbs=None):
        P = self.P
        for m in range(MC):
            wb = wbs[m] if wbs is not None else self.load_w(W, m, KC, rowscale)
            for n in range(self.T // NT):
                ps = P.bank()
                for kc in range(KC):
                    P.op("pe", lambda E, ps=ps, wb=wb, kc=kc, n=n: E.matmul(
                        ps[:, :], lhsT=wb[:, kc, :], rhs=a[:, kc, n * NT:(n + 1) * NT], start=(kc == 0), stop=(kc == KC - 1)),
                        reads=[wb, a], writes=[ps])
                epi(m, n, ps)

    def rstd(self, src, out):
        P = self.P
        sq = self.sq
        for kc in range(8):
            act(P, sq[:, kc, :], src[:, kc, :], AF.Square, [src], [sq])
        for n in range(self.T // NT):
            ps = P.bank()
            for kc in range(8):
                P.op("pe", lambda E, ps=ps, kc=kc, n=n: E.matmul(
                    ps[:, :], lhsT=self.ones[:, :], rhs=sq[:, kc, n * NT:(n + 1) * NT], start=(kc == 0), stop=(kc == 7)),
                    reads=[self.ones, sq], writes=[ps])
            t = self.tmp.get()
            act(P, t[:, :], ps[:, :], AF.Ln, [ps], [t], scale=1.0 / D, bias=EPS)
            act(P, out[:, n * NT:(n + 1) * NT], t[:, :], AF.Exp, [t], [out], scale=-0.5)

    def store(self, dst, m, n, t0, src_res, src_ap):
        self.P.dma("pool", dst.t[m * 128:(m + 1) * 128, t0 + n * NT:t0 + (n + 1) * NT], src_ap, reads=[src_res], writes=[dst])

    def load_act(self, src, t0, dst, KC=8):
        for kc in range(KC):
            self.P.dma("sp", dst[:, kc, :], src.t[kc * 128:(kc + 1) * 128, t0:t0 + self.T], reads=[src], writes=[dst])


def colvec(P, name, dram_res, ncol):
    t = P.sb(name, [128, ncol], F32)
    P.dma("sp", t[:, :], dram_res.t[:, :], reads=[dram_res], writes=[t])
    return t


def _run(nc, in_maps):
    res = run_bass_kernel_spmd(nc, in_maps, core_ids=list(range(NCORE)))
    return res.results


def _cols(v):
    return np.ascontiguousarray(np.asarray(v, np.float32).reshape(-1, 128).T)


TS = 256
NG = 16
M_MAGIC = 12582912.0


def range_reduce(P, eng, out, in_, tmp, reads, writes, shift=0.0):
    res_in = reads
    if shift != 0.0:
        ts(P, eng, out, in_, shift, None, ALU.add, None, res_in, writes)
        in_ = out
        res_in = writes
    ts(P, eng, tmp[0], in_, 1.0 / (2 * PI), M_MAGIC, ALU.mult, ALU.add, res_in, [tmp[1]])
    ts(P, eng, tmp[0], tmp[0], -M_MAGIC, -2 * PI, ALU.add, ALU.mult, [tmp[1]], [tmp[1]])
    tt(P, eng, out, tmp[0], in_, ALU.add, [tmp[1]] + list(res_in), writes)
    ts(P, eng, out, out, 3.14159, -3.14159, ALU.min, ALU.max, writes, writes)


NAMES_T = ["lamre_T", "lamim_T", "logdt_T", "bre_T", "bim_T"]
NAMES_S = ["lamre_S", "lamim_S", "logdt_S"]


def phase_B(nc, dr):
    P = Prog(nc)
    uT = dr["uT_cs"]
    names_T = NAMES_T
    dT = {n: dr[n] for n in names_T}
    names_S = NAMES_S
    dS = {n: dr[n] for n in names_S}
    c1_d, c2_d, iota_d, gmask_d, sgn_d, J_d = dr["c1_S"], dr["c2_S"], dr["iota"], dr["gmask"], dr["sgn"], dr["Jm"]
    dsk = colvec(P, "dskip_cs", dr["dskip_cs"], 2)
    P.mk_banks(4)
    ybank = [P.ps("ybank%d" % i, [128, NT], F32) for i in range(2)]
    igb = P.ps("igb", [128, NT], F32)

    def ld(name, d, ncol):
        return colvec(P, name, d, ncol)

    lt = {n: ld("s_" + n, dT[n], 128) for n in names_T}
    cnt = [0]

    def newT(nm, ncol=128):
        cnt[0] += 1
        return P.sb("%s_%d" % (nm, cnt[0]), [128, ncol], F32)

    def derive(lamre, lamim, logdt, ncol, pfx):
        o = {}
        lr = newT(pfx + "lr", ncol)
        ts(P, "dve", lr[:, :], lamre[:, :], -1e-4, None, ALU.min, None, [lamre], [lr])
        dt = newT(pfx + "dt", ncol)
        act(P, dt[:, :], logdt[:, :], AF.Exp, [logdt], [dt])
        e = newT(pfx + "e", ncol)
        tt(P, "dve", e[:, :], lr[:, :], dt[:, :], ALU.mult, [lr, dt], [e])
        th = newT(pfx + "th", ncol)
        tt(P, "dve", th[:, :], lamim[:, :], dt[:, :], ALU.mult, [lamim, dt], [th])
        thr = newT(pfx + "thr", ncol)
        tmp = newT(pfx + "tmp", ncol)
        range_reduce(P, "dve", thr[:, :], th[:, :], (tmp[:, :], tmp), [th], [thr])
        thc = newT(pfx + "thc", ncol)
        range_reduce(P, "dve", thc[:, :], th[:, :], (tmp[:, :], tmp), [th], [thc], shift=PI / 2)
        mag = newT(pfx + "mag", ncol)
        act(P, mag[:, :], e[:, :], AF.Exp, [e], [mag])
        sn = newT(pfx + "sin", ncol)
        act(P, sn[:, :], thr[:, :], AF.Sin, [thr], [sn])
        cs = newT(pfx + "cos", ncol)
        act(P, cs[:, :], thc[:, :], AF.Sin, [thc], [cs])
        o.update(lr=lr, li=lamim, e=e, thr=thr, mag=mag, sin=sn, cos=cs)
        return o

    dl = derive(lt["lamre_T"], lt["lamim_T"], lt["logdt_T"], 128, "T")

    def mul(a, b, nm):
        t = newT(nm)
        tt(P, "dve", t[:, :], a[:, :], b[:, :], ALU.mult, [a, b], [t])
        return t

    def addsub(a, b, op, nm):
        t = newT(nm)
        tt(P, "dve", t[:, :], a[:, :], b[:, :], op, [a, b], [t])
        return t

    are = mul(dl["mag"], dl["cos"], "are")
    aim = mul(dl["mag"], dl["sin"], "aim")
    den = addsub(mul(dl["lr"], dl["lr"], "lr2"), mul(dl["li"], dl["li"], "li2"), ALU.add, "den")
    rden = newT("rden")
    P.op("dve", lambda E: E.reciprocal(out=rden[:, :], in_=den[:, :]), reads=[den], writes=[rden])
    nr = newT("nr")
    ts(P, "dve", nr[:, :], are[:, :], -1.0, None, ALU.add, None, [are], [nr])
    fre = mul(addsub(mul(nr, dl["lr"], "f1"), mul(aim, dl["li"], "f2"), ALU.add, "f3"), rden, "fre")
    fim = mul(addsub(mul(aim, dl["lr"], "f4"), mul(nr, dl["li"], "f5"), ALU.subtract, "f6"), rden, "fim")
    bbre = addsub(mul(fre, lt["bre_T"], "b1"), mul(fim, lt["bim_T"], "b2"), ALU.subtract, "bbre")
    bbim = addsub(mul(fre, lt["bim_T"], "b3"), mul(fim, lt["bre_T"], "b4"), ALU.add, "bbim")
    nbbim = newT("nbbim")
    ts(P, "dve", nbbim[:, :], bbim[:, :], -1.0, None, ALU.mult, None, [bbim], [nbbim])
    gmask = ld("gmask", gmask_d, 8)
    W1 = P.sb("W1pad", [128, NG, 128], BF16)
    W2 = P.sb("W2pad", [128, NG, 128], BF16)
    for jj in range(2):
        for gl in range(8):
            gg = jj * 8 + gl
            ms = gmask[:, gl:gl + 1]
            sl = slice(jj * 64, (jj + 1) * 64)
            ts(P, "dve", W1[:, gg, 0:64], bbre[:, sl], ms, None, ALU.mult, None, [bbre, gmask], [W1])
            ts(P, "dve", W1[:, gg, 64:128], bbim[:, sl], ms, None, ALU.mult, None, [bbim, gmask], [W1])
            ts(P, "dve", W2[:, gg, 0:64], nbbim[:, sl], ms, None, ALU.mult, None, [nbbim, gmask], [W2])
            ts(P, "dve", W2[:, gg, 64:128], bbre[:, sl], ms, None, ALU.mult, None, [bbre, gmask], [W2])

    ls_ = {n: ld("s_" + n, dS[n], NG) for n in names_S}
    ds = derive(ls_["lamre_S"], ls_["lamim_S"], ls_["logdt_S"], NG, "S")
    rho = ds["mag"]
    iota = ld("iota", iota_d, TS + 1)
    COS = P.sb("COS", [128, NG, TS + 1], F32)
    SIN = P.sb("SIN", [128, NG, TS + 1], F32)
    ARG = P.sb("ARG", [128, NG, TS + 1], F32)
    TMP = P.sb("TMPA", [128, NG, TS + 1], F32)
    Gg = [P.sb("Gg%d" % i, [128, TS], F32) for i in range(NG)]
    GT = [P.sb("GT%d" % i, [128, NG], F32) for i in range(2)]
    for gg in range(NG):
        ts(P, "dve", ARG[:, gg, :], iota[:, :], ds["thr"][:, gg:gg + 1], None, ALU.mult, None, [iota, ds["thr"]], [ARG])
    range_reduce(P, "dve", SIN[:, :, :], ARG[:, :, :], (TMP[:, :, :], TMP), [ARG], [SIN])
    range_reduce(P, "dve", COS[:, :, :], ARG[:, :, :], (TMP[:, :, :], TMP), [ARG], [COS], shift=PI / 2)
    act(P, SIN[:, :, :], SIN[:, :, :], AF.Sin, [SIN], [SIN])
    act(P, COS[:, :, :], COS[:, :, :], AF.Sin, [COS], [COS])
    c1 = ld("c1", c1_d, NG * 16)
    c2 = ld("c2", c2_d, NG * 16)
    sgn = ld("sgn", sgn_d, 1)
    L1 = P.sb("L1pad", [128, NG, 128], BF16)
    L2 = P.sb("L2pad", [128, NG, 128], BF16)
    P.op("pool", lambda E: E.memset(L1[:, :, :], 0.0), writes=[L1])
    P.op("pool", lambda E: E.memset(L2[:, :, :], 0.0), writes=[L2])
    for gg in range(NG):
        gl = gg % 8
        ts(P, "dve", L1[:, gg, gl * 16:(gl + 1) * 16], c1[:, gg * 16:(gg + 1) * 16], sgn[:, 0:1], None, ALU.mult, None, [c1, sgn], [L1])
        ts(P, "dve", L2[:, gg, gl * 16:(gl + 1) * 16], c2[:, gg * 16:(gg + 1) * 16], -1.0, None, ALU.mult, None, [c2], [L2])
    Jm = P.sb("Jm", [128, 128], F32)
    P.dma("sp", Jm[:, :], J_d.t[:, :], reads=[J_d], writes=[Jm])

    ust = Rot(P, "ust", 3, [128, 2, TS], F32)
    ubf = Rot(P, "ubf", 3, [128, 2, TS], BF16)
    t1r = Rot(P, "t1r", 4, [128, TS], F32)
    t2r = Rot(P, "t2r", 4, [128, TS], F32)
    xtr = Rot(P, "xtr", 4, [128, TS], F32)
    h1r = Rot(P, "h1r", 4, [128, TS], BF16)
    h2r = Rot(P, "h2r", 4, [128, TS], BF16)
    S0 = [P.sb("S0_%d" % i, [128, NG], F32) for i in range(2)]
    ys = Rot(P, "ys", 3, [128, 2, TS], BF16)
    P.op("pool", lambda E: E.memset(S0[0][:, :], 0.0), writes=[S0[0]])
    nch = SEQ // TS
    units = []
    chunk_res = {}
    for ch in range(nch):
        for jj in range(2):
            for gl in range(8):
                units.append(dict(ch=ch, jj=jj, gl=gl, gg=jj * 8 + gl))
    nu = len(units)

    def chunk_setup(ch):
        c0 = ch * TS
        us = ust.get()
        ub = ubf.get()
        P.dma("sp", us[:, :, :], uT.t[:, c0:c0 + TS].rearrange("(j p) t -> p j t", p=128), reads=[uT], writes=[us])
        cp(P, "act", ub[:, :, :], us[:, :, :], [us], [ub])
        chunk_res[ch] = dict(us=us, ub=ub, yo=ys.get())

    def s1(i):
        U = units[i]
        ch, jj, gg = U["ch"], U["jj"], U["gg"]
        if jj == 0 and U["gl"] == 0:
            chunk_setup(ch)
        ub = chunk_res[ch]["ub"]
        p1 = P.bank()
        p2 = P.bank()
        P.op("pe", lambda E, p1=p1, gg=gg, ub=ub, jj=jj: E.matmul(p1[:, 0:TS], lhsT=W1[:, gg, :], rhs=ub[:, jj, :], start=True, stop=True), reads=[W1, ub], writes=[p1])
        P.op("pe", lambda E, p2=p2, gg=gg, ub=ub, jj=jj: E.matmul(p2[:, 0:TS], lhsT=W2[:, gg, :], rhs=ub[:, jj, :], start=True, stop=True), reads=[W2, ub], writes=[p2])
        t1 = t1r.get()
        t2 = t2r.get()
        xt = xtr.get()
        tt(P, "dve", t1[:, :], p1[:, 0:TS], COS[:, gg, 1:TS + 1], ALU.mult, [p1, COS], [t1])
        tt(P, "dve", t2[:, :], p2[:, 0:TS], SIN[:, gg, 1:TS + 1], ALU.mult, [p2, SIN], [t2])
        tt(P, "pool", xt[:, :], t1[:, :], t2[:, :], ALU.subtract, [t1, t2], [xt])
        U["xt"] = xt

    def s2(i):
        U = units[i]
        ch, gg = U["ch"], U["gg"]
        gt, s0 = GT[ch % 2], S0[ch % 2]
        xt = U["xt"]
        G = Gg[gg]
        P.op("dve", lambda E, G=G, gg=gg, xt=xt, s0=s0: E.tensor_tensor_scan(
            out=G[:, :], data0=rho[:, gg:gg + 1].to_broadcast([128, TS]), data1=xt[:, :],
            initial=s0[:, gg:gg + 1], op0=ALU.mult, op1=ALU.add), reads=[rho, xt, s0], writes=[G])
        if ch + 1 < nch:
            cp(P, "act", gt[:, gg:gg + 1], G[:, TS - 1:TS], [G], [gt])
        h1 = h1r.get()
        h2 = h2r.get()
        tt(P, "dve", h1[:, :], G[:, :], COS[:, gg, 1:TS + 1], ALU.mult, [G, COS], [h1])
        tt(P, "pool", h2[:, :], G[:, :], SIN[:, gg, 1:TS + 1], ALU.mult, [G, SIN], [h2])
        U["h1"], U["h2"] = h1, h2
        if gg == NG - 1 and ch + 1 < nch:
            s1_ = S0[(ch + 1) % 2]
            P.op("pe", lambda E, gt=gt: E.matmul(igb[:, 0:NG], lhsT=Jm[:, :], rhs=gt[:, :], start=True, stop=True), reads=[Jm, gt], writes=[igb])
            ta = P.sb("bta%d" % ch, [128, NG], F32)
            tb = P.sb("btb%d" % ch, [128, NG], F32)
            tt(P, "dve", ta[:, :], gt[:, :], COS[:, :, TS], ALU.mult, [gt, COS], [ta])
            tt(P, "dve", tb[:, :], igb[:, 0:NG], SIN[:, :, TS], ALU.mult, [igb, SIN], [tb])
            tt(P, "dve", s1_[:, :], ta[:, :], tb[:, :], ALU.add, [ta, tb], [s1_])

    def s3(i):
        U = units[i]
        ch, jj, gl, gg = U["ch"], U["jj"], U["gl"], U["gg"]
        yb = ybank[jj]
        h1, h2 = U["h1"], U["h2"]
        P.op("pe", lambda E, yb=yb, gg=gg, h1=h1, gl=gl: E.matmul(yb[:, 0:TS], lhsT=L1[:, gg, :], rhs=h1[:, :], start=(gl == 0), stop=False), reads=[L1, h1], writes=[yb])
        P.op("pe", lambda E, yb=yb, gg=gg, h2=h2, gl=gl: E.matmul(yb[:, 0:TS], lhsT=L2[:, gg, :], rhs=h2[:, :], start=False, stop=(gl == 7)), reads=[L2, h2], writes=[yb])
        if gl == 7:
            cr = chunk_res[ch]
            yo, us = cr["yo"], cr["us"]
            P.op("dve", lambda E, yo=yo, us=us, yb=yb, jj=jj: E.scalar_tensor_tensor(
                out=yo[:, jj, :], in0=us[:, jj, :], scalar=dsk[:, jj:jj + 1], in1=yb[:, 0:TS], op0=ALU.mult, op1=ALU.add),
                reads=[us, dsk, yb], writes=[yo])
            if jj == 1:
                c0 = ch * TS
                ysrc = dr["y_src"][c0 // 2048]
                P.dma("act", ysrc.t[:, c0 % 2048:c0 % 2048 + TS].rearrange("(j p) t -> p j t", p=128), yo[:, :, :], reads=[yo], writes=[ysrc])
                if (c0 + TS) % 2048 == 0:
                    P.coll("AllGather", ysrc, dr["y_all"][c0 // 2048], GROUPS)

    for j in range(nu + 2):
        if j < nu:
            s1(j)
        if 0 <= j - 1 < nu:
            s2(j - 1)
        if 0 <= j - 2 < nu:
            s3(j - 2)
    P.wait_all("act", dr["y_all"])
    P.emit()


def gelu_tanh(P, dn, out_bf, y, reads):
    s = dn.tmp.get()
    s2 = dn.tmp.get()
    act(P, s[:, :], y, AF.Square, reads, [s])
    ts(P, "dve", s[:, :], s[:, :], 0.044715, 1.0, ALU.mult, ALU.add, [s], [s])
    tt(P, "dve", s2[:, :], s[:, :], y, ALU.mult, [s] + list(reads), [s2])
    act(P, s2[:, :], s2[:, :], AF.Sigmoid, [s2], [s2], scale=1.5957691216057308)
    return s2


def ple(P, dn, Wpg, Wpp, Xb, Pb, X1):
    for m in range(8):
        wg = dn.load_w(Wpg, m, 8)
        wp = dn.load_w(Wpp, m, 2)
        for n in range(dn.T // NT):
            pg = P.bank()
            pp = P.bank()
            for kc in range(8):
                P.op("pe", lambda E, pg=pg, wg=wg, kc=kc, n=n: E.matmul(
                    pg[:, :], lhsT=wg[:, kc, :], rhs=Xb[:, kc, n * NT:(n + 1) * NT], start=(kc == 0), stop=(kc == 7)),
                    reads=[wg, Xb], writes=[pg])
            for kc in range(2):
                P.op("pe", lambda E, pp=pp, wp=wp, kc=kc, n=n: E.matmul(
                    pp[:, :], lhsT=wp[:, kc, :], rhs=Pb[:, kc, n * NT:(n + 1) * NT], start=(kc == 0), stop=(kc == 1)),
                    reads=[wp, Pb], writes=[pp])
            sg = dn.tmp.get()
            act(P, sg[:, :], pg[:, :], AF.Sigmoid, [pg], [sg])
            t = dn.tmp.get()
            tt(P, "dve", t[:, :], sg[:, :], pp[:, :], ALU.mult, [sg, pp], [t])
            tt(P, "pool", X1[:, m, n * NT:(n + 1) * NT], X1[:, m, n * NT:(n + 1) * NT], t[:, :], ALU.add, [X1, t], [X1])


NH = 4
NQT = SEQ // NT


def phase_D(nc, dr):
    P = Prog(nc)
    qT, kT, vT, tri_d = dr["qT_cs"], dr["kT_cs"], dr["vT_cs"], dr["ntri"]
    ident = P.sb("ident", [128, 128], BF16)
    P.op("pool", lambda E: E.memset(ident[:, :], 0.0), writes=[ident])
    P.op("pool", lambda E: E.affine_select(out=ident[:, :], in_=ident[:, :], pattern=[[-1, 128]], compare_op=ALU.not_equal,
                                           fill=1.0, base=0, channel_multiplier=1), reads=[ident], writes=[ident])
    vtb = Rot(P, "vtb", 3, [64, 2048], BF16)
    tpb = P.ps("tpb", [128, NT], BF16)
    zb = [P.ps("zb%d" % i, [128, NT], F32) for i in range(4)]
    ob = [P.ps("ob%d" % i, [64, NT], F32) for i in range(2)]
    st = P.sb("tri_st", [128, 128], F32)
    P.dma("sp", st[:, :], tri_d.t[:, :], reads=[tri_d], writes=[st])
    ntri = P.sb("ntri", [128, 128], BF16)
    cp(P, "dve", ntri[:, :], st[:, :], [st], [ntri])
    negm = P.sb("negm", [128, 4, NT], BF16)
    P.op("pool", lambda E: E.memset(negm[:, :, :], 0.0), writes=[negm])
    for d_ in range(4):
        P.op("pool", lambda E, d_=d_: E.affine_select(
            out=negm[:, d_, :], in_=negm[:, d_, :], pattern=[[1, NT]], compare_op=ALU.is_gt, fill=-30000.0, base=-128 * d_, channel_multiplier=-1),
            reads=[negm], writes=[negm])
    nones = P.sb("nones", [128, 128], BF16)
    P.op("pool", lambda E: E.memset(nones[:, :], -1.0), writes=[nones])
    Qb = [P.sb("Qb%d" % i, [128, SEQ], BF16) for i in range(2)]
    Kb = [P.sb("Kb%d" % i, [128, SEQ], BF16) for i in range(2)]
    Vb = [P.sb("Vb%d" % h, [128, 64 * 64], BF16) for h in range(NH)]
    er = Rot(P, "er", 3, [128, NT], F32)
    spr = Rot(P, "spr", 4, [128, NT], BF16)
    wr = Rot(P, "wr", 4, [128, NT], BF16)
    racc = Rot(P, "racc", 3, [128, NT], BF16)
    ost = Rot(P, "ost", 2, [64, NT], BF16)
    for pr in range(2):
        for c in range(SEQ // 2048):
            P.dma("sp", Qb[pr][:, c * 2048:(c + 1) * 2048], qT.t[pr * 128:(pr + 1) * 128, c * 2048:(c + 1) * 2048], reads=[qT], writes=[Qb[pr]])
            P.dma("sp", Kb[pr][:, c * 2048:(c + 1) * 2048], kT.t[pr * 128:(pr + 1) * 128, c * 2048:(c + 1) * 2048], reads=[kT], writes=[Kb[pr]])

    def load_v(h):
        for c in range(SEQ // 2048):
            vb_ = vtb.get()
            P.dma("sp", vb_[:, :], vT.t[h * 64:(h + 1) * 64, c * 2048:(c + 1) * 2048], reads=[vT], writes=[vb_])
            for k8 in range(2):
                for j in range(8):
                    blk = k8 * 8 + j
                    P.op("pe", lambda E, vb_=vb_, j=j, blk=blk: E.transpose(
                        out=tpb[:, j * 64:(j + 1) * 64], in_=vb_[:, blk * 128:(blk + 1) * 128], identity=ident[0:64, 0:64]),
                        reads=[vb_, ident], writes=[tpb])
                kb0 = c * 16 + k8 * 8
                cp(P, "dve", Vb[h][:, kb0 * 64:(kb0 + 8) * 64], tpb[:, :], [tpb], [Vb[h]])
    load_v(0)
    blocks = []
    for h in range(NH):
        for qt in range(NQT):
            kbs = list(range(4 * qt + 3, -1, -1))
            for idx, kb in enumerate(kbs):
                blocks.append(dict(h=h, qt=qt, kb=kb, idx=idx, n=len(kbs), g=h * NQT + qt))
    nb = len(blocks)

    def operands(B):
        hp, pr = B["h"] % 2, B["h"] // 2
        ksl = Kb[pr][hp * 64:(hp + 1) * 64, B["kb"] * 128:(B["kb"] + 1) * 128]
        qsl = Qb[pr][hp * 64:(hp + 1) * 64, B["qt"] * NT:(B["qt"] + 1) * NT]
        return ksl, qsl, Kb[pr], Qb[pr]

    def mask(t, B):
        base = B["qt"] * NT - 128 * B["kb"]
        P.op("pool", lambda E, t=t, base=base: E.affine_select(
            out=t[:, :], in_=t[:, :], pattern=[[1, NT]], compare_op=ALU.is_gt, fill=0.0, base=base, channel_multiplier=-1),
            reads=[t], writes=[t])

    def stage1a(i):
        B = blocks[i]
        ksl, qsl, kres, qres = operands(B)
        if B["qt"] == 2 and B["idx"] == 0 and B["h"] + 1 < NH:
            load_v(B["h"] + 1)
        z = zb[i % 4]
        P.op("pe", lambda E, z=z, ksl=ksl, qsl=qsl: E.matmul(z[:, :], lhsT=ksl, rhs=qsl, start=True, stop=False), reads=[kres, qres], writes=[z])
        if B["kb"] >= 4 * B["qt"]:
            d_ = B["kb"] - 4 * B["qt"]
            P.op("pe", lambda E, z=z, d_=d_: E.matmul(z[:, :], lhsT=ident[:, :], rhs=negm[:, d_, :], start=False, stop=False), reads=[ident, negm], writes=[z])
        e = er.get()
        act(P, e[:, :], z[:, :], AF.Exp, [z], [e])
        B["e"] = e

    def stage1b(i):
        B = blocks[i]
        e = B["e"]
        sp = spr.get()
        P.op("act", lambda E, sp=sp, e=e: E.activation(out=sp[:, :], in_=e[:, :], func=AF.Ln, scale=1.0, bias=1.0),
             reads=[e], writes=[sp], skip_same=True)
        B["sp"] = sp

    def stage2(i):
        B = blocks[i]
        b = zb[i % 4]
        sp = B["sp"]
        first = B["idx"] == 0
        ra_prev = None if first else blocks[i - 1]["ra"]
        P.op("pe", lambda E, b=b, sp=sp, first=first: E.matmul(b[:, :], lhsT=ntri[:, :], rhs=sp[:, :], start=False, stop=first), reads=[ntri, sp], writes=[b])
        if not first:
            P.op("pe", lambda E, b=b, ra=ra_prev: E.matmul(b[:, :], lhsT=nones[:, :], rhs=ra[:, :], start=False, stop=True), reads=[nones, ra_prev], writes=[b])
        w = wr.get()
        act(P, w[:, :], b[:, :], AF.Exp, [b], [w])
        B["w"] = w
        if B["idx"] + 1 < B["n"]:
            ra = racc.get()
            if first:
                cp(P, "dve", ra[:, :], sp[:, :], [sp], [ra])
            else:
                tt(P, "dve", ra[:, :], ra_prev[:, :], sp[:, :], ALU.add, [ra_prev, sp], [ra])
            B["ra"] = ra

    def stage3(i):
        B = blocks[i]
        o_ps = ob[B["g"] % 2]
        w = B["w"]
        h, kb = B["h"], B["kb"]
        P.op("pe", lambda E, o_ps=o_ps, w=w, h=h, kb=kb, B=B: E.matmul(
            o_ps[:, :], lhsT=Vb[h][:, kb * 64:(kb + 1) * 64], rhs=w[:, :], start=(B["idx"] == 0), stop=(B["idx"] == B["n"] - 1)),
            reads=[Vb[h], w], writes=[o_ps])
        if B["idx"] == B["n"] - 1:
            o = ost.get()
            cp(P, "dve", o[:, :], o_ps[:, :], [o_ps], [o])
            q0 = B["qt"] * NT
            osrc = dr["o_src"][q0 // 2048]
            P.dma("sp", osrc.t[h * 64:(h + 1) * 64, q0 % 2048:q0 % 2048 + NT], o[:, :], reads=[o], writes=[osrc])
            if h == NH - 1 and (q0 + NT) % 2048 == 0:
                P.coll("AllGather", osrc, dr["o_all"][q0 // 2048], GROUPS)

    for j in range(nb + 2):
        if j < nb:
            stage1a(j)
            stage1b(j)
        if 0 <= j - 1 < nb:
            stage2(j - 1)
        if 0 <= j - 2 < nb:
            stage3(j - 2)
    P.wait_all("act", dr["o_all"])
    P.emit()


GROUPS = [[0, 1, 2, 3], [4, 5, 6, 7]]
TOKC = 2048
TP = 1024


def proj_pass(P, dn, src_fn, xs, rs, jobs, col0):
    for kc in range(8):
        srcs = src_fn(kc)
        if isinstance(srcs, tuple):
            P.dma("sp", xs[:, kc, :], srcs[1], reads=[srcs[0]], writes=[xs])
        else:
            w_ = TP // len(srcs)
            for i_, (sr_, sa_) in enumerate(srcs):
                P.dma("sp", xs[:, kc, i_ * w_:(i_ + 1) * w_], sa_, reads=[sr_], writes=[xs])
    dn.rstd(xs, rs)
    done = {}
    for job in jobs:
        xb, gain, wbs, MC, dst = job[:5]
        oscale = job[5] if len(job) > 5 else None
        if id(xb) not in done:
            done[id(xb)] = 1
            for kc in range(8):
                scale_cast(P, kc, xb[:, kc, :], xs[:, kc, :], gain[:, kc:kc + 1], [xs, gain], [xb])

        def epi(m, n, ps, dst=dst, oscale=oscale):
            if oscale is None:
                o = dn.ost.get()
                tt(P, "dve", o[:, :], ps[:, :], rs[:, n * NT:(n + 1) * NT], ALU.mult, [ps, rs], [o])
            else:
                o = dn.ostb.get()
                P.op("dve", lambda E, o=o, ps=ps, n=n: E.scalar_tensor_tensor(
                    out=o[:, :], in0=ps[:, :], scalar=oscale, in1=rs[:, n * NT:(n + 1) * NT], op0=ALU.mult, op1=ALU.mult),
                    reads=[ps, rs], writes=[o])
            dn.store(dst, m, n, col0, o, o[:, :])
        dn.linear(None, 8, MC, xb, epi, wbs=wbs)


def phase_A(nc, dr):
    P = Prog(nc)
    P.mk_banks(6)
    dn = Dense(P, TP)
    g = colvec(P, "g_pre", dr["g_pre"], 8)
    xsr = Rot(P, "xs", 2, [128, 8, TP], F32)
    xb = P.sb("xb", [128, 8, TP], BF16)
    rs = P.sb("rs", [128, TP], F32)
    xT = dr["xT_full"]
    wbs = dn.prep_w("wu", dr["w_in_u"], 8, 2)
    for pa in range(SEQ // TP):
        t0 = pa * TP
        proj_pass(P, dn, lambda kc, t0=t0: (xT, xT.t[kc * 128:(kc + 1) * 128, t0:t0 + TP]), xsr.get(), rs,
                  [(xb, g, wbs, 2, dr["uT_cs"])], t0)
    P.wait_all("pool", [dr["uT_cs"]])
    P.emit()


def mk_select(P, dn, sel):
    ident = P.sb("identf", [128, 128], F32)
    P.op("pool", lambda E: E.memset(ident[:, :], 0.0), writes=[ident])
    P.op("pool", lambda E: E.affine_select(out=ident[:, :], in_=ident[:, :], pattern=[[-1, 128]], compare_op=ALU.not_equal,
                                           fill=1.0, base=0, channel_multiplier=1), reads=[ident], writes=[ident])
    mI = P.sb("mI", [128, 4, 128], BF16)
    for s_ in range(4):
        ts(P, "dve", mI[:, s_, :], ident[:, :], sel[:, s_:s_ + 1], None, ALU.mult, None, [ident, sel], [mI])
    dn.mI = mI


def select4(P, dn, src_fn, sel, dt):
    ps = P.bank()
    for s in range(4):
        it = dn.istb.get()
        sres, sap = src_fn(s)
        P.dma("sp", it[:, :], sap, reads=[sres], writes=[it])
        P.op("pe", lambda E, ps=ps, it=it, s=s: E.matmul(ps[:, :], lhsT=dn.mI[:, s, :], rhs=it[:, :], start=(s == 0), stop=(s == 3)),
             reads=[dn.mI, it], writes=[ps])
    acc = dn.tmp.get()
    act(P, acc[:, :], ps[:, :], AF.Copy, [ps], [acc])
    return acc


def phase_C(nc, dr):
    P = Prog(nc)
    P.mk_banks(7)
    dn = Dense(P, TP)
    dn.istb = Rot(P, "istb", 8, [128, NT], BF16)
    sel = colvec(P, "sel", dr["sel"], 4)
    mk_select(P, dn, sel)
    gpre = colvec(P, "g_pre", dr["g_pre"], 8)
    vl = []
    for i in range(4):
        t_ = P.sb("vec%d" % i, [128, 8], F32)
        P.dma("sp", t_[:, :], dr["vecsC"].t[:, i * 8:(i + 1) * 8], reads=[dr["vecsC"]], writes=[t_])
        vl.append(t_)
    bglu, gpost, gbpre, _unused = vl
    xT, y_all, pT = dr["xT_own"], dr["y_all"], dr["p0T"]
    Gb = P.sb("Gb", [128, 8, TP], BF16)
    SGb = P.sb("SGb", [128, 8, TP], BF16)
    Y2b = P.sb("Y2b", [128, 8, TP], BF16)
    X1 = P.sb("X1", [128, 8, TP], F32)
    Pb = P.sb("Pb", [128, 2, TP], BF16)
    rs = P.sb("rs", [128, TP], F32)
    for pa in range(TOKC // TP):
        t0 = pa * TP
        for kc in range(8):
            P.dma("sp", X1[:, kc, :], xT.t[kc * 128:(kc + 1) * 128, t0:t0 + TP], reads=[xT], writes=[X1])
        dn.rstd(X1, rs)
        for kc in range(8):
            scale_cast(P, kc, Y2b[:, kc, :], X1[:, kc, :], gpre[:, kc:kc + 1], [X1, gpre], [Y2b])

        def epi_gate(m, n, ps):
            t = dn.tmp.get()
            tt(P, "dve", t[:, :], ps[:, :], rs[:, n * NT:(n + 1) * NT], ALU.mult, [ps, rs], [t])
            act(P, SGb[:, m, n * NT:(n + 1) * NT], t[:, :], AF.Silu, [t], [SGb])
        dn.linear(dr["w_in_g"], 8, 8, Y2b, epi_gate)
        for kc in range(8):
            for n in range(TP // NT):
                def ysrc_fn(s, n=n, kc=kc):
                    g0 = s * TOKC + t0 + n * NT
                    ya = y_all[g0 // 2048]
                    return ya, ya.t[kc * 128:(kc + 1) * 128, g0 % 2048:g0 % 2048 + NT]
                y = select4(P, dn, ysrc_fn, sel, BF16)
                s2 = gelu_tanh(P, dn, None, y[:, :], [y])
                tt(P, "pool", Gb[:, kc, n * NT:(n + 1) * NT], s2[:, :], y[:, :], ALU.mult, [s2, y], [Gb])

        def epi_glu(m, n, ps):
            sg = dn.tmp.get()
            act(P, sg[:, :], ps[:, :], AF.Sigmoid, [ps, bglu], [sg], bias=bglu[:, m:m + 1])
            t = dn.tmp.get()
            tt(P, "dve", t[:, :], sg[:, :], Gb[:, m, n * NT:(n + 1) * NT], ALU.mult, [sg, Gb], [t])
            tt(P, "pool", Y2b[:, m, n * NT:(n + 1) * NT], t[:, :], SGb[:, m, n * NT:(n + 1) * NT], ALU.mult, [t, SGb], [Y2b])
        dn.linear(dr["w_glu"], 8, 8, Gb, epi_glu)

        def epi_out(m, n, ps):
            act(P, X1[:, m, n * NT:(n + 1) * NT], ps[:, :], AF.Copy, [ps], [X1])
        dn.linear(dr["w_out0"], 8, 8, Y2b, epi_out)
        dn.rstd(X1, rs)
        for kc in range(8):
            for n in range(TP // NT):
                c0 = t0 + n * NT
                ix = dn.ist.get()
                P.dma("sp", ix[:, :], xT.t[kc * 128:(kc + 1) * 128, c0:c0 + NT], reads=[xT], writes=[ix])
                t = dn.tmp.get()
                P.op("dve", lambda E, t=t, kc=kc, n=n: E.scalar_tensor_tensor(
                    out=t[:, :], in0=X1[:, kc, n * NT:(n + 1) * NT], scalar=gpost[:, kc:kc + 1], in1=rs[:, n * NT:(n + 1) * NT],
                    op0=ALU.mult, op1=ALU.mult), reads=[X1, gpost, rs], writes=[t])
                tt(P, "pool", X1[:, kc, n * NT:(n + 1) * NT], t[:, :], ix[:, :], ALU.add, [t, ix], [X1])
                cp(P, "act", Gb[:, kc, n * NT:(n + 1) * NT], X1[:, kc, n * NT:(n + 1) * NT], [X1], [Gb])
        for kc in range(2):
            for n in range(TP // NT):
                c0 = t0 + n * NT
                ip = dn.ist.get()
                P.dma("sp", ip[:, :], pT.t[kc * 128:(kc + 1) * 128, c0:c0 + NT], reads=[pT], writes=[ip])
                cp(P, "dve", Pb[:, kc, n * NT:(n + 1) * NT], ip[:, :], [ip], [Pb])
        ple(P, dn, dr["w_pg0"], dr["w_pp0"], Gb, Pb, X1)
        dn.rstd(X1, rs)
        for kc in range(8):
            cp(P, "dve", Y2b[:, kc, :], X1[:, kc, :], [X1], [Y2b])
            scale_cast(P, kc + 1, SGb[:, kc, :], X1[:, kc, :], gbpre[:, kc:kc + 1], [X1, gbpre], [SGb])
            P.dma("pool", dr["x1_own"].t[kc * 128:(kc + 1) * 128, t0:t0 + TP], X1[:, kc, :], reads=[X1], writes=[dr["x1_own"]])
            for n in range(TP // NT):
                xsrc = dr["x1_src"][(t0 + n * NT) // NT]
                P.dma("pool", xsrc.t[kc * 128:(kc + 1) * 128, :], Y2b[:, kc, n * NT:(n + 1) * NT], reads=[Y2b], writes=[xsrc])

        def epi_g1(m, n, ps):
            o = dn.ost.get()
            tt(P, "dve", o[:, :], ps[:, :], rs[:, n * NT:(n + 1) * NT], ALU.mult, [ps, rs], [o])
            dn.store(dr["g1T"], m, n, t0, o, o[:, :])
        for k in range(t0 // NT, (t0 + TP) // NT):
            P.coll("AllGather", dr["x1_src"][k], dr["x1_all"][k], GROUPS)
        dn.linear(dr["w_bin_g"], 8, 8, SGb, epi_g1)
    P.wait_all("pool", dr["x1_all"] + [dr["x1_own"], dr["g1T"]])
    P.emit()


def phase_QKV(nc, dr):
    P = Prog(nc)
    P.mk_banks(6)
    dn = Dense(P, TP)
    gkv = colvec(P, "g_kv", dr["g_kv"], 8)
    gbpre = colvec(P, "g_bpre", dr["g_bpre"], 8)
    xsr = Rot(P, "xs", 2, [128, 8, TP], BF16)
    xq = P.sb("xq", [128, 8, TP], BF16)
    xk = P.sb("xk", [128, 8, TP], BF16)
    rs = P.sb("rs", [128, TP], F32)
    xa = dr["x1_all"]
    wq = dn.prep_w("wq", dr["w_q"], 8, 2)
    wk = dn.prep_w("wk", dr["w_k"], 8, 2)
    wv = dn.prep_w("wv", dr["w_v"], 8, 2)
    for pa in range(SEQ // TP):
        t0 = pa * TP
        s, tl = t0 // TOKC, t0 % TOKC
        proj_pass(P, dn, lambda kc, s=s, tl=tl: [(xa[(tl + h_ * NT) // NT], xa[(tl + h_ * NT) // NT].t[s * D + kc * 128:s * D + (kc + 1) * 128, :]) for h_ in range(TP // NT)], xsr.get(), rs,
                  [(xq, gbpre, wq, 2, dr["qT_cs"], 1.0), (xk, gkv, wk, 2, dr["kT_cs"], 0.125), (xk, gkv, wv, 2, dr["vT_cs"], 1.0)], t0)
    P.wait_all("pool", [dr["qT_cs"], dr["kT_cs"], dr["vT_cs"]])
    P.emit()


def phase_E(nc, dr):
    P = Prog(nc)
    P.mk_banks(7)
    dn = Dense(P, TP)
    dn.istb = Rot(P, "istb", 8, [128, NT], BF16)
    sel = colvec(P, "sel", dr["sel"], 4)
    mk_select(P, dn, sel)
    gpost = colvec(P, "g_bpost", dr["g_bpost"], 8)
    Ob = P.sb("Ob", [128, 8, TP], BF16)
    Xb = P.sb("Xb", [128, 8, TP], BF16)
    X1 = P.sb("X1", [128, 8, TP], F32)
    Pb = P.sb("Pb", [128, 2, TP], BF16)
    rs = P.sb("rs", [128, TP], F32)
    x1T, gT, pT, outT = dr["x1_own"], dr["g1T"], dr["p1T"], dr["outT"]
    for pa in range(TOKC // TP):
        t0 = pa * TP
        for kc in range(8):
            for n in range(TP // NT):
                c0 = t0 + n * NT
                def osrc_fn(s, n=n, kc=kc):
                    g0 = s * TOKC + t0 + n * NT
                    oa = dr["o_all"][g0 // 2048]
                    return oa, oa.t[kc * 128:(kc + 1) * 128, g0 % 2048:g0 % 2048 + NT]
                o = select4(P, dn, osrc_fn, sel, BF16)
                ig = dn.ist.get()
                P.dma("sp", ig[:, :], gT.t[kc * 128:(kc + 1) * 128, c0:c0 + NT], reads=[gT], writes=[ig])
                sg = dn.tmp.get()
                act(P, sg[:, :], ig[:, :], AF.Silu, [ig], [sg])
                tt(P, "pool", Ob[:, kc, n * NT:(n + 1) * NT], sg[:, :], o[:, :], ALU.mult, [sg, o], [Ob])

        def epi_out(m, n, ps):
            act(P, X1[:, m, n * NT:(n + 1) * NT], ps[:, :], AF.Copy, [ps], [X1])
        dn.linear(dr["w_out1"], 8, 8, Ob, epi_out)
        dn.rstd(X1, rs)
        for kc in range(8):
            for n in range(TP // NT):
                c0 = t0 + n * NT
                ix = dn.ist.get()
                P.dma("sp", ix[:, :], x1T.t[kc * 128:(kc + 1) * 128, c0:c0 + NT], reads=[x1T], writes=[ix])
                t = dn.tmp.get()
                P.op("dve", lambda E, t=t, kc=kc, n=n: E.scalar_tensor_tensor(
                    out=t[:, :], in0=X1[:, kc, n * NT:(n + 1) * NT], scalar=gpost[:, kc:kc + 1], in1=rs[:, n * NT:(n + 1) * NT],
                    op0=ALU.mult, op1=ALU.mult), reads=[X1, gpost, rs], writes=[t])
                tt(P, "pool", X1[:, kc, n * NT:(n + 1) * NT], t[:, :], ix[:, :], ALU.add, [t, ix], [X1])
                cp(P, "act", Xb[:, kc, n * NT:(n + 1) * NT], X1[:, kc, n * NT:(n + 1) * NT], [X1], [Xb])
        for kc in range(2):
            for n in range(TP // NT):
                c0 = t0 + n * NT
                ip = dn.ist.get()
                P.dma("sp", ip[:, :], pT.t[kc * 128:(kc + 1) * 128, c0:c0 + NT], reads=[pT], writes=[ip])
                cp(P, "dve", Pb[:, kc, n * NT:(n + 1) * NT], ip[:, :], [ip], [Pb])
        ple(P, dn, dr["w_pg1"], dr["w_pp1"], Xb, Pb, X1)
        for kc in range(8):
            P.dma("pool", outT.t[kc * 128:(kc + 1) * 128, t0:t0 + TP], X1[:, kc, :], reads=[X1], writes=[outT])
    P.wait_all("pool", [outT])
    P.emit()


IN_SPECS = {
    "xT_full": ([D, SEQ], F32), "xT_own": ([D, TOKC], F32), "p0T": ([256, TOKC], F32), "p1T": ([256, TOKC], F32),
    "sel": ([128, 4], F32), "g_pre": ([128, 8], F32), "w_in_u": ([D, 256], F32), "w_in_g": ([D, D], F32),
    "lamre_T": ([128, 128], F32), "lamim_T": ([128, 128], F32), "logdt_T": ([128, 128], F32), "bre_T": ([128, 128], F32),
    "bim_T": ([128, 128], F32), "lamre_S": ([128, NG], F32), "lamim_S": ([128, NG], F32), "logdt_S": ([128, NG], F32),
    "c1_S": ([128, NG * 16], F32), "c2_S": ([128, NG * 16], F32), "iota": ([128, TS + 1], F32), "gmask": ([128, 8], F32),
    "sgn": ([128, 1], F32), "Jm": ([128, 128], F32), "dskip_cs": ([128, 2], F32), "vecsC": ([128, 32], F32),
    "w_glu": ([D, D], F32), "w_out0": ([D, D], F32), "w_pg0": ([D, D], F32), "w_pp0": ([256, D], F32),
    "w_bin_g": ([D, D], F32), "g_kv": ([128, 8], F32), "g_bpre": ([128, 8], F32), "w_q": ([D, 256], F32),
    "w_k": ([D, 256], F32), "w_v": ([D, 256], F32), "ntri": ([128, 128], F32), "g_bpost": ([128, 8], F32),
    "w_out1": ([D, D], F32), "w_pg1": ([D, D], F32), "w_pp1": ([256, D], F32),
}
SCRATCH = {
    "uT_cs": ([256, SEQ], F32), "y_src": ([256, 2048], BF16, 4), "y_all": ([D, 2048], BF16, 4), "x1_own": ([D, TOKC], F32),
    "x1_src": ([D, NT], BF16, 4), "x1_all": ([4 * D, NT], BF16, 4), "g1T": ([D, TOKC], F32), "qT_cs": ([256, SEQ], BF16),
    "kT_cs": ([256, SEQ], BF16), "vT_cs": ([256, SEQ], BF16), "o_src": ([256, 2048], BF16, 4), "o_all": ([D, 2048], BF16, 4),
}


def build_fused():
    nc = bass.Bass("TRN2", target_bir_lowering=False)
    dr = {}
    for n, (shp, dt) in IN_SPECS.items():
        dr[n] = Res(n, nc.dram_tensor(n, list(shp), dt, kind="ExternalInput").ap())
    for n, spec in SCRATCH.items():
        shp, dt = spec[0], spec[1]
        if len(spec) == 3:
            dr[n] = [Res("%s%d" % (n, i), nc.dram_tensor("%s%d" % (n, i), list(shp), dt, kind="Internal").ap()) for i in range(spec[2])]
        else:
            dr[n] = Res(n, nc.dram_tensor(n, list(shp), dt, kind="Internal").ap())
    dr["outT"] = Res("outT", nc.dram_tensor("outT", [D, TOKC], F32, kind="ExternalOutput").ap())
    phase_A(nc, dr)
    phase_B(nc, dr)
    phase_C(nc, dr)
    phase_QKV(nc, dr)
    phase_D(nc, dr)
    phase_E(nc, dr)
    Prog.finish()
    return nc


def _f(a):
    return np.ascontiguousarray(np.asarray(a, dtype=np.float32))


def kernel(**inputs):
    inp = {k: np.asarray(v) for k, v in inputs.items()}
    x, p = inp["x"], inp["p"]
    lam_re, lam_im, log_dt = _f(inp["a_lam_re"][0]), _f(inp["a_lam_im"][0]), _f(inp["a_log_dt"][0])
    b_re, b_im, c_re, c_im = _f(inp["a_b_re"][0]), _f(inp["a_b_im"][0]), _f(inp["a_c_re"][0]), _f(inp["a_c_im"][0])
    iota = _f(np.broadcast_to(np.arange(TS + 1, dtype=np.float32), (128, TS + 1)))
    gmask = np.zeros((128, 8), np.float32)
    for gl in range(8):
        gmask[gl * 16:(gl + 1) * 16, gl] = 1.0
    sgn = np.ones((128, 1), np.float32)
    sgn[64:] = -1.0
    J = np.zeros((128, 128), np.float32)
    for q in range(64):
        J[64 + q, q] = -1.0
        J[q, 64 + q] = 1.0
    ntri = np.zeros((128, 128), np.float32)
    for j in range(128):
        ntri[j, :j + 1] = -1.0
    vecsC = np.zeros((128, 32), np.float32)
    for i, v in enumerate([inp["a_b_glu"][0], inp["a_norm_post"][0], inp["b_norm_pre"][0]]):
        vecsC[:, i * 8:(i + 1) * 8] = _cols(v)
    a_w_in, b_w_in, w_kv = _f(inp["a_w_in"][0]), _f(inp["b_w_in"][0]), _f(inp["w_kv"])
    common = {
        "g_pre": _cols(inp["a_norm_pre"][0]), "w_in_g": _f(a_w_in[:, D:]), "iota": iota, "gmask": gmask, "sgn": sgn, "Jm": J,
        "vecsC": vecsC, "w_glu": _f(inp["a_w_glu"][0]), "w_out0": _f(inp["a_w_out"][0]), "w_pg0": _f(inp["ple_w_gate"][0]),
        "w_pp0": _f(inp["ple_w_proj"][0]), "w_bin_g": _f(b_w_in[:, D:]), "g_kv": _cols(inp["kv_norm"]),
        "g_bpre": _cols(inp["b_norm_pre"][0]), "ntri": ntri, "g_bpost": _cols(inp["b_norm_post"][0]),
        "w_out1": _f(inp["b_w_out"][0]), "w_pg1": _f(inp["ple_w_gate"][1]), "w_pp1": _f(inp["ple_w_proj"][1]),
    }
    xT_full = [_f(np.asarray(x[b], np.float32).T) for b in range(2)]
    maps = []
    for c in range(NCORE):
        b, r = c // 4, c % 4
        gs = np.arange(16 * r, 16 * r + 16)
        tsl = slice(r * TOKC, (r + 1) * TOKC)
        csl = slice(256 * r, 256 * r + 256)
        m = dict(common)
        m["xT_full"] = xT_full[b]
        m["xT_own"] = _f(xT_full[b][:, tsl])
        m["p0T"] = _f(np.asarray(p[0, b, tsl, :], np.float32).T)
        m["p1T"] = _f(np.asarray(p[1, b, tsl, :], np.float32).T)
        sel = np.zeros((128, 4), np.float32)
        sel[:, r] = 1.0
        m["sel"] = sel
        m["w_in_u"] = _f(a_w_in[:, csl])
        m["dskip_cs"] = _cols(inp["a_d_skip"][0][csl])
        m["w_q"] = _f(b_w_in[:, csl])
        m["w_k"] = _f(w_kv[:, csl])
        m["w_v"] = _f(w_kv[:, D + 256 * r:D + 256 * r + 256])

        def lt_gp(a):
            t = a[gs].reshape(2, 8, 64)
            t = np.broadcast_to(t[:, :, None, :], (2, 8, 16, 64))
            return _f(t.transpose(1, 2, 0, 3).reshape(128, 128))

        def lt_b(a):
            t = a[gs].reshape(2, 8, 64, 16)
            return _f(t.transpose(1, 3, 0, 2).reshape(128, 128))

        def sp_gp(a):
            t = a[gs].T
            return _f(np.concatenate([t, t], axis=0))
        ldt = np.broadcast_to(log_dt[:, None], (64, 64))
        m["lamre_T"], m["lamim_T"], m["logdt_T"] = lt_gp(lam_re), lt_gp(lam_im), lt_gp(ldt)
        m["bre_T"], m["bim_T"] = lt_b(b_re), lt_b(b_im)
        m["lamre_S"], m["lamim_S"], m["logdt_S"] = sp_gp(lam_re), sp_gp(lam_im), sp_gp(ldt)
        cr = c_re[gs].transpose(2, 0, 1).reshape(64, 256)
        ci = c_im[gs].transpose(2, 0, 1).reshape(64, 256)
        m["c1_S"] = _f(np.concatenate([cr, ci], axis=0))
        m["c2_S"] = _f(np.concatenate([ci, cr], axis=0))
        maps.append(m)
    res = _run(build_fused(), maps)
    out = np.empty((2, SEQ, D), np.float32)
    for c in range(NCORE):
        b, r = c // 4, c % 4
        out[b, r * TOKC:(r + 1) * TOKC, :] = res[c]["outT"].T
    return out
```
